# Optimizing a Trainium2 kernel written in Bass

```python
import math
import jax, jax.numpy as jnp
from jax import lax
import numpy as np

D_MODEL = 2048
BATCH = 4
SEQ = 4096
DEPTH = 1

GRID_W = 64
CTX_LEN = 256

HY_WIDTH = 2048
HY_SHORT_K = 3
HY_EMB_DIM = 33
HY_FILTER_ORDER = 64
HY_MIN_DECAY = math.log(1e-2) / 1.5
HY_MAX_DECAY = math.log(1e-2) / 0.3

GDN_HEADS = 16
GDN_DK = 128
GDN_DV = 128
GDN_SHORT_K = 3
GDN_CHUNK = 64
D_QK = GDN_HEADS * GDN_DK
D_V = GDN_HEADS * GDN_DV

D_FF = -(-8 * D_MODEL // (3 * 256)) * 256

SPLITS = (3 * HY_WIDTH, 2 * D_QK + D_V, D_V, 4 * GDN_HEADS, 2 * D_MODEL)
D_IN = sum(SPLITS)
SPLIT_IDX = [int(s) for s in np.cumsum(SPLITS)[:-1]]

EPS = 1e-6

kernel_name = "hyena_gdn_hybrid_dit_block"


def rmsnorm(x, g):
    xf = x.astype(jnp.float32)
    xf = xf * lax.rsqrt(jnp.mean(xf * xf, axis=-1, keepdims=True) + EPS)
    return xf.astype(x.dtype) * g


def l2norm(x):
    xf = x.astype(jnp.float32)
    return xf * lax.rsqrt(jnp.sum(xf * xf, axis=-1, keepdims=True) + EPS)


def short_conv(x, w):
    k_w = w.shape[0]
    pad = k_w // 2
    n = x.shape[-2]
    xp = jnp.pad(x, [(0, 0)] * (x.ndim - 2) + [(pad, pad), (0, 0)])
    return sum(xp[..., i:i + n, :] * w[i] for i in range(k_w))


def grid_conv(x, w):
    b, n, ch = x.shape
    rows = n // GRID_W
    return short_conv(x.reshape(b, rows, GRID_W, ch), w).reshape(b, n, ch)


def hyena_filters(n, fw1, fb1, fw2, fb2, fw3, fb3, fout, freq):
    f32 = jnp.float32
    t = jnp.linspace(0.0, 1.0, n, dtype=f32)[:, None]
    bands = (HY_EMB_DIM - 1) // 2
    w = 2.0 * math.pi * jnp.arange(n, dtype=f32)[:, None] / n
    f = jnp.linspace(1e-4, bands - 1, bands, dtype=f32)[None, :]
    z = jnp.concatenate([t, jnp.cos(f * w), -jnp.sin(f * w)], axis=-1)
    h = jnp.sin(freq * (z @ fw1 + fb1))
    h = jnp.sin(freq * (h @ fw2 + fb2))
    h = jnp.sin(freq * (h @ fw3 + fb3))
    h = (h @ fout).reshape(n, 2, HY_WIDTH)
    deltas = jnp.abs(jnp.linspace(HY_MIN_DECAY, HY_MAX_DECAY, HY_WIDTH, dtype=f32))
    h = h * jnp.exp(-t * deltas)[:, None, :]
    h_fwd, h_bwd = h[:, 0], h[:, 1]
    return jnp.concatenate([h_fwd, jnp.zeros((1, HY_WIDTH), h.dtype), h_bwd[:0:-1]], axis=0)


def hyena_mix(p_hy, filt, hy_bias, conv_w, conv_fn):
    u = conv_fn(p_hy, conv_w)
    x0, x1, v = jnp.split(u, 3, axis=-1)
    z = x1 * v
    n = z.shape[1]
    zf = jnp.fft.rfft(z.astype(jnp.float32), n=2 * n, axis=1)
    kf = jnp.fft.rfft(filt.astype(jnp.float32), n=2 * n, axis=0)
    y = jnp.fft.irfft(zf * kf[None], n=2 * n, axis=1)[:, :n]
    return x0 * (y.astype(z.dtype) + z * hy_bias)


def gdn_chunked(q, k, v, g, beta, s0):
    f32 = jnp.float32
    b, n, h, dk = q.shape
    dv = v.shape[-1]
    c = GDN_CHUNK
    nc = n // c

    def chunks(t):
        return jnp.moveaxis(t.astype(f32).reshape(b, nc, c, h, *t.shape[3:]), 3, 1)

    q, k, v, g, beta = (chunks(t) for t in (q, k, v, g, beta))
    q = q * (dk ** -0.5)
    g = jnp.cumsum(g, axis=-1)
    tril = jnp.tril(jnp.ones((c, c), bool))
    strict = jnp.tril(jnp.ones((c, c), bool), -1)
    decay = jnp.exp(jnp.where(tril, g[..., :, None] - g[..., None, :], -jnp.inf))
    kb = k * beta[..., None]
    a_mat = jnp.where(strict, jnp.einsum('bhncd,bhnsd->bhncs', kb, k) * decay, 0.0)
    rhs = jnp.concatenate([v * beta[..., None], kb * jnp.exp(g)[..., None]], axis=-1)
    sol = lax.linalg.triangular_solve(jnp.eye(c, dtype=f32) + a_mat, rhs,
                                      left_side=True, lower=True, unit_diagonal=True)
    u, w = sol[..., :dv], sol[..., dv:]
    qk = jnp.where(tril, jnp.einsum('bhncd,bhnsd->bhncs', q, k) * decay, 0.0)
    q_dec = q * jnp.exp(g)[..., None]
    g_last = g[..., -1]
    k_dec = k * jnp.exp(g_last[..., None] - g)[..., None]

    def step(s, xs):
        q_i, w_i, u_i, k_i, qk_i, gl_i = xs
        v_new = u_i - jnp.einsum('bhcd,bhde->bhce', w_i, s)
        o = jnp.einsum('bhcd,bhde->bhce', q_i, s) + jnp.einsum('bhcs,bhse->bhce', qk_i, v_new)
        s = s * jnp.exp(gl_i)[..., None, None] + jnp.einsum('bhcd,bhce->bhde', k_i, v_new)
        return s, o

    xs = tuple(jnp.moveaxis(t, 2, 0) for t in (q_dec, w, u, k_dec, qk, g_last))
    s_final, o = lax.scan(step, s0, xs)
    o = jnp.moveaxis(jnp.moveaxis(o, 0, 2), 1, 3).reshape(b, n, h, dv)
    return o, s_final


def gdn_core(p_qkv, p_scal, conv_fn, s_f0, s_b0, lp):
    b, n, _ = p_qkv.shape
    qkv = jax.nn.silu(conv_fn(p_qkv, lp['gdn_conv']))
    q, k, v = jnp.split(qkv, [D_QK, 2 * D_QK], axis=-1)
    q = l2norm(q.reshape(b, n, GDN_HEADS, GDN_DK))
    k = l2norm(k.reshape(b, n, GDN_HEADS, GDN_DK))
    v = v.reshape(b, n, GDN_HEADS, GDN_DV)
    s = p_scal.astype(jnp.float32).reshape(b, n, 2, 2, GDN_HEADS)
    beta = jax.nn.sigmoid(s[:, :, 0])
    g = -jnp.exp(lp['gdn_a_log']) * jax.nn.softplus(s[:, :, 1] + lp['gdn_dt_bias'])
    o_f, s_f = gdn_chunked(q, k, v, g[:, :, 0], beta[:, :, 0], s_f0)
    rev = lambda t: jnp.flip(t, axis=1)
    o_b, s_b = gdn_chunked(rev(q), rev(k), rev(v), rev(g[:, :, 1]), rev(beta[:, :, 1]), s_b0)
    return (o_f + rev(o_b)).astype(p_qkv.dtype), s_f, s_b


def mixer_out(p_hy, p_z, p_gate, o_gdn, conv_fn, lp):
    n = p_hy.shape[1]
    filt = hyena_filters(n, lp['hy_fw1'], lp['hy_fb1'], lp['hy_fw2'], lp['hy_fb2'],
                         lp['hy_fw3'], lp['hy_fb3'], lp['hy_fout'], lp['hy_freq'])
    y_hy = hyena_mix(p_hy, filt, lp['hy_bias'], lp['hy_conv'], conv_fn) @ lp['w_hy_out']
    b = o_gdn.shape[0]
    z = p_z.reshape(b, n, GDN_HEADS, GDN_DV)
    y_gdn = (rmsnorm(o_gdn, lp['gdn_norm']) * jax.nn.silu(z)).reshape(b, n, D_V) @ lp['w_gdn_out']
    g_hy, g_gdn = jnp.split(jax.nn.sigmoid(p_gate), 2, axis=-1)
    return (g_hy * y_hy + g_gdn * y_gdn) @ lp['w_o']


def swiglu(h, w_up, w_down):
    gate, up = jnp.split(h @ w_up, 2, axis=-1)
    return (jax.nn.silu(gate) * up) @ w_down


def setup_inputs(seed: int = 0) -> dict:
    key = jax.random.key(seed)
    ks = iter(jax.random.split(key, 40))
    f32 = jnp.float32

    def nrm(shape, scale):
        return jax.random.normal(next(ks), shape, f32) * scale

    def gain(shape):
        return 1.0 + nrm(shape, 0.01)

    dt = jnp.exp(jax.random.uniform(next(ks), (DEPTH, 2, GDN_HEADS), f32, math.log(1e-3), math.log(1e-1)))
    return {
        'x': nrm((BATCH, SEQ, D_MODEL), 1.0),
        'c': nrm((BATCH, D_MODEL), 1.0),
        'ctx': nrm((BATCH, CTX_LEN, D_MODEL), 1.0),
        'c_ctx': nrm((D_MODEL,), 1.0),
        'w_ada': nrm((DEPTH, D_MODEL, 6 * D_MODEL), D_MODEL ** -0.5),
        'b_ada': nrm((DEPTH, 6 * D_MODEL), 0.01),
        'norm_mix': gain((DEPTH, D_MODEL)),
        'norm_ffn': gain((DEPTH, D_MODEL)),
        'w_in': nrm((DEPTH, D_MODEL, D_IN), D_MODEL ** -0.5),
        'hy_conv': nrm((DEPTH, HY_SHORT_K, 3 * HY_WIDTH), HY_SHORT_K ** -0.5),
        'hy_bias': nrm((DEPTH, HY_WIDTH), 1.0),
        'hy_fw1': nrm((DEPTH, HY_EMB_DIM, HY_FILTER_ORDER), HY_EMB_DIM ** -0.5),
        'hy_fb1': nrm((DEPTH, HY_FILTER_ORDER), 0.1),
        'hy_fw2': nrm((DEPTH, HY_FILTER_ORDER, HY_FILTER_ORDER), HY_FILTER_ORDER ** -0.5),
        'hy_fb2': nrm((DEPTH, HY_FILTER_ORDER), 0.1),
        'hy_fw3': nrm((DEPTH, HY_FILTER_ORDER, HY_FILTER_ORDER), HY_FILTER_ORDER ** -0.5),
        'hy_fb3': nrm((DEPTH, HY_FILTER_ORDER), 0.1),
        'hy_fout': nrm((DEPTH, HY_FILTER_ORDER, 2 * HY_WIDTH), 0.02),
        'hy_freq': gain((DEPTH, HY_FILTER_ORDER)),
        'gdn_conv': nrm((DEPTH, GDN_SHORT_K, 2 * D_QK + D_V), GDN_SHORT_K ** -0.5),
        'gdn_a_log': jnp.log(jax.random.uniform(next(ks), (DEPTH, 2, GDN_HEADS), f32, 1.0, 16.0)),
        'gdn_dt_bias': dt + jnp.log(-jnp.expm1(-dt)),
        'gdn_norm': gain((DEPTH, GDN_DV)),
        'w_hy_out': nrm((DEPTH, HY_WIDTH, D_MODEL), HY_WIDTH ** -0.5),
        'w_gdn_out': nrm((DEPTH, D_V, D_MODEL), D_V ** -0.5),
        'w_o': nrm((DEPTH, D_MODEL, D_MODEL), D_MODEL ** -0.5),
        'w_up': nrm((DEPTH, D_MODEL, 2 * D_FF), D_MODEL ** -0.5),
        'w_down': nrm((DEPTH, D_FF, D_MODEL), D_FF ** -0.5),
        'norm_final': gain((D_MODEL,)),
    }


def reference(x, c, ctx, c_ctx, w_ada, b_ada, norm_mix, norm_ffn, w_in, hy_conv, hy_bias,
              hy_fw1, hy_fb1, hy_fw2, hy_fb2, hy_fw3, hy_fb3, hy_fout, hy_freq,
              gdn_conv, gdn_a_log, gdn_dt_bias, gdn_norm, w_hy_out, w_gdn_out, w_o,
              w_up, w_down, norm_final):
    b = x.shape[0]
    zero_state = jnp.zeros((b, GDN_HEADS, GDN_DK, GDN_DV), jnp.float32)
    for l in range(DEPTH):
        lp = {
            'hy_conv': hy_conv[l], 'hy_bias': hy_bias[l],
            'hy_fw1': hy_fw1[l], 'hy_fb1': hy_fb1[l], 'hy_fw2': hy_fw2[l], 'hy_fb2': hy_fb2[l],
            'hy_fw3': hy_fw3[l], 'hy_fb3': hy_fb3[l], 'hy_fout': hy_fout[l], 'hy_freq': hy_freq[l],
            'gdn_conv': gdn_conv[l], 'gdn_a_log': gdn_a_log[l], 'gdn_dt_bias': gdn_dt_bias[l],
            'gdn_norm': gdn_norm[l], 'w_hy_out': w_hy_out[l], 'w_gdn_out': w_gdn_out[l], 'w_o': w_o[l],
        }
        mod_lat = (jax.nn.silu(c) @ w_ada[l] + b_ada[l])[:, None, :]
        mod_ctx = (jax.nn.silu(c_ctx) @ w_ada[l] + b_ada[l])[None, None, :]
        sh_a, sc_a, ga_a, sh_f, sc_f, ga_f = jnp.split(mod_lat, 6, axis=-1)
        csh_a, csc_a, cga_a, csh_f, csc_f, cga_f = jnp.split(mod_ctx, 6, axis=-1)

        h_ctx = rmsnorm(ctx, norm_mix[l]) * (1.0 + csc_a) + csh_a
        h_lat = rmsnorm(x, norm_mix[l]) * (1.0 + sc_a) + sh_a
        pc = jnp.split(h_ctx @ w_in[l], SPLIT_IDX, axis=-1)
        pl = jnp.split(h_lat @ w_in[l], SPLIT_IDX, axis=-1)
        o_c, s_f, s_b = gdn_core(pc[1], pc[3], short_conv, zero_state, zero_state, lp)
        o_l, _, _ = gdn_core(pl[1], pl[3], grid_conv, s_f, s_b, lp)
        x = x + ga_a * mixer_out(pl[0], pl[2], pl[4], o_l, grid_conv, lp)

        x = x + ga_f * swiglu(rmsnorm(x, norm_ffn[l]) * (1.0 + sc_f) + sh_f, w_up[l], w_down[l])

        if l < DEPTH - 1:
            ctx = ctx + cga_a * mixer_out(pc[0], pc[2], pc[4], o_c, short_conv, lp)
            ctx = ctx + cga_f * swiglu(rmsnorm(ctx, norm_ffn[l]) * (1.0 + csc_f) + csh_f, w_up[l], w_down[l])
    return rmsnorm(x, norm_final)
```

```python
import contextlib
import math
import numpy as np
import ml_dtypes
import concourse.bass as bass
import concourse.mybir as mybir
from concourse.bass_utils import run_bass_kernel_spmd

F32 = mybir.dt.float32
BF16 = mybir.dt.bfloat16
AF = mybir.ActivationFunctionType
ALU = mybir.AluOpType
AX = mybir.AxisListType

D = 2048
L = 4096
NOWN = 2048
CTX = 256
TT = L + CTX
HEADS = 16
D_IN = 18496
D_FF = 5632
EPS = 1e-6
N_ = 6144

COMPUTE = ("pe", "act", "dve", "pool")
NSLOT = {"sp": 12, "poolq": 12}
QENG = {"sp": "sp", "poolq": "pool"}


class Op:
    __slots__ = ("fn", "waits", "sem", "inc")


class Sched:
    def __init__(self, nc):
        self.nc = nc
        self.streams = {e: [] for e in ("pe", "act", "dve", "pool", "sp")}
        self.cnt = {e: 0 for e in COMPUTE}
        self.slot_uses = {q: [0] * NSLOT[q] for q in NSLOT}
        self.dma_i = {q: 0 for q in NSLOT}
        self.last_w = {}
        self.reads = {}
        self.waited = {e: {} for e in self.streams}
        self.final_tokens = []

    def _need(self, stream, tok, waits):
        if tok is None:
            return
        semname, val, pstream, is_pe = tok
        if pstream == stream and is_pe:
            return
        if self.waited[stream].get(semname, 0) >= val:
            return
        waits[semname] = max(waits.get(semname, 0), val)

    def _deps(self, stream, reads, writes):
        waits = {}
        for k in reads:
            self._need(stream, self.last_w.get(k), waits)
        for k in writes:
            self._need(stream, self.last_w.get(k), waits)
            for t in self.reads.get(k, {}).values():
                self._need(stream, t, waits)
        for s, v in waits.items():
            self.waited[stream][s] = v
        return waits

    def _record(self, tok, reads, writes):
        for k in reads:
            d = self.reads.setdefault(k, {})
            o = d.get(tok[0])
            if o is None or o[1] < tok[1]:
                d[tok[0]] = tok
        for k in writes:
            self.last_w[k] = tok
            self.reads[k] = {}

    def op(self, eng, fn, reads=(), writes=()):
        waits = self._deps(eng, reads, writes)
        self.cnt[eng] += 1
        tok = (eng, self.cnt[eng], eng, eng == "pe")
        o = Op(); o.fn = fn; o.waits = sorted(waits.items()); o.sem = eng; o.inc = 1
        self.streams[eng].append(o)
        self._record(tok, reads, writes)
        return tok

    def dma(self, q, out, in_, reads=(), writes=()):
        stream = QENG[q]
        waits = self._deps(stream, reads, writes)
        i = self.dma_i[q]; self.dma_i[q] += 1
        slot = i % NSLOT[q]
        semname = "%s%d" % (q, slot)
        prev = self.slot_uses[q][slot] * 16
        if prev and self.waited[stream].get(semname, 0) < prev:
            waits[semname] = prev
            self.waited[stream][semname] = prev
        self.slot_uses[q][slot] += 1
        tok = (semname, self.slot_uses[q][slot] * 16, None, False)
        o = Op(); o.waits = sorted(waits.items()); o.sem = semname; o.inc = 16
        o.fn = (lambda e, out=out, in_=in_: e.dma_start(out=out, in_=in_))
        self.streams[stream].append(o)
        self._record(tok, reads, writes)
        return tok

    def barrier(self):
        allv = {e: self.cnt[e] for e in COMPUTE}
        for q in NSLOT:
            for s in range(NSLOT[q]):
                allv["%s%d" % (q, s)] = self.slot_uses[q][s] * 16
        for stream in self.streams:
            waits = {}
            for s, v in allv.items():
                if v and self.waited[stream].get(s, 0) < v and not (s == "pe" and stream == "pe"):
                    waits[s] = v
                    self.waited[stream][s] = v
            if waits:
                o = Op(); o.fn = None; o.waits = sorted(waits.items()); o.sem = None; o.inc = 0
                self.streams[stream].append(o)

    def emit(self):
        nc = self.nc
        names = list(COMPUTE) + ["%s%d" % (q, s) for q in NSLOT for s in range(NSLOT[q])]
        with contextlib.ExitStack() as st:
            sems = {n: st.enter_context(nc.semaphore("s_" + n)) for n in names}
            block = st.enter_context(nc.Block())

            def run(stream, final=False):
                def body(e):
                    for o in self.streams[stream]:
                        for s, v in o.waits:
                            e.wait_ge(sems[s], v)
                        if o.fn is not None:
                            o.fn(e).then_inc(sems[o.sem], o.inc)
                    if final:
                        for (s, v, _, _) in self.final_tokens:
                            e.wait_ge(sems[s], v)
                return body

            block.sync(run("sp", True))
            block.tensor(run("pe"))
            block.scalar(run("act"))
            block.vector(run("dve"))
            block.gpsimd(run("pool"))


class Arena:
    def __init__(self, big, total):
        self.big = big; self.total = total; self.off = 0; self.n = 0

    def reset(self, to=0):
        self.off = to

    def alloc(self, shape, dtype, name=None):
        n = int(np.prod(shape[1:]))
        size = n * (2 if dtype == F32 else 1)
        o = self.off
        self.off += (size + 15) // 16 * 16
        assert self.off <= self.total, ("SBUF arena overflow", self.off, self.total)
        v = self.big[:, o:o + size]
        if dtype == F32:
            v = v.bitcast(F32)
        if len(shape) == 3:
            v = v.rearrange("p (a b) -> p a b", a=shape[1])
        elif len(shape) == 4:
            v = v.rearrange("p (a b c) -> p a b c", a=shape[1], b=shape[2])
        self.n += 1
        key = "%s#%d" % (name or "t", self.n)
        return v[0:shape[0]], key


class Ring:
    def __init__(self, arena, n, shape, dtype, name):
        self.items = [arena.alloc(shape, dtype, name) for _ in range(n)]
        self.i = 0

    def next(self):
        it = self.items[self.i % len(self.items)]
        self.i += 1
        return it


def _make_plan():
    plan = [("scal", 0)]
    for j in range(16):
        plan += [("x1", j), ("hv", j)]
    for typ in ("q", "k", "gv", "x0", "zg"):
        plan += [(typ, j) for j in range(16)]
    plan += [("gate", j) for j in range(32)]
    return plan


SEG = dict(x0=0, x1=2048, hv=4096, q=6144, k=8192, gv=10240, zg=12288, scal=14336, gate=14400)


PLAN = _make_plan()
PLAN_INDEX = {k: i for i, k in enumerate(PLAN)}


def _chunk_major(w, cols):
    K = w.shape[0]
    out = np.empty((len(cols), 128, (K // 128) * 128), np.float32)
    for i, c0 in enumerate(cols):
        blk = w[:, c0:c0 + 128]
        if blk.shape[1] < 128:
            blk = np.concatenate([blk, np.zeros((K, 128 - blk.shape[1]), np.float32)], axis=1)
        out[i] = blk.reshape(K // 128, 128, 128).transpose(1, 0, 2).reshape(128, -1)
    return out


def build(upto=99, dbg=(), phases="DEF"):
    nc = bass.Bass("TRN2", target_bir_lowering=False)
    S = Sched(nc)
    ext = {}

    def inp(name, shape, dt=F32):
        ext[name] = nc.dram_tensor(name, list(shape), dt, kind="ExternalInput")
        return ext[name].ap()

    def scratch(name, shape, dt):
        kind = "ExternalOutput" if name in dbg else "Internal"
        t = nc.dram_tensor(name, list(shape), dt, kind=kind)
        return t.ap()

    x_d = inp("x", [L, D]); ctx_d = inp("ctx", [CTX, D]); cT_d = inp("cT", [128, 16, 2])
    w_ada_d = inp("w_ada", [D, 6 * D]); b_adaT_d = inp("b_adaT", [128, 96])
    nmixT_d = inp("nmixT", [128, 16]); nffnT_d = inp("nffnT", [128, 16])
    w_in_d = inp("w_inc", [145, 128, 16 * 128])
    hyconvT_d = inp("hyconvT", [128, 48, 3]); gdnconvT_d = inp("gdnconvT", [128, 48, 3])
    scalp_d = inp("scalp", [64, 2])
    ident_d = inp("ident", [128, 128])
    out_d = nc.dram_tensor("out", [NOWN, D], F32, kind="ExternalOutput").ap()
    w_hy_out_d = inp("w_hy_outc", [16, 128, 16 * 128]); w_gdn_out_d = inp("w_gdn_outc", [16, 128, 16 * 128]); w_o_d = inp("w_oc", [16, 128, 16 * 128])
    w_up_d = inp("w_upc", [88, 128, 16 * 128]); w_down_d = inp("w_downc", [16, 128, 44 * 128])
    hybT_d = inp("hybT", [128, 16]); gnormT_d = inp("gnormT", [128, 1]); nfinT_d = inp("nfinT", [128, 16])
    Yd = scratch("Yd", [NOWN + 64, D], BF16); Od0 = scratch("Od0", [NOWN, D], F32); Od1 = scratch("Od1", [NOWN, D], F32)
    cmask_d = inp("cmask", [64, 4, 64]); cm2_d = inp("cm2", [64, 5, 64])
    zf_d = inp("zf", [33, N_]); fw1_d = inp("fw1", [33, 64]); fw2_d = inp("fw2", [64, 64]); fw3_d = inp("fw3", [64, 64])
    fsm_d = inp("fsm", [64, 4]); fout_d = inp("fout", [64, 2 * D]); deltas_d = inp("deltas", [128, D]); tau_d = inp("tau", [128, 48])
    W1_d = inp("W1", [128, 48 * 2 * 128]); V_d = inp("Vc", [128, 48 * 2 * 64])
    W2_d = inp("W2", [96, 96]); W2a_d = inp("W2a", [96, 96]); W2b_d = inp("W2b", [96, 96]); L1_d = inp("L1", [96, 96]); L2_d = inp("L2", [96, 96])
    Hc = scratch("Hc", [N_, D], BF16); Ad = scratch("Ad", [128, 96, D], BF16); Bd = scratch("Bd", [128, 96, D], BF16)
    Kd = scratch("Kd", [2, 96, 128, D], BF16)

    X0T = scratch("X0T", [D, NOWN], BF16); ZT = scratch("ZT", [D, L], BF16); Zt = scratch("Zt", [L, D], BF16)
    QT = scratch("QT", [D, TT], BF16); KT = scratch("KT", [D, TT], BF16); VT = scratch("VT", [D, TT], BF16)
    ZGT = scratch("ZGT", [D, NOWN], BF16); SCT = scratch("SCT", [64, TT], F32); GT = scratch("GT", [2 * D, NOWN], BF16)
    MODd = scratch("MODd", [128, 96, 2], F32)

    with contextlib.ExitStack() as st:
        TOTAL = 106000
        big = st.enter_context(nc.sbuf_tensor("big", [128, TOTAL], BF16))
        PSALL = st.enter_context(nc.psum_tensor("psall", [128, 4096], F32))
        psb = [PSALL[:, i * 512:(i + 1) * 512] for i in range(8)]
        A = Arena(big, TOTAL)
        psi = [0]

        def ps_next():
            i = psi[0] % 8; psi[0] += 1
            return psb[i], "psb%d" % i

        def ps_multi(nb):
            i = ((psi[0] + nb - 1) // nb * nb) % 8
            psi[0] = i + nb
            return PSALL[:, i * 512:(i + nb) * 512], ["psb%d" % (i + j) for j in range(nb)]

        ident_f, k_identf = A.alloc([128, 128], F32, "identf")
        ident_b, k_ident = A.alloc([128, 128], BF16, "ident")
        ones_b, k_ones = A.alloc([128, 128], BF16, "ones")
        ones_f, k_onesf = A.alloc([128, 128], F32, "onesf")
        mod, k_mod = A.alloc([128, 96, 2], F32, "mod")
        nmixT, k_nmix = A.alloc([128, 16], F32, "nmix")
        nffnT, k_nffn = A.alloc([128, 16], F32, "nffn")
        scale_a, k_sca = A.alloc([128, 16], F32, "scale_a")
        scale_c, k_scc = A.alloc([128, 16], F32, "scale_c")
        scale_f, k_scf = A.alloc([128, 16], F32, "scale_f")
        hyconvT, k_hyc = A.alloc([128, 48, 3], F32, "hyc")
        gdnconvT, k_gdc = A.alloc([128, 48, 3], F32, "gdc")
        scalp, k_scalp = A.alloc([64, 2], F32, "scalp")
        nega, k_nega = A.alloc([64, 1], F32, "nega")
        negpi, k_negpi = A.alloc([128, 1], F32, "negpi")
        S.op("dve", lambda e: e.memset(negpi, -math.pi), writes=[k_negpi])
        hybT, k_hyb = A.alloc([128, 16], F32, "hyb")
        gnormT, k_gnorm = A.alloc([128, 1], F32, "gnorm")
        nfinT, k_nfin = A.alloc([128, 16], F32, "nfin")
        PERSIST = A.off
        S.dma("sp", hybT, hybT_d, writes=[k_hyb]); S.dma("sp", gnormT, gnormT_d, writes=[k_gnorm]); S.dma("sp", nfinT, nfinT_d, writes=[k_nfin])

        S.dma("sp", ident_f, ident_d, writes=[k_identf])
        S.op("dve", lambda e: e.tensor_copy(out=ident_b, in_=ident_f), reads=[k_identf], writes=[k_ident])
        S.op("dve", lambda e: e.memset(ones_b, 1.0), writes=[k_ones])
        S.op("dve", lambda e: e.memset(ones_f, 1.0), writes=[k_onesf])
        S.dma("sp", nmixT, nmixT_d, writes=[k_nmix]); S.dma("sp", nffnT, nffnT_d, writes=[k_nffn])
        S.dma("sp", hyconvT, hyconvT_d, writes=[k_hyc]); S.dma("sp", gdnconvT, gdnconvT_d, writes=[k_gdc])
        S.dma("sp", scalp, scalp_d, writes=[k_scalp])
        S.op("act", lambda e: e.activation(out=nega[32:64], in_=scalp[32:64, 0:1], func=AF.Exp), reads=[k_scalp], writes=[k_nega])
        S.op("dve", lambda e: e.tensor_scalar(out=nega[32:64], in0=nega[32:64], scalar1=-1.0, scalar2=None, op0=ALU.mult), reads=[k_nega], writes=[k_nega])

        cT, k_cT = A.alloc([128, 16, 2], F32, "cT")
        sT, k_sT = A.alloc([128, 16, 2], BF16, "sT")
        bT, k_bT = A.alloc([128, 96], F32, "bT")
        wr = Ring(A, 2, [128, 16, 1536], BF16, "wada")
        S.dma("sp", cT, cT_d, writes=[k_cT]); S.dma("sp", bT, b_adaT_d, writes=[k_bT])
        S.op("act", lambda e: e.activation(out=sT, in_=cT, func=AF.Silu), reads=[k_cT], writes=[k_sT])
        w_ada_v = w_ada_d.rearrange("(kc p) n -> p kc n", p=128)
        for blk in range(8):
            wt, kw = wr.next()
            S.dma("poolq", wt, w_ada_v[:, :, blk * 1536:(blk + 1) * 1536], writes=[kw])
            ps, kp = ps_next()

            def mmA(e, wt=wt, ps=ps):
                last = None
                for j in range(12):
                    for kc in range(16):
                        last = e.matmul(ps[:, 2 * j:2 * j + 2], wt[:, kc, j * 128:(j + 1) * 128], sT[:, kc, :],
                                        start=(kc == 0), stop=(kc == 15))
                return last
            S.op("pe", mmA, reads=[kw, k_sT], writes=[kp])
            S.op("dve", lambda e, ps=ps, blk=blk: e.tensor_copy(
                out=mod[:, blk * 12:(blk + 1) * 12, :], in_=ps[:, 0:24].rearrange("p (a b) -> p a b", b=2)),
                reads=[kp], writes=[k_mod])
        for j in range(2):
            S.op("dve", lambda e, j=j: e.tensor_tensor(out=mod[:, :, j], in0=mod[:, :, j], in1=bT, op=ALU.add),
                 reads=[k_mod, k_bT], writes=[k_mod])
        for (dst, kd, src, nrm, kn) in ((scale_a, k_sca, mod[:, 16:32, 0], nmixT, k_nmix),
                                        (scale_c, k_scc, mod[:, 16:32, 1], nmixT, k_nmix),
                                        (scale_f, k_scf, mod[:, 64:80, 0], nffnT, k_nffn)):
            S.op("dve", lambda e, dst=dst, src=src, nrm=nrm: e.scalar_tensor_tensor(
                out=dst, in0=src, scalar=1.0, in1=nrm, op0=ALU.add, op1=ALU.mult), reads=[k_mod, kn], writes=[kd])
        if "MODd" in dbg:
            S.final_tokens.append(S.dma("sp", MODd, mod, reads=[k_mod], writes=["MODd"]))
        if upto <= 1:
            S.emit(); return nc

        S.barrier(); A.reset(PERSIST)
        hT, k_hT = A.alloc([128, 16, TT], BF16, "hT")
        PB = A.off
        xr = Ring(A, 2, [128, D], F32, "xt")
        sq, k_sq = A.alloc([128, D], F32, "sq")
        xnr = Ring(A, 2, [128, D], BF16, "xn")
        str_ = Ring(A, 2, [128, 4], F32, "stat")
        for i in range(TT // 128):
            lat = i < L // 128
            src = x_d[i * 128:(i + 1) * 128, :] if lat else ctx_d[(i - 32) * 128:(i - 31) * 128, :]
            sc_t, k_sc = (scale_a, k_sca) if lat else (scale_c, k_scc)
            sh_t = mod[:, 0:16, 0] if lat else mod[:, 0:16, 1]
            xt, kx = xr.next(); xn, kxn = xnr.next(); stt, kst = str_.next()
            S.dma("sp", xt, src, writes=[kx])
            S.op("act", lambda e, xt=xt: e.activation(out=sq, in_=xt, func=AF.Square), reads=[kx], writes=[k_sq])
            S.op("dve", lambda e, stt=stt: e.reduce_sum(out=stt[:, 0:1], in_=sq, axis=AX.X), reads=[k_sq], writes=[kst])
            S.op("dve", lambda e, stt=stt: e.tensor_scalar(out=stt[:, 1:2], in0=stt[:, 0:1], scalar1=1.0 / D, scalar2=EPS,
                                                           op0=ALU.mult, op1=ALU.add), reads=[kst], writes=[kst])
            S.op("dve", lambda e, stt=stt: e.reciprocal(out=stt[:, 2:3], in_=stt[:, 1:2]), reads=[kst], writes=[kst])
            S.op("act", lambda e, stt=stt: e.activation(out=stt[:, 3:4], in_=stt[:, 2:3], func=AF.Sqrt), reads=[kst], writes=[kst])
            S.op("dve", lambda e, stt=stt: e.tensor_tensor(out=stt[:, 0:1], in0=stt[:, 3:4], in1=stt[:, 3:4], op=ALU.mult), reads=[kst], writes=[kst])
            S.op("dve", lambda e, stt=stt: e.tensor_tensor(out=stt[:, 0:1], in0=stt[:, 0:1], in1=stt[:, 1:2], op=ALU.mult), reads=[kst], writes=[kst])
            S.op("dve", lambda e, stt=stt: e.tensor_scalar(out=stt[:, 0:1], in0=stt[:, 0:1], scalar1=-0.5, scalar2=1.5,
                                                           op0=ALU.mult, op1=ALU.add), reads=[kst], writes=[kst])
            S.op("dve", lambda e, stt=stt: e.tensor_tensor(out=stt[:, 3:4], in0=stt[:, 3:4], in1=stt[:, 0:1], op=ALU.mult), reads=[kst], writes=[kst])
            S.op("act", lambda e, xt=xt, xn=xn, stt=stt: e.activation(out=xn, in_=xt, func=AF.Copy, scale=stt[:, 3:4]),
                 reads=[kx, kst], writes=[kxn])
            for g in range(4):
                ps, kp = ps_next()
                psv = ps.bitcast(BF16)

                def tr(e, xn=xn, psv=psv, g=g):
                    last = None
                    for j in range(4):
                        fc = g * 4 + j
                        last = e.transpose(psv[:, j * 128:(j + 1) * 128], xn[:, fc * 128:(fc + 1) * 128], ident_b)
                    return last
                S.op("pe", tr, reads=[kxn, k_ident], writes=[kp])
                for j in range(4):
                    fc = g * 4 + j
                    dst = hT[:, fc, i * 128:(i + 1) * 128]
                    if j % 2 == 0:
                        S.op("act", lambda e, dst=dst, psv=psv, j=j, fc=fc, sc_t=sc_t, sh_t=sh_t: e.activation(
                            out=dst, in_=psv[:, j * 128:(j + 1) * 128], func=AF.Identity, scale=sc_t[:, fc:fc + 1], bias=sh_t[:, fc:fc + 1]),
                            reads=[kp, k_sc, k_mod], writes=[k_hT])
                    else:
                        S.op("dve", lambda e, dst=dst, psv=psv, j=j, fc=fc, sc_t=sc_t, sh_t=sh_t: e.tensor_scalar(
                            out=dst, in0=psv[:, j * 128:(j + 1) * 128], scalar1=sc_t[:, fc:fc + 1], scalar2=sh_t[:, fc:fc + 1],
                            op0=ALU.mult, op1=ALU.add), reads=[kp, k_sc, k_mod], writes=[k_hT])
        if "HTd" in dbg:
            HTd = nc.dram_tensor("HTd", [128, 16, TT], BF16, kind="ExternalOutput").ap()
            S.final_tokens.append(S.dma("sp", HTd, hT, reads=[k_hT], writes=["HTd"]))
        if upto <= 2:
            S.emit(); return nc

        S.barrier(); A.reset(PB)
        wr = Ring(A, 3, [128, 16, 128], BF16, "win")
        ur = Ring(A, 3, [128, 512], F32, "u")
        t2r = Ring(A, 2, [128, 512], F32, "tmp2")
        obr = Ring(A, 3, [128, 512], BF16, "ob")
        sqr = Ring(A, 3, [128, 512], BF16, "sq")
        ofr = Ring(A, 2, [64, 512], F32, "of")
        u1, k_u1 = A.alloc([128, L], BF16, "u1")
        ztr = Ring(A, 2, [128, 4, 128], BF16, "zt")
        BLK_ALL = [(b * 512, 512) for b in range(8)]
        BLK_OWN = BLK_ALL[:NOWN // 512]
        BLK_CTX = [(L, CTX)]

        def conv_epilogue(ps, kp, n, rw, taps, ktaps):
            u, ku = ur.next()
            S.op("act", lambda e: e.activation(out=u[:, 0:n], in_=ps[:, 0:n], func=AF.Copy, scale=taps[:, 1:2]),
                 reads=[kp, ktaps], writes=[ku])
            pv = ps[:, 0:n].rearrange("p (r w) -> p r w", w=rw)
            uv = u[:, 0:n].rearrange("p (r w) -> p r w", w=rw)
            S.op("dve", lambda e: e.scalar_tensor_tensor(out=uv[:, :, 1:rw], in0=pv[:, :, 0:rw - 1], scalar=taps[:, 0:1],
                                                         in1=uv[:, :, 1:rw], op0=ALU.mult, op1=ALU.add), reads=[kp, ktaps, ku], writes=[ku])
            S.op("dve", lambda e: e.scalar_tensor_tensor(out=uv[:, :, 0:rw - 1], in0=pv[:, :, 1:rw], scalar=taps[:, 2:3],
                                                         in1=uv[:, :, 0:rw - 1], op0=ALU.mult, op1=ALU.add), reads=[kp, ktaps, ku], writes=[ku])
            return u, ku

        def do_chunk(typ, j):
            M = 64 if typ == "scal" else 128
            wt, kw = wr.next()
            S.dma("poolq", wt.rearrange("p a b -> p (a b)"), w_in_d[PLAN_INDEX[(typ, j)]], writes=[kw])
            own = typ in ("x0", "zg", "gate")
            blocks = BLK_OWN if own else BLK_ALL
            if typ in ("q", "k", "gv", "scal"):
                blocks = blocks + BLK_CTX
            def blk(t0, n):
                ps, kp = ps_next()

                def mm(e, ps=ps, t0=t0, n=n):
                    last = None
                    for kc in range(16):
                        last = e.matmul(ps[0:M, 0:n], wt[:, kc, 0:M], hT[:, kc, t0:t0 + n], start=(kc == 0), stop=(kc == 15))
                    return last
                S.op("pe", mm, reads=[kw, k_hT], writes=[kp])
                isctx = t0 >= L
                rw = CTX if isctx else 64
                if typ in ("x0", "x1", "hv"):
                    ci = {"x0": 0, "x1": 16, "hv": 32}[typ] + j
                    u, ku = conv_epilogue(ps, kp, n, rw, hyconvT[:, ci, :], k_hyc)
                    if typ == "x0":
                        ob, kob = obr.next()
                        S.op("act", lambda e, ob=ob, u=u: e.activation(out=ob, in_=u, func=AF.Copy), reads=[ku], writes=[kob])
                        S.dma("sp", X0T[j * 128:(j + 1) * 128, t0:t0 + n], ob, reads=[kob], writes=["X0T"])
                    elif typ == "x1":
                        S.op("act", lambda e, u=u, t0=t0: e.activation(out=u1[:, t0:t0 + 512], in_=u, func=AF.Copy), reads=[ku], writes=[k_u1])
                    else:
                        ob, kob = obr.next()
                        S.op("dve", lambda e, ob=ob, u=u, t0=t0: e.tensor_tensor(out=ob, in0=u, in1=u1[:, t0:t0 + 512], op=ALU.mult),
                             reads=[ku, k_u1], writes=[kob])
                        S.dma("sp", ZT[j * 128:(j + 1) * 128, t0:t0 + n], ob, reads=[kob], writes=["ZT"])
                        yield
                        ps2, kp2 = ps_next()
                        ps2v = ps2.bitcast(BF16)

                        def trz(e, ob=ob, ps2v=ps2v):
                            last = None
                            for tb in range(4):
                                last = e.transpose(ps2v[:, tb * 128:(tb + 1) * 128], ob[:, tb * 128:(tb + 1) * 128], ident_b)
                            return last
                        S.op("pe", trz, reads=[kob, k_ident], writes=[kp2])
                        zt, kzt = ztr.next()
                        S.op("act", lambda e, zt=zt, ps2v=ps2v: e.activation(
                            out=zt, in_=ps2v[:, 0:512].rearrange("p (a b) -> p a b", b=128), func=AF.Copy), reads=[kp2], writes=[kzt])
                        S.dma("sp", Zt[t0:t0 + 512, j * 128:(j + 1) * 128].rearrange("(a p) c -> p a c", p=128), zt,
                              reads=[kzt], writes=["Zt"])
                elif typ in ("q", "k", "gv"):
                    ci = {"q": 0, "k": 16, "gv": 32}[typ] + j
                    u, ku = conv_epilogue(ps, kp, n, rw, gdnconvT[:, ci, :], k_gdc)
                    S.op("act", lambda e, u=u, n=n: e.activation(out=u[:, 0:n], in_=u[:, 0:n], func=AF.Silu), reads=[ku], writes=[ku])
                    ob, kob = obr.next()
                    dstT = {"q": QT, "k": KT, "gv": VT}[typ]
                    if typ == "gv":
                        S.op("dve", lambda e, ob=ob, u=u, n=n: e.tensor_copy(out=ob[:, 0:n], in_=u[:, 0:n]), reads=[ku], writes=[kob])
                    else:
                        sqb, ksqb = sqr.next()
                        S.op("dve", lambda e, sqb=sqb, u=u, n=n: e.tensor_tensor(out=sqb[:, 0:n], in0=u[:, 0:n], in1=u[:, 0:n], op=ALU.mult),
                             reads=[ku], writes=[ksqb])
                        yield
                        t2, kt2 = t2r.next()
                        ps2, kp2 = ps_next()
                        S.op("pe", lambda e, ps2=ps2, sqb=sqb, n=n: e.matmul(ps2[:, 0:n], ones_b, sqb[:, 0:n], start=True, stop=True),
                             reads=[ksqb, k_ones], writes=[kp2])
                        S.op("dve", lambda e, t2=t2, ps2=ps2, n=n: e.tensor_scalar(out=t2[:, 0:n], in0=ps2[:, 0:n], scalar1=EPS, scalar2=None, op0=ALU.add),
                             reads=[kp2], writes=[kt2])
                        S.op("dve", lambda e, t2=t2, n=n: e.reciprocal(out=t2[:, 0:n], in_=t2[:, 0:n]), reads=[kt2], writes=[kt2])
                        S.op("act", lambda e, t2=t2, n=n: e.activation(out=t2[:, 0:n], in_=t2[:, 0:n], func=AF.Sqrt), reads=[kt2], writes=[kt2])
                        ob, kob = obr.next()
                        qs = (128.0 ** -0.5) if typ == "q" else 1.0
                        S.op("dve", lambda e, ob=ob, u=u, t2=t2, n=n, qs=qs: e.scalar_tensor_tensor(
                            out=ob[:, 0:n], in0=u[:, 0:n], scalar=qs, in1=t2[:, 0:n], op0=ALU.mult, op1=ALU.mult), reads=[ku, kt2], writes=[kob])
                    S.dma("sp", dstT[j * 128:(j + 1) * 128, t0:t0 + n], ob[:, 0:n], reads=[kob], writes=[typ + "T"])
                elif typ in ("zg", "gate"):
                    ob, kob = obr.next()
                    fn = AF.Silu if typ == "zg" else AF.Sigmoid
                    S.op("act", lambda e, ob=ob, ps=ps, fn=fn: e.activation(out=ob, in_=ps[:, 0:512], func=fn), reads=[kp], writes=[kob])
                    dstT = ZGT if typ == "zg" else GT
                    S.dma("sp", dstT[j * 128:(j + 1) * 128, t0:t0 + n], ob, reads=[kob], writes=[typ + "T"])
                else:
                    of, kof = ofr.next()
                    S.op("act", lambda e, of=of, ps=ps, n=n: e.activation(out=of[0:32, 0:n], in_=ps[0:32, 0:n], func=AF.Sigmoid), reads=[kp], writes=[kof])
                    S.op("act", lambda e, of=of, ps=ps, n=n: e.activation(out=of[32:64, 0:n], in_=ps[32:64, 0:n], func=AF.Exp, bias=scalp[32:64, 1:2]),
                         reads=[kp, k_scalp], writes=[kof])
                    S.op("act", lambda e, of=of, n=n: e.activation(out=of[32:64, 0:n], in_=of[32:64, 0:n], func=AF.Ln, bias=1.0), reads=[kof], writes=[kof])
                    S.op("dve", lambda e, of=of, n=n: e.tensor_scalar(out=of[32:64, 0:n], in0=of[32:64, 0:n], scalar1=nega[32:64, 0:1], scalar2=None, op0=ALU.mult),
                         reads=[kof, k_nega], writes=[kof])
                    S.dma("sp", SCT[:, t0:t0 + n], of[:, 0:n], reads=[kof], writes=["SCT"])
                return
                yield

            pend = None
            for (t0, n) in blocks:
                g = blk(t0, n)
                alive = True
                try:
                    next(g)
                except StopIteration:
                    alive = False
                if pend is not None:
                    for _ in pend:
                        pass
                pend = g if alive else None
            if pend is not None:
                for _ in pend:
                    pass

        plan = list(PLAN)
        import os
        if os.environ.get("K_PLAN_TEST") == "gdn":
            plan = [("scal", 0)] + [(t, j) for t in ("q", "k", "gv") for j in range(16)]
        elif os.environ.get("K_PLAN_TEST") == "hy":
            plan = [pp for j in (0, 7) for pp in (("x1", j), ("hv", j))] + [("x0", 0), ("x0", 7)]
        elif os.environ.get("K_PLAN_TEST"):
            plan = [("scal", 0), ("x1", 1), ("hv", 1), ("q", 2), ("k", 3), ("gv", 4), ("x0", 5), ("zg", 6), ("gate", 7), ("gate", 17)]
        for (typ, j) in plan:
            do_chunk(typ, j)
        for nm in ("X0T", "ZT", "Zt", "QT", "KT", "VT", "ZGT", "SCT", "GT"):
            if nm in dbg:
                pass
        if upto <= 3:
            S.barrier()
            S.emit(); return nc


        def phase_d():
            S.barrier(); A.reset(PERSIST)
            H = HEADS
            HH = 8
            cmask, k_cm = A.alloc([64, 4, 64], F32, "cmask")
            S.dma("sp", cmask, cmask_d, writes=[k_cm])
            cm2, k_cm2 = A.alloc([64, 5, 64], F32, "cm2")
            S.dma("sp", cm2, cm2_d, writes=[k_cm2])
            cmask_b, k_cmb = A.alloc([64, 4, 64], BF16, "cmask_b")
            cm2_b, k_cm2b = A.alloc([64, 5, 64], BF16, "cm2_b")
            S.op("dve", lambda e: e.tensor_copy(out=cm2_b, in_=cm2), reads=[k_cm2], writes=[k_cm2b])
            S.op("dve", lambda e: e.tensor_copy(out=cmask_b, in_=cmask), reads=[k_cm], writes=[k_cmb])
            qr = Ring(A, 2, [128, H, 256], BF16, "qg"); kr = Ring(A, 2, [128, H, 256], BF16, "kg"); vr = Ring(A, 2, [128, H, 256], BF16, "vg")
            scr = Ring(A, 2, [64, 256], F32, "scg")
            identb64 = ident_b[0:64, 0:64].unsqueeze(1).broadcast_to([64, HH, 64])
            identf64 = ident_f[0:64, 0:64].unsqueeze(1).broadcast_to([64, HH, 64])
            QTv = QT.rearrange("(h p) t -> p h t", p=128); KTv = KT.rearrange("(h p) t -> p h t", p=128); VTv = VT.rearrange("(h p) t -> p h t", p=128)

            def al(shape, dt, nm):
                return A.alloc(shape, dt, nm)

            class TS:
                pass
            halves = []
            for hh in range(2):
                T_ = TS()
                T_.S = al([128, HH, 128], F32, "S"); T_.Sb = al([128, HH, 128], BF16, "Sb")
                T_.sct = al([64, 64], F32, "sct"); T_.sm = al([64, 8, HH], F32, "sm")
                T_.ktok = al([64, HH, 128], BF16, "ktok"); T_.vtok = al([64, HH, 128], BF16, "vtok")
                T_.Xg = al([64, HH, 64], F32, "Xg"); T_.Xb = al([64, HH, 64], F32, "Xb")
                T_.Y = al([64, HH, 64], F32, "Y"); T_.EA = al([64, HH, 64], F32, "EA"); T_.EB = al([64, HH, 64], F32, "EB")
                T_.tmpM = al([64, HH, 64], BF16, "tmpM"); T_.tmpB = al([64, HH, 64], BF16, "tmpB"); T_.tmpQ = al([64, HH, 64], BF16, "tmpQ")
                T_.bbc = al([64, HH, 64], BF16, "bbc")
                T_.P0 = al([64, HH, 64], BF16, "P0"); T_.P1 = al([64, HH, 64], BF16, "P1")
                T_.PT0 = al([64, HH, 64], BF16, "PT0"); T_.PT1 = al([64, HH, 64], BF16, "PT1")
                T_.TT = al([64, HH, 64], BF16, "TT"); T_.T = al([64, HH, 64], BF16, "T")
                T_.Am = al([64, HH, 64], BF16, "Am"); T_.ATm = al([64, HH, 64], BF16, "ATm")
                T_.eGbc = al([128, HH, 64], F32, "eGbc")
                T_.vb = al([64, HH, 128], BF16, "vb"); T_.kbg = al([64, HH, 128], BF16, "kbg")
                T_.vnew = al([64, HH, 128], BF16, "vnew")
                T_.slot = []
                for sl in range(2):
                    d = dict(wT=al([128, HH, 64], BF16, "wT"), uu=al([64, HH, 128], F32, "uu"), qdT=al([128, HH, 64], BF16, "qdT"),
                             QKT=al([64, HH, 64], BF16, "QKT"), kdec=al([64, HH, 128], BF16, "kdec"), egl=al([128, HH], F32, "egl"))
                    T_.slot.append(d)
                T_.orr = Ring(A, 2, [64, HH, 128], F32, "o")
                halves.append(T_)

            def bc_s(ap):
                return ap.unsqueeze(2).broadcast_to([64, HH, 64])

            def bc_e(ap):
                return ap.unsqueeze(2).broadcast_to([64, HH, 128])

            def mk(i):
                return cmask[:, i, :].unsqueeze(1).broadcast_to([64, HH, 64])

            def mkb(i):
                return cmask_b[:, i, :].unsqueeze(1).broadcast_to([64, HH, 64])

            def m2(i):
                return cm2_b[:, i, :].unsqueeze(1).broadcast_to([64, HH, 64])

            def hm(psx):
                return psx[0:64, :].rearrange("p (h s) -> p h s", h=HH)

            def hm128(psx):
                return psx[0:64, :].rearrange("p (h d) -> p h d", h=HH)

            def run_dir(dr):
                mA_s, mB_i, mB_s, cumi = ((1, 2, 3, 2) if dr == 0 else (3, 0, 1, 0))
                last_t = 63 if dr == 0 else 0
                for T_ in halves:
                    S.op("dve", lambda e, T_=T_: e.memset(T_.S[0], 0.0), reads=[T_.S[1]], writes=[T_.S[1]])
                    S.op("dve", lambda e, T_=T_: e.memset(T_.Sb[0], 0.0), reads=[T_.Sb[1]], writes=[T_.Sb[1]])
                if dr == 0:
                    groups = [L // 256] + list(range(NOWN // 256))
                else:
                    groups = [L // 256] + list(range(L // 256 - 1, -1, -1))

                def chunk(hh, gi, ci, need_o, g0, qg, kqg, kg, kkg, vg, kvg, scg, kscg, slot):
                    T_ = halves[hh]
                    h0 = hh * HH
                    c0 = ci * 64
                    (Sst, k_S), (Sb, k_Sb), (sct, k_sct), (sm, k_sm) = T_.S, T_.Sb, T_.sct, T_.sm
                    (ktok, k_ktok), (vtok, k_vtok), (Xg, k_Xg), (Xb, k_Xb) = T_.ktok, T_.vtok, T_.Xg, T_.Xb
                    (Y, k_Y), (EA, k_EA), (EB, k_EB), (tmpM, k_tmpM) = T_.Y, T_.EA, T_.EB, T_.tmpM
                    (tmpB, k_tmpB), (tmpQ, k_tmpQ), (bbc, k_bbc) = T_.tmpB, T_.tmpQ, T_.bbc
                    (P0, kP0), (P1, kP1), (PT0, kPT0), (PT1, kPT1) = T_.P0, T_.P1, T_.PT0, T_.PT1
                    (TTm, k_TT), (Tm, k_T), (Am, k_Am), (ATm, k_ATm) = T_.TT, T_.T, T_.Am, T_.ATm
                    (eGbc, k_eGbc), (vb, k_vb), (kbg, k_kbg), (vnew, k_vnew) = T_.eGbc, T_.vb, T_.kbg, T_.vnew
                    sd = T_.slot[slot]
                    (wT, k_wT), (uu, k_uu), (qdT, k_qdT), (QKT, k_QKT), (kdec, k_kdec), (egl, k_egl) = (
                        sd["wT"], sd["uu"], sd["qdT"], sd["QKT"], sd["kdec"], sd["egl"])

                    def mmset(lhs, klhs, rhs, krhs):
                        pX, kpX = ps_next()

                        def f(e):
                            last = None
                            for h in range(HH):
                                last = e.matmul(pX[0:64, h * 64:(h + 1) * 64], lhs[:, h, :], rhs[:, h, :], start=True, stop=True)
                            return last
                        S.op("pe", f, reads=[klhs, krhs], writes=[kpX])
                        return pX, [kpX]
                    ps, kp = ps_next()
                    S.op("pe", lambda e: e.transpose(ps[0:64, 0:64], scg[:, c0:c0 + 64], ident_f[0:64, 0:64]), reads=[kscg, k_identf], writes=[kp])
                    S.op("act", lambda e: e.activation(out=sct, in_=ps[0:64, 0:64], func=AF.Copy), reads=[kp], writes=[k_sct])
                    beta = sct[:, dr * 16 + h0:dr * 16 + h0 + HH]; gg = sct[:, 32 + dr * 16 + h0:32 + dr * 16 + h0 + HH]
                    yield "p"
                    ps2, kp2 = ps_next()
                    S.op("pe", lambda e: e.matmul(ps2[0:64, 0:HH], cmask[:, cumi, :], gg, start=True, stop=True), reads=[k_sct, k_cm], writes=[kp2])
                    S.op("act", lambda e: e.activation(out=sm[:, 0, :], in_=ps2[0:64, 0:HH], func=AF.Copy), reads=[kp2], writes=[k_sm])
                    yield "p"
                    S.op("dve", lambda e: e.tensor_tensor(out=Xg, in0=identf64, in1=bc_s(sm[:, 0, :]), op=ALU.mult), reads=[k_sm, k_identf], writes=[k_Xg])
                    S.op("pool", lambda e: e.tensor_tensor(out=Xb, in0=identf64, in1=bc_s(beta), op=ALU.mult), reads=[k_sct, k_identf], writes=[k_Xb])
                    pG, kpG1 = ps_next(); kpG = [kpG1]
                    pGv = pG.rearrange("p (h s) -> p h s", h=HH)
                    S.op("pe", lambda e: e.matmul(pG[:, 0:512], ones_f[0:64, :], Xg, start=True, stop=True), reads=[k_Xg, k_onesf], writes=kpG)
                    pB, kpB1 = ps_next(); kpB = [kpB1]
                    pBv = pB.rearrange("p (h s) -> p h s", h=HH)
                    S.op("pe", lambda e: e.matmul(pB[0:64, 0:512], ones_f[0:64, 0:64], Xb, start=True, stop=True), reads=[k_Xb, k_onesf], writes=kpB)
                    S.op("dve", lambda e: e.tensor_tensor(out=Y, in0=pGv[0:64], in1=bc_s(sm[:, 0, :]), op=ALU.subtract), reads=kpG + [k_sm], writes=[k_Y])
                    S.op("act", lambda e: e.activation(out=bbc, in_=pBv[0:64], func=AF.Copy), reads=kpB, writes=[k_bbc])
                    S.op("dve", lambda e: e.tensor_tensor(out=sm[:, 3, :], in0=pGv[0:64, :, last_t], in1=sm[:, 0, :], op=ALU.subtract), reads=kpG + [k_sm], writes=[k_sm])
                    S.op("dve", lambda e: e.tensor_scalar(out=egl, in0=pGv[:, :, last_t], scalar1=-80.0, scalar2=None, op0=ALU.max), reads=kpG, writes=[k_egl])
                    if need_o:
                        S.op("dve", lambda e: e.tensor_scalar(out=eGbc, in0=pGv, scalar1=-80.0, scalar2=None, op0=ALU.max), reads=kpG, writes=[k_eGbc])
                    yield "p"
                    S.op("dve", lambda e: e.scalar_tensor_tensor(out=EA, in0=Y, scalar=80.0, in1=mk(mA_s), op0=ALU.min, op1=ALU.mult), reads=[k_Y, k_cm], writes=[k_EA])
                    S.op("act", lambda e: e.activation(out=EA, in_=EA, func=AF.Exp, scale=-1.0), reads=[k_EA], writes=[k_EA])
                    S.op("dve", lambda e: e.scalar_tensor_tensor(out=EB, in0=Y, scalar=-80.0, in1=mk(mB_i), op0=ALU.max, op1=ALU.mult), reads=[k_Y, k_cm], writes=[k_EB])
                    S.op("act", lambda e: e.activation(out=EB, in_=EB, func=AF.Exp), reads=[k_EB], writes=[k_EB])
                    yield "p"
                    S.op("dve", lambda e: e.tensor_scalar(out=sm[:, 1, :], in0=sm[:, 0, :], scalar1=-80.0, scalar2=None, op0=ALU.max), reads=[k_sm], writes=[k_sm])
                    S.op("act", lambda e: e.activation(out=sm[:, 1, :], in_=sm[:, 1, :], func=AF.Exp), reads=[k_sm], writes=[k_sm])
                    S.op("dve", lambda e: e.tensor_tensor(out=sm[:, 2, :], in0=sm[:, 1, :], in1=beta, op=ALU.mult), reads=[k_sm, k_sct], writes=[k_sm])
                    S.op("dve", lambda e: e.tensor_scalar(out=sm[:, 3, :], in0=sm[:, 3, :], scalar1=-80.0, scalar2=None, op0=ALU.max), reads=[k_sm], writes=[k_sm])
                    S.op("act", lambda e: e.activation(out=sm[:, 3, :], in_=sm[:, 3, :], func=AF.Exp), reads=[k_sm], writes=[k_sm])
                    S.op("act", lambda e: e.activation(out=egl, in_=egl, func=AF.Exp), reads=[k_egl], writes=[k_egl])
                    if need_o:
                        S.op("act", lambda e: e.activation(out=eGbc, in_=eGbc, func=AF.Exp), reads=[k_eGbc], writes=[k_eGbc])
                        S.op("pool", lambda e: e.tensor_tensor(out=qdT, in0=qg[:, h0:h0 + HH, c0:c0 + 64], in1=eGbc, op=ALU.mult), reads=[kqg, k_eGbc], writes=[k_qdT])
                    yield "p"
                    for (src, ksrc, dst, kdst) in ((kg, kkg, ktok, k_ktok), (vg, kvg, vtok, k_vtok)):
                        pT, kpT = ps_next()
                        pTv = pT.bitcast(BF16)

                        def trk(e, src=src, pTv=pTv):
                            last = None
                            for h in range(HH):
                                last = e.transpose(pTv[0:64, h * 128:(h + 1) * 128], src[:, h0 + h, c0:c0 + 64], ident_b)
                            return last
                        S.op("pe", trk, reads=[ksrc, k_ident], writes=[kpT])
                        S.op("act", lambda e, dst=dst, pTv=pTv: e.activation(out=dst, in_=pTv[0:64, :].rearrange("p (h d) -> p h d", h=HH), func=AF.Copy),
                             reads=[kpT], writes=[kdst])
                    pK, kpK1 = ps_next(); kpK = [kpK1]
                    pKv = pK.rearrange("p (h s) -> p h s", h=HH)

                    def mmK(e):
                        last = None
                        for h in range(HH):
                            last = e.matmul(pK[0:64, h * 64:(h + 1) * 64], kg[:, h0 + h, c0:c0 + 64], kg[:, h0 + h, c0:c0 + 64], start=True, stop=True)
                        return last
                    S.op("pe", mmK, reads=[kkg], writes=kpK)
                    S.op("dve", lambda e: e.tensor_tensor(out=tmpM, in0=pKv[0:64], in1=EA, op=ALU.mult), reads=kpK + [k_EA], writes=[k_tmpM])
                    S.op("dve", lambda e: e.tensor_tensor(out=tmpB, in0=pKv[0:64], in1=EB, op=ALU.mult), reads=kpK + [k_EB], writes=[k_tmpB])
                    if need_o:
                        pQ, kpQ1 = ps_next(); kpQ = [kpQ1]
                        pQv = pQ.rearrange("p (h s) -> p h s", h=HH)

                        def mmQ(e):
                            last = None
                            for h in range(HH):
                                last = e.matmul(pQ[0:64, h * 64:(h + 1) * 64], kg[:, h0 + h, c0:c0 + 64], qg[:, h0 + h, c0:c0 + 64], start=True, stop=True)
                            return last
                        S.op("pe", mmQ, reads=[kkg, kqg], writes=kpQ)
                        S.op("dve", lambda e: e.tensor_tensor(out=tmpQ, in0=pQv[0:64], in1=EB, op=ALU.mult), reads=kpQ + [k_EB], writes=[k_tmpQ])
                    yield "p"
                    S.op("dve", lambda e: e.tensor_tensor(out=tmpM, in0=tmpM, in1=bc_s(beta), op=ALU.mult), reads=[k_tmpM, k_sct], writes=[k_tmpM])
                    S.op("pool", lambda e: e.tensor_tensor(out=Am, in0=tmpM, in1=mkb(mA_s), op=ALU.mult), reads=[k_tmpM, k_cmb], writes=[k_Am])
                    S.op("dve", lambda e: e.tensor_tensor(out=tmpB, in0=tmpB, in1=bbc, op=ALU.mult), reads=[k_tmpB, k_bbc], writes=[k_tmpB])
                    S.op("pool", lambda e: e.tensor_tensor(out=ATm, in0=tmpB, in1=mkb(mB_s), op=ALU.mult), reads=[k_tmpB, k_cmb], writes=[k_ATm])
                    if need_o:
                        S.op("pool", lambda e: e.tensor_tensor(out=QKT, in0=tmpQ, in1=mkb(mB_i), op=ALU.mult), reads=[k_tmpQ, k_cmb], writes=[k_QKT])
                    yield "p"
                    S.op("pool", lambda e: e.tensor_tensor(out=P0, in0=Am, in1=m2(4), op=ALU.mult), reads=[k_Am, k_cm2b], writes=[kP0])
                    S.op("pool", lambda e: e.tensor_tensor(out=PT0, in0=ATm, in1=m2(4), op=ALU.mult), reads=[k_ATm, k_cm2b], writes=[kPT0])
                    S.op("pool", lambda e: e.tensor_tensor(out=Tm, in0=P0, in1=identb64, op=ALU.add), reads=[kP0, k_ident], writes=[k_T])
                    S.op("pool", lambda e: e.tensor_tensor(out=TTm, in0=PT0, in1=identb64, op=ALU.add), reads=[kPT0, k_ident], writes=[k_TT])
                    p1, kp1 = mmset(PT0, kPT0, P0, kP0)
                    S.op("act", lambda e: e.activation(out=P1, in_=hm(p1), func=AF.Copy), reads=kp1, writes=[kP1])
                    p2_, kp2_ = mmset(P0, kP0, PT0, kPT0)
                    S.op("act", lambda e: e.activation(out=PT1, in_=hm(p2_), func=AF.Copy), reads=kp2_, writes=[kPT1])
                    yield "p"
                    p3, kp3 = mmset(TTm, k_TT, P1, kP1)
                    p4, kp4 = mmset(P1, kP1, TTm, k_TT)
                    S.op("dve", lambda e: e.tensor_tensor(out=Tm, in0=Tm, in1=hm(p3), op=ALU.add), reads=kp3 + [k_T], writes=[k_T])
                    S.op("dve", lambda e: e.tensor_tensor(out=TTm, in0=TTm, in1=hm(p4), op=ALU.add), reads=kp4 + [k_TT], writes=[k_TT])
                    p5, kp5 = mmset(PT1, kPT1, P1, kP1)
                    S.op("act", lambda e: e.activation(out=P0, in_=hm(p5), func=AF.Copy), reads=kp5, writes=[kP0])
                    yield "p"
                    p6, kp6 = mmset(TTm, k_TT, P0, kP0)
                    p7, kp7 = mmset(P0, kP0, TTm, k_TT)
                    S.op("dve", lambda e: e.tensor_tensor(out=Tm, in0=Tm, in1=hm(p6), op=ALU.add), reads=kp6 + [k_T], writes=[k_T])
                    S.op("dve", lambda e: e.tensor_tensor(out=TTm, in0=TTm, in1=hm(p7), op=ALU.add), reads=kp7 + [k_TT], writes=[k_TT])
                    yield "p"
                    for lv in range(3):
                        S.op("pool", lambda e, lv=lv: e.tensor_tensor(out=PT0, in0=ATm, in1=m2(1 + lv), op=ALU.mult), reads=[k_ATm, k_cm2b], writes=[kPT0])
                        pX, kpX = mmset(PT0, kPT0, Tm, k_T)
                        S.op("act", lambda e, pX=pX: e.activation(out=P1, in_=hm(pX), func=AF.Copy), reads=kpX, writes=[kP1])
                        yield "p"
                        p8, kp8 = mmset(P1, kP1, TTm, k_TT)
                        if lv < 2:
                            p9, kp9 = mmset(TTm, k_TT, P1, kP1)
                            S.op("dve", lambda e, p9=p9: e.tensor_tensor(out=Tm, in0=Tm, in1=hm(p9), op=ALU.subtract), reads=kp9 + [k_T], writes=[k_T])
                        S.op("dve", lambda e, p8=p8: e.tensor_tensor(out=TTm, in0=TTm, in1=hm(p8), op=ALU.subtract), reads=kp8 + [k_TT], writes=[k_TT])
                        yield "p"
                    S.op("pool", lambda e: e.tensor_tensor(out=vb, in0=vtok, in1=bc_e(beta), op=ALU.mult), reads=[k_vtok, k_sct], writes=[k_vb])
                    S.op("pool", lambda e: e.tensor_tensor(out=kbg, in0=ktok, in1=bc_e(sm[:, 2, :]), op=ALU.mult), reads=[k_ktok, k_sm], writes=[k_kbg])
                    S.op("pool", lambda e: e.tensor_tensor(out=kdec, in0=ktok, in1=bc_e(sm[:, 3, :]), op=ALU.mult), reads=[k_ktok, k_sm], writes=[k_kdec])
                    pU, kpU = ps_multi(2)

                    def mmU(e):
                        last = None
                        for h in range(HH):
                            last = e.matmul(pU[0:64, h * 128:(h + 1) * 128], TTm[:, h, :], vb[:, h, :], start=True, stop=True)
                        return last
                    S.op("pe", mmU, reads=[k_TT, k_vb], writes=kpU)
                    S.op("act", lambda e: e.activation(out=uu, in_=hm128(pU), func=AF.Copy), reads=kpU, writes=[k_uu])
                    pW, kpW1 = ps_next(); kpW = [kpW1]

                    def mmW(e):
                        last = None
                        for h in range(HH):
                            last = e.matmul(pW[:, h * 64:(h + 1) * 64], kbg[:, h, :], TTm[:, h, :], start=True, stop=True)
                        return last
                    S.op("pe", mmW, reads=[k_kbg, k_TT], writes=kpW)
                    S.op("act", lambda e: e.activation(out=wT, in_=pW.rearrange("p (h s) -> p h s", h=HH), func=AF.Copy), reads=kpW, writes=[k_wT])
                    yield "scan"
                    pV, kpV = ps_multi(2)

                    def mmV(e):
                        last = None
                        for h in range(HH):
                            last = e.matmul(pV[0:64, h * 128:(h + 1) * 128], wT[:, h, :], Sb[:, h, :], start=True, stop=True)
                        return last
                    S.op("pe", mmV, reads=[k_wT, k_Sb], writes=kpV)
                    S.op("dve", lambda e: e.tensor_tensor(out=vnew, in0=uu, in1=hm128(pV), op=ALU.subtract), reads=kpV + [k_uu], writes=[k_vnew])
                    yield "s"
                    if need_o:
                        pO, kpO = ps_multi(2)

                        def mmO(e):
                            last = None
                            for h in range(HH):
                                e.matmul(pO[0:64, h * 128:(h + 1) * 128], qdT[:, h, :], Sb[:, h, :], start=True, stop=False)
                                last = e.matmul(pO[0:64, h * 128:(h + 1) * 128], QKT[:, h, :], vnew[:, h, :], start=False, stop=True)
                            return last
                        S.op("pe", mmO, reads=[k_qdT, k_Sb, k_QKT, k_vnew], writes=kpO)
                        ot, kot = T_.orr.next()
                        S.op("act", lambda e: e.activation(out=ot, in_=hm128(pO), func=AF.Copy), reads=kpO, writes=[kot])
                        Odst = Od0 if dr == 0 else Od1
                        S.dma("sp", Odst[g0 + c0:g0 + c0 + 64, h0 * 128:(h0 + HH) * 128], ot.rearrange("p h d -> p (h d)"), reads=[kot], writes=["Od%d_%d" % (dr, hh)])
                    pS, kpS = ps_multi(2)

                    def mmS(e):
                        last = None
                        for h in range(HH):
                            last = e.matmul(pS[:, h * 128:(h + 1) * 128], kdec[:, h, :], vnew[:, h, :], start=True, stop=True)
                        return last
                    S.op("pe", mmS, reads=[k_kdec, k_vnew], writes=kpS)
                    S.op("dve", lambda e: e.tensor_tensor(out=Sst, in0=Sst, in1=egl.unsqueeze(2).broadcast_to([128, HH, 128]), op=ALU.mult), reads=[k_S, k_egl], writes=[k_S])
                    S.op("dve", lambda e: e.tensor_tensor(out=Sst, in0=Sst, in1=pS.rearrange("p (h d) -> p h d", h=HH), op=ALU.add), reads=kpS + [k_S], writes=[k_S])
                    S.op("act", lambda e: e.activation(out=Sb, in_=Sst, func=AF.Copy), reads=[k_S], writes=[k_Sb])
                    yield "done"

                def advance(gens, until):
                    live = list(gens)
                    while live:
                        for g in list(live):
                            try:
                                v = next(g)
                            except StopIteration:
                                live.remove(g); continue
                            if v == until:
                                live.remove(g)

                pending = None
                nchunk = 0
                for gi in groups:
                    g0 = gi * 256
                    need_o = gi < NOWN // 256
                    qg, kqg = qr.next(); kg, kkg = kr.next(); vg, kvg = vr.next(); scg, kscg = scr.next()
                    if need_o:
                        S.dma("sp", qg, QTv[:, :, g0:g0 + 256], reads=["qT"], writes=[kqg])
                    S.dma("sp", kg, KTv[:, :, g0:g0 + 256], reads=["kT"], writes=[kkg])
                    S.dma("sp", vg, VTv[:, :, g0:g0 + 256], reads=["gvT"], writes=[kvg])
                    S.dma("sp", scg, SCT[:, g0:g0 + 256], reads=["SCT"], writes=[kscg])
                    for ci in (range(4) if dr == 0 else range(3, -1, -1)):
                        gens = [chunk(hh, gi, ci, need_o, g0, qg, kqg, kg, kkg, vg, kvg, scg, kscg, nchunk % 2) for hh in range(2)]
                        advance(gens, "scan")
                        if pending is not None:
                            advance(pending, "done")
                        pending = gens
                        nchunk += 1
                advance(pending, "done")

            for dr_ in range(2):
                run_dir(dr_)

        if "D" in phases:
            phase_d()


        def phase_e():
            CG = 512
            NG = D // CG

            def load_consts(names):
                out = {}
                for nm, src, shp in names:
                    t, k = A.alloc(shp, BF16, nm)
                    S.dma("poolq", t, src, writes=[k])
                    out[nm] = (t, k)
                return out

            S.barrier(); A.reset(PERSIST)
            zf, k_zf = A.alloc([33, N_], F32, "zf")
            fw1, k_fw1 = A.alloc([33, 64], F32, "fw1"); fw2, k_fw2 = A.alloc([64, 64], F32, "fw2"); fw3, k_fw3 = A.alloc([64, 64], F32, "fw3")
            fsm, k_fsm = A.alloc([64, 8], F32, "fsm")
            fout, k_fout = A.alloc([64, 2 * D], F32, "fout")
            dl, k_dl = A.alloc([128, D], F32, "deltas")
            tau, k_tau = A.alloc([128, 48], F32, "tau")
            hr = [A.alloc([64, 512], F32, "hmlp%d" % i) for i in range(3)]
            decr = Ring(A, 2, [128, 512], F32, "dec")
            fm, k_fm = A.alloc([64, 512], F32, "fm")
            hcr = Ring(A, 3, [128, 512], BF16, "hc")
            for (t, k, src) in ((zf, k_zf, zf_d), (fw1, k_fw1, fw1_d), (fw2, k_fw2, fw2_d), (fw3, k_fw3, fw3_d),
                                (fsm[:, 0:4], k_fsm, fsm_d), (fout, k_fout, fout_d), (dl, k_dl, deltas_d), (tau, k_tau, tau_d)):
                S.dma("sp", t, src, writes=[k])
            for i in range(3):
                S.op("dve", lambda e, i=i: e.tensor_tensor(out=fsm[:, 4 + i:5 + i], in0=fsm[:, i:i + 1], in1=fsm[:, 3:4], op=ALU.mult), reads=[k_fsm], writes=[k_fsm])
            for jb in range(N_ // 512):
                j0 = jb * 512
                prev, kprev = zf[:, j0:j0 + 512], k_zf
                for li, (w, kw) in enumerate(((fw1, k_fw1), (fw2, k_fw2), (fw3, k_fw3))):
                    ps, kp = ps_next()
                    S.op("pe", lambda e, ps=ps, w=w, prev=prev: e.matmul(ps[0:64, :], w, prev, start=True, stop=True), reads=[kw, kprev], writes=[kp])
                    ht, kht = hr[li]
                    S.op("act", lambda e, ps=ps, ht=ht, li=li: e.activation(out=ht, in_=ps[0:64, :], func=AF.Identity, scale=fsm[:, 3:4], bias=fsm[:, 4 + li:5 + li]),
                         reads=[kp, k_fsm], writes=[kht])
                    for rep in range(3):
                        S.op("dve", lambda e, ht=ht: e.tensor_scalar(out=fm, in0=ht, scalar1=math.pi, scalar2=-2.0 * math.pi, op0=ALU.is_gt, op1=ALU.mult), reads=[kht], writes=[k_fm])
                        S.op("dve", lambda e, ht=ht: e.tensor_tensor(out=ht, in0=ht, in1=fm, op=ALU.add), reads=[kht, k_fm], writes=[kht])
                        S.op("dve", lambda e, ht=ht: e.tensor_scalar(out=fm, in0=ht, scalar1=-math.pi, scalar2=2.0 * math.pi, op0=ALU.is_lt, op1=ALU.mult), reads=[kht], writes=[k_fm])
                        S.op("dve", lambda e, ht=ht: e.tensor_tensor(out=ht, in0=ht, in1=fm, op=ALU.add), reads=[kht, k_fm], writes=[kht])
                    S.op("act", lambda e, ht=ht: e.activation(out=ht, in_=ht, func=AF.Sin), reads=[kht], writes=[kht])
                    prev, kprev = ht, kht
                for rt in range(4):
                    jt = jb * 4 + rt
                    fcol = 0 if jt < NOWN // 128 else D
                    for cc in range(4):
                        ps, kp = ps_next()
                        S.op("pe", lambda e, ps=ps, prev=prev, rt=rt, cc=cc, fcol=fcol: e.matmul(
                            ps[:, :], prev[:, rt * 128:(rt + 1) * 128], fout[:, fcol + cc * 512:fcol + (cc + 1) * 512], start=True, stop=True),
                            reads=[kprev, k_fout], writes=[kp])
                        dec, kdec = decr.next()
                        S.op("act", lambda e, dec=dec, cc=cc, jt=jt: e.activation(out=dec, in_=dl[:, cc * 512:(cc + 1) * 512], func=AF.Exp, scale=tau[:, jt:jt + 1]),
                             reads=[k_dl, k_tau], writes=[kdec])
                        hc, khc = hcr.next()
                        S.op("dve", lambda e, hc=hc, ps=ps, dec=dec: e.tensor_tensor(out=hc, in0=ps[:, :], in1=dec, op=ALU.mult), reads=[kp, kdec], writes=[khc])
                        S.dma("sp", Hc[jt * 128:(jt + 1) * 128, cc * 512:(cc + 1) * 512], hc, reads=[khc], writes=["Hc"])

            def stage1(src, nm_src, is_filter):
                S.barrier(); A.reset(PERSIST)
                W1, k_W1 = A.alloc([128, 48, 2, 128], BF16, "W1")
                S.dma("poolq", W1.rearrange("p a b c -> p (a b c)"), W1_d, writes=[k_W1])
                ztr = Ring(A, 2, [128, 48, CG], BF16, "zt")
                aor = Ring(A, 4, [128, 2, CG], BF16, "ao")
                K1 = 128 if is_filter else 86
                if not is_filter:
                    for (zt, kz) in ztr.items:
                        S.op("pool", lambda e, zt=zt: e.memset(zt[64:128], 0.0), writes=[kz])
                for cg in range(NG):
                    c0 = cg * CG
                    zt, kz = ztr.next()
                    if is_filter:
                        S.dma("sp", zt, src[:, c0:c0 + CG].rearrange("(a b) c -> a b c", b=48), reads=[nm_src], writes=[kz])
                    else:
                        S.dma("sp", zt[0:85], src[0:4080, c0:c0 + CG].rearrange("(a b) c -> a b c", b=48), reads=[nm_src], writes=[kz])
                        S.dma("sp", zt[85:86, 0:16, :], src[4080:4096, c0:c0 + CG].rearrange("(a b) c -> a b c", a=1), reads=[nm_src], writes=[kz])
                    for t2 in range(48):
                        pp, kpp = ps_multi(2)

                        def mm(e, pp=pp, zt=zt, t2=t2):
                            e.matmul(pp[:, 0:512], W1[0:K1, t2, 0, :], zt[0:K1, t2, :], start=True, stop=True)
                            return e.matmul(pp[:, 512:1024], W1[0:K1, t2, 1, :], zt[0:K1, t2, :], start=True, stop=True)
                        S.op("pe", mm, reads=[k_W1, kz], writes=kpp)
                        ao, kao = aor.next()
                        eng = "act" if t2 % 2 == 0 else "dve"
                        if eng == "act":
                            S.op("act", lambda e, ao=ao, pp=pp: e.activation(out=ao, in_=pp.rearrange("p (r c) -> p r c", r=2), func=AF.Copy), reads=kpp, writes=[kao])
                        else:
                            S.op("dve", lambda e, ao=ao, pp=pp: e.tensor_copy(out=ao, in_=pp.rearrange("p (r c) -> p r c", r=2)), reads=kpp, writes=[kao])
                        S.dma("sp", Ad[:, :, c0:c0 + CG].rearrange("f (r t) c -> f r t c", r=2)[:, :, t2, :], ao, reads=[kao], writes=["Ad"])

            def stage2_filter():
                S.barrier(); A.reset(PERSIST)
                cs = load_consts([("W2a", W2a_d, [96, 96]), ("W2b", W2b_d, [96, 96])])
                atr = Ring(A, 2, [96, 16, CG], BF16, "at")
                kor = Ring(A, 2, [96, 2, 16, CG], BF16, "ko")
                for cg in range(NG):
                    c0 = cg * CG
                    for fb in range(8):
                        at, kat = atr.next()
                        S.dma("sp", at, Ad[fb * 16:(fb + 1) * 16, :, c0:c0 + CG].rearrange("f rt c -> rt f c"), reads=["Ad"], writes=[kat])
                        ko, kko = kor.next()
                        for fi in range(16):
                            pp, kpp = ps_multi(2)

                            def mm(e, pp=pp, at=at, fi=fi):
                                e.matmul(pp[0:96, 0:512], cs["W2a"][0], at[:, fi, :], start=True, stop=True)
                                return e.matmul(pp[0:96, 512:1024], cs["W2b"][0], at[:, fi, :], start=True, stop=True)
                            S.op("pe", mm, reads=[cs["W2a"][1], cs["W2b"][1], kat], writes=kpp)
                            eng = "act" if fi % 2 == 0 else "dve"
                            if eng == "act":
                                S.op("act", lambda e, ko=ko, pp=pp, fi=fi: e.activation(out=ko[:, :, fi, :], in_=pp[0:96].rearrange("p (r c) -> p r c", r=2), func=AF.Copy), reads=kpp, writes=[kko])
                            else:
                                S.op("dve", lambda e, ko=ko, pp=pp, fi=fi: e.tensor_copy(out=ko[:, :, fi, :], in_=pp[0:96].rearrange("p (r c) -> p r c", r=2)), reads=kpp, writes=[kko])
                        for ab in range(2):
                            S.dma("sp", Kd[ab, :, fb * 16:(fb + 1) * 16, c0:c0 + CG], ko[:, ab, :, :], reads=[kko], writes=["Kd"])

            def stage2_data():
                S.barrier(); A.reset(PERSIST)
                cs = load_consts([("W2", W2_d, [96, 96]), ("L1", L1_d, [96, 96]), ("L2", L2_d, [96, 96])])
                atr = Ring(A, 2, [96, 16, CG], BF16, "at")
                kar = Ring(A, 2, [96, 16, CG], BF16, "ka"); kbr = Ring(A, 2, [96, 16, CG], BF16, "kb")
                btr = Ring(A, 2, [96, 16, CG], BF16, "bt")
                p1r = Ring(A, 3, [96, CG], BF16, "p1"); p2r = Ring(A, 3, [96, CG], BF16, "p2")
                for cg in range(NG):
                    c0 = cg * CG
                    for fb in range(8):
                        at, kat = atr.next(); ka, kka = kar.next(); kb, kkb = kbr.next(); bt, kbt = btr.next()
                        S.dma("sp", at, Ad[fb * 16:(fb + 1) * 16, :, c0:c0 + CG].rearrange("f rt c -> rt f c"), reads=["Ad"], writes=[kat])
                        S.dma("sp", ka, Kd[0, :, fb * 16:(fb + 1) * 16, c0:c0 + CG], reads=["Kd"], writes=[kka])
                        S.dma("sp", kb, Kd[1, :, fb * 16:(fb + 1) * 16, c0:c0 + CG], reads=["Kd"], writes=[kkb])
                        for fi in range(16):
                            ps, kp = ps_next()
                            S.op("pe", lambda e, ps=ps, at=at, fi=fi: e.matmul(ps[0:96, :], cs["W2"][0], at[:, fi, :], start=True, stop=True), reads=[cs["W2"][1], kat], writes=[kp])
                            p1, kp1 = p1r.next(); p2, kp2 = p2r.next()
                            S.op("dve", lambda e, ps=ps, p1=p1, ka=ka, fi=fi: e.tensor_tensor(out=p1, in0=ps[0:96, :], in1=ka[:, fi, :], op=ALU.mult), reads=[kp, kka], writes=[kp1])
                            S.op("dve", lambda e, ps=ps, p2=p2, kb=kb, fi=fi: e.tensor_tensor(out=p2, in0=ps[0:96, :], in1=kb[:, fi, :], op=ALU.mult), reads=[kp, kkb], writes=[kp2])
                            ps2, kps2 = ps_next()

                            def mm(e, ps2=ps2, p1=p1, p2=p2):
                                e.matmul(ps2[0:96, :], cs["L1"][0], p1, start=True, stop=False)
                                return e.matmul(ps2[0:96, :], cs["L2"][0], p2, start=False, stop=True)
                            S.op("pe", mm, reads=[cs["L1"][1], cs["L2"][1], kp1, kp2], writes=[kps2])
                            S.op("act", lambda e, ps2=ps2, bt=bt, fi=fi: e.activation(out=bt[:, fi, :], in_=ps2[0:96, :], func=AF.Copy), reads=[kps2], writes=[kbt])
                        S.dma("sp", Bd[fb * 16:(fb + 1) * 16, :, c0:c0 + CG].rearrange("f rt c -> rt f c"), bt, reads=[kbt], writes=["Bd"])

            def stage1_inv():
                S.barrier(); A.reset(PERSIST)
                Vt, k_V = A.alloc([128, 48, 2, 64], BF16, "V")
                S.dma("poolq", Vt.rearrange("p a b c -> p (a b c)"), V_d, writes=[k_V])
                b2r = Ring(A, 2, [128, 2, 8, CG], BF16, "b2")
                yor = Ring(A, 2, [64, 8, CG], BF16, "yo")
                Ydv = Yd[0:43 * 48, :].rearrange("(a b) c -> a b c", b=48)
                for cg in range(NG):
                    c0 = cg * CG
                    for tb in range(6):
                        b2, kb2 = b2r.next(); yo, kyo = yor.next()
                        for ri in range(2):
                            S.dma("sp", b2[:, ri, :, :], Bd[:, ri * 48 + tb * 8:ri * 48 + tb * 8 + 8, c0:c0 + CG], reads=["Bd"], writes=[kb2])
                        for ti in range(8):
                            t2 = tb * 8 + ti
                            ps, kp = ps_next()

                            def mm(e, ps=ps, b2=b2, t2=t2, ti=ti):
                                e.matmul(ps[0:43, :], Vt[:, t2, 0, 0:43], b2[:, 0, ti, :], start=True, stop=False)
                                return e.matmul(ps[0:43, :], Vt[:, t2, 1, 0:43], b2[:, 1, ti, :], start=False, stop=True)
                            S.op("pe", mm, reads=[k_V, kb2], writes=[kp])
                            if ti % 2 == 0:
                                S.op("act", lambda e, ps=ps, yo=yo, ti=ti: e.activation(out=yo[0:43, ti, :], in_=ps[0:43, :], func=AF.Copy), reads=[kp], writes=[kyo])
                            else:
                                S.op("dve", lambda e, ps=ps, yo=yo, ti=ti: e.tensor_copy(out=yo[0:43, ti, :], in_=ps[0:43, :]), reads=[kp], writes=[kyo])
                        S.dma("sp", Ydv[:, tb * 8:(tb + 1) * 8, c0:c0 + CG], yo[0:43], reads=[kyo], writes=["Yd"])

            phase_e0 = None
            stage1(Hc, "Hc", True)
            stage2_filter()
            stage1(Zt, "Zt", False)
            stage2_data()
            stage1_inv()

        if "E" in phases:
            phase_e()

        def phase_fgh():
            S.barrier(); A.reset(PERSIST)
            R1, k_R1 = A.alloc([128, 24576], BF16, "R1")
            GTt = R1[:, 0:16384].rearrange("p (a b) -> p a b", a=32)
            ZGt = R1[:, 16384:24576].rearrange("p (a b) -> p a b", a=16)
            actT = R1[:, 0:22528].rearrange("p (a b) -> p a b", a=44)
            hy, k_hy = A.alloc([128, 16, 512], BF16, "hy")
            gn, k_gn = A.alloc([128, 16, 512], BF16, "gn")
            mixed, k_mx = A.alloc([128, 16, 512], BF16, "mixed")
            x1T, k_x1 = A.alloc([128, 16, 512], F32, "x1T")
            rbc, k_rbc = A.alloc([128, 512], F32, "rbc")
            wr = Ring(A, 3, [128, 16, 128], BF16, "w16")
            wdr = Ring(A, 2, [128, 44, 128], BF16, "w44")
            tfr = Ring(A, 2, [128, 512], F32, "tf")
            tbr = Ring(A, 3, [128, 512], BF16, "tb")
            xtr = Ring(A, 3, [128, D], F32, "xt")
            onr = Ring(A, 1, [128, D], BF16, "on")
            ytr = Ring(A, 2, [128, 4, 128], BF16, "yt")
            st, k_st = A.alloc([128, 64], F32, "st")
            hf, k_hf = hy, k_hy
            sqT, k_sqT = gn, k_gn
            ga_a = mod[:, 32:48, 0]; sh_f = mod[:, 48:64, 0]; ga_f = mod[:, 80:96, 0]
            whv, wgv, wov, wuv, wdv = w_hy_out_d, w_gdn_out_d, w_o_d, w_up_d, w_down_d

            def proj16(wv, c0, rhs, krhs, nkc=16, ring=None):
                wt, kw = (ring or wr).next()
                S.dma("poolq", wt.rearrange("p a b -> p (a b)"), wv[c0 // 128], writes=[kw])
                ps, kp = ps_next()

                def mm(e):
                    last = None
                    for kc in range(nkc):
                        last = e.matmul(ps[:, :], wt[:, kc, :], rhs[:, kc, :], start=(kc == 0), stop=(kc == nkc - 1))
                    return last
                S.op("pe", mm, reads=[kw, krhs], writes=[kp])
                return ps, kp

            def rms_bc(srcT, ksrc):
                S.op("act", lambda e: e.activation(out=sqT, in_=srcT, func=AF.Square), reads=[ksrc], writes=[k_sqT])
                ps, kp = ps_next()

                def mm(e):
                    last = None
                    for fc in range(16):
                        last = e.matmul(ps[:, :], ones_b, sqT[:, fc, :], start=(fc == 0), stop=(fc == 15))
                    return last
                S.op("pe", mm, reads=[k_sqT, k_ones], writes=[kp])
                S.op("dve", lambda e: e.tensor_scalar(out=rbc, in0=ps[:, :], scalar1=1.0 / D, scalar2=EPS, op0=ALU.mult, op1=ALU.add), reads=[kp], writes=[k_rbc])
                S.op("dve", lambda e: e.reciprocal(out=rbc, in_=rbc), reads=[k_rbc], writes=[k_rbc])
                S.op("act", lambda e: e.activation(out=rbc, in_=rbc, func=AF.Sqrt), reads=[k_rbc], writes=[k_rbc])

            for tb in range(NOWN // 512):
                t0 = tb * 512
                S.dma("sp", GTt, GT[:, t0:t0 + 512].rearrange("(a p) t -> p a t", p=128), writes=[k_R1])
                S.dma("sp", ZGt, ZGT[:, t0:t0 + 512].rearrange("(a p) t -> p a t", p=128), writes=[k_R1])
                for cc in range(16):
                    x0t, kx0 = tbr.next(); zt, kz = tbr.next(); yt, kyt = ytr.next()
                    S.dma("sp", x0t, X0T[cc * 128:(cc + 1) * 128, t0:t0 + 512], reads=["X0T"], writes=[kx0])
                    S.dma("sp", zt, ZT[cc * 128:(cc + 1) * 128, t0:t0 + 512], reads=["ZT"], writes=[kz])
                    S.dma("sp", yt, Yd[t0:t0 + 512, cc * 128:(cc + 1) * 128].rearrange("(a p) c -> p a c", p=128), reads=["Yd"], writes=[kyt])
                    ps, kp = ps_next(); psv = ps.bitcast(BF16)

                    def trY(e, yt=yt, psv=psv):
                        last = None
                        for a in range(4):
                            last = e.transpose(psv[:, a * 128:(a + 1) * 128], yt[:, a, :], ident_b)
                        return last
                    S.op("pe", trY, reads=[kyt, k_ident], writes=[kp])
                    tf, ktf = tfr.next()
                    S.op("dve", lambda e, tf=tf, zt=zt, psv=psv, cc=cc: e.scalar_tensor_tensor(
                        out=tf, in0=zt, scalar=hybT[:, cc:cc + 1], in1=psv[:, 0:512], op0=ALU.mult, op1=ALU.add), reads=[kz, kp, k_hyb], writes=[ktf])
                    S.op("dve", lambda e, tf=tf, x0t=x0t, cc=cc: e.tensor_tensor(out=hy[:, cc, :], in0=tf, in1=x0t, op=ALU.mult), reads=[ktf, kx0], writes=[k_hy])
                for tt in range(4):
                    ot, kot = xtr.next(); on, kon = onr.next()
                    tq, ktq = xtr.next()
                    S.dma("sp", ot, Od0[t0 + tt * 128:t0 + (tt + 1) * 128, :], reads=["Od0_0", "Od0_1"], writes=[kot])
                    S.dma("sp", tq, Od1[t0 + tt * 128:t0 + (tt + 1) * 128, :], reads=["Od1_0", "Od1_1"], writes=[ktq])
                    S.op("dve", lambda e, tq=tq, ot=ot: e.tensor_tensor(out=ot, in0=ot, in1=tq, op=ALU.add), reads=[kot, ktq], writes=[kot])
                    S.op("act", lambda e, tq=tq, ot=ot: e.activation(out=tq, in_=ot, func=AF.Square), reads=[kot], writes=[ktq])
                    S.op("dve", lambda e, tq=tq: e.reduce_sum(out=st[:, 0:16], in_=tq.rearrange("p (h e) -> p h e", h=16), axis=AX.X), reads=[ktq], writes=[k_st])
                    S.op("dve", lambda e: e.tensor_scalar(out=st[:, 16:32], in0=st[:, 0:16], scalar1=1.0 / 128, scalar2=EPS, op0=ALU.mult, op1=ALU.add), reads=[k_st], writes=[k_st])
                    S.op("dve", lambda e: e.reciprocal(out=st[:, 32:48], in_=st[:, 16:32]), reads=[k_st], writes=[k_st])
                    S.op("act", lambda e: e.activation(out=st[:, 48:64], in_=st[:, 32:48], func=AF.Sqrt), reads=[k_st], writes=[k_st])
                    for h in range(16):
                        eng = "act" if h % 2 == 0 else "dve"
                        if eng == "act":
                            S.op("act", lambda e, on=on, ot=ot, h=h: e.activation(out=on[:, h * 128:(h + 1) * 128], in_=ot[:, h * 128:(h + 1) * 128],
                                                                            func=AF.Copy, scale=st[:, 48 + h:49 + h]), reads=[kot, k_st], writes=[kon])
                        else:
                            S.op("dve", lambda e, on=on, ot=ot, h=h: e.tensor_scalar(out=on[:, h * 128:(h + 1) * 128], in0=ot[:, h * 128:(h + 1) * 128],
                                                                               scalar1=st[:, 48 + h:49 + h], scalar2=None, op0=ALU.mult), reads=[kot, k_st], writes=[kon])
                    for g in range(4):
                        ps, kp = ps_next(); psv = ps.bitcast(BF16)

                        def trO(e, on=on, psv=psv, g=g):
                            last = None
                            for j in range(4):
                                h = g * 4 + j
                                last = e.transpose(psv[:, j * 128:(j + 1) * 128], on[:, h * 128:(h + 1) * 128], ident_b)
                            return last
                        S.op("pe", trO, reads=[kon, k_ident], writes=[kp])
                        for j in range(4):
                            h = g * 4 + j
                            S.op("dve", lambda e, psv=psv, j=j, h=h, tt=tt: e.scalar_tensor_tensor(
                                out=gn[:, h, tt * 128:(tt + 1) * 128], in0=psv[:, j * 128:(j + 1) * 128], scalar=gnormT[:, 0:1],
                                in1=ZGt[:, h, tt * 128:(tt + 1) * 128], op0=ALU.mult, op1=ALU.mult), reads=[kp, k_gnorm, k_R1], writes=[k_gn])
                for m in range(16):
                    ps, kp = proj16(whv, m * 128, hy, k_hy)
                    tf, ktf = tfr.next()
                    S.op("dve", lambda e, tf=tf, ps=ps, m=m: e.tensor_tensor(out=tf, in0=ps[:, :], in1=GTt[:, m, :], op=ALU.mult), reads=[kp, k_R1], writes=[ktf])
                    ps2, kp2 = proj16(wgv, m * 128, gn, k_gn)
                    tf2, ktf2 = tfr.next()
                    S.op("dve", lambda e, tf2=tf2, ps2=ps2, m=m: e.tensor_tensor(out=tf2, in0=ps2[:, :], in1=GTt[:, 16 + m, :], op=ALU.mult), reads=[kp2, k_R1], writes=[ktf2])
                    S.op("dve", lambda e, tf=tf, tf2=tf2, m=m: e.tensor_tensor(out=mixed[:, m, :], in0=tf, in1=tf2, op=ALU.add), reads=[ktf, ktf2], writes=[k_mx])
                for tt in range(4):
                    xt, kx = xtr.next()
                    S.dma("sp", xt, x_d[t0 + tt * 128:t0 + (tt + 1) * 128, :], writes=[kx])
                    for g in range(4):
                        ps, kp = ps_next()

                        def trX(e, xt=xt, ps=ps, g=g):
                            last = None
                            for j in range(4):
                                fc = g * 4 + j
                                last = e.transpose(ps[:, j * 128:(j + 1) * 128], xt[:, fc * 128:(fc + 1) * 128], ident_f)
                            return last
                        S.op("pe", trX, reads=[kx, k_identf], writes=[kp])
                        S.op("act", lambda e, ps=ps, g=g, tt=tt: e.activation(
                            out=x1T[:, g * 4:(g + 1) * 4, tt * 128:(tt + 1) * 128], in_=ps[:, :].rearrange("p (a b) -> p a b", a=4), func=AF.Copy),
                            reads=[kp], writes=[k_x1])
                for m in range(16):
                    ps, kp = proj16(wov, m * 128, mixed, k_mx)
                    S.op("dve", lambda e, ps=ps, m=m: e.scalar_tensor_tensor(out=x1T[:, m, :], in0=ps[:, :], scalar=ga_a[:, m:m + 1], in1=x1T[:, m, :],
                                                                           op0=ALU.mult, op1=ALU.add), reads=[kp, k_mod, k_x1], writes=[k_x1])
                rms_bc(x1T, k_x1)
                for fc in range(16):
                    tf, ktf = tfr.next()
                    S.op("dve", lambda e, tf=tf, fc=fc: e.scalar_tensor_tensor(out=tf, in0=x1T[:, fc, :], scalar=scale_f[:, fc:fc + 1], in1=rbc,
                                                                             op0=ALU.mult, op1=ALU.mult), reads=[k_x1, k_scf, k_rbc], writes=[ktf])
                    S.op("act", lambda e, tf=tf, fc=fc: e.activation(out=hf[:, fc, :], in_=tf, func=AF.Identity, bias=sh_f[:, fc:fc + 1]),
                         reads=[ktf, k_mod], writes=[k_hf])
                for j in range(44):
                    psg, kpg = proj16(wuv, j * 128, hf, k_hf)
                    psu, kpu = proj16(wuv, D_FF + j * 128, hf, k_hf)
                    tf, ktf = tfr.next()
                    S.op("act", lambda e, tf=tf, psg=psg: e.activation(out=tf, in_=psg[:, :], func=AF.Silu), reads=[kpg], writes=[ktf])
                    S.op("dve", lambda e, tf=tf, psu=psu, j=j: e.tensor_tensor(out=actT[:, j, :], in0=tf, in1=psu[:, :], op=ALU.mult), reads=[ktf, kpu], writes=[k_R1])
                for m in range(16):
                    ps, kp = proj16(wdv, m * 128, actT, k_R1, nkc=44, ring=wdr)
                    S.op("dve", lambda e, ps=ps, m=m: e.scalar_tensor_tensor(out=x1T[:, m, :], in0=ps[:, :], scalar=ga_f[:, m:m + 1], in1=x1T[:, m, :],
                                                                           op0=ALU.mult, op1=ALU.add), reads=[kp, k_mod, k_x1], writes=[k_x1])
                rms_bc(x1T, k_x1)
                for fc in range(16):
                    S.op("dve", lambda e, fc=fc: e.scalar_tensor_tensor(out=x1T[:, fc, :], in0=x1T[:, fc, :], scalar=nfinT[:, fc:fc + 1], in1=rbc,
                                                                      op0=ALU.mult, op1=ALU.mult), reads=[k_x1, k_nfin, k_rbc], writes=[k_x1])
                for tt in range(4):
                    xo, kxo = xtr.next()
                    for g in range(4):
                        ps, kp = ps_next()

                        def trB(e, ps=ps, g=g, tt=tt):
                            last = None
                            for j in range(4):
                                fc = g * 4 + j
                                last = e.transpose(ps[:, j * 128:(j + 1) * 128], x1T[:, fc, tt * 128:(tt + 1) * 128], ident_f)
                            return last
                        S.op("pe", trB, reads=[k_x1, k_identf], writes=[kp])
                        S.op("act", lambda e, ps=ps, xo=xo, g=g: e.activation(out=xo[:, g * 512:(g + 1) * 512], in_=ps[:, :], func=AF.Copy), reads=[kp], writes=[kxo])
                    S.final_tokens.append(S.dma("sp", out_d[t0 + tt * 128:t0 + (tt + 1) * 128, :], xo, reads=[kxo], writes=["out"]))

        if "F" in phases:
            phase_fgh()
        S.emit()
    return nc


def _fm(v, n):
    return np.ascontiguousarray(np.asarray(v, np.float32).reshape(n, 128).T)


def _hyena_consts():
    n = L
    j = np.arange(N_)
    idx = np.where(j < NOWN, j, N_ - j).astype(np.float64)
    idx[NOWN] = 0
    tt = idx / (n - 1)
    bands = 16
    w = 2.0 * np.pi * idx / n
    f = np.linspace(1e-4, bands - 1, bands)
    zf = np.concatenate([tt[None, :], np.cos(f[:, None] * w[None, :]), -np.sin(f[:, None] * w[None, :])], axis=0)
    tau = -tt.copy(); tau[NOWN] = -30.0
    deltas = np.abs(np.linspace(math.log(1e-2) / 1.5, math.log(1e-2) / 0.3, D))
    t1 = np.arange(128)[:, None, None, None]; t2 = np.arange(48)[None, :, None, None]; f1 = np.arange(128)[None, None, None, :]
    th = 2 * np.pi * (t1 * f1 / 128.0 + t2 * f1 / float(N_))
    W1 = np.concatenate([np.cos(th), -np.sin(th)], axis=2)
    a = np.arange(48)
    th2 = 2 * np.pi * np.outer(a, a) / 48.0
    c2, s2 = np.cos(th2), np.sin(th2)
    W2 = np.block([[c2, -s2], [s2, c2]])
    W2a = np.block([[c2, c2], [s2, s2]])
    W2b = np.block([[-s2, -s2], [c2, c2]])
    L1 = np.block([[c2, s2], [-s2, c2]])
    L2 = np.block([[-s2, c2], [-c2, -s2]])
    f1v = np.arange(128)[:, None, None, None]; t2v = np.arange(48)[None, :, None, None]; t1v = np.arange(64)[None, None, None, :]
    ph = 2 * np.pi * f1v * (t2v / float(N_) + t1v / 128.0)
    V = np.concatenate([np.cos(ph), -np.sin(ph)], axis=2) / float(N_)
    f32 = lambda x: np.ascontiguousarray(x, dtype=np.float32)
    return dict(zf=f32(zf), tau=f32(tau.reshape(48, 128).T), deltas=f32(np.broadcast_to(deltas[None, :], (128, D))),
                W1=f32(W1.reshape(128, -1)), Vc=f32(V.reshape(128, -1)), W2=f32(W2), W2a=f32(W2a), W2b=f32(W2b), L1=f32(L1), L2=f32(L2))


def make_in_maps(inp):
    maps = []
    ident = np.eye(128, dtype=np.float32)
    tri = np.tril(np.ones((64, 64), np.float32))
    cmask = np.ascontiguousarray(np.stack([tri, np.tril(tri, -1), tri.T, np.triu(tri.T, 1)], axis=1))
    ii = np.arange(64)
    def same(b): return (ii[:, None] // b == ii[None, :] // b).astype(np.float32)
    cm2 = np.ascontiguousarray(np.stack([same(8), same(16) - same(8), same(32) - same(16), same(64) - same(32), -same(8)], axis=1))
    hyc_consts = _hyena_consts()
    w_in0 = inp["w_in"][0]
    cols = [SEG[t] + j * 128 for (t, j) in PLAN]
    w_inc_base = _chunk_major(w_in0, cols)
    shared = dict(
        w_hy_outc=_chunk_major(inp["w_hy_out"][0], [m * 128 for m in range(16)]),
        w_gdn_outc=_chunk_major(inp["w_gdn_out"][0], [m * 128 for m in range(16)]),
        w_oc=_chunk_major(inp["w_o"][0], [m * 128 for m in range(16)]),
        w_upc=_chunk_major(inp["w_up"][0], [j * 128 for j in range(88)]),
        w_downc=_chunk_major(inp["w_down"][0], [m * 128 for m in range(16)]),
    )
    w_inc_flip = None
    cache = {}

    def _w_inc_for(flip):
        if flip not in cache:
            if not flip:
                cache[flip] = w_inc_base
            else:
                wsc = w_in0[:, 14336:14400].reshape(D, 2, 2, 16)[:, :, ::-1, :].reshape(D, 64)
                arr = w_inc_base.copy()
                arr[PLAN_INDEX[("scal", 0)]] = _chunk_major(wsc, [0])[0]
                cache[flip] = arr
        return cache[flip]

    for core in range(8):
        b, half = core // 2, core % 2
        flip = half == 1
        x = inp["x"][b]; ctx = inp["ctx"][b]
        if flip:
            x = x[::-1]; ctx = ctx[::-1]
        cT = np.stack([_fm(inp["c"][b], 16), _fm(inp["c_ctx"], 16)], axis=-1)
        w_in = inp["w_in"][0]
        w_scal = w_in[:, 14336:14400]
        a_log = inp["gdn_a_log"][0]; dtb = inp["gdn_dt_bias"][0]
        hyc = inp["hy_conv"][0]; gdc = inp["gdn_conv"][0]
        if flip:
            w_scal = w_scal.reshape(D, 2, 2, 16)[:, :, ::-1, :].reshape(D, 64)
            a_log = a_log[::-1]; dtb = dtb[::-1]
            hyc = hyc[::-1]; gdc = gdc[::-1]
        scalp = np.zeros((64, 2), np.float32)
        scalp[32:64, 0] = a_log.reshape(32); scalp[32:64, 1] = dtb.reshape(32)
        hyconvT = np.ascontiguousarray(hyc.reshape(3, 48, 128).transpose(2, 1, 0))
        gdnconvT = np.ascontiguousarray(gdc.reshape(3, 48, 128).transpose(2, 1, 0))
        maps.append(dict(
            x=np.ascontiguousarray(x), ctx=np.ascontiguousarray(ctx), cT=np.ascontiguousarray(cT),
            w_ada=inp["w_ada"][0], b_adaT=_fm(inp["b_ada"][0], 96),
            nmixT=_fm(inp["norm_mix"][0], 16), nffnT=_fm(inp["norm_ffn"][0], 16),
            w_inc=_w_inc_for(flip),
            hyconvT=hyconvT, gdnconvT=gdnconvT, scalp=scalp, ident=ident, cmask=cmask, cm2=cm2,
            fw1=inp["hy_fw1"][0], fw2=inp["hy_fw2"][0], fw3=inp["hy_fw3"][0],
            fsm=np.ascontiguousarray(np.stack([inp["hy_fb1"][0], inp["hy_fb2"][0], inp["hy_fb3"][0], inp["hy_freq"][0]], axis=1)),
            fout=(np.ascontiguousarray(np.concatenate([inp["hy_fout"][0][:, D:], inp["hy_fout"][0][:, :D]], axis=1)) if flip else inp["hy_fout"][0]),
            **hyc_consts,
            **shared,
            hybT=_fm(inp["hy_bias"][0], 16), gnormT=_fm(inp["gdn_norm"][0], 1), nfinT=_fm(inp["norm_final"], 16),
        ))
    return maps


def kernel(**inputs):
    inp = {k: np.asarray(v) for k, v in inputs.items()}
    nc = build()
    maps = make_in_maps(inp)
    res = run_bass_kernel_spmd(nc, maps, core_ids=list(range(8)))
    out = np.empty((4, L, D), np.float32)
    for core in range(8):
        b, half = core // 2, core % 2
        o = res.results[core]["out"]
        if half == 0:
            out[b, :NOWN] = o
        else:
            out[b, NOWN:] = o[::-1]
    return out
```

```python
import contextlib
import math
import numpy as np
import ml_dtypes
import concourse.bass as bass
import concourse.mybir as mybir
from concourse.bass_utils import run_bass_kernel_spmd

F32 = mybir.dt.float32
BF16 = mybir.dt.bfloat16
AF = mybir.ActivationFunctionType
ALU = mybir.AluOpType
AX = mybir.AxisListType

D = 2048
L = 4096
NOWN = 2048
CTX = 256
TT = L + CTX
HEADS = 16
D_IN = 18496
D_FF = 5632
EPS = 1e-6
N_ = 6144

COMPUTE = ("pe", "act", "dve", "pool")
NSLOT = {"sp": 12, "poolq": 12}
QENG = {"sp": "sp", "poolq": "pool"}


class Op:
    __slots__ = ("fn", "waits", "sem", "inc")


class Sched:
    def __init__(self, nc):
        self.nc = nc
        self.streams = {e: [] for e in ("pe", "act", "dve", "pool", "sp")}
        self.cnt = {e: 0 for e in COMPUTE}
        self.slot_uses = {q: [0] * NSLOT[q] for q in NSLOT}
        self.dma_i = {q: 0 for q in NSLOT}
        self.last_w = {}
        self.reads = {}
        self.waited = {e: {} for e in self.streams}
        self.final_tokens = []

    def _need(self, stream, tok, waits):
        if tok is None:
            return
        semname, val, pstream, is_pe = tok
        if pstream == stream and is_pe:
            return
        if self.waited[stream].get(semname, 0) >= val:
            return
        waits[semname] = max(waits.get(semname, 0), val)

    def _deps(self, stream, reads, writes):
        waits = {}
        for k in reads:
            self._need(stream, self.last_w.get(k), waits)
        for k in writes:
            self._need(stream, self.last_w.get(k), waits)
            for t in self.reads.get(k, {}).values():
                self._need(stream, t, waits)
        for s, v in waits.items():
            self.waited[stream][s] = v
        return waits

    def _record(self, tok, reads, writes):
        for k in reads:
            d = self.reads.setdefault(k, {})
            o = d.get(tok[0])
            if o is None or o[1] < tok[1]:
                d[tok[0]] = tok
        for k in writes:
            self.last_w[k] = tok
            self.reads[k] = {}

    def op(self, eng, fn, reads=(), writes=()):
        waits = self._deps(eng, reads, writes)
        self.cnt[eng] += 1
        tok = (eng, self.cnt[eng], eng, eng == "pe")
        o = Op(); o.fn = fn; o.waits = sorted(waits.items()); o.sem = eng; o.inc = 1
        self.streams[eng].append(o)
        self._record(tok, reads, writes)
        return tok

    def dma(self, q, out, in_, reads=(), writes=()):
        stream = QENG[q]
        waits = self._deps(stream, reads, writes)
        i = self.dma_i[q]; self.dma_i[q] += 1
        slot = i % NSLOT[q]
        semname = "%s%d" % (q, slot)
        prev = self.slot_uses[q][slot] * 16
        if prev and self.waited[stream].get(semname, 0) < prev:
            waits[semname] = prev
            self.waited[stream][semname] = prev
        self.slot_uses[q][slot] += 1
        tok = (semname, self.slot_uses[q][slot] * 16, None, False)
        o = Op(); o.waits = sorted(waits.items()); o.sem = semname; o.inc = 16
        o.fn = (lambda e, out=out, in_=in_: e.dma_start(out=out, in_=in_))
        self.streams[stream].append(o)
        self._record(tok, reads, writes)
        return tok

    def barrier(self):
        allv = {e: self.cnt[e] for e in COMPUTE}
        for q in NSLOT:
            for s in range(NSLOT[q]):
                allv["%s%d" % (q, s)] = self.slot_uses[q][s] * 16
        for stream in self.streams:
            waits = {}
            for s, v in allv.items():
                if v and self.waited[stream].get(s, 0) < v and not (s == "pe" and stream == "pe"):
                    waits[s] = v
                    self.waited[stream][s] = v
            if waits:
                o = Op(); o.fn = None; o.waits = sorted(waits.items()); o.sem = None; o.inc = 0
                self.streams[stream].append(o)

    def emit(self):
        nc = self.nc
        names = list(COMPUTE) + ["%s%d" % (q, s) for q in NSLOT for s in range(NSLOT[q])]
        with contextlib.ExitStack() as st:
            sems = {n: st.enter_context(nc.semaphore("s_" + n)) for n in names}
            block = st.enter_context(nc.Block())

            def run(stream, final=False):
                def body(e):
                    for o in self.streams[stream]:
                        for s, v in o.waits:
                            e.wait_ge(sems[s], v)
                        if o.fn is not None:
                            o.fn(e).then_inc(sems[o.sem], o.inc)
                    if final:
                        for (s, v, _, _) in self.final_tokens:
                            e.wait_ge(sems[s], v)
                return body

            block.sync(run("sp", True))
            block.tensor(run("pe"))
            block.scalar(run("act"))
            block.vector(run("dve"))
            block.gpsimd(run("pool"))


class Arena:
    def __init__(self, big, total):
        self.big = big; self.total = total; self.off = 0; self.n = 0

    def reset(self, to=0):
        self.off = to

    def alloc(self, shape, dtype, name=None):
        n = int(np.prod(shape[1:]))
        size = n * (2 if dtype == F32 else 1)
        o = self.off
        self.off += (size + 15) // 16 * 16
        assert self.off <= self.total, ("SBUF arena overflow", self.off, self.total)
        v = self.big[:, o:o + size]
        if dtype == F32:
            v = v.bitcast(F32)
        if len(shape) == 3:
            v = v.rearrange("p (a b) -> p a b", a=shape[1])
        elif len(shape) == 4:
            v = v.rearrange("p (a b c) -> p a b c", a=shape[1], b=shape[2])
        self.n += 1
        key = "%s#%d" % (name or "t", self.n)
        return v[0:shape[0]], key


class Ring:
    def __init__(self, arena, n, shape, dtype, name):
        self.items = [arena.alloc(shape, dtype, name) for _ in range(n)]
        self.i = 0

    def next(self):
        it = self.items[self.i % len(self.items)]
        self.i += 1
        return it


def _make_plan():
    plan = [("scal", 0)]
    for j in range(16):
        plan += [("x1", j), ("hv", j)]
    for typ in ("q", "k", "gv", "x0", "zg"):
        plan += [(typ, j) for j in range(16)]
    plan += [("gate", j) for j in range(32)]
    return plan


SEG = dict(x0=0, x1=2048, hv=4096, q=6144, k=8192, gv=10240, zg=12288, scal=14336, gate=14400)


PLAN = _make_plan()
PLAN_INDEX = {k: i for i, k in enumerate(PLAN)}


def _chunk_major(w, cols):
    K = w.shape[0]
    out = np.empty((len(cols), 128, (K // 128) * 128), np.float32)
    for i, c0 in enumerate(cols):
        blk = w[:, c0:c0 + 128]
        if blk.shape[1] < 128:
            blk = np.concatenate([blk, np.zeros((K, 128 - blk.shape[1]), np.float32)], axis=1)
        out[i] = blk.reshape(K // 128, 128, 128).transpose(1, 0, 2).reshape(128, -1)
    return out


def build(upto=99, dbg=(), phases="DEF"):
    nc = bass.Bass("TRN2", target_bir_lowering=False)
    S = Sched(nc)
    ext = {}

    def inp(name, shape, dt=F32):
        ext[name] = nc.dram_tensor(name, list(shape), dt, kind="ExternalInput")
        return ext[name].ap()

    def scratch(name, shape, dt):
        kind = "ExternalOutput" if name in dbg else "Internal"
        t = nc.dram_tensor(name, list(shape), dt, kind=kind)
        return t.ap()

    x_d = inp("x", [L, D]); ctx_d = inp("ctx", [CTX, D]); cT_d = inp("cT", [128, 16, 2])
    w_ada_d = inp("w_ada", [D, 6 * D]); b_adaT_d = inp("b_adaT", [128, 96])
    nmixT_d = inp("nmixT", [128, 16]); nffnT_d = inp("nffnT", [128, 16])
    w_in_d = inp("w_inc", [145, 128, 16 * 128])
    hyconvT_d = inp("hyconvT", [128, 48, 3]); gdnconvT_d = inp("gdnconvT", [128, 48, 3])
    scalp_d = inp("scalp", [64, 2])
    ident_d = inp("ident", [128, 128])
    out_d = nc.dram_tensor("out", [NOWN, D], F32, kind="ExternalOutput").ap()
    w_hy_out_d = inp("w_hy_outc", [16, 128, 16 * 128]); w_gdn_out_d = inp("w_gdn_outc", [16, 128, 16 * 128]); w_o_d = inp("w_oc", [16, 128, 16 * 128])
    w_up_d = inp("w_upc", [88, 128, 16 * 128]); w_down_d = inp("w_downc", [16, 128, 44 * 128])
    hybT_d = inp("hybT", [128, 16]); gnormT_d = inp("gnormT", [128, 1]); nfinT_d = inp("nfinT", [128, 16])
    Yd = scratch("Yd", [NOWN + 64, D], BF16); Od0 = scratch("Od0", [NOWN, D], F32); Od1 = scratch("Od1", [NOWN, D], F32)
    cmask_d = inp("cmask", [64, 4, 64]); cm2_d = inp("cm2", [64, 5, 64])
    zf_d = inp("zf", [33, N_]); fw1_d = inp("fw1", [33, 64]); fw2_d = inp("fw2", [64, 64]); fw3_d = inp("fw3", [64, 64])
    fsm_d = inp("fsm", [64, 4]); fout_d = inp("fout", [64, 2 * D]); deltas_d = inp("deltas", [128, D]); tau_d = inp("tau", [128, 48])
    W1_d = inp("W1", [128, 48 * 2 * 128]); V_d = inp("Vc", [128, 48 * 2 * 64])
    W2_d = inp("W2", [96, 96]); W2a_d = inp("W2a", [96, 96]); W2b_d = inp("W2b", [96, 96]); L1_d = inp("L1", [96, 96]); L2_d = inp("L2", [96, 96])
    Hc = scratch("Hc", [N_, D], BF16); Ad = scratch("Ad", [128, 96, D], BF16); Bd = scratch("Bd", [128, 96, D], BF16)
    Kd = scratch("Kd", [2, 96, 128, D], BF16)

    X0T = scratch("X0T", [D, NOWN], BF16); ZT = scratch("ZT", [D, L], BF16); Zt = scratch("Zt", [L, D], BF16)
    QT = scratch("QT", [D, TT], BF16); KT = scratch("KT", [D, TT], BF16); VT = scratch("VT", [D, TT], BF16)
    ZGT = scratch("ZGT", [D, NOWN], BF16); SCT = scratch("SCT", [64, TT], F32); GT = scratch("GT", [2 * D, NOWN], BF16)
    MODd = scratch("MODd", [128, 96, 2], F32)

    with contextlib.ExitStack() as st:
        TOTAL = 106000
        big = st.enter_context(nc.sbuf_tensor("big", [128, TOTAL], BF16))
        PSALL = st.enter_context(nc.psum_tensor("psall", [128, 4096], F32))
        psb = [PSALL[:, i * 512:(i + 1) * 512] for i in range(8)]
        A = Arena(big, TOTAL)
        psi = [0]

        def ps_next():
            i = psi[0] % 8; psi[0] += 1
            return psb[i], "psb%d" % i

        def ps_multi(nb):
            i = ((psi[0] + nb - 1) // nb * nb) % 8
            psi[0] = i + nb
            return PSALL[:, i * 512:(i + nb) * 512], ["psb%d" % (i + j) for j in range(nb)]

        ident_f, k_identf = A.alloc([128, 128], F32, "identf")
        ident_b, k_ident = A.alloc([128, 128], BF16, "ident")
        ones_b, k_ones = A.alloc([128, 128], BF16, "ones")
        ones_f, k_onesf = A.alloc([128, 128], F32, "onesf")
        mod, k_mod = A.alloc([128, 96, 2], F32, "mod")
        nmixT, k_nmix = A.alloc([128, 16], F32, "nmix")
        nffnT, k_nffn = A.alloc([128, 16], F32, "nffn")
        scale_a, k_sca = A.alloc([128, 16], F32, "scale_a")
        scale_c, k_scc = A.alloc([128, 16], F32, "scale_c")
        scale_f, k_scf = A.alloc([128, 16], F32, "scale_f")
        hyconvT, k_hyc = A.alloc([128, 48, 3], F32, "hyc")
        gdnconvT, k_gdc = A.alloc([128, 48, 3], F32, "gdc")
        scalp, k_scalp = A.alloc([64, 2], F32, "scalp")
        nega, k_nega = A.alloc([64, 1], F32, "nega")
        negpi, k_negpi = A.alloc([128, 1], F32, "negpi")
        S.op("dve", lambda e: e.memset(negpi, -math.pi), writes=[k_negpi])
        hybT, k_hyb = A.alloc([128, 16], F32, "hyb")
        gnormT, k_gnorm = A.alloc([128, 1], F32, "gnorm")
        nfinT, k_nfin = A.alloc([128, 16], F32, "nfin")
        PERSIST = A.off
        S.dma("sp", hybT, hybT_d, writes=[k_hyb]); S.dma("sp", gnormT, gnormT_d, writes=[k_gnorm]); S.dma("sp", nfinT, nfinT_d, writes=[k_nfin])

        S.dma("sp", ident_f, ident_d, writes=[k_identf])
        S.op("dve", lambda e: e.tensor_copy(out=ident_b, in_=ident_f), reads=[k_identf], writes=[k_ident])
        S.op("dve", lambda e: e.memset(ones_b, 1.0), writes=[k_ones])
        S.op("dve", lambda e: e.memset(ones_f, 1.0), writes=[k_onesf])
        S.dma("sp", nmixT, nmixT_d, writes=[k_nmix]); S.dma("sp", nffnT, nffnT_d, writes=[k_nffn])
        S.dma("sp", hyconvT, hyconvT_d, writes=[k_hyc]); S.dma("sp", gdnconvT, gdnconvT_d, writes=[k_gdc])
        S.dma("sp", scalp, scalp_d, writes=[k_scalp])
        S.op("act", lambda e: e.activation(out=nega[32:64], in_=scalp[32:64, 0:1], func=AF.Exp), reads=[k_scalp], writes=[k_nega])
        S.op("dve", lambda e: e.tensor_scalar(out=nega[32:64], in0=nega[32:64], scalar1=-1.0, scalar2=None, op0=ALU.mult), reads=[k_nega], writes=[k_nega])

        cT, k_cT = A.alloc([128, 16, 2], F32, "cT")
        sT, k_sT = A.alloc([128, 16, 2], BF16, "sT")
        bT, k_bT = A.alloc([128, 96], F32, "bT")
        wr = Ring(A, 2, [128, 16, 1536], BF16, "wada")
        S.dma("sp", cT, cT_d, writes=[k_cT]); S.dma("sp", bT, b_adaT_d, writes=[k_bT])
        S.op("act", lambda e: e.activation(out=sT, in_=cT, func=AF.Silu), reads=[k_cT], writes=[k_sT])
        w_ada_v = w_ada_d.rearrange("(kc p) n -> p kc n", p=128)
        for blk in range(8):
            wt, kw = wr.next()
            S.dma("poolq", wt, w_ada_v[:, :, blk * 1536:(blk + 1) * 1536], writes=[kw])
            ps, kp = ps_next()

            def mmA(e, wt=wt, ps=ps):
                last = None
                for j in range(12):
                    for kc in range(16):
                        last = e.matmul(ps[:, 2 * j:2 * j + 2], wt[:, kc, j * 128:(j + 1) * 128], sT[:, kc, :],
                                        start=(kc == 0), stop=(kc == 15))
                return last
            S.op("pe", mmA, reads=[kw, k_sT], writes=[kp])
            S.op("dve", lambda e, ps=ps, blk=blk: e.tensor_copy(
                out=mod[:, blk * 12:(blk + 1) * 12, :], in_=ps[:, 0:24].rearrange("p (a b) -> p a b", b=2)),
                reads=[kp], writes=[k_mod])
        for j in range(2):
            S.op("dve", lambda e, j=j: e.tensor_tensor(out=mod[:, :, j], in0=mod[:, :, j], in1=bT, op=ALU.add),
                 reads=[k_mod, k_bT], writes=[k_mod])
        for (dst, kd, src, nrm, kn) in ((scale_a, k_sca, mod[:, 16:32, 0], nmixT, k_nmix),
                                        (scale_c, k_scc, mod[:, 16:32, 1], nmixT, k_nmix),
                                        (scale_f, k_scf, mod[:, 64:80, 0], nffnT, k_nffn)):
            S.op("dve", lambda e, dst=dst, src=src, nrm=nrm: e.scalar_tensor_tensor(
                out=dst, in0=src, scalar=1.0, in1=nrm, op0=ALU.add, op1=ALU.mult), reads=[k_mod, kn], writes=[kd])
        if "MODd" in dbg:
            S.final_tokens.append(S.dma("sp", MODd, mod, reads=[k_mod], writes=["MODd"]))
        if upto <= 1:
            S.emit(); return nc

        S.barrier(); A.reset(PERSIST)
        hT, k_hT = A.alloc([128, 16, TT], BF16, "hT")
        PB = A.off
        xr = Ring(A, 2, [128, D], F32, "xt")
        sq, k_sq = A.alloc([128, D], F32, "sq")
        xnr = Ring(A, 2, [128, D], BF16, "xn")
        str_ = Ring(A, 2, [128, 4], F32, "stat")
        for i in range(TT // 128):
            lat = i < L // 128
            src = x_d[i * 128:(i + 1) * 128, :] if lat else ctx_d[(i - 32) * 128:(i - 31) * 128, :]
            sc_t, k_sc = (scale_a, k_sca) if lat else (scale_c, k_scc)
            sh_t = mod[:, 0:16, 0] if lat else mod[:, 0:16, 1]
            xt, kx = xr.next(); xn, kxn = xnr.next(); stt, kst = str_.next()
            S.dma("sp", xt, src, writes=[kx])
            S.op("act", lambda e, xt=xt: e.activation(out=sq, in_=xt, func=AF.Square), reads=[kx], writes=[k_sq])
            S.op("dve", lambda e, stt=stt: e.reduce_sum(out=stt[:, 0:1], in_=sq, axis=AX.X), reads=[k_sq], writes=[kst])
            S.op("dve", lambda e, stt=stt: e.tensor_scalar(out=stt[:, 1:2], in0=stt[:, 0:1], scalar1=1.0 / D, scalar2=EPS,
                                                           op0=ALU.mult, op1=ALU.add), reads=[kst], writes=[kst])
            S.op("dve", lambda e, stt=stt: e.reciprocal(out=stt[:, 2:3], in_=stt[:, 1:2]), reads=[kst], writes=[kst])
            S.op("act", lambda e, stt=stt: e.activation(out=stt[:, 3:4], in_=stt[:, 2:3], func=AF.Sqrt), reads=[kst], writes=[kst])
            S.op("dve", lambda e, stt=stt: e.tensor_tensor(out=stt[:, 0:1], in0=stt[:, 3:4], in1=stt[:, 3:4], op=ALU.mult), reads=[kst], writes=[kst])
            S.op("dve", lambda e, stt=stt: e.tensor_tensor(out=stt[:, 0:1], in0=stt[:, 0:1], in1=stt[:, 1:2], op=ALU.mult), reads=[kst], writes=[kst])
            S.op("dve", lambda e, stt=stt: e.tensor_scalar(out=stt[:, 0:1], in0=stt[:, 0:1], scalar1=-0.5, scalar2=1.5,
                                                           op0=ALU.mult, op1=ALU.add), reads=[kst], writes=[kst])
            S.op("dve", lambda e, stt=stt: e.tensor_tensor(out=stt[:, 3:4], in0=stt[:, 3:4], in1=stt[:, 0:1], op=ALU.mult), reads=[kst], writes=[kst])
            S.op("act", lambda e, xt=xt, xn=xn, stt=stt: e.activation(out=xn, in_=xt, func=AF.Copy, scale=stt[:, 3:4]),
                 reads=[kx, kst], writes=[kxn])
            for g in range(4):
                ps, kp = ps_next()
                psv = ps.bitcast(BF16)

                def tr(e, xn=xn, psv=psv, g=g):
                    last = None
                    for j in range(4):
                        fc = g * 4 + j
                        last = e.transpose(psv[:, j * 128:(j + 1) * 128], xn[:, fc * 128:(fc + 1) * 128], ident_b)
                    return last
                S.op("pe", tr, reads=[kxn, k_ident], writes=[kp])
                for j in range(4):
                    fc = g * 4 + j
                    dst = hT[:, fc, i * 128:(i + 1) * 128]
                    if j % 2 == 0:
                        S.op("act", lambda e, dst=dst, psv=psv, j=j, fc=fc, sc_t=sc_t, sh_t=sh_t: e.activation(
                            out=dst, in_=psv[:, j * 128:(j + 1) * 128], func=AF.Identity, scale=sc_t[:, fc:fc + 1], bias=sh_t[:, fc:fc + 1]),
                            reads=[kp, k_sc, k_mod], writes=[k_hT])
                    else:
                        S.op("dve", lambda e, dst=dst, psv=psv, j=j, fc=fc, sc_t=sc_t, sh_t=sh_t: e.tensor_scalar(
                            out=dst, in0=psv[:, j * 128:(j + 1) * 128], scalar1=sc_t[:, fc:fc + 1], scalar2=sh_t[:, fc:fc + 1],
                            op0=ALU.mult, op1=ALU.add), reads=[kp, k_sc, k_mod], writes=[k_hT])
        if "HTd" in dbg:
            HTd = nc.dram_tensor("HTd", [128, 16, TT], BF16, kind="ExternalOutput").ap()
            S.final_tokens.append(S.dma("sp", HTd, hT, reads=[k_hT], writes=["HTd"]))
        if upto <= 2:
            S.emit(); return nc

        S.barrier(); A.reset(PB)
        wr = Ring(A, 3, [128, 16, 128], BF16, "win")
        ur = Ring(A, 3, [128, 512], F32, "u")
        t2r = Ring(A, 2, [128, 512], F32, "tmp2")
        obr = Ring(A, 3, [128, 512], BF16, "ob")
        sqr = Ring(A, 3, [128, 512], BF16, "sq")
        ofr = Ring(A, 2, [64, 512], F32, "of")
        u1, k_u1 = A.alloc([128, L], BF16, "u1")
        ztr = Ring(A, 2, [128, 4, 128], BF16, "zt")
        BLK_ALL = [(b * 512, 512) for b in range(8)]
        BLK_OWN = BLK_ALL[:NOWN // 512]
        BLK_CTX = [(L, CTX)]

        def conv_epilogue(ps, kp, n, rw, taps, ktaps):
            u, ku = ur.next()
            S.op("act", lambda e: e.activation(out=u[:, 0:n], in_=ps[:, 0:n], func=AF.Copy, scale=taps[:, 1:2]),
                 reads=[kp, ktaps], writes=[ku])
            pv = ps[:, 0:n].rearrange("p (r w) -> p r w", w=rw)
            uv = u[:, 0:n].rearrange("p (r w) -> p r w", w=rw)
            S.op("dve", lambda e: e.scalar_tensor_tensor(out=uv[:, :, 1:rw], in0=pv[:, :, 0:rw - 1], scalar=taps[:, 0:1],
                                                         in1=uv[:, :, 1:rw], op0=ALU.mult, op1=ALU.add), reads=[kp, ktaps, ku], writes=[ku])
            S.op("dve", lambda e: e.scalar_tensor_tensor(out=uv[:, :, 0:rw - 1], in0=pv[:, :, 1:rw], scalar=taps[:, 2:3],
                                                         in1=uv[:, :, 0:rw - 1], op0=ALU.mult, op1=ALU.add), reads=[kp, ktaps, ku], writes=[ku])
            return u, ku

        def do_chunk(typ, j):
            M = 64 if typ == "scal" else 128
            wt, kw = wr.next()
            S.dma("poolq", wt.rearrange("p a b -> p (a b)"), w_in_d[PLAN_INDEX[(typ, j)]], writes=[kw])
            own = typ in ("x0", "zg", "gate")
            blocks = BLK_OWN if own else BLK_ALL
            if typ in ("q", "k", "gv", "scal"):
                blocks = blocks + BLK_CTX
            def blk(t0, n):
                ps, kp = ps_next()

                def mm(e, ps=ps, t0=t0, n=n):
                    last = None
                    for kc in range(16):
                        last = e.matmul(ps[0:M, 0:n], wt[:, kc, 0:M], hT[:, kc, t0:t0 + n], start=(kc == 0), stop=(kc == 15))
                    return last
                S.op("pe", mm, reads=[kw, k_hT], writes=[kp])
                isctx = t0 >= L
                rw = CTX if isctx else 64
                if typ in ("x0", "x1", "hv"):
                    ci = {"x0": 0, "x1": 16, "hv": 32}[typ] + j
                    u, ku = conv_epilogue(ps, kp, n, rw, hyconvT[:, ci, :], k_hyc)
                    if typ == "x0":
                        ob, kob = obr.next()
                        S.op("act", lambda e, ob=ob, u=u: e.activation(out=ob, in_=u, func=AF.Copy), reads=[ku], writes=[kob])
                        S.dma("sp", X0T[j * 128:(j + 1) * 128, t0:t0 + n], ob, reads=[kob], writes=["X0T"])
                    elif typ == "x1":
                        S.op("act", lambda e, u=u, t0=t0: e.activation(out=u1[:, t0:t0 + 512], in_=u, func=AF.Copy), reads=[ku], writes=[k_u1])
                    else:
                        ob, kob = obr.next()
                        S.op("dve", lambda e, ob=ob, u=u, t0=t0: e.tensor_tensor(out=ob, in0=u, in1=u1[:, t0:t0 + 512], op=ALU.mult),
                             reads=[ku, k_u1], writes=[kob])
                        S.dma("sp", ZT[j * 128:(j + 1) * 128, t0:t0 + n], ob, reads=[kob], writes=["ZT"])
                        yield
                        ps2, kp2 = ps_next()
                        ps2v = ps2.bitcast(BF16)

                        def trz(e, ob=ob, ps2v=ps2v):
                            last = None
                            for tb in range(4):
                                last = e.transpose(ps2v[:, tb * 128:(tb + 1) * 128], ob[:, tb * 128:(tb + 1) * 128], ident_b)
                            return last
                        S.op("pe", trz, reads=[kob, k_ident], writes=[kp2])
                        zt, kzt = ztr.next()
                        S.op("act", lambda e, zt=zt, ps2v=ps2v: e.activation(
                            out=zt, in_=ps2v[:, 0:512].rearrange("p (a b) -> p a b", b=128), func=AF.Copy), reads=[kp2], writes=[kzt])
                        S.dma("sp", Zt[t0:t0 + 512, j * 128:(j + 1) * 128].rearrange("(a p) c -> p a c", p=128), zt,
                              reads=[kzt], writes=["Zt"])
                elif typ in ("q", "k", "gv"):
                    ci = {"q": 0, "k": 16, "gv": 32}[typ] + j
                    u, ku = conv_epilogue(ps, kp, n, rw, gdnconvT[:, ci, :], k_gdc)
                    S.op("act", lambda e, u=u, n=n: e.activation(out=u[:, 0:n], in_=u[:, 0:n], func=AF.Silu), reads=[ku], writes=[ku])
                    ob, kob = obr.next()
                    dstT = {"q": QT, "k": KT, "gv": VT}[typ]
                    if typ == "gv":
                        S.op("dve", lambda e, ob=ob, u=u, n=n: e.tensor_copy(out=ob[:, 0:n], in_=u[:, 0:n]), reads=[ku], writes=[kob])
                    else:
                        sqb, ksqb = sqr.next()
                        S.op("dve", lambda e, sqb=sqb, u=u, n=n: e.tensor_tensor(out=sqb[:, 0:n], in0=u[:, 0:n], in1=u[:, 0:n], op=ALU.mult),
                             reads=[ku], writes=[ksqb])
                        yield
                        t2, kt2 = t2r.next()
                        ps2, kp2 = ps_next()
                        S.op("pe", lambda e, ps2=ps2, sqb=sqb, n=n: e.matmul(ps2[:, 0:n], ones_b, sqb[:, 0:n], start=True, stop=True),
                             reads=[ksqb, k_ones], writes=[kp2])
                        S.op("dve", lambda e, t2=t2, ps2=ps2, n=n: e.tensor_scalar(out=t2[:, 0:n], in0=ps2[:, 0:n], scalar1=EPS, scalar2=None, op0=ALU.add),
                             reads=[kp2], writes=[kt2])
                        S.op("dve", lambda e, t2=t2, n=n: e.reciprocal(out=t2[:, 0:n], in_=t2[:, 0:n]), reads=[kt2], writes=[kt2])
                        S.op("act", lambda e, t2=t2, n=n: e.activation(out=t2[:, 0:n], in_=t2[:, 0:n], func=AF.Sqrt), reads=[kt2], writes=[kt2])
                        ob, kob = obr.next()
                        qs = (128.0 ** -0.5) if typ == "q" else 1.0
                        S.op("dve", lambda e, ob=ob, u=u, t2=t2, n=n, qs=qs: e.scalar_tensor_tensor(
                            out=ob[:, 0:n], in0=u[:, 0:n], scalar=qs, in1=t2[:, 0:n], op0=ALU.mult, op1=ALU.mult), reads=[ku, kt2], writes=[kob])
                    S.dma("sp", dstT[j * 128:(j + 1) * 128, t0:t0 + n], ob[:, 0:n], reads=[kob], writes=[typ + "T"])
                elif typ in ("zg", "gate"):
                    ob, kob = obr.next()
                    fn = AF.Silu if typ == "zg" else AF.Sigmoid
                    S.op("act", lambda e, ob=ob, ps=ps, fn=fn: e.activation(out=ob, in_=ps[:, 0:512], func=fn), reads=[kp], writes=[kob])
                    dstT = ZGT if typ == "zg" else GT
                    S.dma("sp", dstT[j * 128:(j + 1) * 128, t0:t0 + n], ob, reads=[kob], writes=[typ + "T"])
                else:
                    of, kof = ofr.next()
                    S.op("act", lambda e, of=of, ps=ps, n=n: e.activation(out=of[0:32, 0:n], in_=ps[0:32, 0:n], func=AF.Sigmoid), reads=[kp], writes=[kof])
                    S.op("act", lambda e, of=of, ps=ps, n=n: e.activation(out=of[32:64, 0:n], in_=ps[32:64, 0:n], func=AF.Exp, bias=scalp[32:64, 1:2]),
                         reads=[kp, k_scalp], writes=[kof])
                    S.op("act", lambda e, of=of, n=n: e.activation(out=of[32:64, 0:n], in_=of[32:64, 0:n], func=AF.Ln, bias=1.0), reads=[kof], writes=[kof])
                    S.op("dve", lambda e, of=of, n=n: e.tensor_scalar(out=of[32:64, 0:n], in0=of[32:64, 0:n], scalar1=nega[32:64, 0:1], scalar2=None, op0=ALU.mult),
                         reads=[kof, k_nega], writes=[kof])
                    S.dma("sp", SCT[:, t0:t0 + n], of[:, 0:n], reads=[kof], writes=["SCT"])
                return
                yield

            pend = None
            for (t0, n) in blocks:
                g = blk(t0, n)
                alive = True
                try:
                    next(g)
                except StopIteration:
                    alive = False
                if pend is not None:
                    for _ in pend:
                        pass
                pend = g if alive else None
            if pend is not None:
                for _ in pend:
                    pass

        plan = list(PLAN)
        import os
        if os.environ.get("K_PLAN_TEST") == "gdn":
            plan = [("scal", 0)] + [(t, j) for t in ("q", "k", "gv") for j in range(16)]
        elif os.environ.get("K_PLAN_TEST") == "hy":
            plan = [pp for j in (0, 7) for pp in (("x1", j), ("hv", j))] + [("x0", 0), ("x0", 7)]
        elif os.environ.get("K_PLAN_TEST"):
            plan = [("scal", 0), ("x1", 1), ("hv", 1), ("q", 2), ("k", 3), ("gv", 4), ("x0", 5), ("zg", 6), ("gate", 7), ("gate", 17)]
        for (typ, j) in plan:
            do_chunk(typ, j)
        for nm in ("X0T", "ZT", "Zt", "QT", "KT", "VT", "ZGT", "SCT", "GT"):
            if nm in dbg:
                pass
        if upto <= 3:
            S.barrier()
            S.emit(); return nc


        def phase_d():
            S.barrier(); A.reset(PERSIST)
            H = HEADS
            HH = 8
            cmask, k_cm = A.alloc([64, 4, 64], F32, "cmask")
            S.dma("sp", cmask, cmask_d, writes=[k_cm])
            cm2, k_cm2 = A.alloc([64, 5, 64], F32, "cm2")
            S.dma("sp", cm2, cm2_d, writes=[k_cm2])
            cmask_b, k_cmb = A.alloc([64, 4, 64], BF16, "cmask_b")
            cm2_b, k_cm2b = A.alloc([64, 5, 64], BF16, "cm2_b")
            S.op("dve", lambda e: e.tensor_copy(out=cm2_b, in_=cm2), reads=[k_cm2], writes=[k_cm2b])
            S.op("dve", lambda e: e.tensor_copy(out=cmask_b, in_=cmask), reads=[k_cm], writes=[k_cmb])
            qr = Ring(A, 2, [128, H, 256], BF16, "qg"); kr = Ring(A, 2, [128, H, 256], BF16, "kg"); vr = Ring(A, 2, [128, H, 256], BF16, "vg")
            scr = Ring(A, 2, [64, 256], F32, "scg")
            identb64 = ident_b[0:64, 0:64].unsqueeze(1).broadcast_to([64, HH, 64])
            identf64 = ident_f[0:64, 0:64].unsqueeze(1).broadcast_to([64, HH, 64])
            QTv = QT.rearrange("(h p) t -> p h t", p=128); KTv = KT.rearrange("(h p) t -> p h t", p=128); VTv = VT.rearrange("(h p) t -> p h t", p=128)

            def al(shape, dt, nm):
                return A.alloc(shape, dt, nm)

            class TS:
                pass
            halves = []
            for hh in range(2):
                T_ = TS()
                T_.S = al([128, HH, 128], F32, "S"); T_.Sb = al([128, HH, 128], BF16, "Sb")
                T_.sct = al([64, 64], F32, "sct"); T_.sm = al([64, 8, HH], F32, "sm")
                T_.ktok = al([64, HH, 128], BF16, "ktok"); T_.vtok = al([64, HH, 128], BF16, "vtok")
                T_.Xg = al([64, HH, 64], F32, "Xg"); T_.Xb = al([64, HH, 64], F32, "Xb")
                T_.Y = al([64, HH, 64], F32, "Y"); T_.EA = al([64, HH, 64], F32, "EA"); T_.EB = al([64, HH, 64], F32, "EB")
                T_.tmpM = al([64, HH, 64], BF16, "tmpM"); T_.tmpB = al([64, HH, 64], BF16, "tmpB"); T_.tmpQ = al([64, HH, 64], BF16, "tmpQ")
                T_.bbc = al([64, HH, 64], BF16, "bbc")
                T_.P0 = al([64, HH, 64], BF16, "P0"); T_.P1 = al([64, HH, 64], BF16, "P1")
                T_.PT0 = al([64, HH, 64], BF16, "PT0"); T_.PT1 = al([64, HH, 64], BF16, "PT1")
                T_.TT = al([64, HH, 64], BF16, "TT"); T_.T = al([64, HH, 64], BF16, "T")
                T_.Am = al([64, HH, 64], BF16, "Am"); T_.ATm = al([64, HH, 64], BF16, "ATm")
                T_.eGbc = al([128, HH, 64], F32, "eGbc")
                T_.vb = al([64, HH, 128], BF16, "vb"); T_.kbg = al([64, HH, 128], BF16, "kbg")
                T_.vnew = al([64, HH, 128], BF16, "vnew")
                T_.slot = []
                for sl in range(2):
                    d = dict(wT=al([128, HH, 64], BF16, "wT"), uu=al([64, HH, 128], F32, "uu"), qdT=al([128, HH, 64], BF16, "qdT"),
                             QKT=al([64, HH, 64], BF16, "QKT"), kdec=al([64, HH, 128], BF16, "kdec"), egl=al([128, HH], F32, "egl"))
                    T_.slot.append(d)
                T_.orr = Ring(A, 2, [64, HH, 128], F32, "o")
                halves.append(T_)

            def bc_s(ap):
                return ap.unsqueeze(2).broadcast_to([64, HH, 64])

            def bc_e(ap):
                return ap.unsqueeze(2).broadcast_to([64, HH, 128])

            def mk(i):
                return cmask[:, i, :].unsqueeze(1).broadcast_to([64, HH, 64])

            def mkb(i):
                return cmask_b[:, i, :].unsqueeze(1).broadcast_to([64, HH, 64])

            def m2(i):
                return cm2_b[:, i, :].unsqueeze(1).broadcast_to([64, HH, 64])

            def hm(psx):
                return psx[0:64, :].rearrange("p (h s) -> p h s", h=HH)

            def hm128(psx):
                return psx[0:64, :].rearrange("p (h d) -> p h d", h=HH)

            def run_dir(dr):
                mA_s, mB_i, mB_s, cumi = ((1, 2, 3, 2) if dr == 0 else (3, 0, 1, 0))
                last_t = 63 if dr == 0 else 0
                for T_ in halves:
                    S.op("dve", lambda e, T_=T_: e.memset(T_.S[0], 0.0), reads=[T_.S[1]], writes=[T_.S[1]])
                    S.op("dve", lambda e, T_=T_: e.memset(T_.Sb[0], 0.0), reads=[T_.Sb[1]], writes=[T_.Sb[1]])
                if dr == 0:
                    groups = [L // 256] + list(range(NOWN // 256))
                else:
                    groups = [L // 256] + list(range(L // 256 - 1, -1, -1))

                def chunk(hh, gi, ci, need_o, g0, qg, kqg, kg, kkg, vg, kvg, scg, kscg, slot):
                    T_ = halves[hh]
                    h0 = hh * HH
                    c0 = ci * 64
                    (Sst, k_S), (Sb, k_Sb), (sct, k_sct), (sm, k_sm) = T_.S, T_.Sb, T_.sct, T_.sm
                    (ktok, k_ktok), (vtok, k_vtok), (Xg, k_Xg), (Xb, k_Xb) = T_.ktok, T_.vtok, T_.Xg, T_.Xb
                    (Y, k_Y), (EA, k_EA), (EB, k_EB), (tmpM, k_tmpM) = T_.Y, T_.EA, T_.EB, T_.tmpM
                    (tmpB, k_tmpB), (tmpQ, k_tmpQ), (bbc, k_bbc) = T_.tmpB, T_.tmpQ, T_.bbc
                    (P0, kP0), (P1, kP1), (PT0, kPT0), (PT1, kPT1) = T_.P0, T_.P1, T_.PT0, T_.PT1
                    (TTm, k_TT), (Tm, k_T), (Am, k_Am), (ATm, k_ATm) = T_.TT, T_.T, T_.Am, T_.ATm
                    (eGbc, k_eGbc), (vb, k_vb), (kbg, k_kbg), (vnew, k_vnew) = T_.eGbc, T_.vb, T_.kbg, T_.vnew
                    sd = T_.slot[slot]
                    (wT, k_wT), (uu, k_uu), (qdT, k_qdT), (QKT, k_QKT), (kdec, k_kdec), (egl, k_egl) = (
                        sd["wT"], sd["uu"], sd["qdT"], sd["QKT"], sd["kdec"], sd["egl"])

                    def mmset(lhs, klhs, rhs, krhs):
                        pX, kpX = ps_next()

                        def f(e):
                            last = None
                            for h in range(HH):
                                last = e.matmul(pX[0:64, h * 64:(h + 1) * 64], lhs[:, h, :], rhs[:, h, :], start=True, stop=True)
                            return last
                        S.op("pe", f, reads=[klhs, krhs], writes=[kpX])
                        return pX, [kpX]
                    ps, kp = ps_next()
                    S.op("pe", lambda e: e.transpose(ps[0:64, 0:64], scg[:, c0:c0 + 64], ident_f[0:64, 0:64]), reads=[kscg, k_identf], writes=[kp])
                    S.op("act", lambda e: e.activation(out=sct, in_=ps[0:64, 0:64], func=AF.Copy), reads=[kp], writes=[k_sct])
                    beta = sct[:, dr * 16 + h0:dr * 16 + h0 + HH]; gg = sct[:, 32 + dr * 16 + h0:32 + dr * 16 + h0 + HH]
                    yield "p"
                    ps2, kp2 = ps_next()
                    S.op("pe", lambda e: e.matmul(ps2[0:64, 0:HH], cmask[:, cumi, :], gg, start=True, stop=True), reads=[k_sct, k_cm], writes=[kp2])
                    S.op("act", lambda e: e.activation(out=sm[:, 0, :], in_=ps2[0:64, 0:HH], func=AF.Copy), reads=[kp2], writes=[k_sm])
                    yield "p"
                    S.op("dve", lambda e: e.tensor_tensor(out=Xg, in0=identf64, in1=bc_s(sm[:, 0, :]), op=ALU.mult), reads=[k_sm, k_identf], writes=[k_Xg])
                    S.op("pool", lambda e: e.tensor_tensor(out=Xb, in0=identf64, in1=bc_s(beta), op=ALU.mult), reads=[k_sct, k_identf], writes=[k_Xb])
                    pG, kpG1 = ps_next(); kpG = [kpG1]
                    pGv = pG.rearrange("p (h s) -> p h s", h=HH)
                    S.op("pe", lambda e: e.matmul(pG[:, 0:512], ones_f[0:64, :], Xg, start=True, stop=True), reads=[k_Xg, k_onesf], writes=kpG)
                    pB, kpB1 = ps_next(); kpB = [kpB1]
                    pBv = pB.rearrange("p (h s) -> p h s", h=HH)
                    S.op("pe", lambda e: e.matmul(pB[0:64, 0:512], ones_f[0:64, 0:64], Xb, start=True, stop=True), reads=[k_Xb, k_onesf], writes=kpB)
                    S.op("dve", lambda e: e.tensor_tensor(out=Y, in0=pGv[0:64], in1=bc_s(sm[:, 0, :]), op=ALU.subtract), reads=kpG + [k_sm], writes=[k_Y])
                    S.op("act", lambda e: e.activation(out=bbc, in_=pBv[0:64], func=AF.Copy), reads=kpB, writes=[k_bbc])
                    S.op("dve", lambda e: e.tensor_tensor(out=sm[:, 3, :], in0=pGv[0:64, :, last_t], in1=sm[:, 0, :], op=ALU.subtract), reads=kpG + [k_sm], writes=[k_sm])
                    S.op("dve", lambda e: e.tensor_scalar(out=egl, in0=pGv[:, :, last_t], scalar1=-80.0, scalar2=None, op0=ALU.max), reads=kpG, writes=[k_egl])
                    if need_o:
                        S.op("dve", lambda e: e.tensor_scalar(out=eGbc, in0=pGv, scalar1=-80.0, scalar2=None, op0=ALU.max), reads=kpG, writes=[k_eGbc])
                    yield "p"
                    S.op("dve", lambda e: e.scalar_tensor_tensor(out=EA, in0=Y, scalar=80.0, in1=mk(mA_s), op0=ALU.min, op1=ALU.mult), reads=[k_Y, k_cm], writes=[k_EA])
                    S.op("act", lambda e: e.activation(out=EA, in_=EA, func=AF.Exp, scale=-1.0), reads=[k_EA], writes=[k_EA])
                    S.op("dve", lambda e: e.scalar_tensor_tensor(out=EB, in0=Y, scalar=-80.0, in1=mk(mB_i), op0=ALU.max, op1=ALU.mult), reads=[k_Y, k_cm], writes=[k_EB])
                    S.op("act", lambda e: e.activation(out=EB, in_=EB, func=AF.Exp), reads=[k_EB], writes=[k_EB])
                    yield "p"
                    S.op("dve", lambda e: e.tensor_scalar(out=sm[:, 1, :], in0=sm[:, 0, :], scalar1=-80.0, scalar2=None, op0=ALU.max), reads=[k_sm], writes=[k_sm])
                    S.op("act", lambda e: e.activation(out=sm[:, 1, :], in_=sm[:, 1, :], func=AF.Exp), reads=[k_sm], writes=[k_sm])
                    S.op("dve", lambda e: e.tensor_tensor(out=sm[:, 2, :], in0=sm[:, 1, :], in1=beta, op=ALU.mult), reads=[k_sm, k_sct], writes=[k_sm])
                    S.op("dve", lambda e: e.tensor_scalar(out=sm[:, 3, :], in0=sm[:, 3, :], scalar1=-80.0, scalar2=None, op0=ALU.max), reads=[k_sm], writes=[k_sm])
                    S.op("act", lambda e: e.activation(out=sm[:, 3, :], in_=sm[:, 3, :], func=AF.Exp), reads=[k_sm], writes=[k_sm])
                    S.op("act", lambda e: e.activation(out=egl, in_=egl, func=AF.Exp), reads=[k_egl], writes=[k_egl])
                    if need_o:
                        S.op("act", lambda e: e.activation(out=eGbc, in_=eGbc, func=AF.Exp), reads=[k_eGbc], writes=[k_eGbc])
                        S.op("pool", lambda e: e.tensor_tensor(out=qdT, in0=qg[:, h0:h0 + HH, c0:c0 + 64], in1=eGbc, op=ALU.mult), reads=[kqg, k_eGbc], writes=[k_qdT])
                    yield "p"
                    for (src, ksrc, dst, kdst) in ((kg, kkg, ktok, k_ktok), (vg, kvg, vtok, k_vtok)):
                        pT, kpT = ps_next()
                        pTv = pT.bitcast(BF16)

                        def trk(e, src=src, pTv=pTv):
                            last = None
                            for h in range(HH):
                                last = e.transpose(pTv[0:64, h * 128:(h + 1) * 128], src[:, h0 + h, c0:c0 + 64], ident_b)
                            return last
                        S.op("pe", trk, reads=[ksrc, k_ident], writes=[kpT])
                        S.op("act", lambda e, dst=dst, pTv=pTv: e.activation(out=dst, in_=pTv[0:64, :].rearrange("p (h d) -> p h d", h=HH), func=AF.Copy),
                             reads=[kpT], writes=[kdst])
                    pK, kpK1 = ps_next(); kpK = [kpK1]
                    pKv = pK.rearrange("p (h s) -> p h s", h=HH)

                    def mmK(e):
                        last = None
                        for h in range(HH):
                            last = e.matmul(pK[0:64, h * 64:(h + 1) * 64], kg[:, h0 + h, c0:c0 + 64], kg[:, h0 + h, c0:c0 + 64], start=True, stop=True)
                        return last
                    S.op("pe", mmK, reads=[kkg], writes=kpK)
                    S.op("dve", lambda e: e.tensor_tensor(out=tmpM, in0=pKv[0:64], in1=EA, op=ALU.mult), reads=kpK + [k_EA], writes=[k_tmpM])
                    S.op("dve", lambda e: e.tensor_tensor(out=tmpB, in0=pKv[0:64], in1=EB, op=ALU.mult), reads=kpK + [k_EB], writes=[k_tmpB])
                    if need_o:
                        pQ, kpQ1 = ps_next(); kpQ = [kpQ1]
                        pQv = pQ.rearrange("p (h s) -> p h s", h=HH)

                        def mmQ(e):
                            last = None
                            for h in range(HH):
                                last = e.matmul(pQ[0:64, h * 64:(h + 1) * 64], kg[:, h0 + h, c0:c0 + 64], qg[:, h0 + h, c0:c0 + 64], start=True, stop=True)
                            return last
                        S.op("pe", mmQ, reads=[kkg, kqg], writes=kpQ)
                        S.op("dve", lambda e: e.tensor_tensor(out=tmpQ, in0=pQv[0:64], in1=EB, op=ALU.mult), reads=kpQ + [k_EB], writes=[k_tmpQ])
                    yield "p"
                    S.op("dve", lambda e: e.tensor_tensor(out=tmpM, in0=tmpM, in1=bc_s(beta), op=ALU.mult), reads=[k_tmpM, k_sct], writes=[k_tmpM])
                    S.op("pool", lambda e: e.tensor_tensor(out=Am, in0=tmpM, in1=mkb(mA_s), op=ALU.mult), reads=[k_tmpM, k_cmb], writes=[k_Am])
                    S.op("dve", lambda e: e.tensor_tensor(out=tmpB, in0=tmpB, in1=bbc, op=ALU.mult), reads=[k_tmpB, k_bbc], writes=[k_tmpB])
                    S.op("pool", lambda e: e.tensor_tensor(out=ATm, in0=tmpB, in1=mkb(mB_s), op=ALU.mult), reads=[k_tmpB, k_cmb], writes=[k_ATm])
                    if need_o:
                        S.op("pool", lambda e: e.tensor_tensor(out=QKT, in0=tmpQ, in1=mkb(mB_i), op=ALU.mult), reads=[k_tmpQ, k_cmb], writes=[k_QKT])
                    yield "p"
                    S.op("pool", lambda e: e.tensor_tensor(out=P0, in0=Am, in1=m2(4), op=ALU.mult), reads=[k_Am, k_cm2b], writes=[kP0])
                    S.op("pool", lambda e: e.tensor_tensor(out=PT0, in0=ATm, in1=m2(4), op=ALU.mult), reads=[k_ATm, k_cm2b], writes=[kPT0])
                    S.op("pool", lambda e: e.tensor_tensor(out=Tm, in0=P0, in1=identb64, op=ALU.add), reads=[kP0, k_ident], writes=[k_T])
                    S.op("pool", lambda e: e.tensor_tensor(out=TTm, in0=PT0, in1=identb64, op=ALU.add), reads=[kPT0, k_ident], writes=[k_TT])
                    p1, kp1 = mmset(PT0, kPT0, P0, kP0)
                    S.op("act", lambda e: e.activation(out=P1, in_=hm(p1), func=AF.Copy), reads=kp1, writes=[kP1])
                    p2_, kp2_ = mmset(P0, kP0, PT0, kPT0)
                    S.op("act", lambda e: e.activation(out=PT1, in_=hm(p2_), func=AF.Copy), reads=kp2_, writes=[kPT1])
                    yield "p"
                    p3, kp3 = mmset(TTm, k_TT, P1, kP1)
                    p4, kp4 = mmset(P1, kP1, TTm, k_TT)
                    S.op("dve", lambda e: e.tensor_tensor(out=Tm, in0=Tm, in1=hm(p3), op=ALU.add), reads=kp3 + [k_T], writes=[k_T])
                    S.op("dve", lambda e: e.tensor_tensor(out=TTm, in0=TTm, in1=hm(p4), op=ALU.add), reads=kp4 + [k_TT], writes=[k_TT])
                    p5, kp5 = mmset(PT1, kPT1, P1, kP1)
                    S.op("act", lambda e: e.activation(out=P0, in_=hm(p5), func=AF.Copy), reads=kp5, writes=[kP0])
                    yield "p"
                    p6, kp6 = mmset(TTm, k_TT, P0, kP0)
                    p7, kp7 = mmset(P0, kP0, TTm, k_TT)
                    S.op("dve", lambda e: e.tensor_tensor(out=Tm, in0=Tm, in1=hm(p6), op=ALU.add), reads=kp6 + [k_T], writes=[k_T])
                    S.op("dve", lambda e: e.tensor_tensor(out=TTm, in0=TTm, in1=hm(p7), op=ALU.add), reads=kp7 + [k_TT], writes=[k_TT])
                    yield "p"
                    for lv in range(3):
                        S.op("pool", lambda e, lv=lv: e.tensor_tensor(out=PT0, in0=ATm, in1=m2(1 + lv), op=ALU.mult), reads=[k_ATm, k_cm2b], writes=[kPT0])
                        pX, kpX = mmset(PT0, kPT0, Tm, k_T)
                        S.op("act", lambda e, pX=pX: e.activation(out=P1, in_=hm(pX), func=AF.Copy), reads=kpX, writes=[kP1])
                        yield "p"
                        p8, kp8 = mmset(P1, kP1, TTm, k_TT)
                        if lv < 2:
                            p9, kp9 = mmset(TTm, k_TT, P1, kP1)
                            S.op("dve", lambda e, p9=p9: e.tensor_tensor(out=Tm, in0=Tm, in1=hm(p9), op=ALU.subtract), reads=kp9 + [k_T], writes=[k_T])
                        S.op("dve", lambda e, p8=p8: e.tensor_tensor(out=TTm, in0=TTm, in1=hm(p8), op=ALU.subtract), reads=kp8 + [k_TT], writes=[k_TT])
                        yield "p"
                    S.op("pool", lambda e: e.tensor_tensor(out=vb, in0=vtok, in1=bc_e(beta), op=ALU.mult), reads=[k_vtok, k_sct], writes=[k_vb])
                    S.op("pool", lambda e: e.tensor_tensor(out=kbg, in0=ktok, in1=bc_e(sm[:, 2, :]), op=ALU.mult), reads=[k_ktok, k_sm], writes=[k_kbg])
                    S.op("pool", lambda e: e.tensor_tensor(out=kdec, in0=ktok, in1=bc_e(sm[:, 3, :]), op=ALU.mult), reads=[k_ktok, k_sm], writes=[k_kdec])
                    pU, kpU = ps_multi(2)

                    def mmU(e):
                        last = None
                        for h in range(HH):
                            last = e.matmul(pU[0:64, h * 128:(h + 1) * 128], TTm[:, h, :], vb[:, h, :], start=True, stop=True)
                        return last
                    S.op("pe", mmU, reads=[k_TT, k_vb], writes=kpU)
                    S.op("act", lambda e: e.activation(out=uu, in_=hm128(pU), func=AF.Copy), reads=kpU, writes=[k_uu])
                    pW, kpW1 = ps_next(); kpW = [kpW1]

                    def mmW(e):
                        last = None
                        for h in range(HH):
                            last = e.matmul(pW[:, h * 64:(h + 1) * 64], kbg[:, h, :], TTm[:, h, :], start=True, stop=True)
                        return last
                    S.op("pe", mmW, reads=[k_kbg, k_TT], writes=kpW)
                    S.op("act", lambda e: e.activation(out=wT, in_=pW.rearrange("p (h s) -> p h s", h=HH), func=AF.Copy), reads=kpW, writes=[k_wT])
                    yield "scan"
                    pV, kpV = ps_multi(2)

                    def mmV(e):
                        last = None
                        for h in range(HH):
                            last = e.matmul(pV[0:64, h * 128:(h + 1) * 128], wT[:, h, :], Sb[:, h, :], start=True, stop=True)
                        return last
                    S.op("pe", mmV, reads=[k_wT, k_Sb], writes=kpV)
                    S.op("dve", lambda e: e.tensor_tensor(out=vnew, in0=uu, in1=hm128(pV), op=ALU.subtract), reads=kpV + [k_uu], writes=[k_vnew])
                    yield "s"
                    if need_o:
                        pO, kpO = ps_multi(2)

                        def mmO(e):
                            last = None
                            for h in range(HH):
                                e.matmul(pO[0:64, h * 128:(h + 1) * 128], qdT[:, h, :], Sb[:, h, :], start=True, stop=False)
                                last = e.matmul(pO[0:64, h * 128:(h + 1) * 128], QKT[:, h, :], vnew[:, h, :], start=False, stop=True)
                            return last
                        S.op("pe", mmO, reads=[k_qdT, k_Sb, k_QKT, k_vnew], writes=kpO)
                        ot, kot = T_.orr.next()
                        S.op("act", lambda e: e.activation(out=ot, in_=hm128(pO), func=AF.Copy), reads=kpO, writes=[kot])
                        Odst = Od0 if dr == 0 else Od1
                        S.dma("sp", Odst[g0 + c0:g0 + c0 + 64, h0 * 128:(h0 + HH) * 128], ot.rearrange("p h d -> p (h d)"), reads=[kot], writes=["Od%d_%d" % (dr, hh)])
                    pS, kpS = ps_multi(2)

                    def mmS(e):
                        last = None
                        for h in range(HH):
                            last = e.matmul(pS[:, h * 128:(h + 1) * 128], kdec[:, h, :], vnew[:, h, :], start=True, stop=True)
                        return last
                    S.op("pe", mmS, reads=[k_kdec, k_vnew], writes=kpS)
                    S.op("dve", lambda e: e.tensor_tensor(out=Sst, in0=Sst, in1=egl.unsqueeze(2).broadcast_to([128, HH, 128]), op=ALU.mult), reads=[k_S, k_egl], writes=[k_S])
                    S.op("dve", lambda e: e.tensor_tensor(out=Sst, in0=Sst, in1=pS.rearrange("p (h d) -> p h d", h=HH), op=ALU.add), reads=kpS + [k_S], writes=[k_S])
                    S.op("act", lambda e: e.activation(out=Sb, in_=Sst, func=AF.Copy), reads=[k_S], writes=[k_Sb])
                    yield "done"

                def advance(gens, until):
                    live = list(gens)
                    while live:
                        for g in list(live):
                            try:
                                v = next(g)
                            except StopIteration:
                                live.remove(g); continue
                            if v == until:
                                live.remove(g)

                pending = None
                nchunk = 0
                for gi in groups:
                    g0 = gi * 256
                    need_o = gi < NOWN // 256
                    qg, kqg = qr.next(); kg, kkg = kr.next(); vg, kvg = vr.next(); scg, kscg = scr.next()
                    if need_o:
                        S.dma("sp", qg, QTv[:, :, g0:g0 + 256], reads=["qT"], writes=[kqg])
                    S.dma("sp", kg, KTv[:, :, g0:g0 + 256], reads=["kT"], writes=[kkg])
                    S.dma("sp", vg, VTv[:, :, g0:g0 + 256], reads=["gvT"], writes=[kvg])
                    S.dma("sp", scg, SCT[:, g0:g0 + 256], reads=["SCT"], writes=[kscg])
                    for ci in (range(4) if dr == 0 else range(3, -1, -1)):
                        gens = [chunk(hh, gi, ci, need_o, g0, qg, kqg, kg, kkg, vg, kvg, scg, kscg, nchunk % 2) for hh in range(2)]
                        advance(gens, "scan")
                        if pending is not None:
                            advance(pending, "done")
                        pending = gens
                        nchunk += 1
                advance(pending, "done")

            for dr_ in range(2):
                run_dir(dr_)

        if "D" in phases:
            phase_d()


        def phase_e():
            CG = 512
            NG = D // CG

            def load_consts(names):
                out = {}
                for nm, src, shp in names:
                    t, k = A.alloc(shp, BF16, nm)
                    S.dma("poolq", t, src, writes=[k])
                    out[nm] = (t, k)
                return out

            S.barrier(); A.reset(PERSIST)
            zf, k_zf = A.alloc([33, N_], F32, "zf")
            fw1, k_fw1 = A.alloc([33, 64], F32, "fw1"); fw2, k_fw2 = A.alloc([64, 64], F32, "fw2"); fw3, k_fw3 = A.alloc([64, 64], F32, "fw3")
            fsm, k_fsm = A.alloc([64, 8], F32, "fsm")
            fout, k_fout = A.alloc([64, 2 * D], F32, "fout")
            dl, k_dl = A.alloc([128, D], F32, "deltas")
            tau, k_tau = A.alloc([128, 48], F32, "tau")
            hr = [A.alloc([64, 512], F32, "hmlp%d" % i) for i in range(3)]
            decr = Ring(A, 2, [128, 512], F32, "dec")
            fm, k_fm = A.alloc([64, 512], F32, "fm")
            hcr = Ring(A, 3, [128, 512], BF16, "hc")
            for (t, k, src) in ((zf, k_zf, zf_d), (fw1, k_fw1, fw1_d), (fw2, k_fw2, fw2_d), (fw3, k_fw3, fw3_d),
                                (fsm[:, 0:4], k_fsm, fsm_d), (fout, k_fout, fout_d), (dl, k_dl, deltas_d), (tau, k_tau, tau_d)):
                S.dma("sp", t, src, writes=[k])
            for i in range(3):
                S.op("dve", lambda e, i=i: e.tensor_tensor(out=fsm[:, 4 + i:5 + i], in0=fsm[:, i:i + 1], in1=fsm[:, 3:4], op=ALU.mult), reads=[k_fsm], writes=[k_fsm])
            for jb in range(N_ // 512):
                j0 = jb * 512
                prev, kprev = zf[:, j0:j0 + 512], k_zf
                for li, (w, kw) in enumerate(((fw1, k_fw1), (fw2, k_fw2), (fw3, k_fw3))):
                    ps, kp = ps_next()
                    S.op("pe", lambda e, ps=ps, w=w, prev=prev: e.matmul(ps[0:64, :], w, prev, start=True, stop=True), reads=[kw, kprev], writes=[kp])
                    ht, kht = hr[li]
                    S.op("act", lambda e, ps=ps, ht=ht, li=li: e.activation(out=ht, in_=ps[0:64, :], func=AF.Identity, scale=fsm[:, 3:4], bias=fsm[:, 4 + li:5 + li]),
                         reads=[kp, k_fsm], writes=[kht])
                    for rep in range(3):
                        S.op("dve", lambda e, ht=ht: e.tensor_scalar(out=fm, in0=ht, scalar1=math.pi, scalar2=-2.0 * math.pi, op0=ALU.is_gt, op1=ALU.mult), reads=[kht], writes=[k_fm])
                        S.op("dve", lambda e, ht=ht: e.tensor_tensor(out=ht, in0=ht, in1=fm, op=ALU.add), reads=[kht, k_fm], writes=[kht])
                        S.op("dve", lambda e, ht=ht: e.tensor_scalar(out=fm, in0=ht, scalar1=-math.pi, scalar2=2.0 * math.pi, op0=ALU.is_lt, op1=ALU.mult), reads=[kht], writes=[k_fm])
                        S.op("dve", lambda e, ht=ht: e.tensor_tensor(out=ht, in0=ht, in1=fm, op=ALU.add), reads=[kht, k_fm], writes=[kht])
                    S.op("act", lambda e, ht=ht: e.activation(out=ht, in_=ht, func=AF.Sin), reads=[kht], writes=[kht])
                    prev, kprev = ht, kht
                for rt in range(4):
                    jt = jb * 4 + rt
                    fcol = 0 if jt < NOWN // 128 else D
                    for cc in range(4):
                        ps, kp = ps_next()
                        S.op("pe", lambda e, ps=ps, prev=prev, rt=rt, cc=cc, fcol=fcol: e.matmul(
                            ps[:, :], prev[:, rt * 128:(rt + 1) * 128], fout[:, fcol + cc * 512:fcol + (cc + 1) * 512], start=True, stop=True),
                            reads=[kprev, k_fout], writes=[kp])
                        dec, kdec = decr.next()
                        S.op("act", lambda e, dec=dec, cc=cc, jt=jt: e.activation(out=dec, in_=dl[:, cc * 512:(cc + 1) * 512], func=AF.Exp, scale=tau[:, jt:jt + 1]),
                             reads=[k_dl, k_tau], writes=[kdec])
                        hc, khc = hcr.next()
                        S.op("dve", lambda e, hc=hc, ps=ps, dec=dec: e.tensor_tensor(out=hc, in0=ps[:, :], in1=dec, op=ALU.mult), reads=[kp, kdec], writes=[khc])
                        S.dma("sp", Hc[jt * 128:(jt + 1) * 128, cc * 512:(cc + 1) * 512], hc, reads=[khc], writes=["Hc"])

            def stage1(src, nm_src, is_filter):
                S.barrier(); A.reset(PERSIST)
                W1, k_W1 = A.alloc([128, 48, 2, 128], BF16, "W1")
                S.dma("poolq", W1.rearrange("p a b c -> p (a b c)"), W1_d, writes=[k_W1])
                ztr = Ring(A, 2, [128, 48, CG], BF16, "zt")
                aor = Ring(A, 4, [128, 2, CG], BF16, "ao")
                K1 = 128 if is_filter else 86
                if not is_filter:
                    for (zt, kz) in ztr.items:
                        S.op("pool", lambda e, zt=zt: e.memset(zt[64:128], 0.0), writes=[kz])
                for cg in range(NG):
                    c0 = cg * CG
                    zt, kz = ztr.next()
                    if is_filter:
                        S.dma("sp", zt, src[:, c0:c0 + CG].rearrange("(a b) c -> a b c", b=48), reads=[nm_src], writes=[kz])
                    else:
                        S.dma("sp", zt[0:85], src[0:4080, c0:c0 + CG].rearrange("(a b) c -> a b c", b=48), reads=[nm_src], writes=[kz])
                        S.dma("sp", zt[85:86, 0:16, :], src[4080:4096, c0:c0 + CG].rearrange("(a b) c -> a b c", a=1), reads=[nm_src], writes=[kz])
                    for t2 in range(48):
                        pp, kpp = ps_multi(2)

                        def mm(e, pp=pp, zt=zt, t2=t2):
                            e.matmul(pp[:, 0:512], W1[0:K1, t2, 0, :], zt[0:K1, t2, :], start=True, stop=True)
                            return e.matmul(pp[:, 512:1024], W1[0:K1, t2, 1, :], zt[0:K1, t2, :], start=True, stop=True)
                        S.op("pe", mm, reads=[k_W1, kz], writes=kpp)
                        ao, kao = aor.next()
                        eng = "act" if t2 % 2 == 0 else "dve"
                        if eng == "act":
                            S.op("act", lambda e, ao=ao, pp=pp: e.activation(out=ao, in_=pp.rearrange("p (r c) -> p r c", r=2), func=AF.Copy), reads=kpp, writes=[kao])
                        else:
                            S.op("dve", lambda e, ao=ao, pp=pp: e.tensor_copy(out=ao, in_=pp.rearrange("p (r c) -> p r c", r=2)), reads=kpp, writes=[kao])
                        S.dma("sp", Ad[:, :, c0:c0 + CG].rearrange("f (r t) c -> f r t c", r=2)[:, :, t2, :], ao, reads=[kao], writes=["Ad"])

            def stage2_filter():
                S.barrier(); A.reset(PERSIST)
                cs = load_consts([("W2a", W2a_d, [96, 96]), ("W2b", W2b_d, [96, 96])])
                atr = Ring(A, 2, [96, 16, CG], BF16, "at")
                kor = Ring(A, 2, [96, 2, 16, CG], BF16, "ko")
                for cg in range(NG):
                    c0 = cg * CG
                    for fb in range(8):
                        at, kat = atr.next()
                        S.dma("sp", at, Ad[fb * 16:(fb + 1) * 16, :, c0:c0 + CG].rearrange("f rt c -> rt f c"), reads=["Ad"], writes=[kat])
                        ko, kko = kor.next()
                        for fi in range(16):
                            pp, kpp = ps_multi(2)

                            def mm(e, pp=pp, at=at, fi=fi):
                                e.matmul(pp[0:96, 0:512], cs["W2a"][0], at[:, fi, :], start=True, stop=True)
                                return e.matmul(pp[0:96, 512:1024], cs["W2b"][0], at[:, fi, :], start=True, stop=True)
                            S.op("pe", mm, reads=[cs["W2a"][1], cs["W2b"][1], kat], writes=kpp)
                            eng = "act" if fi % 2 == 0 else "dve"
                            if eng == "act":
                                S.op("act", lambda e, ko=ko, pp=pp, fi=fi: e.activation(out=ko[:, :, fi, :], in_=pp[0:96].rearrange("p (r c) -> p r c", r=2), func=AF.Copy), reads=kpp, writes=[kko])
                            else:
                                S.op("dve", lambda e, ko=ko, pp=pp, fi=fi: e.tensor_copy(out=ko[:, :, fi, :], in_=pp[0:96].rearrange("p (r c) -> p r c", r=2)), reads=kpp, writes=[kko])
                        for ab in range(2):
                            S.dma("sp", Kd[ab, :, fb * 16:(fb + 1) * 16, c0:c0 + CG], ko[:, ab, :, :], reads=[kko], writes=["Kd"])

            def stage2_data():
                S.barrier(); A.reset(PERSIST)
                cs = load_consts([("W2", W2_d, [96, 96]), ("L1", L1_d, [96, 96]), ("L2", L2_d, [96, 96])])
                atr = Ring(A, 2, [96, 16, CG], BF16, "at")
                kar = Ring(A, 2, [96, 16, CG], BF16, "ka"); kbr = Ring(A, 2, [96, 16, CG], BF16, "kb")
                btr = Ring(A, 2, [96, 16, CG], BF16, "bt")
                p1r = Ring(A, 3, [96, CG], BF16, "p1"); p2r = Ring(A, 3, [96, CG], BF16, "p2")
                for cg in range(NG):
                    c0 = cg * CG
                    for fb in range(8):
                        at, kat = atr.next(); ka, kka = kar.next(); kb, kkb = kbr.next(); bt, kbt = btr.next()
                        S.dma("sp", at, Ad[fb * 16:(fb + 1) * 16, :, c0:c0 + CG].rearrange("f rt c -> rt f c"), reads=["Ad"], writes=[kat])
                        S.dma("sp", ka, Kd[0, :, fb * 16:(fb + 1) * 16, c0:c0 + CG], reads=["Kd"], writes=[kka])
                        S.dma("sp", kb, Kd[1, :, fb * 16:(fb + 1) * 16, c0:c0 + CG], reads=["Kd"], writes=[kkb])
                        pend = None
                        for fi in range(16):
                            ps, kp = ps_next()
                            S.op("pe", lambda e, ps=ps, at=at, fi=fi: e.matmul(ps[0:96, :], cs["W2"][0], at[:, fi, :], start=True, stop=True), reads=[cs["W2"][1], kat], writes=[kp])
                            p1, kp1 = p1r.next(); p2, kp2 = p2r.next()
                            S.op("dve", lambda e, ps=ps, p1=p1, ka=ka, fi=fi: e.tensor_tensor(out=p1, in0=ps[0:96, :], in1=ka[:, fi, :], op=ALU.mult), reads=[kp, kka], writes=[kp1])
                            S.op("dve", lambda e, ps=ps, p2=p2, kb=kb, fi=fi: e.tensor_tensor(out=p2, in0=ps[0:96, :], in1=kb[:, fi, :], op=ALU.mult), reads=[kp, kkb], writes=[kp2])

                            def part2(p1=p1, kp1=kp1, p2=p2, kp2=kp2, fi=fi, bt=bt, kbt=kbt):
                                ps2, kps2 = ps_next()

                                def mm(e):
                                    e.matmul(ps2[0:96, :], cs["L1"][0], p1, start=True, stop=False)
                                    return e.matmul(ps2[0:96, :], cs["L2"][0], p2, start=False, stop=True)
                                S.op("pe", mm, reads=[cs["L1"][1], cs["L2"][1], kp1, kp2], writes=[kps2])
                                S.op("act", lambda e: e.activation(out=bt[:, fi, :], in_=ps2[0:96, :], func=AF.Copy), reads=[kps2], writes=[kbt])
                            if pend is not None:
                                pend()
                            pend = part2
                        pend()
                        S.dma("sp", Bd[fb * 16:(fb + 1) * 16, :, c0:c0 + CG].rearrange("f rt c -> rt f c"), bt, reads=[kbt], writes=["Bd"])

            def stage1_inv():
                S.barrier(); A.reset(PERSIST)
                Vt, k_V = A.alloc([128, 48, 2, 64], BF16, "V")
                S.dma("poolq", Vt.rearrange("p a b c -> p (a b c)"), V_d, writes=[k_V])
                b2r = Ring(A, 2, [128, 2, 8, CG], BF16, "b2")
                yor = Ring(A, 2, [64, 8, CG], BF16, "yo")
                Ydv = Yd[0:43 * 48, :].rearrange("(a b) c -> a b c", b=48)
                for cg in range(NG):
                    c0 = cg * CG
                    for tb in range(6):
                        b2, kb2 = b2r.next(); yo, kyo = yor.next()
                        for ri in range(2):
                            S.dma("sp", b2[:, ri, :, :], Bd[:, ri * 48 + tb * 8:ri * 48 + tb * 8 + 8, c0:c0 + CG], reads=["Bd"], writes=[kb2])
                        for ti in range(8):
                            t2 = tb * 8 + ti
                            ps, kp = ps_next()

                            def mm(e, ps=ps, b2=b2, t2=t2, ti=ti):
                                e.matmul(ps[0:43, :], Vt[:, t2, 0, 0:43], b2[:, 0, ti, :], start=True, stop=False)
                                return e.matmul(ps[0:43, :], Vt[:, t2, 1, 0:43], b2[:, 1, ti, :], start=False, stop=True)
                            S.op("pe", mm, reads=[k_V, kb2], writes=[kp])
                            if ti % 2 == 0:
                                S.op("act", lambda e, ps=ps, yo=yo, ti=ti: e.activation(out=yo[0:43, ti, :], in_=ps[0:43, :], func=AF.Copy), reads=[kp], writes=[kyo])
                            else:
                                S.op("dve", lambda e, ps=ps, yo=yo, ti=ti: e.tensor_copy(out=yo[0:43, ti, :], in_=ps[0:43, :]), reads=[kp], writes=[kyo])
                        S.dma("sp", Ydv[:, tb * 8:(tb + 1) * 8, c0:c0 + CG], yo[0:43], reads=[kyo], writes=["Yd"])

            phase_e0 = None
            stage1(Hc, "Hc", True)
            stage2_filter()
            stage1(Zt, "Zt", False)
            stage2_data()
            stage1_inv()

        if "E" in phases:
            phase_e()

        def phase_fgh():
            S.barrier(); A.reset(PERSIST)
            R1, k_R1 = A.alloc([128, 24576], BF16, "R1")
            GTt = R1[:, 0:16384].rearrange("p (a b) -> p a b", a=32)
            ZGt = R1[:, 16384:24576].rearrange("p (a b) -> p a b", a=16)
            actT = R1[:, 0:22528].rearrange("p (a b) -> p a b", a=44)
            hy, k_hy = A.alloc([128, 16, 512], BF16, "hy")
            gn, k_gn = A.alloc([128, 16, 512], BF16, "gn")
            mixed, k_mx = A.alloc([128, 16, 512], BF16, "mixed")
            x1T, k_x1 = A.alloc([128, 16, 512], F32, "x1T")
            rbc, k_rbc = A.alloc([128, 512], F32, "rbc")
            wr = Ring(A, 3, [128, 16, 128], BF16, "w16")
            wdr = Ring(A, 2, [128, 44, 128], BF16, "w44")
            tfr = Ring(A, 2, [128, 512], F32, "tf")
            tbr = Ring(A, 3, [128, 512], BF16, "tb")
            xtr = Ring(A, 3, [128, D], F32, "xt")
            onr = Ring(A, 1, [128, D], BF16, "on")
            ytr = Ring(A, 2, [128, 4, 128], BF16, "yt")
            st, k_st = A.alloc([128, 64], F32, "st")
            hf, k_hf = hy, k_hy
            sqT, k_sqT = gn, k_gn
            ga_a = mod[:, 32:48, 0]; sh_f = mod[:, 48:64, 0]; ga_f = mod[:, 80:96, 0]
            whv, wgv, wov, wuv, wdv = w_hy_out_d, w_gdn_out_d, w_o_d, w_up_d, w_down_d

            def proj16(wv, c0, rhs, krhs, nkc=16, ring=None):
                wt, kw = (ring or wr).next()
                S.dma("poolq", wt.rearrange("p a b -> p (a b)"), wv[c0 // 128], writes=[kw])
                ps, kp = ps_next()

                def mm(e):
                    last = None
                    for kc in range(nkc):
                        last = e.matmul(ps[:, :], wt[:, kc, :], rhs[:, kc, :], start=(kc == 0), stop=(kc == nkc - 1))
                    return last
                S.op("pe", mm, reads=[kw, krhs], writes=[kp])
                return ps, kp

            def rms_bc(srcT, ksrc):
                S.op("act", lambda e: e.activation(out=sqT, in_=srcT, func=AF.Square), reads=[ksrc], writes=[k_sqT])
                ps, kp = ps_next()

                def mm(e):
                    last = None
                    for fc in range(16):
                        last = e.matmul(ps[:, :], ones_b, sqT[:, fc, :], start=(fc == 0), stop=(fc == 15))
                    return last
                S.op("pe", mm, reads=[k_sqT, k_ones], writes=[kp])
                S.op("dve", lambda e: e.tensor_scalar(out=rbc, in0=ps[:, :], scalar1=1.0 / D, scalar2=EPS, op0=ALU.mult, op1=ALU.add), reads=[kp], writes=[k_rbc])
                S.op("dve", lambda e: e.reciprocal(out=rbc, in_=rbc), reads=[k_rbc], writes=[k_rbc])
                S.op("act", lambda e: e.activation(out=rbc, in_=rbc, func=AF.Sqrt), reads=[k_rbc], writes=[k_rbc])

            for tb in range(NOWN // 512):
                t0 = tb * 512
                S.dma("sp", GTt, GT[:, t0:t0 + 512].rearrange("(a p) t -> p a t", p=128), writes=[k_R1])
                S.dma("sp", ZGt, ZGT[:, t0:t0 + 512].rearrange("(a p) t -> p a t", p=128), writes=[k_R1])
                for cc in range(16):
                    x0t, kx0 = tbr.next(); zt, kz = tbr.next(); yt, kyt = ytr.next()
                    S.dma("sp", x0t, X0T[cc * 128:(cc + 1) * 128, t0:t0 + 512], reads=["X0T"], writes=[kx0])
                    S.dma("sp", zt, ZT[cc * 128:(cc + 1) * 128, t0:t0 + 512], reads=["ZT"], writes=[kz])
                    S.dma("sp", yt, Yd[t0:t0 + 512, cc * 128:(cc + 1) * 128].rearrange("(a p) c -> p a c", p=128), reads=["Yd"], writes=[kyt])
                    ps, kp = ps_next(); psv = ps.bitcast(BF16)

                    def trY(e, yt=yt, psv=psv):
                        last = None
                        for a in range(4):
                            last = e.transpose(psv[:, a * 128:(a + 1) * 128], yt[:, a, :], ident_b)
                        return last
                    S.op("pe", trY, reads=[kyt, k_ident], writes=[kp])
                    tf, ktf = tfr.next()
                    S.op("dve", lambda e, tf=tf, zt=zt, psv=psv, cc=cc: e.scalar_tensor_tensor(
                        out=tf, in0=zt, scalar=hybT[:, cc:cc + 1], in1=psv[:, 0:512], op0=ALU.mult, op1=ALU.add), reads=[kz, kp, k_hyb], writes=[ktf])
                    S.op("dve", lambda e, tf=tf, x0t=x0t, cc=cc: e.tensor_tensor(out=hy[:, cc, :], in0=tf, in1=x0t, op=ALU.mult), reads=[ktf, kx0], writes=[k_hy])
                for tt in range(4):
                    ot, kot = xtr.next(); on, kon = onr.next()
                    tq, ktq = xtr.next()
                    S.dma("sp", ot, Od0[t0 + tt * 128:t0 + (tt + 1) * 128, :], reads=["Od0_0", "Od0_1"], writes=[kot])
                    S.dma("sp", tq, Od1[t0 + tt * 128:t0 + (tt + 1) * 128, :], reads=["Od1_0", "Od1_1"], writes=[ktq])
                    S.op("dve", lambda e, tq=tq, ot=ot: e.tensor_tensor(out=ot, in0=ot, in1=tq, op=ALU.add), reads=[kot, ktq], writes=[kot])
                    S.op("act", lambda e, tq=tq, ot=ot: e.activation(out=tq, in_=ot, func=AF.Square), reads=[kot], writes=[ktq])
                    S.op("dve", lambda e, tq=tq: e.reduce_sum(out=st[:, 0:16], in_=tq.rearrange("p (h e) -> p h e", h=16), axis=AX.X), reads=[ktq], writes=[k_st])
                    S.op("dve", lambda e: e.tensor_scalar(out=st[:, 16:32], in0=st[:, 0:16], scalar1=1.0 / 128, scalar2=EPS, op0=ALU.mult, op1=ALU.add), reads=[k_st], writes=[k_st])
                    S.op("dve", lambda e: e.reciprocal(out=st[:, 32:48], in_=st[:, 16:32]), reads=[k_st], writes=[k_st])
                    S.op("act", lambda e: e.activation(out=st[:, 48:64], in_=st[:, 32:48], func=AF.Sqrt), reads=[k_st], writes=[k_st])
                    for h in range(16):
                        eng = "act" if h % 2 == 0 else "dve"
                        if eng == "act":
                            S.op("act", lambda e, on=on, ot=ot, h=h: e.activation(out=on[:, h * 128:(h + 1) * 128], in_=ot[:, h * 128:(h + 1) * 128],
                                                                            func=AF.Copy, scale=st[:, 48 + h:49 + h]), reads=[kot, k_st], writes=[kon])
                        else:
                            S.op("dve", lambda e, on=on, ot=ot, h=h: e.tensor_scalar(out=on[:, h * 128:(h + 1) * 128], in0=ot[:, h * 128:(h + 1) * 128],
                                                                               scalar1=st[:, 48 + h:49 + h], scalar2=None, op0=ALU.mult), reads=[kot, k_st], writes=[kon])
                    for g in range(4):
                        ps, kp = ps_next(); psv = ps.bitcast(BF16)

                        def trO(e, on=on, psv=psv, g=g):
                            last = None
                            for j in range(4):
                                h = g * 4 + j
                                last = e.transpose(psv[:, j * 128:(j + 1) * 128], on[:, h * 128:(h + 1) * 128], ident_b)
                            return last
                        S.op("pe", trO, reads=[kon, k_ident], writes=[kp])
                        for j in range(4):
                            h = g * 4 + j
                            S.op("dve", lambda e, psv=psv, j=j, h=h, tt=tt: e.scalar_tensor_tensor(
                                out=gn[:, h, tt * 128:(tt + 1) * 128], in0=psv[:, j * 128:(j + 1) * 128], scalar=gnormT[:, 0:1],
                                in1=ZGt[:, h, tt * 128:(tt + 1) * 128], op0=ALU.mult, op1=ALU.mult), reads=[kp, k_gnorm, k_R1], writes=[k_gn])
                for m in range(16):
                    ps, kp = proj16(whv, m * 128, hy, k_hy)
                    tf, ktf = tfr.next()
                    S.op("dve", lambda e, tf=tf, ps=ps, m=m: e.tensor_tensor(out=tf, in0=ps[:, :], in1=GTt[:, m, :], op=ALU.mult), reads=[kp, k_R1], writes=[ktf])
                    ps2, kp2 = proj16(wgv, m * 128, gn, k_gn)
                    tf2, ktf2 = tfr.next()
                    S.op("dve", lambda e, tf2=tf2, ps2=ps2, m=m: e.tensor_tensor(out=tf2, in0=ps2[:, :], in1=GTt[:, 16 + m, :], op=ALU.mult), reads=[kp2, k_R1], writes=[ktf2])
                    S.op("dve", lambda e, tf=tf, tf2=tf2, m=m: e.tensor_tensor(out=mixed[:, m, :], in0=tf, in1=tf2, op=ALU.add), reads=[ktf, ktf2], writes=[k_mx])
                for tt in range(4):
                    xt, kx = xtr.next()
                    S.dma("sp", xt, x_d[t0 + tt * 128:t0 + (tt + 1) * 128, :], writes=[kx])
                    for g in range(4):
                        ps, kp = ps_next()

                        def trX(e, xt=xt, ps=ps, g=g):
                            last = None
                            for j in range(4):
                                fc = g * 4 + j
                                last = e.transpose(ps[:, j * 128:(j + 1) * 128], xt[:, fc * 128:(fc + 1) * 128], ident_f)
                            return last
                        S.op("pe", trX, reads=[kx, k_identf], writes=[kp])
                        S.op("act", lambda e, ps=ps, g=g, tt=tt: e.activation(
                            out=x1T[:, g * 4:(g + 1) * 4, tt * 128:(tt + 1) * 128], in_=ps[:, :].rearrange("p (a b) -> p a b", a=4), func=AF.Copy),
                            reads=[kp], writes=[k_x1])
                for m in range(16):
                    ps, kp = proj16(wov, m * 128, mixed, k_mx)
                    S.op("dve", lambda e, ps=ps, m=m: e.scalar_tensor_tensor(out=x1T[:, m, :], in0=ps[:, :], scalar=ga_a[:, m:m + 1], in1=x1T[:, m, :],
                                                                           op0=ALU.mult, op1=ALU.add), reads=[kp, k_mod, k_x1], writes=[k_x1])
                rms_bc(x1T, k_x1)
                for fc in range(16):
                    tf, ktf = tfr.next()
                    S.op("dve", lambda e, tf=tf, fc=fc: e.scalar_tensor_tensor(out=tf, in0=x1T[:, fc, :], scalar=scale_f[:, fc:fc + 1], in1=rbc,
                                                                             op0=ALU.mult, op1=ALU.mult), reads=[k_x1, k_scf, k_rbc], writes=[ktf])
                    S.op("act", lambda e, tf=tf, fc=fc: e.activation(out=hf[:, fc, :], in_=tf, func=AF.Identity, bias=sh_f[:, fc:fc + 1]),
                         reads=[ktf, k_mod], writes=[k_hf])
                for j in range(44):
                    psg, kpg = proj16(wuv, j * 128, hf, k_hf)
                    psu, kpu = proj16(wuv, D_FF + j * 128, hf, k_hf)
                    tf, ktf = tfr.next()
                    S.op("act", lambda e, tf=tf, psg=psg: e.activation(out=tf, in_=psg[:, :], func=AF.Silu), reads=[kpg], writes=[ktf])
                    S.op("dve", lambda e, tf=tf, psu=psu, j=j: e.tensor_tensor(out=actT[:, j, :], in0=tf, in1=psu[:, :], op=ALU.mult), reads=[ktf, kpu], writes=[k_R1])
                for m in range(16):
                    ps, kp = proj16(wdv, m * 128, actT, k_R1, nkc=44, ring=wdr)
                    S.op("dve", lambda e, ps=ps, m=m: e.scalar_tensor_tensor(out=x1T[:, m, :], in0=ps[:, :], scalar=ga_f[:, m:m + 1], in1=x1T[:, m, :],
                                                                           op0=ALU.mult, op1=ALU.add), reads=[kp, k_mod, k_x1], writes=[k_x1])
                rms_bc(x1T, k_x1)
                for fc in range(16):
                    S.op("dve", lambda e, fc=fc: e.scalar_tensor_tensor(out=x1T[:, fc, :], in0=x1T[:, fc, :], scalar=nfinT[:, fc:fc + 1], in1=rbc,
                                                                      op0=ALU.mult, op1=ALU.mult), reads=[k_x1, k_nfin, k_rbc], writes=[k_x1])
                for tt in range(4):
                    xo, kxo = xtr.next()
                    for g in range(4):
                        ps, kp = ps_next()

                        def trB(e, ps=ps, g=g, tt=tt):
                            last = None
                            for j in range(4):
                                fc = g * 4 + j
                                last = e.transpose(ps[:, j * 128:(j + 1) * 128], x1T[:, fc, tt * 128:(tt + 1) * 128], ident_f)
                            return last
                        S.op("pe", trB, reads=[k_x1, k_identf], writes=[kp])
                        S.op("act", lambda e, ps=ps, xo=xo, g=g: e.activation(out=xo[:, g * 512:(g + 1) * 512], in_=ps[:, :], func=AF.Copy), reads=[kp], writes=[kxo])
                    S.final_tokens.append(S.dma("sp", out_d[t0 + tt * 128:t0 + (tt + 1) * 128, :], xo, reads=[kxo], writes=["out"]))

        if "F" in phases:
            phase_fgh()
        S.emit()
    return nc


def _fm(v, n):
    return np.ascontiguousarray(np.asarray(v, np.float32).reshape(n, 128).T)


def _hyena_consts():
    n = L
    j = np.arange(N_)
    idx = np.where(j < NOWN, j, N_ - j).astype(np.float64)
    idx[NOWN] = 0
    tt = idx / (n - 1)
    bands = 16
    w = 2.0 * np.pi * idx / n
    f = np.linspace(1e-4, bands - 1, bands)
    zf = np.concatenate([tt[None, :], np.cos(f[:, None] * w[None, :]), -np.sin(f[:, None] * w[None, :])], axis=0)
    tau = -tt.copy(); tau[NOWN] = -30.0
    deltas = np.abs(np.linspace(math.log(1e-2) / 1.5, math.log(1e-2) / 0.3, D))
    t1 = np.arange(128)[:, None, None, None]; t2 = np.arange(48)[None, :, None, None]; f1 = np.arange(128)[None, None, None, :]
    th = 2 * np.pi * (t1 * f1 / 128.0 + t2 * f1 / float(N_))
    W1 = np.concatenate([np.cos(th), -np.sin(th)], axis=2)
    a = np.arange(48)
    th2 = 2 * np.pi * np.outer(a, a) / 48.0
    c2, s2 = np.cos(th2), np.sin(th2)
    W2 = np.block([[c2, -s2], [s2, c2]])
    W2a = np.block([[c2, c2], [s2, s2]])
    W2b = np.block([[-s2, -s2], [c2, c2]])
    L1 = np.block([[c2, s2], [-s2, c2]])
    L2 = np.block([[-s2, c2], [-c2, -s2]])
    f1v = np.arange(128)[:, None, None, None]; t2v = np.arange(48)[None, :, None, None]; t1v = np.arange(64)[None, None, None, :]
    ph = 2 * np.pi * f1v * (t2v / float(N_) + t1v / 128.0)
    V = np.concatenate([np.cos(ph), -np.sin(ph)], axis=2) / float(N_)
    f32 = lambda x: np.ascontiguousarray(x, dtype=np.float32)
    return dict(zf=f32(zf), tau=f32(tau.reshape(48, 128).T), deltas=f32(np.broadcast_to(deltas[None, :], (128, D))),
                W1=f32(W1.reshape(128, -1)), Vc=f32(V.reshape(128, -1)), W2=f32(W2), W2a=f32(W2a), W2b=f32(W2b), L1=f32(L1), L2=f32(L2))


def make_in_maps(inp):
    maps = []
    ident = np.eye(128, dtype=np.float32)
    tri = np.tril(np.ones((64, 64), np.float32))
    cmask = np.ascontiguousarray(np.stack([tri, np.tril(tri, -1), tri.T, np.triu(tri.T, 1)], axis=1))
    ii = np.arange(64)
    def same(b): return (ii[:, None] // b == ii[None, :] // b).astype(np.float32)
    cm2 = np.ascontiguousarray(np.stack([same(8), same(16) - same(8), same(32) - same(16), same(64) - same(32), -same(8)], axis=1))
    hyc_consts = _hyena_consts()
    w_in0 = inp["w_in"][0]
    cols = [SEG[t] + j * 128 for (t, j) in PLAN]
    w_inc_base = _chunk_major(w_in0, cols)
    shared = dict(
        w_hy_outc=_chunk_major(inp["w_hy_out"][0], [m * 128 for m in range(16)]),
        w_gdn_outc=_chunk_major(inp["w_gdn_out"][0], [m * 128 for m in range(16)]),
        w_oc=_chunk_major(inp["w_o"][0], [m * 128 for m in range(16)]),
        w_upc=_chunk_major(inp["w_up"][0], [j * 128 for j in range(88)]),
        w_downc=_chunk_major(inp["w_down"][0], [m * 128 for m in range(16)]),
    )
    w_inc_flip = None
    cache = {}

    def _w_inc_for(flip):
        if flip not in cache:
            if not flip:
                cache[flip] = w_inc_base
            else:
                wsc = w_in0[:, 14336:14400].reshape(D, 2, 2, 16)[:, :, ::-1, :].reshape(D, 64)
                arr = w_inc_base.copy()
                arr[PLAN_INDEX[("scal", 0)]] = _chunk_major(wsc, [0])[0]
                cache[flip] = arr
        return cache[flip]

    for core in range(8):
        b, half = core // 2, core % 2
        flip = half == 1
        x = inp["x"][b]; ctx = inp["ctx"][b]
        if flip:
            x = x[::-1]; ctx = ctx[::-1]
        cT = np.stack([_fm(inp["c"][b], 16), _fm(inp["c_ctx"], 16)], axis=-1)
        w_in = inp["w_in"][0]
        w_scal = w_in[:, 14336:14400]
        a_log = inp["gdn_a_log"][0]; dtb = inp["gdn_dt_bias"][0]
        hyc = inp["hy_conv"][0]; gdc = inp["gdn_conv"][0]
        if flip:
            w_scal = w_scal.reshape(D, 2, 2, 16)[:, :, ::-1, :].reshape(D, 64)
            a_log = a_log[::-1]; dtb = dtb[::-1]
            hyc = hyc[::-1]; gdc = gdc[::-1]
        scalp = np.zeros((64, 2), np.float32)
        scalp[32:64, 0] = a_log.reshape(32); scalp[32:64, 1] = dtb.reshape(32)
        hyconvT = np.ascontiguousarray(hyc.reshape(3, 48, 128).transpose(2, 1, 0))
        gdnconvT = np.ascontiguousarray(gdc.reshape(3, 48, 128).transpose(2, 1, 0))
        maps.append(dict(
            x=np.ascontiguousarray(x), ctx=np.ascontiguousarray(ctx), cT=np.ascontiguousarray(cT),
            w_ada=inp["w_ada"][0], b_adaT=_fm(inp["b_ada"][0], 96),
            nmixT=_fm(inp["norm_mix"][0], 16), nffnT=_fm(inp["norm_ffn"][0], 16),
            w_inc=_w_inc_for(flip),
            hyconvT=hyconvT, gdnconvT=gdnconvT, scalp=scalp, ident=ident, cmask=cmask, cm2=cm2,
            fw1=inp["hy_fw1"][0], fw2=inp["hy_fw2"][0], fw3=inp["hy_fw3"][0],
            fsm=np.ascontiguousarray(np.stack([inp["hy_fb1"][0], inp["hy_fb2"][0], inp["hy_fb3"][0], inp["hy_freq"][0]], axis=1)),
            fout=(np.ascontiguousarray(np.concatenate([inp["hy_fout"][0][:, D:], inp["hy_fout"][0][:, :D]], axis=1)) if flip else inp["hy_fout"][0]),
            **hyc_consts,
            **shared,
            hybT=_fm(inp["hy_bias"][0], 16), gnormT=_fm(inp["gdn_norm"][0], 1), nfinT=_fm(inp["norm_final"], 16),
        ))
    return maps


def kernel(**inputs):
    inp = {k: np.asarray(v) for k, v in inputs.items()}
    nc = build()
    maps = make_in_maps(inp)
    res = run_bass_kernel_spmd(nc, maps, core_ids=list(range(8)))
    out = np.empty((4, L, D), np.float32)
    for core in range(8):
        b, half = core // 2, core % 2
        o = res.results[core]["out"]
        if half == 0:
            out[b, :NOWN] = o
        else:
            out[b, NOWN:] = o[::-1]
    return out
```

```python
import contextlib
import math
import numpy as np
import ml_dtypes
import concourse.bass as bass
import concourse.mybir as mybir
from concourse.bass_utils import run_bass_kernel_spmd

F32 = mybir.dt.float32
BF16 = mybir.dt.bfloat16
AF = mybir.ActivationFunctionType
ALU = mybir.AluOpType
AX = mybir.AxisListType

D = 2048
L = 4096
NOWN = 2048
CTX = 256
TT = L + CTX
HEADS = 16
D_IN = 18496
D_FF = 5632
EPS = 1e-6
N_ = 6144

COMPUTE = ("pe", "act", "dve", "pool")
NSLOT = {"sp": 12, "poolq": 12}
QENG = {"sp": "sp", "poolq": "pool"}


class Op:
    __slots__ = ("fn", "waits", "sem", "inc")


class Sched:
    def __init__(self, nc):
        self.nc = nc
        self.streams = {e: [] for e in ("pe", "act", "dve", "pool", "sp")}
        self.cnt = {e: 0 for e in COMPUTE}
        self.slot_uses = {q: [0] * NSLOT[q] for q in NSLOT}
        self.dma_i = {q: 0 for q in NSLOT}
        self.last_w = {}
        self.reads = {}
        self.waited = {e: {} for e in self.streams}
        self.final_tokens = []

    def _need(self, stream, tok, waits):
        if tok is None:
            return
        semname, val, pstream, is_pe = tok
        if pstream == stream and is_pe:
            return
        if self.waited[stream].get(semname, 0) >= val:
            return
        waits[semname] = max(waits.get(semname, 0), val)

    def _deps(self, stream, reads, writes):
        waits = {}
        for k in reads:
            self._need(stream, self.last_w.get(k), waits)
        for k in writes:
            self._need(stream, self.last_w.get(k), waits)
            for t in self.reads.get(k, {}).values():
                self._need(stream, t, waits)
        for s, v in waits.items():
            self.waited[stream][s] = v
        return waits

    def _record(self, tok, reads, writes):
        for k in reads:
            d = self.reads.setdefault(k, {})
            o = d.get(tok[0])
            if o is None or o[1] < tok[1]:
                d[tok[0]] = tok
        for k in writes:
            self.last_w[k] = tok
            self.reads[k] = {}

    def op(self, eng, fn, reads=(), writes=()):
        waits = self._deps(eng, reads, writes)
        self.cnt[eng] += 1
        tok = (eng, self.cnt[eng], eng, eng == "pe")
        o = Op(); o.fn = fn; o.waits = sorted(waits.items()); o.sem = eng; o.inc = 1
        self.streams[eng].append(o)
        self._record(tok, reads, writes)
        return tok

    def dma(self, q, out, in_, reads=(), writes=()):
        reads = [k for k in reads if "#" in k]
        writes = [k for k in writes if "#" in k]
        stream = QENG[q]
        waits = self._deps(stream, reads, writes)
        i = self.dma_i[q]; self.dma_i[q] += 1
        slot = i % NSLOT[q]
        semname = "%s%d" % (q, slot)
        prev = self.slot_uses[q][slot] * 16
        if prev and self.waited[stream].get(semname, 0) < prev:
            waits[semname] = prev
            self.waited[stream][semname] = prev
        self.slot_uses[q][slot] += 1
        tok = (semname, self.slot_uses[q][slot] * 16, None, False)
        o = Op(); o.waits = sorted(waits.items()); o.sem = semname; o.inc = 16
        o.fn = (lambda e, out=out, in_=in_: e.dma_start(out=out, in_=in_))
        self.streams[stream].append(o)
        self._record(tok, reads, writes)
        return tok

    def barrier(self):
        allv = {e: self.cnt[e] for e in COMPUTE}
        for q in NSLOT:
            for s in range(NSLOT[q]):
                allv["%s%d" % (q, s)] = self.slot_uses[q][s] * 16
        for stream in self.streams:
            waits = {}
            for s, v in allv.items():
                if v and self.waited[stream].get(s, 0) < v and not (s == "pe" and stream == "pe"):
                    waits[s] = v
                    self.waited[stream][s] = v
            if waits:
                o = Op(); o.fn = None; o.waits = sorted(waits.items()); o.sem = None; o.inc = 0
                self.streams[stream].append(o)

    def emit(self):
        nc = self.nc
        names = list(COMPUTE) + ["%s%d" % (q, s) for q in NSLOT for s in range(NSLOT[q])]
        with contextlib.ExitStack() as st:
            sems = {n: st.enter_context(nc.semaphore("s_" + n)) for n in names}
            block = st.enter_context(nc.Block())

            def run(stream, final=False):
                def body(e):
                    for o in self.streams[stream]:
                        for s, v in o.waits:
                            e.wait_ge(sems[s], v)
                        if o.fn is not None:
                            o.fn(e).then_inc(sems[o.sem], o.inc)
                    if final:
                        for (s, v, _, _) in self.final_tokens:
                            e.wait_ge(sems[s], v)
                return body

            block.sync(run("sp", True))
            block.tensor(run("pe"))
            block.scalar(run("act"))
            block.vector(run("dve"))
            block.gpsimd(run("pool"))


class Arena:
    def __init__(self, big, total):
        self.big = big; self.total = total; self.off = 0; self.n = 0

    def reset(self, to=0):
        self.off = to

    def alloc(self, shape, dtype, name=None):
        n = int(np.prod(shape[1:]))
        size = n * (2 if dtype == F32 else 1)
        o = self.off
        self.off += (size + 15) // 16 * 16
        assert self.off <= self.total, ("SBUF arena overflow", self.off, self.total)
        v = self.big[:, o:o + size]
        if dtype == F32:
            v = v.bitcast(F32)
        if len(shape) == 3:
            v = v.rearrange("p (a b) -> p a b", a=shape[1])
        elif len(shape) == 4:
            v = v.rearrange("p (a b c) -> p a b c", a=shape[1], b=shape[2])
        self.n += 1
        key = "%s#%d" % (name or "t", self.n)
        return v[0:shape[0]], key


class Ring:
    def __init__(self, arena, n, shape, dtype, name):
        self.items = [arena.alloc(shape, dtype, name) for _ in range(n)]
        self.i = 0

    def next(self):
        it = self.items[self.i % len(self.items)]
        self.i += 1
        return it


def _make_plan():
    plan = [("scal", 0)]
    for j in range(16):
        plan += [("x1", j), ("hv", j)]
    for typ in ("q", "k", "gv", "x0", "zg"):
        plan += [(typ, j) for j in range(16)]
    plan += [("gate", j) for j in range(32)]
    return plan


SEG = dict(x0=0, x1=2048, hv=4096, q=6144, k=8192, gv=10240, zg=12288, scal=14336, gate=14400)


PLAN = _make_plan()
PLAN_INDEX = {k: i for i, k in enumerate(PLAN)}


def _chunk_major(w, cols):
    K = w.shape[0]
    out = np.empty((len(cols), 128, (K // 128) * 128), np.float32)
    for i, c0 in enumerate(cols):
        blk = w[:, c0:c0 + 128]
        if blk.shape[1] < 128:
            blk = np.concatenate([blk, np.zeros((K, 128 - blk.shape[1]), np.float32)], axis=1)
        out[i] = blk.reshape(K // 128, 128, 128).transpose(1, 0, 2).reshape(128, -1)
    return out


def build(upto=99, dbg=(), phases="DEF"):
    nc = bass.Bass("TRN2", target_bir_lowering=False)
    S = Sched(nc)
    ext = {}

    def inp(name, shape, dt=F32):
        ext[name] = nc.dram_tensor(name, list(shape), dt, kind="ExternalInput")
        return ext[name].ap()

    def scratch(name, shape, dt):
        kind = "ExternalOutput" if name in dbg else "Internal"
        t = nc.dram_tensor(name, list(shape), dt, kind=kind)
        return t.ap()

    x_d = inp("x", [L, D]); ctx_d = inp("ctx", [CTX, D]); cT_d = inp("cT", [128, 16, 2])
    w_ada_d = inp("w_ada", [D, 6 * D]); b_adaT_d = inp("b_adaT", [128, 96])
    nmixT_d = inp("nmixT", [128, 16]); nffnT_d = inp("nffnT", [128, 16])
    w_in_d = inp("w_inc", [145, 128, 16 * 128])
    hyconvT_d = inp("hyconvT", [128, 48, 3]); gdnconvT_d = inp("gdnconvT", [128, 48, 3])
    scalp_d = inp("scalp", [64, 2])
    ident_d = inp("ident", [128, 128])
    out_d = nc.dram_tensor("out", [NOWN, D], F32, kind="ExternalOutput").ap()
    w_hy_out_d = inp("w_hy_outc", [16, 128, 16 * 128]); w_gdn_out_d = inp("w_gdn_outc", [16, 128, 16 * 128]); w_o_d = inp("w_oc", [16, 128, 16 * 128])
    w_up_d = inp("w_upc", [88, 128, 16 * 128]); w_down_d = inp("w_downc", [16, 128, 44 * 128])
    hybT_d = inp("hybT", [128, 16]); gnormT_d = inp("gnormT", [128, 1]); nfinT_d = inp("nfinT", [128, 16])
    Yd = scratch("Yd", [NOWN + 64, D], BF16); Od0 = scratch("Od0", [NOWN, D], F32); Od1 = scratch("Od1", [NOWN, D], F32)
    cmask_d = inp("cmask", [64, 4, 64]); cm2_d = inp("cm2", [64, 5, 64])
    zf_d = inp("zf", [33, N_]); fw1_d = inp("fw1", [33, 64]); fw2_d = inp("fw2", [64, 64]); fw3_d = inp("fw3", [64, 64])
    fsm_d = inp("fsm", [64, 4]); fout_d = inp("fout", [64, 2 * D]); deltas_d = inp("deltas", [128, D]); tau_d = inp("tau", [128, 48])
    W1_d = inp("W1", [128, 48 * 2 * 128]); V_d = inp("Vc", [128, 48 * 2 * 64])
    W2_d = inp("W2", [96, 96]); W2a_d = inp("W2a", [96, 96]); W2b_d = inp("W2b", [96, 96]); L1_d = inp("L1", [96, 96]); L2_d = inp("L2", [96, 96])
    Hc = scratch("Hc", [N_, D], BF16); Ad = scratch("Ad", [128, 96, D], BF16); Bd = scratch("Bd", [128, 96, D], BF16)
    Kd = scratch("Kd", [2, 96, 128, D], BF16)

    X0T = scratch("X0T", [D, NOWN], BF16); ZT = scratch("ZT", [D, L], BF16); Zt = scratch("Zt", [L, D], BF16)
    QT = scratch("QT", [D, TT], BF16); KT = scratch("KT", [D, TT], BF16); VT = scratch("VT", [D, TT], BF16)
    ZGT = scratch("ZGT", [D, NOWN], BF16); SCT = scratch("SCT", [64, TT], F32); GT = scratch("GT", [2 * D, NOWN], BF16)
    MODd = scratch("MODd", [128, 96, 2], F32)

    with contextlib.ExitStack() as st:
        TOTAL = 106000
        big = st.enter_context(nc.sbuf_tensor("big", [128, TOTAL], BF16))
        PSALL = st.enter_context(nc.psum_tensor("psall", [128, 4096], F32))
        psb = [PSALL[:, i * 512:(i + 1) * 512] for i in range(8)]
        A = Arena(big, TOTAL)
        psi = [0]

        def ps_next():
            i = psi[0] % 8; psi[0] += 1
            return psb[i], "psb%d" % i

        def ps_multi(nb):
            i = ((psi[0] + nb - 1) // nb * nb) % 8
            psi[0] = i + nb
            return PSALL[:, i * 512:(i + nb) * 512], ["psb%d" % (i + j) for j in range(nb)]

        ident_f, k_identf = A.alloc([128, 128], F32, "identf")
        ident_b, k_ident = A.alloc([128, 128], BF16, "ident")
        ones_b, k_ones = A.alloc([128, 128], BF16, "ones")
        ones_f, k_onesf = A.alloc([128, 128], F32, "onesf")
        mod, k_mod = A.alloc([128, 96, 2], F32, "mod")
        nmixT, k_nmix = A.alloc([128, 16], F32, "nmix")
        nffnT, k_nffn = A.alloc([128, 16], F32, "nffn")
        scale_a, k_sca = A.alloc([128, 16], F32, "scale_a")
        scale_c, k_scc = A.alloc([128, 16], F32, "scale_c")
        scale_f, k_scf = A.alloc([128, 16], F32, "scale_f")
        hyconvT, k_hyc = A.alloc([128, 48, 3], F32, "hyc")
        gdnconvT, k_gdc = A.alloc([128, 48, 3], F32, "gdc")
        scalp, k_scalp = A.alloc([64, 2], F32, "scalp")
        nega, k_nega = A.alloc([64, 1], F32, "nega")
        negpi, k_negpi = A.alloc([128, 1], F32, "negpi")
        S.op("dve", lambda e: e.memset(negpi, -math.pi), writes=[k_negpi])
        hybT, k_hyb = A.alloc([128, 16], F32, "hyb")
        gnormT, k_gnorm = A.alloc([128, 1], F32, "gnorm")
        nfinT, k_nfin = A.alloc([128, 16], F32, "nfin")
        PERSIST = A.off
        S.dma("sp", hybT, hybT_d, writes=[k_hyb]); S.dma("sp", gnormT, gnormT_d, writes=[k_gnorm]); S.dma("sp", nfinT, nfinT_d, writes=[k_nfin])

        S.dma("sp", ident_f, ident_d, writes=[k_identf])
        S.op("dve", lambda e: e.tensor_copy(out=ident_b, in_=ident_f), reads=[k_identf], writes=[k_ident])
        S.op("dve", lambda e: e.memset(ones_b, 1.0), writes=[k_ones])
        S.op("dve", lambda e: e.memset(ones_f, 1.0), writes=[k_onesf])
        S.dma("sp", nmixT, nmixT_d, writes=[k_nmix]); S.dma("sp", nffnT, nffnT_d, writes=[k_nffn])
        S.dma("sp", hyconvT, hyconvT_d, writes=[k_hyc]); S.dma("sp", gdnconvT, gdnconvT_d, writes=[k_gdc])
        S.dma("sp", scalp, scalp_d, writes=[k_scalp])
        S.op("act", lambda e: e.activation(out=nega[32:64], in_=scalp[32:64, 0:1], func=AF.Exp), reads=[k_scalp], writes=[k_nega])
        S.op("dve", lambda e: e.tensor_scalar(out=nega[32:64], in0=nega[32:64], scalar1=-1.0, scalar2=None, op0=ALU.mult), reads=[k_nega], writes=[k_nega])

        cT, k_cT = A.alloc([128, 16, 2], F32, "cT")
        sT, k_sT = A.alloc([128, 16, 2], BF16, "sT")
        bT, k_bT = A.alloc([128, 96], F32, "bT")
        wr = Ring(A, 2, [128, 16, 1536], BF16, "wada")
        S.dma("sp", cT, cT_d, writes=[k_cT]); S.dma("sp", bT, b_adaT_d, writes=[k_bT])
        S.op("act", lambda e: e.activation(out=sT, in_=cT, func=AF.Silu), reads=[k_cT], writes=[k_sT])
        w_ada_v = w_ada_d.rearrange("(kc p) n -> p kc n", p=128)
        for blk in range(8):
            wt, kw = wr.next()
            S.dma("poolq", wt, w_ada_v[:, :, blk * 1536:(blk + 1) * 1536], writes=[kw])
            ps, kp = ps_next()

            def mmA(e, wt=wt, ps=ps):
                last = None
                for j in range(12):
                    for kc in range(16):
                        last = e.matmul(ps[:, 2 * j:2 * j + 2], wt[:, kc, j * 128:(j + 1) * 128], sT[:, kc, :],
                                        start=(kc == 0), stop=(kc == 15))
                return last
            S.op("pe", mmA, reads=[kw, k_sT], writes=[kp])
            S.op("dve", lambda e, ps=ps, blk=blk: e.tensor_copy(
                out=mod[:, blk * 12:(blk + 1) * 12, :], in_=ps[:, 0:24].rearrange("p (a b) -> p a b", b=2)),
                reads=[kp], writes=[k_mod])
        for j in range(2):
            S.op("dve", lambda e, j=j: e.tensor_tensor(out=mod[:, :, j], in0=mod[:, :, j], in1=bT, op=ALU.add),
                 reads=[k_mod, k_bT], writes=[k_mod])
        for (dst, kd, src, nrm, kn) in ((scale_a, k_sca, mod[:, 16:32, 0], nmixT, k_nmix),
                                        (scale_c, k_scc, mod[:, 16:32, 1], nmixT, k_nmix),
                                        (scale_f, k_scf, mod[:, 64:80, 0], nffnT, k_nffn)):
            S.op("dve", lambda e, dst=dst, src=src, nrm=nrm: e.scalar_tensor_tensor(
                out=dst, in0=src, scalar=1.0, in1=nrm, op0=ALU.add, op1=ALU.mult), reads=[k_mod, kn], writes=[kd])
        if "MODd" in dbg:
            S.final_tokens.append(S.dma("sp", MODd, mod, reads=[k_mod], writes=["MODd"]))
        if upto <= 1:
            S.emit(); return nc

        S.barrier(); A.reset(PERSIST)
        hT, k_hT = A.alloc([128, 16, TT], BF16, "hT")
        PB = A.off
        xr = Ring(A, 2, [128, D], F32, "xt")
        sq, k_sq = A.alloc([128, D], F32, "sq")
        xnr = Ring(A, 2, [128, D], BF16, "xn")
        str_ = Ring(A, 2, [128, 4], F32, "stat")
        for i in range(TT // 128):
            lat = i < L // 128
            src = x_d[i * 128:(i + 1) * 128, :] if lat else ctx_d[(i - 32) * 128:(i - 31) * 128, :]
            sc_t, k_sc = (scale_a, k_sca) if lat else (scale_c, k_scc)
            sh_t = mod[:, 0:16, 0] if lat else mod[:, 0:16, 1]
            xt, kx = xr.next(); xn, kxn = xnr.next(); stt, kst = str_.next()
            S.dma("sp", xt, src, writes=[kx])
            S.op("act", lambda e, xt=xt: e.activation(out=sq, in_=xt, func=AF.Square), reads=[kx], writes=[k_sq])
            S.op("dve", lambda e, stt=stt: e.reduce_sum(out=stt[:, 0:1], in_=sq, axis=AX.X), reads=[k_sq], writes=[kst])
            S.op("dve", lambda e, stt=stt: e.tensor_scalar(out=stt[:, 1:2], in0=stt[:, 0:1], scalar1=1.0 / D, scalar2=EPS,
                                                           op0=ALU.mult, op1=ALU.add), reads=[kst], writes=[kst])
            S.op("dve", lambda e, stt=stt: e.reciprocal(out=stt[:, 2:3], in_=stt[:, 1:2]), reads=[kst], writes=[kst])
            S.op("act", lambda e, stt=stt: e.activation(out=stt[:, 3:4], in_=stt[:, 2:3], func=AF.Sqrt), reads=[kst], writes=[kst])
            S.op("dve", lambda e, stt=stt: e.tensor_tensor(out=stt[:, 0:1], in0=stt[:, 3:4], in1=stt[:, 3:4], op=ALU.mult), reads=[kst], writes=[kst])
            S.op("dve", lambda e, stt=stt: e.tensor_tensor(out=stt[:, 0:1], in0=stt[:, 0:1], in1=stt[:, 1:2], op=ALU.mult), reads=[kst], writes=[kst])
            S.op("dve", lambda e, stt=stt: e.tensor_scalar(out=stt[:, 0:1], in0=stt[:, 0:1], scalar1=-0.5, scalar2=1.5,
                                                           op0=ALU.mult, op1=ALU.add), reads=[kst], writes=[kst])
            S.op("dve", lambda e, stt=stt: e.tensor_tensor(out=stt[:, 3:4], in0=stt[:, 3:4], in1=stt[:, 0:1], op=ALU.mult), reads=[kst], writes=[kst])
            S.op("act", lambda e, xt=xt, xn=xn, stt=stt: e.activation(out=xn, in_=xt, func=AF.Copy, scale=stt[:, 3:4]),
                 reads=[kx, kst], writes=[kxn])
            for g in range(4):
                ps, kp = ps_next()
                psv = ps.bitcast(BF16)

                def tr(e, xn=xn, psv=psv, g=g):
                    last = None
                    for j in range(4):
                        fc = g * 4 + j
                        last = e.transpose(psv[:, j * 128:(j + 1) * 128], xn[:, fc * 128:(fc + 1) * 128], ident_b)
                    return last
                S.op("pe", tr, reads=[kxn, k_ident], writes=[kp])
                for j in range(4):
                    fc = g * 4 + j
                    dst = hT[:, fc, i * 128:(i + 1) * 128]
                    if j % 2 == 0:
                        S.op("act", lambda e, dst=dst, psv=psv, j=j, fc=fc, sc_t=sc_t, sh_t=sh_t: e.activation(
                            out=dst, in_=psv[:, j * 128:(j + 1) * 128], func=AF.Identity, scale=sc_t[:, fc:fc + 1], bias=sh_t[:, fc:fc + 1]),
                            reads=[kp, k_sc, k_mod], writes=[k_hT])
                    else:
                        S.op("dve", lambda e, dst=dst, psv=psv, j=j, fc=fc, sc_t=sc_t, sh_t=sh_t: e.tensor_scalar(
                            out=dst, in0=psv[:, j * 128:(j + 1) * 128], scalar1=sc_t[:, fc:fc + 1], scalar2=sh_t[:, fc:fc + 1],
                            op0=ALU.mult, op1=ALU.add), reads=[kp, k_sc, k_mod], writes=[k_hT])
        if "HTd" in dbg:
            HTd = nc.dram_tensor("HTd", [128, 16, TT], BF16, kind="ExternalOutput").ap()
            S.final_tokens.append(S.dma("sp", HTd, hT, reads=[k_hT], writes=["HTd"]))
        if upto <= 2:
            S.emit(); return nc

        S.barrier(); A.reset(PB)
        wr = Ring(A, 3, [128, 16, 128], BF16, "win")
        ur = Ring(A, 3, [128, 512], F32, "u")
        t2r = Ring(A, 2, [128, 512], F32, "tmp2")
        obr = Ring(A, 3, [128, 512], BF16, "ob")
        sqr = Ring(A, 3, [128, 512], BF16, "sq")
        ofr = Ring(A, 2, [64, 512], F32, "of")
        u1, k_u1 = A.alloc([128, L], BF16, "u1")
        ztr = Ring(A, 2, [128, 4, 128], BF16, "zt")
        BLK_ALL = [(b * 512, 512) for b in range(8)]
        BLK_OWN = BLK_ALL[:NOWN // 512]
        BLK_CTX = [(L, CTX)]

        def conv_epilogue(ps, kp, n, rw, taps, ktaps):
            u, ku = ur.next()
            S.op("act", lambda e: e.activation(out=u[:, 0:n], in_=ps[:, 0:n], func=AF.Copy, scale=taps[:, 1:2]),
                 reads=[kp, ktaps], writes=[ku])
            pv = ps[:, 0:n].rearrange("p (r w) -> p r w", w=rw)
            uv = u[:, 0:n].rearrange("p (r w) -> p r w", w=rw)
            S.op("dve", lambda e: e.scalar_tensor_tensor(out=uv[:, :, 1:rw], in0=pv[:, :, 0:rw - 1], scalar=taps[:, 0:1],
                                                         in1=uv[:, :, 1:rw], op0=ALU.mult, op1=ALU.add), reads=[kp, ktaps, ku], writes=[ku])
            S.op("dve", lambda e: e.scalar_tensor_tensor(out=uv[:, :, 0:rw - 1], in0=pv[:, :, 1:rw], scalar=taps[:, 2:3],
                                                         in1=uv[:, :, 0:rw - 1], op0=ALU.mult, op1=ALU.add), reads=[kp, ktaps, ku], writes=[ku])
            return u, ku

        def do_chunk(typ, j):
            M = 64 if typ == "scal" else 128
            wt, kw = wr.next()
            S.dma("poolq", wt.rearrange("p a b -> p (a b)"), w_in_d[PLAN_INDEX[(typ, j)]], writes=[kw])
            own = typ in ("x0", "zg", "gate")
            blocks = BLK_OWN if own else BLK_ALL
            if typ in ("q", "k", "gv", "scal"):
                blocks = blocks + BLK_CTX
            def blk(t0, n):
                ps, kp = ps_next()

                def mm(e, ps=ps, t0=t0, n=n):
                    last = None
                    for kc in range(16):
                        last = e.matmul(ps[0:M, 0:n], wt[:, kc, 0:M], hT[:, kc, t0:t0 + n], start=(kc == 0), stop=(kc == 15))
                    return last
                S.op("pe", mm, reads=[kw, k_hT], writes=[kp])
                isctx = t0 >= L
                rw = CTX if isctx else 64
                if typ in ("x0", "x1", "hv"):
                    ci = {"x0": 0, "x1": 16, "hv": 32}[typ] + j
                    u, ku = conv_epilogue(ps, kp, n, rw, hyconvT[:, ci, :], k_hyc)
                    if typ == "x0":
                        ob, kob = obr.next()
                        S.op("act", lambda e, ob=ob, u=u: e.activation(out=ob, in_=u, func=AF.Copy), reads=[ku], writes=[kob])
                        S.dma("sp", X0T[j * 128:(j + 1) * 128, t0:t0 + n], ob, reads=[kob], writes=["X0T"])
                    elif typ == "x1":
                        S.op("act", lambda e, u=u, t0=t0: e.activation(out=u1[:, t0:t0 + 512], in_=u, func=AF.Copy), reads=[ku], writes=[k_u1 + "@%d" % t0])
                    else:
                        ob, kob = obr.next()
                        S.op("dve", lambda e, ob=ob, u=u, t0=t0: e.tensor_tensor(out=ob, in0=u, in1=u1[:, t0:t0 + 512], op=ALU.mult),
                             reads=[ku, k_u1 + "@%d" % t0], writes=[kob])
                        S.dma("sp", ZT[j * 128:(j + 1) * 128, t0:t0 + n], ob, reads=[kob], writes=["ZT"])
                        yield
                        ps2, kp2 = ps_next()
                        ps2v = ps2.bitcast(BF16)

                        def trz(e, ob=ob, ps2v=ps2v):
                            last = None
                            for tb in range(4):
                                last = e.transpose(ps2v[:, tb * 128:(tb + 1) * 128], ob[:, tb * 128:(tb + 1) * 128], ident_b)
                            return last
                        S.op("pe", trz, reads=[kob, k_ident], writes=[kp2])
                        zt, kzt = ztr.next()
                        S.op("act", lambda e, zt=zt, ps2v=ps2v: e.activation(
                            out=zt, in_=ps2v[:, 0:512].rearrange("p (a b) -> p a b", b=128), func=AF.Copy), reads=[kp2], writes=[kzt])
                        S.dma("sp", Zt[t0:t0 + 512, j * 128:(j + 1) * 128].rearrange("(a p) c -> p a c", p=128), zt,
                              reads=[kzt], writes=["Zt"])
                elif typ in ("q", "k", "gv"):
                    ci = {"q": 0, "k": 16, "gv": 32}[typ] + j
                    u, ku = conv_epilogue(ps, kp, n, rw, gdnconvT[:, ci, :], k_gdc)
                    S.op("act", lambda e, u=u, n=n: e.activation(out=u[:, 0:n], in_=u[:, 0:n], func=AF.Silu), reads=[ku], writes=[ku])
                    ob, kob = obr.next()
                    dstT = {"q": QT, "k": KT, "gv": VT}[typ]
                    if typ == "gv":
                        S.op("dve", lambda e, ob=ob, u=u, n=n: e.tensor_copy(out=ob[:, 0:n], in_=u[:, 0:n]), reads=[ku], writes=[kob])
                    else:
                        sqb, ksqb = sqr.next()
                        S.op("dve", lambda e, sqb=sqb, u=u, n=n: e.tensor_tensor(out=sqb[:, 0:n], in0=u[:, 0:n], in1=u[:, 0:n], op=ALU.mult),
                             reads=[ku], writes=[ksqb])
                        yield
                        t2, kt2 = t2r.next()
                        ps2, kp2 = ps_next()
                        S.op("pe", lambda e, ps2=ps2, sqb=sqb, n=n: e.matmul(ps2[:, 0:n], ones_b, sqb[:, 0:n], start=True, stop=True),
                             reads=[ksqb, k_ones], writes=[kp2])
                        S.op("dve", lambda e, t2=t2, ps2=ps2, n=n: e.tensor_scalar(out=t2[:, 0:n], in0=ps2[:, 0:n], scalar1=EPS, scalar2=None, op0=ALU.add),
                             reads=[kp2], writes=[kt2])
                        S.op("dve", lambda e, t2=t2, n=n: e.reciprocal(out=t2[:, 0:n], in_=t2[:, 0:n]), reads=[kt2], writes=[kt2])
                        S.op("act", lambda e, t2=t2, n=n: e.activation(out=t2[:, 0:n], in_=t2[:, 0:n], func=AF.Sqrt), reads=[kt2], writes=[kt2])
                        ob, kob = obr.next()
                        qs = (128.0 ** -0.5) if typ == "q" else 1.0
                        S.op("dve", lambda e, ob=ob, u=u, t2=t2, n=n, qs=qs: e.scalar_tensor_tensor(
                            out=ob[:, 0:n], in0=u[:, 0:n], scalar=qs, in1=t2[:, 0:n], op0=ALU.mult, op1=ALU.mult), reads=[ku, kt2], writes=[kob])
                    S.dma("sp", dstT[j * 128:(j + 1) * 128, t0:t0 + n], ob[:, 0:n], reads=[kob], writes=[typ + "T"])
                elif typ in ("zg", "gate"):
                    ob, kob = obr.next()
                    fn = AF.Silu if typ == "zg" else AF.Sigmoid
                    S.op("act", lambda e, ob=ob, ps=ps, fn=fn: e.activation(out=ob, in_=ps[:, 0:512], func=fn), reads=[kp], writes=[kob])
                    dstT = ZGT if typ == "zg" else GT
                    S.dma("sp", dstT[j * 128:(j + 1) * 128, t0:t0 + n], ob, reads=[kob], writes=[typ + "T"])
                else:
                    of, kof = ofr.next()
                    S.op("act", lambda e, of=of, ps=ps, n=n: e.activation(out=of[0:32, 0:n], in_=ps[0:32, 0:n], func=AF.Sigmoid), reads=[kp], writes=[kof])
                    S.op("act", lambda e, of=of, ps=ps, n=n: e.activation(out=of[32:64, 0:n], in_=ps[32:64, 0:n], func=AF.Exp, bias=scalp[32:64, 1:2]),
                         reads=[kp, k_scalp], writes=[kof])
                    S.op("act", lambda e, of=of, n=n: e.activation(out=of[32:64, 0:n], in_=of[32:64, 0:n], func=AF.Ln, bias=1.0), reads=[kof], writes=[kof])
                    S.op("dve", lambda e, of=of, n=n: e.tensor_scalar(out=of[32:64, 0:n], in0=of[32:64, 0:n], scalar1=nega[32:64, 0:1], scalar2=None, op0=ALU.mult),
                         reads=[kof, k_nega], writes=[kof])
                    S.dma("sp", SCT[:, t0:t0 + n], of[:, 0:n], reads=[kof], writes=["SCT"])
                return
                yield

            pend = None
            for (t0, n) in blocks:
                g = blk(t0, n)
                alive = True
                try:
                    next(g)
                except StopIteration:
                    alive = False
                if pend is not None:
                    for _ in pend:
                        pass
                pend = g if alive else None
            if pend is not None:
                for _ in pend:
                    pass

        plan = list(PLAN)
        import os
        if os.environ.get("K_PLAN_TEST") == "gdn":
            plan = [("scal", 0)] + [(t, j) for t in ("q", "k", "gv") for j in range(16)]
        elif os.environ.get("K_PLAN_TEST") == "hy":
            plan = [pp for j in (0, 7) for pp in (("x1", j), ("hv", j))] + [("x0", 0), ("x0", 7)]
        elif os.environ.get("K_PLAN_TEST"):
            plan = [("scal", 0), ("x1", 1), ("hv", 1), ("q", 2), ("k", 3), ("gv", 4), ("x0", 5), ("zg", 6), ("gate", 7), ("gate", 17)]
        for (typ, j) in plan:
            do_chunk(typ, j)
        for nm in ("X0T", "ZT", "Zt", "QT", "KT", "VT", "ZGT", "SCT", "GT"):
            if nm in dbg:
                pass
        if upto <= 3:
            S.barrier()
            S.emit(); return nc


        def phase_d():
            S.barrier(); A.reset(PERSIST)
            H = HEADS
            HH = 8
            cmask, k_cm = A.alloc([64, 4, 64], F32, "cmask")
            S.dma("sp", cmask, cmask_d, writes=[k_cm])
            cm2, k_cm2 = A.alloc([64, 5, 64], F32, "cm2")
            S.dma("sp", cm2, cm2_d, writes=[k_cm2])
            cmask_b, k_cmb = A.alloc([64, 4, 64], BF16, "cmask_b")
            cm2_b, k_cm2b = A.alloc([64, 5, 64], BF16, "cm2_b")
            S.op("dve", lambda e: e.tensor_copy(out=cm2_b, in_=cm2), reads=[k_cm2], writes=[k_cm2b])
            S.op("dve", lambda e: e.tensor_copy(out=cmask_b, in_=cmask), reads=[k_cm], writes=[k_cmb])
            qr = Ring(A, 2, [128, H, 256], BF16, "qg"); kr = Ring(A, 2, [128, H, 256], BF16, "kg"); vr = Ring(A, 2, [128, H, 256], BF16, "vg")
            scr = Ring(A, 2, [64, 256], F32, "scg")
            identb64 = ident_b[0:64, 0:64].unsqueeze(1).broadcast_to([64, HH, 64])
            identf64 = ident_f[0:64, 0:64].unsqueeze(1).broadcast_to([64, HH, 64])
            QTv = QT.rearrange("(h p) t -> p h t", p=128); KTv = KT.rearrange("(h p) t -> p h t", p=128); VTv = VT.rearrange("(h p) t -> p h t", p=128)

            def al(shape, dt, nm):
                return A.alloc(shape, dt, nm)

            class TS:
                pass
            halves = []
            for hh in range(2):
                T_ = TS()
                T_.S = al([128, HH, 128], F32, "S"); T_.Sb = al([128, HH, 128], BF16, "Sb")
                T_.sct = al([64, 64], F32, "sct"); T_.sm = al([64, 8, HH], F32, "sm")
                T_.ktok = al([64, HH, 128], BF16, "ktok"); T_.vtok = al([64, HH, 128], BF16, "vtok")
                T_.Xg = al([64, HH, 64], F32, "Xg"); T_.Xb = al([64, HH, 64], F32, "Xb")
                T_.Y = al([64, HH, 64], F32, "Y"); T_.EA = al([64, HH, 64], F32, "EA"); T_.EB = al([64, HH, 64], F32, "EB")
                T_.tmpM = al([64, HH, 64], BF16, "tmpM"); T_.tmpB = al([64, HH, 64], BF16, "tmpB"); T_.tmpQ = al([64, HH, 64], BF16, "tmpQ")
                T_.bbc = al([64, HH, 64], BF16, "bbc")
                T_.P0 = al([64, HH, 64], BF16, "P0"); T_.P1 = al([64, HH, 64], BF16, "P1")
                T_.PT0 = al([64, HH, 64], BF16, "PT0"); T_.PT1 = al([64, HH, 64], BF16, "PT1")
                T_.TT = al([64, HH, 64], BF16, "TT"); T_.T = al([64, HH, 64], BF16, "T")
                T_.Am = al([64, HH, 64], BF16, "Am"); T_.ATm = al([64, HH, 64], BF16, "ATm")
                T_.eGbc = al([128, HH, 64], F32, "eGbc")
                T_.vb = al([64, HH, 128], BF16, "vb"); T_.kbg = al([64, HH, 128], BF16, "kbg")
                T_.vnew = al([64, HH, 128], BF16, "vnew")
                T_.slot = []
                for sl in range(2):
                    d = dict(wT=al([128, HH, 64], BF16, "wT"), uu=al([64, HH, 128], F32, "uu"), qdT=al([128, HH, 64], BF16, "qdT"),
                             QKT=al([64, HH, 64], BF16, "QKT"), kdec=al([64, HH, 128], BF16, "kdec"), egl=al([128, HH], F32, "egl"))
                    T_.slot.append(d)
                T_.orr = Ring(A, 2, [64, HH, 128], F32, "o")
                halves.append(T_)

            def bc_s(ap):
                return ap.unsqueeze(2).broadcast_to([64, HH, 64])

            def bc_e(ap):
                return ap.unsqueeze(2).broadcast_to([64, HH, 128])

            def mk(i):
                return cmask[:, i, :].unsqueeze(1).broadcast_to([64, HH, 64])

            def mkb(i):
                return cmask_b[:, i, :].unsqueeze(1).broadcast_to([64, HH, 64])

            def m2(i):
                return cm2_b[:, i, :].unsqueeze(1).broadcast_to([64, HH, 64])

            def hm(psx):
                return psx[0:64, :].rearrange("p (h s) -> p h s", h=HH)

            def hm128(psx):
                return psx[0:64, :].rearrange("p (h d) -> p h d", h=HH)

            def run_dir(dr):
                mA_s, mB_i, mB_s, cumi = ((1, 2, 3, 2) if dr == 0 else (3, 0, 1, 0))
                last_t = 63 if dr == 0 else 0
                for T_ in halves:
                    S.op("dve", lambda e, T_=T_: e.memset(T_.S[0], 0.0), reads=[T_.S[1]], writes=[T_.S[1]])
                    S.op("dve", lambda e, T_=T_: e.memset(T_.Sb[0], 0.0), reads=[T_.Sb[1]], writes=[T_.Sb[1]])
                if dr == 0:
                    groups = [L // 256] + list(range(NOWN // 256))
                else:
                    groups = [L // 256] + list(range(L // 256 - 1, -1, -1))

                def chunk(hh, gi, ci, need_o, g0, qg, kqg, kg, kkg, vg, kvg, scg, kscg, slot):
                    T_ = halves[hh]
                    h0 = hh * HH
                    c0 = ci * 64
                    (Sst, k_S), (Sb, k_Sb), (sct, k_sct), (sm, k_sm) = T_.S, T_.Sb, T_.sct, T_.sm
                    (ktok, k_ktok), (vtok, k_vtok), (Xg, k_Xg), (Xb, k_Xb) = T_.ktok, T_.vtok, T_.Xg, T_.Xb
                    (Y, k_Y), (EA, k_EA), (EB, k_EB), (tmpM, k_tmpM) = T_.Y, T_.EA, T_.EB, T_.tmpM
                    (tmpB, k_tmpB), (tmpQ, k_tmpQ), (bbc, k_bbc) = T_.tmpB, T_.tmpQ, T_.bbc
                    (P0, kP0), (P1, kP1), (PT0, kPT0), (PT1, kPT1) = T_.P0, T_.P1, T_.PT0, T_.PT1
                    (TTm, k_TT), (Tm, k_T), (Am, k_Am), (ATm, k_ATm) = T_.TT, T_.T, T_.Am, T_.ATm
                    (eGbc, k_eGbc), (vb, k_vb), (kbg, k_kbg), (vnew, k_vnew) = T_.eGbc, T_.vb, T_.kbg, T_.vnew
                    sd = T_.slot[slot]
                    (wT, k_wT), (uu, k_uu), (qdT, k_qdT), (QKT, k_QKT), (kdec, k_kdec), (egl, k_egl) = (
                        sd["wT"], sd["uu"], sd["qdT"], sd["QKT"], sd["kdec"], sd["egl"])

                    def mmset(lhs, klhs, rhs, krhs):
                        pX, kpX = ps_next()

                        def f(e):
                            last = None
                            for h in range(HH):
                                last = e.matmul(pX[0:64, h * 64:(h + 1) * 64], lhs[:, h, :], rhs[:, h, :], start=True, stop=True)
                            return last
                        S.op("pe", f, reads=[klhs, krhs], writes=[kpX])
                        return pX, [kpX]
                    ps, kp = ps_next()
                    S.op("pe", lambda e: e.transpose(ps[0:64, 0:64], scg[:, c0:c0 + 64], ident_f[0:64, 0:64]), reads=[kscg, k_identf], writes=[kp])
                    S.op("act", lambda e: e.activation(out=sct, in_=ps[0:64, 0:64], func=AF.Copy), reads=[kp], writes=[k_sct])
                    beta = sct[:, dr * 16 + h0:dr * 16 + h0 + HH]; gg = sct[:, 32 + dr * 16 + h0:32 + dr * 16 + h0 + HH]
                    yield "p"
                    ps2, kp2 = ps_next()
                    S.op("pe", lambda e: e.matmul(ps2[0:64, 0:HH], cmask[:, cumi, :], gg, start=True, stop=True), reads=[k_sct, k_cm], writes=[kp2])
                    S.op("act", lambda e: e.activation(out=sm[:, 0, :], in_=ps2[0:64, 0:HH], func=AF.Copy), reads=[kp2], writes=[k_sm])
                    yield "p"
                    S.op("dve", lambda e: e.tensor_tensor(out=Xg, in0=identf64, in1=bc_s(sm[:, 0, :]), op=ALU.mult), reads=[k_sm, k_identf], writes=[k_Xg])
                    S.op("pool", lambda e: e.tensor_tensor(out=Xb, in0=identf64, in1=bc_s(beta), op=ALU.mult), reads=[k_sct, k_identf], writes=[k_Xb])
                    pG, kpG1 = ps_next(); kpG = [kpG1]
                    pGv = pG.rearrange("p (h s) -> p h s", h=HH)
                    S.op("pe", lambda e: e.matmul(pG[:, 0:512], ones_f[0:64, :], Xg, start=True, stop=True), reads=[k_Xg, k_onesf], writes=kpG)
                    pB, kpB1 = ps_next(); kpB = [kpB1]
                    pBv = pB.rearrange("p (h s) -> p h s", h=HH)
                    S.op("pe", lambda e: e.matmul(pB[0:64, 0:512], ones_f[0:64, 0:64], Xb, start=True, stop=True), reads=[k_Xb, k_onesf], writes=kpB)
                    S.op("dve", lambda e: e.tensor_tensor(out=Y, in0=pGv[0:64], in1=bc_s(sm[:, 0, :]), op=ALU.subtract), reads=kpG + [k_sm], writes=[k_Y])
                    S.op("act", lambda e: e.activation(out=bbc, in_=pBv[0:64], func=AF.Copy), reads=kpB, writes=[k_bbc])
                    S.op("dve", lambda e: e.tensor_tensor(out=sm[:, 3, :], in0=pGv[0:64, :, last_t], in1=sm[:, 0, :], op=ALU.subtract), reads=kpG + [k_sm], writes=[k_sm])
                    S.op("dve", lambda e: e.tensor_scalar(out=egl, in0=pGv[:, :, last_t], scalar1=-80.0, scalar2=None, op0=ALU.max), reads=kpG, writes=[k_egl])
                    if need_o:
                        S.op("dve", lambda e: e.tensor_scalar(out=eGbc, in0=pGv, scalar1=-80.0, scalar2=None, op0=ALU.max), reads=kpG, writes=[k_eGbc])
                    yield "p"
                    S.op("dve", lambda e: e.scalar_tensor_tensor(out=EA, in0=Y, scalar=80.0, in1=mk(mA_s), op0=ALU.min, op1=ALU.mult), reads=[k_Y, k_cm], writes=[k_EA])
                    S.op("act", lambda e: e.activation(out=EA, in_=EA, func=AF.Exp, scale=-1.0), reads=[k_EA], writes=[k_EA])
                    S.op("dve", lambda e: e.scalar_tensor_tensor(out=EB, in0=Y, scalar=-80.0, in1=mk(mB_i), op0=ALU.max, op1=ALU.mult), reads=[k_Y, k_cm], writes=[k_EB])
                    S.op("act", lambda e: e.activation(out=EB, in_=EB, func=AF.Exp), reads=[k_EB], writes=[k_EB])
                    yield "p"
                    S.op("dve", lambda e: e.tensor_scalar(out=sm[:, 1, :], in0=sm[:, 0, :], scalar1=-80.0, scalar2=None, op0=ALU.max), reads=[k_sm], writes=[k_sm])
                    S.op("act", lambda e: e.activation(out=sm[:, 1, :], in_=sm[:, 1, :], func=AF.Exp), reads=[k_sm], writes=[k_sm])
                    S.op("dve", lambda e: e.tensor_tensor(out=sm[:, 2, :], in0=sm[:, 1, :], in1=beta, op=ALU.mult), reads=[k_sm, k_sct], writes=[k_sm])
                    S.op("dve", lambda e: e.tensor_scalar(out=sm[:, 3, :], in0=sm[:, 3, :], scalar1=-80.0, scalar2=None, op0=ALU.max), reads=[k_sm], writes=[k_sm])
                    S.op("act", lambda e: e.activation(out=sm[:, 3, :], in_=sm[:, 3, :], func=AF.Exp), reads=[k_sm], writes=[k_sm])
                    S.op("act", lambda e: e.activation(out=egl, in_=egl, func=AF.Exp), reads=[k_egl], writes=[k_egl])
                    if need_o:
                        S.op("act", lambda e: e.activation(out=eGbc, in_=eGbc, func=AF.Exp), reads=[k_eGbc], writes=[k_eGbc])
                        S.op("pool", lambda e: e.tensor_tensor(out=qdT, in0=qg[:, h0:h0 + HH, c0:c0 + 64], in1=eGbc, op=ALU.mult), reads=[kqg, k_eGbc], writes=[k_qdT])
                    yield "p"
                    for (src, ksrc, dst, kdst) in ((kg, kkg, ktok, k_ktok), (vg, kvg, vtok, k_vtok)):
                        pT, kpT = ps_next()
                        pTv = pT.bitcast(BF16)

                        def trk(e, src=src, pTv=pTv):
                            last = None
                            for h in range(HH):
                                last = e.transpose(pTv[0:64, h * 128:(h + 1) * 128], src[:, h0 + h, c0:c0 + 64], ident_b)
                            return last
                        S.op("pe", trk, reads=[ksrc, k_ident], writes=[kpT])
                        S.op("act", lambda e, dst=dst, pTv=pTv: e.activation(out=dst, in_=pTv[0:64, :].rearrange("p (h d) -> p h d", h=HH), func=AF.Copy),
                             reads=[kpT], writes=[kdst])
                    pK, kpK1 = ps_next(); kpK = [kpK1]
                    pKv = pK.rearrange("p (h s) -> p h s", h=HH)

                    def mmK(e):
                        last = None
                        for h in range(HH):
                            last = e.matmul(pK[0:64, h * 64:(h + 1) * 64], kg[:, h0 + h, c0:c0 + 64], kg[:, h0 + h, c0:c0 + 64], start=True, stop=True)
                        return last
                    S.op("pe", mmK, reads=[kkg], writes=kpK)
                    S.op("dve", lambda e: e.tensor_tensor(out=tmpM, in0=pKv[0:64], in1=EA, op=ALU.mult), reads=kpK + [k_EA], writes=[k_tmpM])
                    S.op("dve", lambda e: e.tensor_tensor(out=tmpB, in0=pKv[0:64], in1=EB, op=ALU.mult), reads=kpK + [k_EB], writes=[k_tmpB])
                    if need_o:
                        pQ, kpQ1 = ps_next(); kpQ = [kpQ1]
                        pQv = pQ.rearrange("p (h s) -> p h s", h=HH)

                        def mmQ(e):
                            last = None
                            for h in range(HH):
                                last = e.matmul(pQ[0:64, h * 64:(h + 1) * 64], kg[:, h0 + h, c0:c0 + 64], qg[:, h0 + h, c0:c0 + 64], start=True, stop=True)
                            return last
                        S.op("pe", mmQ, reads=[kkg, kqg], writes=kpQ)
                        S.op("dve", lambda e: e.tensor_tensor(out=tmpQ, in0=pQv[0:64], in1=EB, op=ALU.mult), reads=kpQ + [k_EB], writes=[k_tmpQ])
                    yield "p"
                    S.op("dve", lambda e: e.tensor_tensor(out=tmpM, in0=tmpM, in1=bc_s(beta), op=ALU.mult), reads=[k_tmpM, k_sct], writes=[k_tmpM])
                    S.op("pool", lambda e: e.tensor_tensor(out=Am, in0=tmpM, in1=mkb(mA_s), op=ALU.mult), reads=[k_tmpM, k_cmb], writes=[k_Am])
                    S.op("dve", lambda e: e.tensor_tensor(out=tmpB, in0=tmpB, in1=bbc, op=ALU.mult), reads=[k_tmpB, k_bbc], writes=[k_tmpB])
                    S.op("pool", lambda e: e.tensor_tensor(out=ATm, in0=tmpB, in1=mkb(mB_s), op=ALU.mult), reads=[k_tmpB, k_cmb], writes=[k_ATm])
                    if need_o:
                        S.op("pool", lambda e: e.tensor_tensor(out=QKT, in0=tmpQ, in1=mkb(mB_i), op=ALU.mult), reads=[k_tmpQ, k_cmb], writes=[k_QKT])
                    yield "p"
                    S.op("pool", lambda e: e.tensor_tensor(out=P0, in0=Am, in1=m2(4), op=ALU.mult), reads=[k_Am, k_cm2b], writes=[kP0])
                    S.op("pool", lambda e: e.tensor_tensor(out=PT0, in0=ATm, in1=m2(4), op=ALU.mult), reads=[k_ATm, k_cm2b], writes=[kPT0])
                    S.op("pool", lambda e: e.tensor_tensor(out=Tm, in0=P0, in1=identb64, op=ALU.add), reads=[kP0, k_ident], writes=[k_T])
                    S.op("pool", lambda e: e.tensor_tensor(out=TTm, in0=PT0, in1=identb64, op=ALU.add), reads=[kPT0, k_ident], writes=[k_TT])
                    p1, kp1 = mmset(PT0, kPT0, P0, kP0)
                    S.op("act", lambda e: e.activation(out=P1, in_=hm(p1), func=AF.Copy), reads=kp1, writes=[kP1])
                    p2_, kp2_ = mmset(P0, kP0, PT0, kPT0)
                    S.op("act", lambda e: e.activation(out=PT1, in_=hm(p2_), func=AF.Copy), reads=kp2_, writes=[kPT1])
                    yield "p"
                    p3, kp3 = mmset(TTm, k_TT, P1, kP1)
                    p4, kp4 = mmset(P1, kP1, TTm, k_TT)
                    S.op("dve", lambda e: e.tensor_tensor(out=Tm, in0=Tm, in1=hm(p3), op=ALU.add), reads=kp3 + [k_T], writes=[k_T])
                    S.op("dve", lambda e: e.tensor_tensor(out=TTm, in0=TTm, in1=hm(p4), op=ALU.add), reads=kp4 + [k_TT], writes=[k_TT])
                    p5, kp5 = mmset(PT1, kPT1, P1, kP1)
                    S.op("act", lambda e: e.activation(out=P0, in_=hm(p5), func=AF.Copy), reads=kp5, writes=[kP0])
                    yield "p"
                    p6, kp6 = mmset(TTm, k_TT, P0, kP0)
                    p7, kp7 = mmset(P0, kP0, TTm, k_TT)
                    S.op("dve", lambda e: e.tensor_tensor(out=Tm, in0=Tm, in1=hm(p6), op=ALU.add), reads=kp6 + [k_T], writes=[k_T])
                    S.op("dve", lambda e: e.tensor_tensor(out=TTm, in0=TTm, in1=hm(p7), op=ALU.add), reads=kp7 + [k_TT], writes=[k_TT])
                    yield "p"
                    for lv in range(3):
                        S.op("pool", lambda e, lv=lv: e.tensor_tensor(out=PT0, in0=ATm, in1=m2(1 + lv), op=ALU.mult), reads=[k_ATm, k_cm2b], writes=[kPT0])
                        pX, kpX = mmset(PT0, kPT0, Tm, k_T)
                        S.op("act", lambda e, pX=pX: e.activation(out=P1, in_=hm(pX), func=AF.Copy), reads=kpX, writes=[kP1])
                        yield "p"
                        p8, kp8 = mmset(P1, kP1, TTm, k_TT)
                        if lv < 2:
                            p9, kp9 = mmset(TTm, k_TT, P1, kP1)
                            S.op("dve", lambda e, p9=p9: e.tensor_tensor(out=Tm, in0=Tm, in1=hm(p9), op=ALU.subtract), reads=kp9 + [k_T], writes=[k_T])
                        S.op("dve", lambda e, p8=p8: e.tensor_tensor(out=TTm, in0=TTm, in1=hm(p8), op=ALU.subtract), reads=kp8 + [k_TT], writes=[k_TT])
                        yield "p"
                    S.op("pool", lambda e: e.tensor_tensor(out=vb, in0=vtok, in1=bc_e(beta), op=ALU.mult), reads=[k_vtok, k_sct], writes=[k_vb])
                    S.op("pool", lambda e: e.tensor_tensor(out=kbg, in0=ktok, in1=bc_e(sm[:, 2, :]), op=ALU.mult), reads=[k_ktok, k_sm], writes=[k_kbg])
                    S.op("pool", lambda e: e.tensor_tensor(out=kdec, in0=ktok, in1=bc_e(sm[:, 3, :]), op=ALU.mult), reads=[k_ktok, k_sm], writes=[k_kdec])
                    pU, kpU = ps_multi(2)

                    def mmU(e):
                        last = None
                        for h in range(HH):
                            last = e.matmul(pU[0:64, h * 128:(h + 1) * 128], TTm[:, h, :], vb[:, h, :], start=True, stop=True)
                        return last
                    S.op("pe", mmU, reads=[k_TT, k_vb], writes=kpU)
                    S.op("act", lambda e: e.activation(out=uu, in_=hm128(pU), func=AF.Copy), reads=kpU, writes=[k_uu])
                    pW, kpW1 = ps_next(); kpW = [kpW1]

                    def mmW(e):
                        last = None
                        for h in range(HH):
                            last = e.matmul(pW[:, h * 64:(h + 1) * 64], kbg[:, h, :], TTm[:, h, :], start=True, stop=True)
                        return last
                    S.op("pe", mmW, reads=[k_kbg, k_TT], writes=kpW)
                    S.op("act", lambda e: e.activation(out=wT, in_=pW.rearrange("p (h s) -> p h s", h=HH), func=AF.Copy), reads=kpW, writes=[k_wT])
                    yield "scan"
                    pV, kpV = ps_multi(2)

                    def mmV(e):
                        last = None
                        for h in range(HH):
                            last = e.matmul(pV[0:64, h * 128:(h + 1) * 128], wT[:, h, :], Sb[:, h, :], start=True, stop=True)
                        return last
                    S.op("pe", mmV, reads=[k_wT, k_Sb], writes=kpV)
                    S.op("dve", lambda e: e.tensor_tensor(out=vnew, in0=uu, in1=hm128(pV), op=ALU.subtract), reads=kpV + [k_uu], writes=[k_vnew])
                    yield "s"
                    if need_o:
                        pO, kpO = ps_multi(2)

                        def mmO(e):
                            last = None
                            for h in range(HH):
                                e.matmul(pO[0:64, h * 128:(h + 1) * 128], qdT[:, h, :], Sb[:, h, :], start=True, stop=False)
                                last = e.matmul(pO[0:64, h * 128:(h + 1) * 128], QKT[:, h, :], vnew[:, h, :], start=False, stop=True)
                            return last
                        S.op("pe", mmO, reads=[k_qdT, k_Sb, k_QKT, k_vnew], writes=kpO)
                        ot, kot = T_.orr.next()
                        S.op("act", lambda e: e.activation(out=ot, in_=hm128(pO), func=AF.Copy), reads=kpO, writes=[kot])
                        Odst = Od0 if dr == 0 else Od1
                        S.dma("sp", Odst[g0 + c0:g0 + c0 + 64, h0 * 128:(h0 + HH) * 128], ot.rearrange("p h d -> p (h d)"), reads=[kot], writes=["Od%d_%d" % (dr, hh)])
                    pS, kpS = ps_multi(2)

                    def mmS(e):
                        last = None
                        for h in range(HH):
                            last = e.matmul(pS[:, h * 128:(h + 1) * 128], kdec[:, h, :], vnew[:, h, :], start=True, stop=True)
                        return last
                    S.op("pe", mmS, reads=[k_kdec, k_vnew], writes=kpS)
                    S.op("dve", lambda e: e.tensor_tensor(out=Sst, in0=Sst, in1=egl.unsqueeze(2).broadcast_to([128, HH, 128]), op=ALU.mult), reads=[k_S, k_egl], writes=[k_S])
                    S.op("dve", lambda e: e.tensor_tensor(out=Sst, in0=Sst, in1=pS.rearrange("p (h d) -> p h d", h=HH), op=ALU.add), reads=kpS + [k_S], writes=[k_S])
                    S.op("act", lambda e: e.activation(out=Sb, in_=Sst, func=AF.Copy), reads=[k_S], writes=[k_Sb])
                    yield "done"

                def advance(gens, until):
                    live = list(gens)
                    while live:
                        for g in list(live):
                            try:
                                v = next(g)
                            except StopIteration:
                                live.remove(g); continue
                            if v == until:
                                live.remove(g)

                pending = None
                nchunk = 0
                for gi in groups:
                    g0 = gi * 256
                    need_o = gi < NOWN // 256
                    qg, kqg = qr.next(); kg, kkg = kr.next(); vg, kvg = vr.next(); scg, kscg = scr.next()
                    if need_o:
                        S.dma("sp", qg, QTv[:, :, g0:g0 + 256], reads=["qT"], writes=[kqg])
                    S.dma("sp", kg, KTv[:, :, g0:g0 + 256], reads=["kT"], writes=[kkg])
                    S.dma("sp", vg, VTv[:, :, g0:g0 + 256], reads=["gvT"], writes=[kvg])
                    S.dma("sp", scg, SCT[:, g0:g0 + 256], reads=["SCT"], writes=[kscg])
                    for ci in (range(4) if dr == 0 else range(3, -1, -1)):
                        gens = [chunk(hh, gi, ci, need_o, g0, qg, kqg, kg, kkg, vg, kvg, scg, kscg, nchunk % 2) for hh in range(2)]
                        advance(gens, "scan")
                        if pending is not None:
                            advance(pending, "done")
                        pending = gens
                        nchunk += 1
                advance(pending, "done")

            for dr_ in range(2):
                run_dir(dr_)

        if "D" in phases:
            phase_d()


        def phase_e():
            CG = 512
            NG = D // CG

            def load_consts(names):
                out = {}
                for nm, src, shp in names:
                    t, k = A.alloc(shp, BF16, nm)
                    S.dma("poolq", t, src, writes=[k])
                    out[nm] = (t, k)
                return out

            S.barrier(); A.reset(PERSIST)
            zf, k_zf = A.alloc([33, N_], F32, "zf")
            fw1, k_fw1 = A.alloc([33, 64], F32, "fw1"); fw2, k_fw2 = A.alloc([64, 64], F32, "fw2"); fw3, k_fw3 = A.alloc([64, 64], F32, "fw3")
            fsm, k_fsm = A.alloc([64, 8], F32, "fsm")
            fout, k_fout = A.alloc([64, 2 * D], F32, "fout")
            dl, k_dl = A.alloc([128, D], F32, "deltas")
            tau, k_tau = A.alloc([128, 48], F32, "tau")
            hr = [A.alloc([64, 512], F32, "hmlp%d" % i) for i in range(3)]
            decr = Ring(A, 2, [128, 512], F32, "dec")
            fm, k_fm = A.alloc([64, 512], F32, "fm")
            hcr = Ring(A, 3, [128, 512], BF16, "hc")
            for (t, k, src) in ((zf, k_zf, zf_d), (fw1, k_fw1, fw1_d), (fw2, k_fw2, fw2_d), (fw3, k_fw3, fw3_d),
                                (fsm[:, 0:4], k_fsm, fsm_d), (fout, k_fout, fout_d), (dl, k_dl, deltas_d), (tau, k_tau, tau_d)):
                S.dma("sp", t, src, writes=[k])
            for i in range(3):
                S.op("dve", lambda e, i=i: e.tensor_tensor(out=fsm[:, 4 + i:5 + i], in0=fsm[:, i:i + 1], in1=fsm[:, 3:4], op=ALU.mult), reads=[k_fsm], writes=[k_fsm])
            for jb in range(N_ // 512):
                j0 = jb * 512
                prev, kprev = zf[:, j0:j0 + 512], k_zf
                for li, (w, kw) in enumerate(((fw1, k_fw1), (fw2, k_fw2), (fw3, k_fw3))):
                    ps, kp = ps_next()
                    S.op("pe", lambda e, ps=ps, w=w, prev=prev: e.matmul(ps[0:64, :], w, prev, start=True, stop=True), reads=[kw, kprev], writes=[kp])
                    ht, kht = hr[li]
                    S.op("act", lambda e, ps=ps, ht=ht, li=li: e.activation(out=ht, in_=ps[0:64, :], func=AF.Identity, scale=fsm[:, 3:4], bias=fsm[:, 4 + li:5 + li]),
                         reads=[kp, k_fsm], writes=[kht])
                    for rep in range(3):
                        S.op("dve", lambda e, ht=ht: e.tensor_scalar(out=fm, in0=ht, scalar1=math.pi, scalar2=-2.0 * math.pi, op0=ALU.is_gt, op1=ALU.mult), reads=[kht], writes=[k_fm])
                        S.op("dve", lambda e, ht=ht: e.tensor_tensor(out=ht, in0=ht, in1=fm, op=ALU.add), reads=[kht, k_fm], writes=[kht])
                        S.op("dve", lambda e, ht=ht: e.tensor_scalar(out=fm, in0=ht, scalar1=-math.pi, scalar2=2.0 * math.pi, op0=ALU.is_lt, op1=ALU.mult), reads=[kht], writes=[k_fm])
                        S.op("dve", lambda e, ht=ht: e.tensor_tensor(out=ht, in0=ht, in1=fm, op=ALU.add), reads=[kht, k_fm], writes=[kht])
                    S.op("act", lambda e, ht=ht: e.activation(out=ht, in_=ht, func=AF.Sin), reads=[kht], writes=[kht])
                    prev, kprev = ht, kht
                for rt in range(4):
                    jt = jb * 4 + rt
                    fcol = 0 if jt < NOWN // 128 else D
                    for cc in range(4):
                        ps, kp = ps_next()
                        S.op("pe", lambda e, ps=ps, prev=prev, rt=rt, cc=cc, fcol=fcol: e.matmul(
                            ps[:, :], prev[:, rt * 128:(rt + 1) * 128], fout[:, fcol + cc * 512:fcol + (cc + 1) * 512], start=True, stop=True),
                            reads=[kprev, k_fout], writes=[kp])
                        dec, kdec = decr.next()
                        S.op("act", lambda e, dec=dec, cc=cc, jt=jt: e.activation(out=dec, in_=dl[:, cc * 512:(cc + 1) * 512], func=AF.Exp, scale=tau[:, jt:jt + 1]),
                             reads=[k_dl, k_tau], writes=[kdec])
                        hc, khc = hcr.next()
                        S.op("dve", lambda e, hc=hc, ps=ps, dec=dec: e.tensor_tensor(out=hc, in0=ps[:, :], in1=dec, op=ALU.mult), reads=[kp, kdec], writes=[khc])
                        S.dma("sp", Hc[jt * 128:(jt + 1) * 128, cc * 512:(cc + 1) * 512], hc, reads=[khc], writes=["Hc"])

            def stage1(src, nm_src, is_filter):
                S.barrier(); A.reset(PERSIST)
                W1, k_W1 = A.alloc([128, 48, 2, 128], BF16, "W1")
                S.dma("poolq", W1.rearrange("p a b c -> p (a b c)"), W1_d, writes=[k_W1])
                ztr = Ring(A, 2, [128, 48, CG], BF16, "zt")
                aor = Ring(A, 4, [128, 2, CG], BF16, "ao")
                K1 = 128 if is_filter else 86
                if not is_filter:
                    for (zt, kz) in ztr.items:
                        S.op("pool", lambda e, zt=zt: e.memset(zt[64:128], 0.0), writes=[kz])
                for cg in range(NG):
                    c0 = cg * CG
                    zt, kz = ztr.next()
                    if is_filter:
                        S.dma("sp", zt, src[:, c0:c0 + CG].rearrange("(a b) c -> a b c", b=48), reads=[nm_src], writes=[kz])
                    else:
                        S.dma("sp", zt[0:85], src[0:4080, c0:c0 + CG].rearrange("(a b) c -> a b c", b=48), reads=[nm_src], writes=[kz])
                        S.dma("sp", zt[85:86, 0:16, :], src[4080:4096, c0:c0 + CG].rearrange("(a b) c -> a b c", a=1), reads=[nm_src], writes=[kz])
                    for t2 in range(48):
                        pp, kpp = ps_multi(2)

                        def mm(e, pp=pp, zt=zt, t2=t2):
                            e.matmul(pp[:, 0:512], W1[0:K1, t2, 0, :], zt[0:K1, t2, :], start=True, stop=True)
                            return e.matmul(pp[:, 512:1024], W1[0:K1, t2, 1, :], zt[0:K1, t2, :], start=True, stop=True)
                        S.op("pe", mm, reads=[k_W1, kz], writes=kpp)
                        ao, kao = aor.next()
                        eng = "act" if t2 % 2 == 0 else "dve"
                        if eng == "act":
                            S.op("act", lambda e, ao=ao, pp=pp: e.activation(out=ao, in_=pp.rearrange("p (r c) -> p r c", r=2), func=AF.Copy), reads=kpp, writes=[kao])
                        else:
                            S.op("dve", lambda e, ao=ao, pp=pp: e.tensor_copy(out=ao, in_=pp.rearrange("p (r c) -> p r c", r=2)), reads=kpp, writes=[kao])
                        S.dma("sp", Ad[:, :, c0:c0 + CG].rearrange("f (r t) c -> f r t c", r=2)[:, :, t2, :], ao, reads=[kao], writes=["Ad"])

            def stage2_filter():
                S.barrier(); A.reset(PERSIST)
                cs = load_consts([("W2a", W2a_d, [96, 96]), ("W2b", W2b_d, [96, 96])])
                atr = Ring(A, 2, [96, 16, CG], BF16, "at")
                kor = Ring(A, 2, [96, 2, 16, CG], BF16, "ko")
                for cg in range(NG):
                    c0 = cg * CG
                    for fb in range(8):
                        at, kat = atr.next()
                        S.dma("sp", at, Ad[fb * 16:(fb + 1) * 16, :, c0:c0 + CG].rearrange("f rt c -> rt f c"), reads=["Ad"], writes=[kat])
                        ko, kko = kor.next()
                        for fi in range(16):
                            pp, kpp = ps_multi(2)

                            def mm(e, pp=pp, at=at, fi=fi):
                                e.matmul(pp[0:96, 0:512], cs["W2a"][0], at[:, fi, :], start=True, stop=True)
                                return e.matmul(pp[0:96, 512:1024], cs["W2b"][0], at[:, fi, :], start=True, stop=True)
                            S.op("pe", mm, reads=[cs["W2a"][1], cs["W2b"][1], kat], writes=kpp)
                            eng = "act" if fi % 2 == 0 else "dve"
                            if eng == "act":
                                S.op("act", lambda e, ko=ko, pp=pp, fi=fi: e.activation(out=ko[:, :, fi, :], in_=pp[0:96].rearrange("p (r c) -> p r c", r=2), func=AF.Copy), reads=kpp, writes=[kko])
                            else:
                                S.op("dve", lambda e, ko=ko, pp=pp, fi=fi: e.tensor_copy(out=ko[:, :, fi, :], in_=pp[0:96].rearrange("p (r c) -> p r c", r=2)), reads=kpp, writes=[kko])
                        for ab in range(2):
                            S.dma("sp", Kd[ab, :, fb * 16:(fb + 1) * 16, c0:c0 + CG], ko[:, ab, :, :], reads=[kko], writes=["Kd"])

            def stage2_data():
                S.barrier(); A.reset(PERSIST)
                cs = load_consts([("W2", W2_d, [96, 96]), ("L1", L1_d, [96, 96]), ("L2", L2_d, [96, 96])])
                atr = Ring(A, 2, [96, 16, CG], BF16, "at")
                kar = Ring(A, 2, [96, 16, CG], BF16, "ka"); kbr = Ring(A, 2, [96, 16, CG], BF16, "kb")
                btr = Ring(A, 2, [96, 16, CG], BF16, "bt")
                p1r = Ring(A, 3, [96, CG], BF16, "p1"); p2r = Ring(A, 3, [96, CG], BF16, "p2")
                for cg in range(NG):
                    c0 = cg * CG
                    for fb in range(8):
                        at, kat = atr.next(); ka, kka = kar.next(); kb, kkb = kbr.next(); bt, kbt = btr.next()
                        S.dma("sp", at, Ad[fb * 16:(fb + 1) * 16, :, c0:c0 + CG].rearrange("f rt c -> rt f c"), reads=["Ad"], writes=[kat])
                        S.dma("sp", ka, Kd[0, :, fb * 16:(fb + 1) * 16, c0:c0 + CG], reads=["Kd"], writes=[kka])
                        S.dma("sp", kb, Kd[1, :, fb * 16:(fb + 1) * 16, c0:c0 + CG], reads=["Kd"], writes=[kkb])
                        pend = None
                        for fi in range(16):
                            ps, kp = ps_next()
                            S.op("pe", lambda e, ps=ps, at=at, fi=fi: e.matmul(ps[0:96, :], cs["W2"][0], at[:, fi, :], start=True, stop=True), reads=[cs["W2"][1], kat], writes=[kp])
                            p1, kp1 = p1r.next(); p2, kp2 = p2r.next()
                            S.op("dve", lambda e, ps=ps, p1=p1, ka=ka, fi=fi: e.tensor_tensor(out=p1, in0=ps[0:96, :], in1=ka[:, fi, :], op=ALU.mult), reads=[kp, kka], writes=[kp1])
                            S.op("dve", lambda e, ps=ps, p2=p2, kb=kb, fi=fi: e.tensor_tensor(out=p2, in0=ps[0:96, :], in1=kb[:, fi, :], op=ALU.mult), reads=[kp, kkb], writes=[kp2])

                            def part2(p1=p1, kp1=kp1, p2=p2, kp2=kp2, fi=fi, bt=bt, kbt=kbt):
                                ps2, kps2 = ps_next()

                                def mm(e):
                                    e.matmul(ps2[0:96, :], cs["L1"][0], p1, start=True, stop=False)
                                    return e.matmul(ps2[0:96, :], cs["L2"][0], p2, start=False, stop=True)
                                S.op("pe", mm, reads=[cs["L1"][1], cs["L2"][1], kp1, kp2], writes=[kps2])
                                S.op("act", lambda e: e.activation(out=bt[:, fi, :], in_=ps2[0:96, :], func=AF.Copy), reads=[kps2], writes=[kbt])
                            if pend is not None:
                                pend()
                            pend = part2
                        pend()
                        S.dma("sp", Bd[fb * 16:(fb + 1) * 16, :, c0:c0 + CG].rearrange("f rt c -> rt f c"), bt, reads=[kbt], writes=["Bd"])

            def stage1_inv():
                S.barrier(); A.reset(PERSIST)
                Vt, k_V = A.alloc([128, 48, 2, 64], BF16, "V")
                S.dma("poolq", Vt.rearrange("p a b c -> p (a b c)"), V_d, writes=[k_V])
                b2r = Ring(A, 2, [128, 2, 8, CG], BF16, "b2")
                yor = Ring(A, 2, [64, 8, CG], BF16, "yo")
                Ydv = Yd[0:43 * 48, :].rearrange("(a b) c -> a b c", b=48)
                for cg in range(NG):
                    c0 = cg * CG
                    for tb in range(6):
                        b2, kb2 = b2r.next(); yo, kyo = yor.next()
                        for ri in range(2):
                            S.dma("sp", b2[:, ri, :, :], Bd[:, ri * 48 + tb * 8:ri * 48 + tb * 8 + 8, c0:c0 + CG], reads=["Bd"], writes=[kb2])
                        for ti in range(8):
                            t2 = tb * 8 + ti
                            ps, kp = ps_next()

                            def mm(e, ps=ps, b2=b2, t2=t2, ti=ti):
                                e.matmul(ps[0:43, :], Vt[:, t2, 0, 0:43], b2[:, 0, ti, :], start=True, stop=False)
                                return e.matmul(ps[0:43, :], Vt[:, t2, 1, 0:43], b2[:, 1, ti, :], start=False, stop=True)
                            S.op("pe", mm, reads=[k_V, kb2], writes=[kp])
                            if ti % 2 == 0:
                                S.op("act", lambda e, ps=ps, yo=yo, ti=ti: e.activation(out=yo[0:43, ti, :], in_=ps[0:43, :], func=AF.Copy), reads=[kp], writes=[kyo])
                            else:
                                S.op("dve", lambda e, ps=ps, yo=yo, ti=ti: e.tensor_copy(out=yo[0:43, ti, :], in_=ps[0:43, :]), reads=[kp], writes=[kyo])
                        S.dma("sp", Ydv[:, tb * 8:(tb + 1) * 8, c0:c0 + CG], yo[0:43], reads=[kyo], writes=["Yd"])

            phase_e0 = None
            stage1(Hc, "Hc", True)
            stage2_filter()
            stage1(Zt, "Zt", False)
            stage2_data()
            stage1_inv()

        if "E" in phases:
            phase_e()

        def phase_fgh():
            S.barrier(); A.reset(PERSIST)
            R1, k_R1 = A.alloc([128, 24576], BF16, "R1")
            GTt = R1[:, 0:16384].rearrange("p (a b) -> p a b", a=32)
            ZGt = R1[:, 16384:24576].rearrange("p (a b) -> p a b", a=16)
            actT = R1[:, 0:22528].rearrange("p (a b) -> p a b", a=44)
            hy, k_hy = A.alloc([128, 16, 512], BF16, "hy")
            gn, k_gn = A.alloc([128, 16, 512], BF16, "gn")
            mixed, k_mx = A.alloc([128, 16, 512], BF16, "mixed")
            x1T, k_x1 = A.alloc([128, 16, 512], F32, "x1T")
            rbc, k_rbc = A.alloc([128, 512], F32, "rbc")
            wr = Ring(A, 3, [128, 16, 128], BF16, "w16")
            wdr = Ring(A, 2, [128, 44, 128], BF16, "w44")
            tfr = Ring(A, 2, [128, 512], F32, "tf")
            tbr = Ring(A, 3, [128, 512], BF16, "tb")
            xtr = Ring(A, 3, [128, D], F32, "xt")
            onr = Ring(A, 1, [128, D], BF16, "on")
            ytr = Ring(A, 2, [128, 4, 128], BF16, "yt")
            st, k_st = A.alloc([128, 64], F32, "st")
            hf, k_hf = hy, k_hy
            sqT, k_sqT = gn, k_gn
            ga_a = mod[:, 32:48, 0]; sh_f = mod[:, 48:64, 0]; ga_f = mod[:, 80:96, 0]
            whv, wgv, wov, wuv, wdv = w_hy_out_d, w_gdn_out_d, w_o_d, w_up_d, w_down_d

            def proj16(wv, c0, rhs, krhs, nkc=16, ring=None):
                wt, kw = (ring or wr).next()
                S.dma("poolq", wt.rearrange("p a b -> p (a b)"), wv[c0 // 128], writes=[kw])
                ps, kp = ps_next()

                def mm(e):
                    last = None
                    for kc in range(nkc):
                        last = e.matmul(ps[:, :], wt[:, kc, :], rhs[:, kc, :], start=(kc == 0), stop=(kc == nkc - 1))
                    return last
                S.op("pe", mm, reads=[kw, krhs], writes=[kp])
                return ps, kp

            def rms_bc(srcT, ksrc):
                S.op("act", lambda e: e.activation(out=sqT, in_=srcT, func=AF.Square), reads=[ksrc], writes=[k_sqT])
                ps, kp = ps_next()

                def mm(e):
                    last = None
                    for fc in range(16):
                        last = e.matmul(ps[:, :], ones_b, sqT[:, fc, :], start=(fc == 0), stop=(fc == 15))
                    return last
                S.op("pe", mm, reads=[k_sqT, k_ones], writes=[kp])
                S.op("dve", lambda e: e.tensor_scalar(out=rbc, in0=ps[:, :], scalar1=1.0 / D, scalar2=EPS, op0=ALU.mult, op1=ALU.add), reads=[kp], writes=[k_rbc])
                S.op("dve", lambda e: e.reciprocal(out=rbc, in_=rbc), reads=[k_rbc], writes=[k_rbc])
                S.op("act", lambda e: e.activation(out=rbc, in_=rbc, func=AF.Sqrt), reads=[k_rbc], writes=[k_rbc])

            for tb in range(NOWN // 512):
                t0 = tb * 512
                S.dma("sp", GTt, GT[:, t0:t0 + 512].rearrange("(a p) t -> p a t", p=128), writes=[k_R1])
                S.dma("sp", ZGt, ZGT[:, t0:t0 + 512].rearrange("(a p) t -> p a t", p=128), writes=[k_R1])
                for cc in range(16):
                    x0t, kx0 = tbr.next(); zt, kz = tbr.next(); yt, kyt = ytr.next()
                    S.dma("sp", x0t, X0T[cc * 128:(cc + 1) * 128, t0:t0 + 512], reads=["X0T"], writes=[kx0])
                    S.dma("sp", zt, ZT[cc * 128:(cc + 1) * 128, t0:t0 + 512], reads=["ZT"], writes=[kz])
                    S.dma("sp", yt, Yd[t0:t0 + 512, cc * 128:(cc + 1) * 128].rearrange("(a p) c -> p a c", p=128), reads=["Yd"], writes=[kyt])
                    ps, kp = ps_next(); psv = ps.bitcast(BF16)

                    def trY(e, yt=yt, psv=psv):
                        last = None
                        for a in range(4):
                            last = e.transpose(psv[:, a * 128:(a + 1) * 128], yt[:, a, :], ident_b)
                        return last
                    S.op("pe", trY, reads=[kyt, k_ident], writes=[kp])
                    tf, ktf = tfr.next()
                    S.op("dve", lambda e, tf=tf, zt=zt, psv=psv, cc=cc: e.scalar_tensor_tensor(
                        out=tf, in0=zt, scalar=hybT[:, cc:cc + 1], in1=psv[:, 0:512], op0=ALU.mult, op1=ALU.add), reads=[kz, kp, k_hyb], writes=[ktf])
                    S.op("dve", lambda e, tf=tf, x0t=x0t, cc=cc: e.tensor_tensor(out=hy[:, cc, :], in0=tf, in1=x0t, op=ALU.mult), reads=[ktf, kx0], writes=[k_hy])
                for tt in range(4):
                    ot, kot = xtr.next(); on, kon = onr.next()
                    tq, ktq = xtr.next()
                    S.dma("sp", ot, Od0[t0 + tt * 128:t0 + (tt + 1) * 128, :], reads=["Od0_0", "Od0_1"], writes=[kot])
                    S.dma("sp", tq, Od1[t0 + tt * 128:t0 + (tt + 1) * 128, :], reads=["Od1_0", "Od1_1"], writes=[ktq])
                    S.op("dve", lambda e, tq=tq, ot=ot: e.tensor_tensor(out=ot, in0=ot, in1=tq, op=ALU.add), reads=[kot, ktq], writes=[kot])
                    S.op("act", lambda e, tq=tq, ot=ot: e.activation(out=tq, in_=ot, func=AF.Square), reads=[kot], writes=[ktq])
                    S.op("dve", lambda e, tq=tq: e.reduce_sum(out=st[:, 0:16], in_=tq.rearrange("p (h e) -> p h e", h=16), axis=AX.X), reads=[ktq], writes=[k_st])
                    S.op("dve", lambda e: e.tensor_scalar(out=st[:, 16:32], in0=st[:, 0:16], scalar1=1.0 / 128, scalar2=EPS, op0=ALU.mult, op1=ALU.add), reads=[k_st], writes=[k_st])
                    S.op("dve", lambda e: e.reciprocal(out=st[:, 32:48], in_=st[:, 16:32]), reads=[k_st], writes=[k_st])
                    S.op("act", lambda e: e.activation(out=st[:, 48:64], in_=st[:, 32:48], func=AF.Sqrt), reads=[k_st], writes=[k_st])
                    for h in range(16):
                        eng = "act" if h % 2 == 0 else "dve"
                        if eng == "act":
                            S.op("act", lambda e, on=on, ot=ot, h=h: e.activation(out=on[:, h * 128:(h + 1) * 128], in_=ot[:, h * 128:(h + 1) * 128],
                                                                            func=AF.Copy, scale=st[:, 48 + h:49 + h]), reads=[kot, k_st], writes=[kon])
                        else:
                            S.op("dve", lambda e, on=on, ot=ot, h=h: e.tensor_scalar(out=on[:, h * 128:(h + 1) * 128], in0=ot[:, h * 128:(h + 1) * 128],
                                                                               scalar1=st[:, 48 + h:49 + h], scalar2=None, op0=ALU.mult), reads=[kot, k_st], writes=[kon])
                    for g in range(4):
                        ps, kp = ps_next(); psv = ps.bitcast(BF16)

                        def trO(e, on=on, psv=psv, g=g):
                            last = None
                            for j in range(4):
                                h = g * 4 + j
                                last = e.transpose(psv[:, j * 128:(j + 1) * 128], on[:, h * 128:(h + 1) * 128], ident_b)
                            return last
                        S.op("pe", trO, reads=[kon, k_ident], writes=[kp])
                        for j in range(4):
                            h = g * 4 + j
                            S.op("dve", lambda e, psv=psv, j=j, h=h, tt=tt: e.scalar_tensor_tensor(
                                out=gn[:, h, tt * 128:(tt + 1) * 128], in0=psv[:, j * 128:(j + 1) * 128], scalar=gnormT[:, 0:1],
                                in1=ZGt[:, h, tt * 128:(tt + 1) * 128], op0=ALU.mult, op1=ALU.mult), reads=[kp, k_gnorm, k_R1], writes=[k_gn])
                for m in range(16):
                    ps, kp = proj16(whv, m * 128, hy, k_hy)
                    tf, ktf = tfr.next()
                    S.op("dve", lambda e, tf=tf, ps=ps, m=m: e.tensor_tensor(out=tf, in0=ps[:, :], in1=GTt[:, m, :], op=ALU.mult), reads=[kp, k_R1], writes=[ktf])
                    ps2, kp2 = proj16(wgv, m * 128, gn, k_gn)
                    tf2, ktf2 = tfr.next()
                    S.op("dve", lambda e, tf2=tf2, ps2=ps2, m=m: e.tensor_tensor(out=tf2, in0=ps2[:, :], in1=GTt[:, 16 + m, :], op=ALU.mult), reads=[kp2, k_R1], writes=[ktf2])
                    S.op("dve", lambda e, tf=tf, tf2=tf2, m=m: e.tensor_tensor(out=mixed[:, m, :], in0=tf, in1=tf2, op=ALU.add), reads=[ktf, ktf2], writes=[k_mx])
                for tt in range(4):
                    xt, kx = xtr.next()
                    S.dma("sp", xt, x_d[t0 + tt * 128:t0 + (tt + 1) * 128, :], writes=[kx])
                    for g in range(4):
                        ps, kp = ps_next()

                        def trX(e, xt=xt, ps=ps, g=g):
                            last = None
                            for j in range(4):
                                fc = g * 4 + j
                                last = e.transpose(ps[:, j * 128:(j + 1) * 128], xt[:, fc * 128:(fc + 1) * 128], ident_f)
                            return last
                        S.op("pe", trX, reads=[kx, k_identf], writes=[kp])
                        S.op("act", lambda e, ps=ps, g=g, tt=tt: e.activation(
                            out=x1T[:, g * 4:(g + 1) * 4, tt * 128:(tt + 1) * 128], in_=ps[:, :].rearrange("p (a b) -> p a b", a=4), func=AF.Copy),
                            reads=[kp], writes=[k_x1])
                for m in range(16):
                    ps, kp = proj16(wov, m * 128, mixed, k_mx)
                    S.op("dve", lambda e, ps=ps, m=m: e.scalar_tensor_tensor(out=x1T[:, m, :], in0=ps[:, :], scalar=ga_a[:, m:m + 1], in1=x1T[:, m, :],
                                                                           op0=ALU.mult, op1=ALU.add), reads=[kp, k_mod, k_x1], writes=[k_x1])
                rms_bc(x1T, k_x1)
                for fc in range(16):
                    tf, ktf = tfr.next()
                    S.op("dve", lambda e, tf=tf, fc=fc: e.scalar_tensor_tensor(out=tf, in0=x1T[:, fc, :], scalar=scale_f[:, fc:fc + 1], in1=rbc,
                                                                             op0=ALU.mult, op1=ALU.mult), reads=[k_x1, k_scf, k_rbc], writes=[ktf])
                    S.op("act", lambda e, tf=tf, fc=fc: e.activation(out=hf[:, fc, :], in_=tf, func=AF.Identity, bias=sh_f[:, fc:fc + 1]),
                         reads=[ktf, k_mod], writes=[k_hf])
                for j in range(44):
                    psg, kpg = proj16(wuv, j * 128, hf, k_hf)
                    psu, kpu = proj16(wuv, D_FF + j * 128, hf, k_hf)
                    tf, ktf = tfr.next()
                    S.op("act", lambda e, tf=tf, psg=psg: e.activation(out=tf, in_=psg[:, :], func=AF.Silu), reads=[kpg], writes=[ktf])
                    S.op("dve", lambda e, tf=tf, psu=psu, j=j: e.tensor_tensor(out=actT[:, j, :], in0=tf, in1=psu[:, :], op=ALU.mult), reads=[ktf, kpu], writes=[k_R1])
                for m in range(16):
                    ps, kp = proj16(wdv, m * 128, actT, k_R1, nkc=44, ring=wdr)
                    S.op("dve", lambda e, ps=ps, m=m: e.scalar_tensor_tensor(out=x1T[:, m, :], in0=ps[:, :], scalar=ga_f[:, m:m + 1], in1=x1T[:, m, :],
                                                                           op0=ALU.mult, op1=ALU.add), reads=[kp, k_mod, k_x1], writes=[k_x1])
                rms_bc(x1T, k_x1)
                for fc in range(16):
                    S.op("dve", lambda e, fc=fc: e.scalar_tensor_tensor(out=x1T[:, fc, :], in0=x1T[:, fc, :], scalar=nfinT[:, fc:fc + 1], in1=rbc,
                                                                      op0=ALU.mult, op1=ALU.mult), reads=[k_x1, k_nfin, k_rbc], writes=[k_x1])
                for tt in range(4):
                    xo, kxo = xtr.next()
                    for g in range(4):
                        ps, kp = ps_next()

                        def trB(e, ps=ps, g=g, tt=tt):
                            last = None
                            for j in range(4):
                                fc = g * 4 + j
                                last = e.transpose(ps[:, j * 128:(j + 1) * 128], x1T[:, fc, tt * 128:(tt + 1) * 128], ident_f)
                            return last
                        S.op("pe", trB, reads=[k_x1, k_identf], writes=[kp])
                        S.op("act", lambda e, ps=ps, xo=xo, g=g: e.activation(out=xo[:, g * 512:(g + 1) * 512], in_=ps[:, :], func=AF.Copy), reads=[kp], writes=[kxo])
                    S.final_tokens.append(S.dma("sp", out_d[t0 + tt * 128:t0 + (tt + 1) * 128, :], xo, reads=[kxo], writes=["out"]))

        if "F" in phases:
            phase_fgh()
        S.emit()
    return nc


def _fm(v, n):
    return np.ascontiguousarray(np.asarray(v, np.float32).reshape(n, 128).T)


def _hyena_consts():
    n = L
    j = np.arange(N_)
    idx = np.where(j < NOWN, j, N_ - j).astype(np.float64)
    idx[NOWN] = 0
    tt = idx / (n - 1)
    bands = 16
    w = 2.0 * np.pi * idx / n
    f = np.linspace(1e-4, bands - 1, bands)
    zf = np.concatenate([tt[None, :], np.cos(f[:, None] * w[None, :]), -np.sin(f[:, None] * w[None, :])], axis=0)
    tau = -tt.copy(); tau[NOWN] = -30.0
    deltas = np.abs(np.linspace(math.log(1e-2) / 1.5, math.log(1e-2) / 0.3, D))
    t1 = np.arange(128)[:, None, None, None]; t2 = np.arange(48)[None, :, None, None]; f1 = np.arange(128)[None, None, None, :]
    th = 2 * np.pi * (t1 * f1 / 128.0 + t2 * f1 / float(N_))
    W1 = np.concatenate([np.cos(th), -np.sin(th)], axis=2)
    a = np.arange(48)
    th2 = 2 * np.pi * np.outer(a, a) / 48.0
    c2, s2 = np.cos(th2), np.sin(th2)
    W2 = np.block([[c2, -s2], [s2, c2]])
    W2a = np.block([[c2, c2], [s2, s2]])
    W2b = np.block([[-s2, -s2], [c2, c2]])
    L1 = np.block([[c2, s2], [-s2, c2]])
    L2 = np.block([[-s2, c2], [-c2, -s2]])
    f1v = np.arange(128)[:, None, None, None]; t2v = np.arange(48)[None, :, None, None]; t1v = np.arange(64)[None, None, None, :]
    ph = 2 * np.pi * f1v * (t2v / float(N_) + t1v / 128.0)
    V = np.concatenate([np.cos(ph), -np.sin(ph)], axis=2) / float(N_)
    f32 = lambda x: np.ascontiguousarray(x, dtype=np.float32)
    return dict(zf=f32(zf), tau=f32(tau.reshape(48, 128).T), deltas=f32(np.broadcast_to(deltas[None, :], (128, D))),
                W1=f32(W1.reshape(128, -1)), Vc=f32(V.reshape(128, -1)), W2=f32(W2), W2a=f32(W2a), W2b=f32(W2b), L1=f32(L1), L2=f32(L2))


def make_in_maps(inp):
    maps = []
    ident = np.eye(128, dtype=np.float32)
    tri = np.tril(np.ones((64, 64), np.float32))
    cmask = np.ascontiguousarray(np.stack([tri, np.tril(tri, -1), tri.T, np.triu(tri.T, 1)], axis=1))
    ii = np.arange(64)
    def same(b): return (ii[:, None] // b == ii[None, :] // b).astype(np.float32)
    cm2 = np.ascontiguousarray(np.stack([same(8), same(16) - same(8), same(32) - same(16), same(64) - same(32), -same(8)], axis=1))
    hyc_consts = _hyena_consts()
    w_in0 = inp["w_in"][0]
    cols = [SEG[t] + j * 128 for (t, j) in PLAN]
    w_inc_base = _chunk_major(w_in0, cols)
    shared = dict(
        w_hy_outc=_chunk_major(inp["w_hy_out"][0], [m * 128 for m in range(16)]),
        w_gdn_outc=_chunk_major(inp["w_gdn_out"][0], [m * 128 for m in range(16)]),
        w_oc=_chunk_major(inp["w_o"][0], [m * 128 for m in range(16)]),
        w_upc=_chunk_major(inp["w_up"][0], [j * 128 for j in range(88)]),
        w_downc=_chunk_major(inp["w_down"][0], [m * 128 for m in range(16)]),
    )
    w_inc_flip = None
    cache = {}

    def _w_inc_for(flip):
        if flip not in cache:
            if not flip:
                cache[flip] = w_inc_base
            else:
                wsc = w_in0[:, 14336:14400].reshape(D, 2, 2, 16)[:, :, ::-1, :].reshape(D, 64)
                arr = w_inc_base.copy()
                arr[PLAN_INDEX[("scal", 0)]] = _chunk_major(wsc, [0])[0]
                cache[flip] = arr
        return cache[flip]

    for core in range(8):
        b, half = core // 2, core % 2
        flip = half == 1
        x = inp["x"][b]; ctx = inp["ctx"][b]
        if flip:
            x = x[::-1]; ctx = ctx[::-1]
        cT = np.stack([_fm(inp["c"][b], 16), _fm(inp["c_ctx"], 16)], axis=-1)
        w_in = inp["w_in"][0]
        w_scal = w_in[:, 14336:14400]
        a_log = inp["gdn_a_log"][0]; dtb = inp["gdn_dt_bias"][0]
        hyc = inp["hy_conv"][0]; gdc = inp["gdn_conv"][0]
        if flip:
            w_scal = w_scal.reshape(D, 2, 2, 16)[:, :, ::-1, :].reshape(D, 64)
            a_log = a_log[::-1]; dtb = dtb[::-1]
            hyc = hyc[::-1]; gdc = gdc[::-1]
        scalp = np.zeros((64, 2), np.float32)
        scalp[32:64, 0] = a_log.reshape(32); scalp[32:64, 1] = dtb.reshape(32)
        hyconvT = np.ascontiguousarray(hyc.reshape(3, 48, 128).transpose(2, 1, 0))
        gdnconvT = np.ascontiguousarray(gdc.reshape(3, 48, 128).transpose(2, 1, 0))
        maps.append(dict(
            x=np.ascontiguousarray(x), ctx=np.ascontiguousarray(ctx), cT=np.ascontiguousarray(cT),
            w_ada=inp["w_ada"][0], b_adaT=_fm(inp["b_ada"][0], 96),
            nmixT=_fm(inp["norm_mix"][0], 16), nffnT=_fm(inp["norm_ffn"][0], 16),
            w_inc=_w_inc_for(flip),
            hyconvT=hyconvT, gdnconvT=gdnconvT, scalp=scalp, ident=ident, cmask=cmask, cm2=cm2,
            fw1=inp["hy_fw1"][0], fw2=inp["hy_fw2"][0], fw3=inp["hy_fw3"][0],
            fsm=np.ascontiguousarray(np.stack([inp["hy_fb1"][0], inp["hy_fb2"][0], inp["hy_fb3"][0], inp["hy_freq"][0]], axis=1)),
            fout=(np.ascontiguousarray(np.concatenate([inp["hy_fout"][0][:, D:], inp["hy_fout"][0][:, :D]], axis=1)) if flip else inp["hy_fout"][0]),
            **hyc_consts,
            **shared,
            hybT=_fm(inp["hy_bias"][0], 16), gnormT=_fm(inp["gdn_norm"][0], 1), nfinT=_fm(inp["norm_final"], 16),
        ))
    return maps


def kernel(**inputs):
    inp = {k: np.asarray(v) for k, v in inputs.items()}
    nc = build()
    maps = make_in_maps(inp)
    res = run_bass_kernel_spmd(nc, maps, core_ids=list(range(8)))
    out = np.empty((4, L, D), np.float32)
    for core in range(8):
        b, half = core // 2, core % 2
        o = res.results[core]["out"]
        if half == 0:
            out[b, :NOWN] = o
        else:
            out[b, NOWN:] = o[::-1]
    return out
```

```python
import contextlib
import math
import numpy as np
import ml_dtypes
import concourse.bass as bass
import concourse.mybir as mybir
from concourse.bass_utils import run_bass_kernel_spmd

F32 = mybir.dt.float32
BF16 = mybir.dt.bfloat16
AF = mybir.ActivationFunctionType
ALU = mybir.AluOpType
AX = mybir.AxisListType

D = 2048
L = 4096
NOWN = 2048
CTX = 256
TT = L + CTX
HEADS = 16
D_IN = 18496
D_FF = 5632
EPS = 1e-6
N_ = 6144

COMPUTE = ("pe", "act", "dve", "pool")
NSLOT = {"sp": 12, "poolq": 12}
QENG = {"sp": "sp", "poolq": "pool"}


class Op:
    __slots__ = ("fn", "waits", "sem", "inc")


class Sched:
    def __init__(self, nc):
        self.nc = nc
        self.streams = {e: [] for e in ("pe", "act", "dve", "pool", "sp")}
        self.cnt = {e: 0 for e in COMPUTE}
        self.slot_uses = {q: [0] * NSLOT[q] for q in NSLOT}
        self.dma_i = {q: 0 for q in NSLOT}
        self.last_w = {}
        self.reads = {}
        self.waited = {e: {} for e in self.streams}
        self.final_tokens = []

    def _need(self, stream, tok, waits):
        if tok is None:
            return
        semname, val, pstream, is_pe = tok
        if pstream == stream and is_pe:
            return
        if self.waited[stream].get(semname, 0) >= val:
            return
        waits[semname] = max(waits.get(semname, 0), val)

    def _deps(self, stream, reads, writes):
        waits = {}
        for k in reads:
            self._need(stream, self.last_w.get(k), waits)
        for k in writes:
            self._need(stream, self.last_w.get(k), waits)
            for t in self.reads.get(k, {}).values():
                self._need(stream, t, waits)
        for s, v in waits.items():
            self.waited[stream][s] = v
        return waits

    def _record(self, tok, reads, writes):
        for k in reads:
            d = self.reads.setdefault(k, {})
            o = d.get(tok[0])
            if o is None or o[1] < tok[1]:
                d[tok[0]] = tok
        for k in writes:
            self.last_w[k] = tok
            self.reads[k] = {}

    def op(self, eng, fn, reads=(), writes=()):
        waits = self._deps(eng, reads, writes)
        self.cnt[eng] += 1
        tok = (eng, self.cnt[eng], eng, eng == "pe")
        o = Op(); o.fn = fn; o.waits = sorted(waits.items()); o.sem = eng; o.inc = 1
        self.streams[eng].append(o)
        self._record(tok, reads, writes)
        return tok

    def dma(self, q, out, in_, reads=(), writes=()):
        reads = [k for k in reads if "#" in k]
        writes = [k for k in writes if "#" in k]
        stream = QENG[q]
        waits = self._deps(stream, reads, writes)
        i = self.dma_i[q]; self.dma_i[q] += 1
        slot = i % NSLOT[q]
        semname = "%s%d" % (q, slot)
        prev = self.slot_uses[q][slot] * 16
        if prev and self.waited[stream].get(semname, 0) < prev:
            waits[semname] = prev
            self.waited[stream][semname] = prev
        self.slot_uses[q][slot] += 1
        tok = (semname, self.slot_uses[q][slot] * 16, None, False)
        o = Op(); o.waits = sorted(waits.items()); o.sem = semname; o.inc = 16
        o.fn = (lambda e, out=out, in_=in_: e.dma_start(out=out, in_=in_))
        self.streams[stream].append(o)
        self._record(tok, reads, writes)
        return tok

    def barrier(self):
        allv = {e: self.cnt[e] for e in COMPUTE}
        for q in NSLOT:
            for s in range(NSLOT[q]):
                allv["%s%d" % (q, s)] = self.slot_uses[q][s] * 16
        for stream in self.streams:
            waits = {}
            for s, v in allv.items():
                if v and self.waited[stream].get(s, 0) < v and not (s == "pe" and stream == "pe"):
                    waits[s] = v
                    self.waited[stream][s] = v
            if waits:
                o = Op(); o.fn = None; o.waits = sorted(waits.items()); o.sem = None; o.inc = 0
                self.streams[stream].append(o)

    def emit(self):
        nc = self.nc
        names = list(COMPUTE) + ["%s%d" % (q, s) for q in NSLOT for s in range(NSLOT[q])]
        with contextlib.ExitStack() as st:
            sems = {n: st.enter_context(nc.semaphore("s_" + n)) for n in names}
            block = st.enter_context(nc.Block())

            def run(stream, final=False):
                def body(e):
                    for o in self.streams[stream]:
                        for s, v in o.waits:
                            e.wait_ge(sems[s], v)
                        if o.fn is not None:
                            o.fn(e).then_inc(sems[o.sem], o.inc)
                    if final:
                        for (s, v, _, _) in self.final_tokens:
                            e.wait_ge(sems[s], v)
                return body

            block.sync(run("sp", True))
            block.tensor(run("pe"))
            block.scalar(run("act"))
            block.vector(run("dve"))
            block.gpsimd(run("pool"))


class Arena:
    def __init__(self, big, total):
        self.big = big; self.total = total; self.off = 0; self.n = 0

    def reset(self, to=0):
        self.off = to

    def alloc(self, shape, dtype, name=None):
        n = int(np.prod(shape[1:]))
        size = n * (2 if dtype == F32 else 1)
        o = self.off
        self.off += (size + 15) // 16 * 16
        assert self.off <= self.total, ("SBUF arena overflow", self.off, self.total)
        v = self.big[:, o:o + size]
        if dtype == F32:
            v = v.bitcast(F32)
        if len(shape) == 3:
            v = v.rearrange("p (a b) -> p a b", a=shape[1])
        elif len(shape) == 4:
            v = v.rearrange("p (a b c) -> p a b c", a=shape[1], b=shape[2])
        self.n += 1
        key = "%s#%d" % (name or "t", self.n)
        return v[0:shape[0]], key


class Ring:
    def __init__(self, arena, n, shape, dtype, name):
        self.items = [arena.alloc(shape, dtype, name) for _ in range(n)]
        self.i = 0

    def next(self):
        it = self.items[self.i % len(self.items)]
        self.i += 1
        return it


def _make_plan():
    plan = [("scal", 0)]
    for j in range(16):
        plan += [("x1", j), ("hv", j)]
    for typ in ("q", "k", "gv", "x0", "zg"):
        plan += [(typ, j) for j in range(16)]
    plan += [("gate", j) for j in range(32)]
    return plan


SEG = dict(x0=0, x1=2048, hv=4096, q=6144, k=8192, gv=10240, zg=12288, scal=14336, gate=14400)


PLAN = _make_plan()
PLAN_INDEX = {k: i for i, k in enumerate(PLAN)}


def _chunk_major(w, cols):
    K = w.shape[0]
    out = np.empty((len(cols), 128, (K // 128) * 128), np.float32)
    for i, c0 in enumerate(cols):
        blk = w[:, c0:c0 + 128]
        if blk.shape[1] < 128:
            blk = np.concatenate([blk, np.zeros((K, 128 - blk.shape[1]), np.float32)], axis=1)
        out[i] = blk.reshape(K // 128, 128, 128).transpose(1, 0, 2).reshape(128, -1)
    return out


def build(upto=99, dbg=(), phases="DEF"):
    nc = bass.Bass("TRN2", target_bir_lowering=False)
    S = Sched(nc)
    ext = {}

    def inp(name, shape, dt=F32):
        ext[name] = nc.dram_tensor(name, list(shape), dt, kind="ExternalInput")
        return ext[name].ap()

    def scratch(name, shape, dt):
        kind = "ExternalOutput" if name in dbg else "Internal"
        t = nc.dram_tensor(name, list(shape), dt, kind=kind)
        return t.ap()

    x_d = inp("x", [L, D]); ctx_d = inp("ctx", [CTX, D]); cT_d = inp("cT", [128, 16, 2])
    w_ada_d = inp("w_ada", [D, 6 * D]); b_adaT_d = inp("b_adaT", [128, 96])
    nmixT_d = inp("nmixT", [128, 16]); nffnT_d = inp("nffnT", [128, 16])
    w_in_d = inp("w_inc", [145, 128, 16 * 128])
    hyconvT_d = inp("hyconvT", [128, 48, 3]); gdnconvT_d = inp("gdnconvT", [128, 48, 3])
    scalp_d = inp("scalp", [64, 2])
    ident_d = inp("ident", [128, 128])
    out_d = nc.dram_tensor("out", [NOWN, D], F32, kind="ExternalOutput").ap()
    w_hy_out_d = inp("w_hy_outc", [16, 128, 16 * 128]); w_gdn_out_d = inp("w_gdn_outc", [16, 128, 16 * 128]); w_o_d = inp("w_oc", [16, 128, 16 * 128])
    w_up_d = inp("w_upc", [88, 128, 16 * 128]); w_down_d = inp("w_downc", [16, 128, 44 * 128])
    hybT_d = inp("hybT", [128, 16]); gnormT_d = inp("gnormT", [128, 1]); nfinT_d = inp("nfinT", [128, 16])
    Yd = scratch("Yd", [NOWN + 64, D], BF16); Od0 = scratch("Od0", [NOWN, D], F32); Od1 = scratch("Od1", [NOWN, D], F32)
    cmask_d = inp("cmask", [64, 4, 64]); cm2_d = inp("cm2", [64, 5, 64])
    zf_d = inp("zf", [33, N_]); fw1_d = inp("fw1", [33, 64]); fw2_d = inp("fw2", [64, 64]); fw3_d = inp("fw3", [64, 64])
    fsm_d = inp("fsm", [64, 4]); fout_d = inp("fout", [64, 2 * D]); deltas_d = inp("deltas", [128, D]); tau_d = inp("tau", [128, 48])
    W1_d = inp("W1", [128, 48 * 2 * 128]); V_d = inp("Vc", [128, 48 * 2 * 64])
    W2_d = inp("W2", [96, 96]); W2a_d = inp("W2a", [96, 96]); W2b_d = inp("W2b", [96, 96]); L1_d = inp("L1", [96, 96]); L2_d = inp("L2", [96, 96])
    Hc = scratch("Hc", [N_, D], BF16); Ad = scratch("Ad", [128, 96, D], BF16); Bd = scratch("Bd", [128, 96, D], BF16)
    Kd = scratch("Kd", [2, 96, 128, D], BF16)

    X0T = scratch("X0T", [D, NOWN], BF16); ZT = scratch("ZT", [D, L], BF16); Zt = scratch("Zt", [L, D], BF16)
    QT = scratch("QT", [D, TT], BF16); KT = scratch("KT", [D, TT], BF16); VT = scratch("VT", [D, TT], BF16)
    ZGT = scratch("ZGT", [D, NOWN], BF16); SCT = scratch("SCT", [64, TT], F32); GT = scratch("GT", [2 * D, NOWN], BF16)
    MODd = scratch("MODd", [128, 96, 2], F32)

    with contextlib.ExitStack() as st:
        TOTAL = 106000
        big = st.enter_context(nc.sbuf_tensor("big", [128, TOTAL], BF16))
        PSALL = st.enter_context(nc.psum_tensor("psall", [128, 4096], F32))
        psb = [PSALL[:, i * 512:(i + 1) * 512] for i in range(8)]
        A = Arena(big, TOTAL)
        psi = [0]

        def ps_next():
            i = psi[0] % 8; psi[0] += 1
            return psb[i], "psb%d" % i

        def ps_multi(nb):
            i = ((psi[0] + nb - 1) // nb * nb) % 8
            psi[0] = i + nb
            return PSALL[:, i * 512:(i + nb) * 512], ["psb%d" % (i + j) for j in range(nb)]

        ident_f, k_identf = A.alloc([128, 128], F32, "identf")
        ident_b, k_ident = A.alloc([128, 128], BF16, "ident")
        ones_b, k_ones = A.alloc([128, 128], BF16, "ones")
        ones_f, k_onesf = A.alloc([128, 128], F32, "onesf")
        mod, k_mod = A.alloc([128, 96, 2], F32, "mod")
        nmixT, k_nmix = A.alloc([128, 16], F32, "nmix")
        nffnT, k_nffn = A.alloc([128, 16], F32, "nffn")
        scale_a, k_sca = A.alloc([128, 16], F32, "scale_a")
        scale_c, k_scc = A.alloc([128, 16], F32, "scale_c")
        scale_f, k_scf = A.alloc([128, 16], F32, "scale_f")
        hyconvT, k_hyc = A.alloc([128, 48, 3], F32, "hyc")
        gdnconvT, k_gdc = A.alloc([128, 48, 3], F32, "gdc")
        scalp, k_scalp = A.alloc([64, 2], F32, "scalp")
        nega, k_nega = A.alloc([64, 1], F32, "nega")
        negpi, k_negpi = A.alloc([128, 1], F32, "negpi")
        S.op("dve", lambda e: e.memset(negpi, -math.pi), writes=[k_negpi])
        hybT, k_hyb = A.alloc([128, 16], F32, "hyb")
        gnormT, k_gnorm = A.alloc([128, 1], F32, "gnorm")
        nfinT, k_nfin = A.alloc([128, 16], F32, "nfin")
        PERSIST = A.off
        S.dma("sp", hybT, hybT_d, writes=[k_hyb]); S.dma("sp", gnormT, gnormT_d, writes=[k_gnorm]); S.dma("sp", nfinT, nfinT_d, writes=[k_nfin])

        S.dma("sp", ident_f, ident_d, writes=[k_identf])
        S.op("dve", lambda e: e.tensor_copy(out=ident_b, in_=ident_f), reads=[k_identf], writes=[k_ident])
        S.op("dve", lambda e: e.memset(ones_b, 1.0), writes=[k_ones])
        S.op("dve", lambda e: e.memset(ones_f, 1.0), writes=[k_onesf])
        S.dma("sp", nmixT, nmixT_d, writes=[k_nmix]); S.dma("sp", nffnT, nffnT_d, writes=[k_nffn])
        S.dma("sp", hyconvT, hyconvT_d, writes=[k_hyc]); S.dma("sp", gdnconvT, gdnconvT_d, writes=[k_gdc])
        S.dma("sp", scalp, scalp_d, writes=[k_scalp])
        S.op("act", lambda e: e.activation(out=nega[32:64], in_=scalp[32:64, 0:1], func=AF.Exp), reads=[k_scalp], writes=[k_nega])
        S.op("dve", lambda e: e.tensor_scalar(out=nega[32:64], in0=nega[32:64], scalar1=-1.0, scalar2=None, op0=ALU.mult), reads=[k_nega], writes=[k_nega])

        cT, k_cT = A.alloc([128, 16, 2], F32, "cT")
        sT, k_sT = A.alloc([128, 16, 2], BF16, "sT")
        bT, k_bT = A.alloc([128, 96], F32, "bT")
        wr = Ring(A, 2, [128, 16, 1536], BF16, "wada")
        S.dma("sp", cT, cT_d, writes=[k_cT]); S.dma("sp", bT, b_adaT_d, writes=[k_bT])
        S.op("act", lambda e: e.activation(out=sT, in_=cT, func=AF.Silu), reads=[k_cT], writes=[k_sT])
        w_ada_v = w_ada_d.rearrange("(kc p) n -> p kc n", p=128)
        for blk in range(8):
            wt, kw = wr.next()
            S.dma("poolq", wt, w_ada_v[:, :, blk * 1536:(blk + 1) * 1536], writes=[kw])
            ps, kp = ps_next()

            def mmA(e, wt=wt, ps=ps):
                last = None
                for j in range(12):
                    for kc in range(16):
                        last = e.matmul(ps[:, 2 * j:2 * j + 2], wt[:, kc, j * 128:(j + 1) * 128], sT[:, kc, :],
                                        start=(kc == 0), stop=(kc == 15))
                return last
            S.op("pe", mmA, reads=[kw, k_sT], writes=[kp])
            S.op("dve", lambda e, ps=ps, blk=blk: e.tensor_copy(
                out=mod[:, blk * 12:(blk + 1) * 12, :], in_=ps[:, 0:24].rearrange("p (a b) -> p a b", b=2)),
                reads=[kp], writes=[k_mod])
        for j in range(2):
            S.op("dve", lambda e, j=j: e.tensor_tensor(out=mod[:, :, j], in0=mod[:, :, j], in1=bT, op=ALU.add),
                 reads=[k_mod, k_bT], writes=[k_mod])
        for (dst, kd, src, nrm, kn) in ((scale_a, k_sca, mod[:, 16:32, 0], nmixT, k_nmix),
                                        (scale_c, k_scc, mod[:, 16:32, 1], nmixT, k_nmix),
                                        (scale_f, k_scf, mod[:, 64:80, 0], nffnT, k_nffn)):
            S.op("dve", lambda e, dst=dst, src=src, nrm=nrm: e.scalar_tensor_tensor(
                out=dst, in0=src, scalar=1.0, in1=nrm, op0=ALU.add, op1=ALU.mult), reads=[k_mod, kn], writes=[kd])
        if "MODd" in dbg:
            S.final_tokens.append(S.dma("sp", MODd, mod, reads=[k_mod], writes=["MODd"]))
        if upto <= 1:
            S.emit(); return nc

        S.barrier(); A.reset(PERSIST)
        hT, k_hT = A.alloc([128, 16, TT], BF16, "hT")
        PB = A.off
        xr = Ring(A, 2, [128, D], F32, "xt")
        sq, k_sq = A.alloc([128, D], F32, "sq")
        xnr = Ring(A, 2, [128, D], BF16, "xn")
        str_ = Ring(A, 2, [128, 4], F32, "stat")
        for i in range(TT // 128):
            lat = i < L // 128
            src = x_d[i * 128:(i + 1) * 128, :] if lat else ctx_d[(i - 32) * 128:(i - 31) * 128, :]
            sc_t, k_sc = (scale_a, k_sca) if lat else (scale_c, k_scc)
            sh_t = mod[:, 0:16, 0] if lat else mod[:, 0:16, 1]
            xt, kx = xr.next(); xn, kxn = xnr.next(); stt, kst = str_.next()
            S.dma("sp", xt, src, writes=[kx])
            S.op("act", lambda e, xt=xt: e.activation(out=sq, in_=xt, func=AF.Square), reads=[kx], writes=[k_sq])
            S.op("dve", lambda e, stt=stt: e.reduce_sum(out=stt[:, 0:1], in_=sq, axis=AX.X), reads=[k_sq], writes=[kst])
            S.op("dve", lambda e, stt=stt: e.tensor_scalar(out=stt[:, 1:2], in0=stt[:, 0:1], scalar1=1.0 / D, scalar2=EPS,
                                                           op0=ALU.mult, op1=ALU.add), reads=[kst], writes=[kst])
            S.op("dve", lambda e, stt=stt: e.reciprocal(out=stt[:, 2:3], in_=stt[:, 1:2]), reads=[kst], writes=[kst])
            S.op("act", lambda e, stt=stt: e.activation(out=stt[:, 3:4], in_=stt[:, 2:3], func=AF.Sqrt), reads=[kst], writes=[kst])
            S.op("dve", lambda e, stt=stt: e.tensor_tensor(out=stt[:, 0:1], in0=stt[:, 3:4], in1=stt[:, 3:4], op=ALU.mult), reads=[kst], writes=[kst])
            S.op("dve", lambda e, stt=stt: e.tensor_tensor(out=stt[:, 0:1], in0=stt[:, 0:1], in1=stt[:, 1:2], op=ALU.mult), reads=[kst], writes=[kst])
            S.op("dve", lambda e, stt=stt: e.tensor_scalar(out=stt[:, 0:1], in0=stt[:, 0:1], scalar1=-0.5, scalar2=1.5,
                                                           op0=ALU.mult, op1=ALU.add), reads=[kst], writes=[kst])
            S.op("dve", lambda e, stt=stt: e.tensor_tensor(out=stt[:, 3:4], in0=stt[:, 3:4], in1=stt[:, 0:1], op=ALU.mult), reads=[kst], writes=[kst])
            S.op("act", lambda e, xt=xt, xn=xn, stt=stt: e.activation(out=xn, in_=xt, func=AF.Copy, scale=stt[:, 3:4]),
                 reads=[kx, kst], writes=[kxn])
            for g in range(4):
                ps, kp = ps_next()
                psv = ps.bitcast(BF16)

                def tr(e, xn=xn, psv=psv, g=g):
                    last = None
                    for j in range(4):
                        fc = g * 4 + j
                        last = e.transpose(psv[:, j * 128:(j + 1) * 128], xn[:, fc * 128:(fc + 1) * 128], ident_b)
                    return last
                S.op("pe", tr, reads=[kxn, k_ident], writes=[kp])
                for j in range(4):
                    fc = g * 4 + j
                    dst = hT[:, fc, i * 128:(i + 1) * 128]
                    if j % 2 == 0:
                        S.op("act", lambda e, dst=dst, psv=psv, j=j, fc=fc, sc_t=sc_t, sh_t=sh_t: e.activation(
                            out=dst, in_=psv[:, j * 128:(j + 1) * 128], func=AF.Identity, scale=sc_t[:, fc:fc + 1], bias=sh_t[:, fc:fc + 1]),
                            reads=[kp, k_sc, k_mod], writes=[k_hT])
                    else:
                        S.op("dve", lambda e, dst=dst, psv=psv, j=j, fc=fc, sc_t=sc_t, sh_t=sh_t: e.tensor_scalar(
                            out=dst, in0=psv[:, j * 128:(j + 1) * 128], scalar1=sc_t[:, fc:fc + 1], scalar2=sh_t[:, fc:fc + 1],
                            op0=ALU.mult, op1=ALU.add), reads=[kp, k_sc, k_mod], writes=[k_hT])
        if "HTd" in dbg:
            HTd = nc.dram_tensor("HTd", [128, 16, TT], BF16, kind="ExternalOutput").ap()
            S.final_tokens.append(S.dma("sp", HTd, hT, reads=[k_hT], writes=["HTd"]))
        if upto <= 2:
            S.emit(); return nc

        S.barrier(); A.reset(PB)
        wr = Ring(A, 3, [128, 16, 128], BF16, "win")
        ur = Ring(A, 3, [128, 512], F32, "u")
        t2r = Ring(A, 2, [128, 512], F32, "tmp2")
        obr = Ring(A, 3, [128, 512], BF16, "ob")
        sqr = Ring(A, 3, [128, 512], BF16, "sq")
        ofr = Ring(A, 2, [64, 512], F32, "of")
        u1, k_u1 = A.alloc([128, L], BF16, "u1")
        ztr = Ring(A, 2, [128, 4, 128], BF16, "zt")
        BLK_ALL = [(b * 512, 512) for b in range(8)]
        BLK_OWN = BLK_ALL[:NOWN // 512]
        BLK_CTX = [(L, CTX)]

        def conv_epilogue(ps, kp, n, rw, taps, ktaps):
            u, ku = ur.next()
            S.op("act", lambda e: e.activation(out=u[:, 0:n], in_=ps[:, 0:n], func=AF.Copy, scale=taps[:, 1:2]),
                 reads=[kp, ktaps], writes=[ku])
            pv = ps[:, 0:n].rearrange("p (r w) -> p r w", w=rw)
            uv = u[:, 0:n].rearrange("p (r w) -> p r w", w=rw)
            S.op("dve", lambda e: e.scalar_tensor_tensor(out=uv[:, :, 1:rw], in0=pv[:, :, 0:rw - 1], scalar=taps[:, 0:1],
                                                         in1=uv[:, :, 1:rw], op0=ALU.mult, op1=ALU.add), reads=[kp, ktaps, ku], writes=[ku])
            S.op("dve", lambda e: e.scalar_tensor_tensor(out=uv[:, :, 0:rw - 1], in0=pv[:, :, 1:rw], scalar=taps[:, 2:3],
                                                         in1=uv[:, :, 0:rw - 1], op0=ALU.mult, op1=ALU.add), reads=[kp, ktaps, ku], writes=[ku])
            return u, ku

        def do_chunk(typ, j):
            M = 64 if typ == "scal" else 128
            wt, kw = wr.next()
            S.dma("poolq", wt.rearrange("p a b -> p (a b)"), w_in_d[PLAN_INDEX[(typ, j)]], writes=[kw])
            own = typ in ("x0", "zg", "gate")
            blocks = BLK_OWN if own else BLK_ALL
            if typ in ("q", "k", "gv", "scal"):
                blocks = blocks + BLK_CTX
            def blk(t0, n):
                ps, kp = ps_next()

                def mm(e, ps=ps, t0=t0, n=n):
                    last = None
                    for kc in range(16):
                        last = e.matmul(ps[0:M, 0:n], wt[:, kc, 0:M], hT[:, kc, t0:t0 + n], start=(kc == 0), stop=(kc == 15))
                    return last
                S.op("pe", mm, reads=[kw, k_hT], writes=[kp])
                isctx = t0 >= L
                rw = CTX if isctx else 64
                if typ in ("x0", "x1", "hv"):
                    ci = {"x0": 0, "x1": 16, "hv": 32}[typ] + j
                    u, ku = conv_epilogue(ps, kp, n, rw, hyconvT[:, ci, :], k_hyc)
                    if typ == "x0":
                        ob, kob = obr.next()
                        S.op("act", lambda e, ob=ob, u=u: e.activation(out=ob, in_=u, func=AF.Copy), reads=[ku], writes=[kob])
                        S.dma("sp", X0T[j * 128:(j + 1) * 128, t0:t0 + n], ob, reads=[kob], writes=["X0T"])
                    elif typ == "x1":
                        S.op("act", lambda e, u=u, t0=t0: e.activation(out=u1[:, t0:t0 + 512], in_=u, func=AF.Copy), reads=[ku], writes=[k_u1 + "@%d" % t0])
                    else:
                        ob, kob = obr.next()
                        S.op("dve", lambda e, ob=ob, u=u, t0=t0: e.tensor_tensor(out=ob, in0=u, in1=u1[:, t0:t0 + 512], op=ALU.mult),
                             reads=[ku, k_u1 + "@%d" % t0], writes=[kob])
                        S.dma("sp", ZT[j * 128:(j + 1) * 128, t0:t0 + n], ob, reads=[kob], writes=["ZT"])
                        yield
                        ps2, kp2 = ps_next()
                        ps2v = ps2.bitcast(BF16)

                        def trz(e, ob=ob, ps2v=ps2v):
                            last = None
                            for tb in range(4):
                                last = e.transpose(ps2v[:, tb * 128:(tb + 1) * 128], ob[:, tb * 128:(tb + 1) * 128], ident_b)
                            return last
                        S.op("pe", trz, reads=[kob, k_ident], writes=[kp2])
                        zt, kzt = ztr.next()
                        S.op("act", lambda e, zt=zt, ps2v=ps2v: e.activation(
                            out=zt, in_=ps2v[:, 0:512].rearrange("p (a b) -> p a b", b=128), func=AF.Copy), reads=[kp2], writes=[kzt])
                        S.dma("sp", Zt[t0:t0 + 512, j * 128:(j + 1) * 128].rearrange("(a p) c -> p a c", p=128), zt,
                              reads=[kzt], writes=["Zt"])
                elif typ in ("q", "k", "gv"):
                    ci = {"q": 0, "k": 16, "gv": 32}[typ] + j
                    u, ku = conv_epilogue(ps, kp, n, rw, gdnconvT[:, ci, :], k_gdc)
                    S.op("act", lambda e, u=u, n=n: e.activation(out=u[:, 0:n], in_=u[:, 0:n], func=AF.Silu), reads=[ku], writes=[ku])
                    ob, kob = obr.next()
                    dstT = {"q": QT, "k": KT, "gv": VT}[typ]
                    if typ == "gv":
                        S.op("dve", lambda e, ob=ob, u=u, n=n: e.tensor_copy(out=ob[:, 0:n], in_=u[:, 0:n]), reads=[ku], writes=[kob])
                    else:
                        sqb, ksqb = sqr.next()
                        S.op("dve", lambda e, sqb=sqb, u=u, n=n: e.tensor_tensor(out=sqb[:, 0:n], in0=u[:, 0:n], in1=u[:, 0:n], op=ALU.mult),
                             reads=[ku], writes=[ksqb])
                        yield
                        t2, kt2 = t2r.next()
                        ps2, kp2 = ps_next()
                        S.op("pe", lambda e, ps2=ps2, sqb=sqb, n=n: e.matmul(ps2[:, 0:n], ones_b, sqb[:, 0:n], start=True, stop=True),
                             reads=[ksqb, k_ones], writes=[kp2])
                        S.op("dve", lambda e, t2=t2, ps2=ps2, n=n: e.tensor_scalar(out=t2[:, 0:n], in0=ps2[:, 0:n], scalar1=EPS, scalar2=None, op0=ALU.add),
                             reads=[kp2], writes=[kt2])
                        S.op("dve", lambda e, t2=t2, n=n: e.reciprocal(out=t2[:, 0:n], in_=t2[:, 0:n]), reads=[kt2], writes=[kt2])
                        S.op("act", lambda e, t2=t2, n=n: e.activation(out=t2[:, 0:n], in_=t2[:, 0:n], func=AF.Sqrt), reads=[kt2], writes=[kt2])
                        ob, kob = obr.next()
                        qs = (128.0 ** -0.5) if typ == "q" else 1.0
                        S.op("dve", lambda e, ob=ob, u=u, t2=t2, n=n, qs=qs: e.scalar_tensor_tensor(
                            out=ob[:, 0:n], in0=u[:, 0:n], scalar=qs, in1=t2[:, 0:n], op0=ALU.mult, op1=ALU.mult), reads=[ku, kt2], writes=[kob])
                    S.dma("sp", dstT[j * 128:(j + 1) * 128, t0:t0 + n], ob[:, 0:n], reads=[kob], writes=[typ + "T"])
                elif typ in ("zg", "gate"):
                    ob, kob = obr.next()
                    fn = AF.Silu if typ == "zg" else AF.Sigmoid
                    S.op("act", lambda e, ob=ob, ps=ps, fn=fn: e.activation(out=ob, in_=ps[:, 0:512], func=fn), reads=[kp], writes=[kob])
                    dstT = ZGT if typ == "zg" else GT
                    S.dma("sp", dstT[j * 128:(j + 1) * 128, t0:t0 + n], ob, reads=[kob], writes=[typ + "T"])
                else:
                    of, kof = ofr.next()
                    S.op("act", lambda e, of=of, ps=ps, n=n: e.activation(out=of[0:32, 0:n], in_=ps[0:32, 0:n], func=AF.Sigmoid), reads=[kp], writes=[kof])
                    S.op("act", lambda e, of=of, ps=ps, n=n: e.activation(out=of[32:64, 0:n], in_=ps[32:64, 0:n], func=AF.Exp, bias=scalp[32:64, 1:2]),
                         reads=[kp, k_scalp], writes=[kof])
                    S.op("act", lambda e, of=of, n=n: e.activation(out=of[32:64, 0:n], in_=of[32:64, 0:n], func=AF.Ln, bias=1.0), reads=[kof], writes=[kof])
                    S.op("dve", lambda e, of=of, n=n: e.tensor_scalar(out=of[32:64, 0:n], in0=of[32:64, 0:n], scalar1=nega[32:64, 0:1], scalar2=None, op0=ALU.mult),
                         reads=[kof, k_nega], writes=[kof])
                    S.dma("sp", SCT[:, t0:t0 + n], of[:, 0:n], reads=[kof], writes=["SCT"])
                return
                yield

            pend = None
            for (t0, n) in blocks:
                g = blk(t0, n)
                alive = True
                try:
                    next(g)
                except StopIteration:
                    alive = False
                if pend is not None:
                    for _ in pend:
                        pass
                pend = g if alive else None
            if pend is not None:
                for _ in pend:
                    pass

        plan = list(PLAN)
        import os
        if os.environ.get("K_PLAN_TEST") == "gdn":
            plan = [("scal", 0)] + [(t, j) for t in ("q", "k", "gv") for j in range(16)]
        elif os.environ.get("K_PLAN_TEST") == "hy":
            plan = [pp for j in (0, 7) for pp in (("x1", j), ("hv", j))] + [("x0", 0), ("x0", 7)]
        elif os.environ.get("K_PLAN_TEST"):
            plan = [("scal", 0), ("x1", 1), ("hv", 1), ("q", 2), ("k", 3), ("gv", 4), ("x0", 5), ("zg", 6), ("gate", 7), ("gate", 17)]
        for (typ, j) in plan:
            do_chunk(typ, j)
        for nm in ("X0T", "ZT", "Zt", "QT", "KT", "VT", "ZGT", "SCT", "GT"):
            if nm in dbg:
                pass
        if upto <= 3:
            S.barrier()
            S.emit(); return nc


        def phase_d():
            S.barrier(); A.reset(PERSIST)
            H = HEADS
            HH = 8
            cmask, k_cm = A.alloc([64, 4, 64], F32, "cmask")
            S.dma("sp", cmask, cmask_d, writes=[k_cm])
            cm2, k_cm2 = A.alloc([64, 5, 64], F32, "cm2")
            S.dma("sp", cm2, cm2_d, writes=[k_cm2])
            cmask_b, k_cmb = A.alloc([64, 4, 64], BF16, "cmask_b")
            cm2_b, k_cm2b = A.alloc([64, 5, 64], BF16, "cm2_b")
            S.op("dve", lambda e: e.tensor_copy(out=cm2_b, in_=cm2), reads=[k_cm2], writes=[k_cm2b])
            S.op("dve", lambda e: e.tensor_copy(out=cmask_b, in_=cmask), reads=[k_cm], writes=[k_cmb])
            qr = Ring(A, 2, [128, H, 256], BF16, "qg"); kr = Ring(A, 2, [128, H, 256], BF16, "kg"); vr = Ring(A, 2, [128, H, 256], BF16, "vg")
            scr = Ring(A, 2, [64, 256], F32, "scg")
            identb64 = ident_b[0:64, 0:64].unsqueeze(1).broadcast_to([64, HH, 64])
            identf64 = ident_f[0:64, 0:64].unsqueeze(1).broadcast_to([64, HH, 64])
            QTv = QT.rearrange("(h p) t -> p h t", p=128); KTv = KT.rearrange("(h p) t -> p h t", p=128); VTv = VT.rearrange("(h p) t -> p h t", p=128)

            def al(shape, dt, nm):
                return A.alloc(shape, dt, nm)

            class TS:
                pass
            halves = []
            for hh in range(2):
                T_ = TS()
                T_.S = al([128, HH, 128], F32, "S"); T_.Sb = al([128, HH, 128], BF16, "Sb")
                T_.sct = al([64, 64], F32, "sct"); T_.sm = al([64, 8, HH], F32, "sm")
                T_.ktok = al([64, HH, 128], BF16, "ktok"); T_.vtok = al([64, HH, 128], BF16, "vtok")
                T_.Xg = al([64, HH, 64], F32, "Xg"); T_.Xb = al([64, HH, 64], F32, "Xb")
                T_.Y = al([64, HH, 64], F32, "Y"); T_.EA = al([64, HH, 64], F32, "EA"); T_.EB = al([64, HH, 64], F32, "EB")
                T_.tmpM = al([64, HH, 64], BF16, "tmpM"); T_.tmpB = al([64, HH, 64], BF16, "tmpB"); T_.tmpQ = al([64, HH, 64], BF16, "tmpQ")
                T_.bbc = al([64, HH, 64], BF16, "bbc")
                T_.AO = [al([64, HH, 64], BF16, "AO%d" % i) for i in range(3)]
                T_.P0 = al([64, HH, 64], BF16, "P0"); T_.P1 = al([64, HH, 64], BF16, "P1")
                T_.PT0 = al([64, HH, 64], BF16, "PT0"); T_.PT1 = al([64, HH, 64], BF16, "PT1")
                T_.TT = al([64, HH, 64], BF16, "TT"); T_.T = al([64, HH, 64], BF16, "T")
                T_.Am = al([64, HH, 64], BF16, "Am"); T_.ATm = al([64, HH, 64], BF16, "ATm")
                T_.eGbc = al([128, HH, 64], F32, "eGbc")
                T_.vb = al([64, HH, 128], BF16, "vb"); T_.kbg = al([64, HH, 128], BF16, "kbg")
                T_.vnew = al([64, HH, 128], BF16, "vnew")
                T_.slot = []
                for sl in range(2):
                    d = dict(wT=al([128, HH, 64], BF16, "wT"), uu=al([64, HH, 128], F32, "uu"), qdT=al([128, HH, 64], BF16, "qdT"),
                             QKT=al([64, HH, 64], BF16, "QKT"), kdec=al([64, HH, 128], BF16, "kdec"), egl=al([128, HH], F32, "egl"))
                    T_.slot.append(d)
                T_.orr = Ring(A, 2, [64, HH, 128], F32, "o")
                halves.append(T_)

            def bc_s(ap):
                return ap.unsqueeze(2).broadcast_to([64, HH, 64])

            def bc_e(ap):
                return ap.unsqueeze(2).broadcast_to([64, HH, 128])

            def mk(i):
                return cmask[:, i, :].unsqueeze(1).broadcast_to([64, HH, 64])

            def mkb(i):
                return cmask_b[:, i, :].unsqueeze(1).broadcast_to([64, HH, 64])

            def m2(i):
                return cm2_b[:, i, :].unsqueeze(1).broadcast_to([64, HH, 64])

            def hm(psx):
                return psx[0:64, :].rearrange("p (h s) -> p h s", h=HH)

            def hm128(psx):
                return psx[0:64, :].rearrange("p (h d) -> p h d", h=HH)

            def run_dir(dr):
                mA_s, mB_i, mB_s, cumi = ((1, 2, 3, 2) if dr == 0 else (3, 0, 1, 0))
                last_t = 63 if dr == 0 else 0
                for T_ in halves:
                    S.op("dve", lambda e, T_=T_: e.memset(T_.S[0], 0.0), reads=[T_.S[1]], writes=[T_.S[1]])
                    S.op("dve", lambda e, T_=T_: e.memset(T_.Sb[0], 0.0), reads=[T_.Sb[1]], writes=[T_.Sb[1]])
                if dr == 0:
                    groups = [L // 256] + list(range(NOWN // 256))
                else:
                    groups = [L // 256] + list(range(L // 256 - 1, -1, -1))

                def chunk(hh, gi, ci, need_o, g0, qg, kqg, kg, kkg, vg, kvg, scg, kscg, slot):
                    T_ = halves[hh]
                    h0 = hh * HH
                    c0 = ci * 64
                    (Sst, k_S), (Sb, k_Sb), (sct, k_sct), (sm, k_sm) = T_.S, T_.Sb, T_.sct, T_.sm
                    (ktok, k_ktok), (vtok, k_vtok), (Xg, k_Xg), (Xb, k_Xb) = T_.ktok, T_.vtok, T_.Xg, T_.Xb
                    (Y, k_Y), (EA, k_EA), (EB, k_EB), (tmpM, k_tmpM) = T_.Y, T_.EA, T_.EB, T_.tmpM
                    (tmpB, k_tmpB), (tmpQ, k_tmpQ), (bbc, k_bbc) = T_.tmpB, T_.tmpQ, T_.bbc
                    (P0, kP0), (P1, kP1), (PT0, kPT0), (PT1, kPT1) = T_.P0, T_.P1, T_.PT0, T_.PT1
                    (TTm, k_TT), (Tm, k_T), (Am, k_Am), (ATm, k_ATm) = T_.TT, T_.T, T_.Am, T_.ATm
                    (eGbc, k_eGbc), (vb, k_vb), (kbg, k_kbg), (vnew, k_vnew) = T_.eGbc, T_.vb, T_.kbg, T_.vnew
                    sd = T_.slot[slot]
                    (wT, k_wT), (uu, k_uu), (qdT, k_qdT), (QKT, k_QKT), (kdec, k_kdec), (egl, k_egl) = (
                        sd["wT"], sd["uu"], sd["qdT"], sd["QKT"], sd["kdec"], sd["egl"])

                    def mmset(lhs, klhs, rhs, krhs):
                        pX, kpX = ps_next()

                        def f(e):
                            last = None
                            for h in range(HH):
                                last = e.matmul(pX[0:64, h * 64:(h + 1) * 64], lhs[:, h, :], rhs[:, h, :], start=True, stop=True)
                            return last
                        S.op("pe", f, reads=[klhs, krhs], writes=[kpX])
                        return pX, [kpX]
                    ps, kp = ps_next()
                    S.op("pe", lambda e: e.transpose(ps[0:64, 0:64], scg[:, c0:c0 + 64], ident_f[0:64, 0:64]), reads=[kscg, k_identf], writes=[kp])
                    S.op("act", lambda e: e.activation(out=sct, in_=ps[0:64, 0:64], func=AF.Copy), reads=[kp], writes=[k_sct])
                    beta = sct[:, dr * 16 + h0:dr * 16 + h0 + HH]; gg = sct[:, 32 + dr * 16 + h0:32 + dr * 16 + h0 + HH]
                    yield "p"
                    ps2, kp2 = ps_next()
                    S.op("pe", lambda e: e.matmul(ps2[0:64, 0:HH], cmask[:, cumi, :], gg, start=True, stop=True), reads=[k_sct, k_cm], writes=[kp2])
                    S.op("act", lambda e: e.activation(out=sm[:, 0, :], in_=ps2[0:64, 0:HH], func=AF.Copy), reads=[kp2], writes=[k_sm])
                    yield "p"
                    S.op("dve", lambda e: e.tensor_tensor(out=Xg, in0=identf64, in1=bc_s(sm[:, 0, :]), op=ALU.mult), reads=[k_sm, k_identf], writes=[k_Xg])
                    S.op("pool", lambda e: e.tensor_tensor(out=Xb, in0=identf64, in1=bc_s(beta), op=ALU.mult), reads=[k_sct, k_identf], writes=[k_Xb])
                    pG, kpG1 = ps_next(); kpG = [kpG1]
                    pGv = pG.rearrange("p (h s) -> p h s", h=HH)
                    S.op("pe", lambda e: e.matmul(pG[:, 0:512], ones_f[0:64, :], Xg, start=True, stop=True), reads=[k_Xg, k_onesf], writes=kpG)
                    pB, kpB1 = ps_next(); kpB = [kpB1]
                    pBv = pB.rearrange("p (h s) -> p h s", h=HH)
                    S.op("pe", lambda e: e.matmul(pB[0:64, 0:512], ones_f[0:64, 0:64], Xb, start=True, stop=True), reads=[k_Xb, k_onesf], writes=kpB)
                    S.op("dve", lambda e: e.tensor_tensor(out=Y, in0=pGv[0:64], in1=bc_s(sm[:, 0, :]), op=ALU.subtract), reads=kpG + [k_sm], writes=[k_Y])
                    S.op("act", lambda e: e.activation(out=bbc, in_=pBv[0:64], func=AF.Copy), reads=kpB, writes=[k_bbc])
                    S.op("dve", lambda e: e.tensor_tensor(out=sm[:, 3, :], in0=pGv[0:64, :, last_t], in1=sm[:, 0, :], op=ALU.subtract), reads=kpG + [k_sm], writes=[k_sm])
                    S.op("dve", lambda e: e.tensor_scalar(out=egl, in0=pGv[:, :, last_t], scalar1=-80.0, scalar2=None, op0=ALU.max), reads=kpG, writes=[k_egl])
                    if need_o:
                        S.op("dve", lambda e: e.tensor_scalar(out=eGbc, in0=pGv, scalar1=-80.0, scalar2=None, op0=ALU.max), reads=kpG, writes=[k_eGbc])
                    yield "p"
                    S.op("dve", lambda e: e.scalar_tensor_tensor(out=EA, in0=Y, scalar=80.0, in1=mk(mA_s), op0=ALU.min, op1=ALU.mult), reads=[k_Y, k_cm], writes=[k_EA])
                    S.op("act", lambda e: e.activation(out=EA, in_=EA, func=AF.Exp, scale=-1.0), reads=[k_EA], writes=[k_EA])
                    S.op("dve", lambda e: e.scalar_tensor_tensor(out=EB, in0=Y, scalar=-80.0, in1=mk(mB_i), op0=ALU.max, op1=ALU.mult), reads=[k_Y, k_cm], writes=[k_EB])
                    S.op("act", lambda e: e.activation(out=EB, in_=EB, func=AF.Exp), reads=[k_EB], writes=[k_EB])
                    yield "p"
                    S.op("dve", lambda e: e.tensor_scalar(out=sm[:, 1, :], in0=sm[:, 0, :], scalar1=-80.0, scalar2=None, op0=ALU.max), reads=[k_sm], writes=[k_sm])
                    S.op("act", lambda e: e.activation(out=sm[:, 1, :], in_=sm[:, 1, :], func=AF.Exp), reads=[k_sm], writes=[k_sm])
                    S.op("dve", lambda e: e.tensor_tensor(out=sm[:, 2, :], in0=sm[:, 1, :], in1=beta, op=ALU.mult), reads=[k_sm, k_sct], writes=[k_sm])
                    S.op("dve", lambda e: e.tensor_scalar(out=sm[:, 3, :], in0=sm[:, 3, :], scalar1=-80.0, scalar2=None, op0=ALU.max), reads=[k_sm], writes=[k_sm])
                    S.op("act", lambda e: e.activation(out=sm[:, 3, :], in_=sm[:, 3, :], func=AF.Exp), reads=[k_sm], writes=[k_sm])
                    S.op("act", lambda e: e.activation(out=egl, in_=egl, func=AF.Exp), reads=[k_egl], writes=[k_egl])
                    if need_o:
                        S.op("act", lambda e: e.activation(out=eGbc, in_=eGbc, func=AF.Exp), reads=[k_eGbc], writes=[k_eGbc])
                        S.op("pool", lambda e: e.tensor_tensor(out=qdT, in0=qg[:, h0:h0 + HH, c0:c0 + 64], in1=eGbc, op=ALU.mult), reads=[kqg, k_eGbc], writes=[k_qdT])
                    yield "p"
                    for (src, ksrc, dst, kdst) in ((kg, kkg, ktok, k_ktok), (vg, kvg, vtok, k_vtok)):
                        pT, kpT = ps_next()
                        pTv = pT.bitcast(BF16)

                        def trk(e, src=src, pTv=pTv):
                            last = None
                            for h in range(HH):
                                last = e.transpose(pTv[0:64, h * 128:(h + 1) * 128], src[:, h0 + h, c0:c0 + 64], ident_b)
                            return last
                        S.op("pe", trk, reads=[ksrc, k_ident], writes=[kpT])
                        S.op("act", lambda e, dst=dst, pTv=pTv: e.activation(out=dst, in_=pTv[0:64, :].rearrange("p (h d) -> p h d", h=HH), func=AF.Copy),
                             reads=[kpT], writes=[kdst])
                    pK, kpK1 = ps_next(); kpK = [kpK1]
                    pKv = pK.rearrange("p (h s) -> p h s", h=HH)

                    def mmK(e):
                        last = None
                        for h in range(HH):
                            last = e.matmul(pK[0:64, h * 64:(h + 1) * 64], kg[:, h0 + h, c0:c0 + 64], kg[:, h0 + h, c0:c0 + 64], start=True, stop=True)
                        return last
                    S.op("pe", mmK, reads=[kkg], writes=kpK)
                    S.op("dve", lambda e: e.tensor_tensor(out=tmpM, in0=pKv[0:64], in1=EA, op=ALU.mult), reads=kpK + [k_EA], writes=[k_tmpM])
                    S.op("dve", lambda e: e.tensor_tensor(out=tmpB, in0=pKv[0:64], in1=EB, op=ALU.mult), reads=kpK + [k_EB], writes=[k_tmpB])
                    if need_o:
                        pQ, kpQ1 = ps_next(); kpQ = [kpQ1]
                        pQv = pQ.rearrange("p (h s) -> p h s", h=HH)

                        def mmQ(e):
                            last = None
                            for h in range(HH):
                                last = e.matmul(pQ[0:64, h * 64:(h + 1) * 64], kg[:, h0 + h, c0:c0 + 64], qg[:, h0 + h, c0:c0 + 64], start=True, stop=True)
                            return last
                        S.op("pe", mmQ, reads=[kkg, kqg], writes=kpQ)
                        S.op("dve", lambda e: e.tensor_tensor(out=tmpQ, in0=pQv[0:64], in1=EB, op=ALU.mult), reads=kpQ + [k_EB], writes=[k_tmpQ])
                    yield "p"
                    S.op("dve", lambda e: e.tensor_tensor(out=tmpM, in0=tmpM, in1=bc_s(beta), op=ALU.mult), reads=[k_tmpM, k_sct], writes=[k_tmpM])
                    S.op("pool", lambda e: e.tensor_tensor(out=Am, in0=tmpM, in1=mkb(mA_s), op=ALU.mult), reads=[k_tmpM, k_cmb], writes=[k_Am])
                    S.op("dve", lambda e: e.tensor_tensor(out=tmpB, in0=tmpB, in1=bbc, op=ALU.mult), reads=[k_tmpB, k_bbc], writes=[k_tmpB])
                    S.op("pool", lambda e: e.tensor_tensor(out=ATm, in0=tmpB, in1=mkb(mB_s), op=ALU.mult), reads=[k_tmpB, k_cmb], writes=[k_ATm])
                    if need_o:
                        S.op("pool", lambda e: e.tensor_tensor(out=QKT, in0=tmpQ, in1=mkb(mB_i), op=ALU.mult), reads=[k_tmpQ, k_cmb], writes=[k_QKT])
                    for lv in range(3):
                        AOt, kAO = T_.AO[lv]
                        S.op("pool", lambda e, lv=lv, AOt=AOt: e.tensor_tensor(out=AOt, in0=ATm, in1=m2(1 + lv), op=ALU.mult), reads=[k_ATm, k_cm2b], writes=[kAO])
                    yield "p"
                    S.op("pool", lambda e: e.tensor_tensor(out=P0, in0=Am, in1=m2(4), op=ALU.mult), reads=[k_Am, k_cm2b], writes=[kP0])
                    S.op("pool", lambda e: e.tensor_tensor(out=PT0, in0=ATm, in1=m2(4), op=ALU.mult), reads=[k_ATm, k_cm2b], writes=[kPT0])
                    S.op("pool", lambda e: e.tensor_tensor(out=Tm, in0=P0, in1=identb64, op=ALU.add), reads=[kP0, k_ident], writes=[k_T])
                    S.op("pool", lambda e: e.tensor_tensor(out=TTm, in0=PT0, in1=identb64, op=ALU.add), reads=[kPT0, k_ident], writes=[k_TT])
                    p1, kp1 = mmset(PT0, kPT0, P0, kP0)
                    S.op("act", lambda e: e.activation(out=P1, in_=hm(p1), func=AF.Copy), reads=kp1, writes=[kP1])
                    p2_, kp2_ = mmset(P0, kP0, PT0, kPT0)
                    S.op("act", lambda e: e.activation(out=PT1, in_=hm(p2_), func=AF.Copy), reads=kp2_, writes=[kPT1])
                    yield "p"
                    p3, kp3 = mmset(TTm, k_TT, P1, kP1)
                    p4, kp4 = mmset(P1, kP1, TTm, k_TT)
                    S.op("dve", lambda e: e.tensor_tensor(out=Tm, in0=Tm, in1=hm(p3), op=ALU.add), reads=kp3 + [k_T], writes=[k_T])
                    S.op("dve", lambda e: e.tensor_tensor(out=TTm, in0=TTm, in1=hm(p4), op=ALU.add), reads=kp4 + [k_TT], writes=[k_TT])
                    p5, kp5 = mmset(PT1, kPT1, P1, kP1)
                    S.op("act", lambda e: e.activation(out=P0, in_=hm(p5), func=AF.Copy), reads=kp5, writes=[kP0])
                    yield "p"
                    p6, kp6 = mmset(TTm, k_TT, P0, kP0)
                    p7, kp7 = mmset(P0, kP0, TTm, k_TT)
                    S.op("dve", lambda e: e.tensor_tensor(out=Tm, in0=Tm, in1=hm(p6), op=ALU.add), reads=kp6 + [k_T], writes=[k_T])
                    S.op("dve", lambda e: e.tensor_tensor(out=TTm, in0=TTm, in1=hm(p7), op=ALU.add), reads=kp7 + [k_TT], writes=[k_TT])
                    yield "p"
                    for lv in range(3):
                        AOt, kAO = T_.AO[lv]
                        pX, kpX = mmset(AOt, kAO, Tm, k_T)
                        S.op("act", lambda e, pX=pX: e.activation(out=P1, in_=hm(pX), func=AF.Copy), reads=kpX, writes=[kP1])
                        yield "p"
                        p8, kp8 = mmset(P1, kP1, TTm, k_TT)
                        if lv < 2:
                            p9, kp9 = mmset(TTm, k_TT, P1, kP1)
                            S.op("dve", lambda e, p9=p9: e.tensor_tensor(out=Tm, in0=Tm, in1=hm(p9), op=ALU.subtract), reads=kp9 + [k_T], writes=[k_T])
                        S.op("dve", lambda e, p8=p8: e.tensor_tensor(out=TTm, in0=TTm, in1=hm(p8), op=ALU.subtract), reads=kp8 + [k_TT], writes=[k_TT])
                        yield "p"
                    S.op("pool", lambda e: e.tensor_tensor(out=vb, in0=vtok, in1=bc_e(beta), op=ALU.mult), reads=[k_vtok, k_sct], writes=[k_vb])
                    S.op("pool", lambda e: e.tensor_tensor(out=kbg, in0=ktok, in1=bc_e(sm[:, 2, :]), op=ALU.mult), reads=[k_ktok, k_sm], writes=[k_kbg])
                    S.op("pool", lambda e: e.tensor_tensor(out=kdec, in0=ktok, in1=bc_e(sm[:, 3, :]), op=ALU.mult), reads=[k_ktok, k_sm], writes=[k_kdec])
                    pU, kpU = ps_multi(2)

                    def mmU(e):
                        last = None
                        for h in range(HH):
                            last = e.matmul(pU[0:64, h * 128:(h + 1) * 128], TTm[:, h, :], vb[:, h, :], start=True, stop=True)
                        return last
                    S.op("pe", mmU, reads=[k_TT, k_vb], writes=kpU)
                    S.op("act", lambda e: e.activation(out=uu, in_=hm128(pU), func=AF.Copy), reads=kpU, writes=[k_uu])
                    pW, kpW1 = ps_next(); kpW = [kpW1]

                    def mmW(e):
                        last = None
                        for h in range(HH):
                            last = e.matmul(pW[:, h * 64:(h + 1) * 64], kbg[:, h, :], TTm[:, h, :], start=True, stop=True)
                        return last
                    S.op("pe", mmW, reads=[k_kbg, k_TT], writes=kpW)
                    S.op("act", lambda e: e.activation(out=wT, in_=pW.rearrange("p (h s) -> p h s", h=HH), func=AF.Copy), reads=kpW, writes=[k_wT])
                    yield "scan"
                    pV, kpV = ps_multi(2)

                    def mmV(e):
                        last = None
                        for h in range(HH):
                            last = e.matmul(pV[0:64, h * 128:(h + 1) * 128], wT[:, h, :], Sb[:, h, :], start=True, stop=True)
                        return last
                    S.op("pe", mmV, reads=[k_wT, k_Sb], writes=kpV)
                    S.op("dve", lambda e: e.tensor_tensor(out=vnew, in0=uu, in1=hm128(pV), op=ALU.subtract), reads=kpV + [k_uu], writes=[k_vnew])
                    yield "s"
                    if need_o:
                        pO, kpO = ps_multi(2)

                        def mmO(e):
                            last = None
                            for h in range(HH):
                                e.matmul(pO[0:64, h * 128:(h + 1) * 128], qdT[:, h, :], Sb[:, h, :], start=True, stop=False)
                                last = e.matmul(pO[0:64, h * 128:(h + 1) * 128], QKT[:, h, :], vnew[:, h, :], start=False, stop=True)
                            return last
                        S.op("pe", mmO, reads=[k_qdT, k_Sb, k_QKT, k_vnew], writes=kpO)
                        ot, kot = T_.orr.next()
                        S.op("act", lambda e: e.activation(out=ot, in_=hm128(pO), func=AF.Copy), reads=kpO, writes=[kot])
                        Odst = Od0 if dr == 0 else Od1
                        S.dma("sp", Odst[g0 + c0:g0 + c0 + 64, h0 * 128:(h0 + HH) * 128], ot.rearrange("p h d -> p (h d)"), reads=[kot], writes=["Od%d_%d" % (dr, hh)])
                    pS, kpS = ps_multi(2)

                    def mmS(e):
                        last = None
                        for h in range(HH):
                            last = e.matmul(pS[:, h * 128:(h + 1) * 128], kdec[:, h, :], vnew[:, h, :], start=True, stop=True)
                        return last
                    S.op("pe", mmS, reads=[k_kdec, k_vnew], writes=kpS)
                    S.op("dve", lambda e: e.tensor_tensor(out=Sst, in0=Sst, in1=egl.unsqueeze(2).broadcast_to([128, HH, 128]), op=ALU.mult), reads=[k_S, k_egl], writes=[k_S])
                    S.op("dve", lambda e: e.tensor_tensor(out=Sst, in0=Sst, in1=pS.rearrange("p (h d) -> p h d", h=HH), op=ALU.add), reads=kpS + [k_S], writes=[k_S])
                    S.op("act", lambda e: e.activation(out=Sb, in_=Sst, func=AF.Copy), reads=[k_S], writes=[k_Sb])
                    yield "done"

                def advance(gens, until):
                    live = list(gens)
                    while live:
                        for g in list(live):
                            try:
                                v = next(g)
                            except StopIteration:
                                live.remove(g); continue
                            if v == until:
                                live.remove(g)

                pending = None
                nchunk = 0
                for gi in groups:
                    g0 = gi * 256
                    need_o = gi < NOWN // 256
                    qg, kqg = qr.next(); kg, kkg = kr.next(); vg, kvg = vr.next(); scg, kscg = scr.next()
                    if need_o:
                        S.dma("sp", qg, QTv[:, :, g0:g0 + 256], reads=["qT"], writes=[kqg])
                    S.dma("sp", kg, KTv[:, :, g0:g0 + 256], reads=["kT"], writes=[kkg])
                    S.dma("sp", vg, VTv[:, :, g0:g0 + 256], reads=["gvT"], writes=[kvg])
                    S.dma("sp", scg, SCT[:, g0:g0 + 256], reads=["SCT"], writes=[kscg])
                    for ci in (range(4) if dr == 0 else range(3, -1, -1)):
                        gens = [chunk(hh, gi, ci, need_o, g0, qg, kqg, kg, kkg, vg, kvg, scg, kscg, nchunk % 2) for hh in range(2)]
                        advance(gens, "scan")
                        if pending is not None:
                            advance(pending, "done")
                        pending = gens
                        nchunk += 1
                advance(pending, "done")

            for dr_ in range(2):
                run_dir(dr_)

        if "D" in phases:
            phase_d()


        def phase_e():
            CG = 512
            NG = D // CG

            def load_consts(names):
                out = {}
                for nm, src, shp in names:
                    t, k = A.alloc(shp, BF16, nm)
                    S.dma("poolq", t, src, writes=[k])
                    out[nm] = (t, k)
                return out

            S.barrier(); A.reset(PERSIST)
            zf, k_zf = A.alloc([33, N_], F32, "zf")
            fw1, k_fw1 = A.alloc([33, 64], F32, "fw1"); fw2, k_fw2 = A.alloc([64, 64], F32, "fw2"); fw3, k_fw3 = A.alloc([64, 64], F32, "fw3")
            fsm, k_fsm = A.alloc([64, 8], F32, "fsm")
            fout, k_fout = A.alloc([64, 2 * D], F32, "fout")
            dl, k_dl = A.alloc([128, D], F32, "deltas")
            tau, k_tau = A.alloc([128, 48], F32, "tau")
            hr = [A.alloc([64, 512], F32, "hmlp%d" % i) for i in range(3)]
            decr = Ring(A, 2, [128, 512], F32, "dec")
            fm, k_fm = A.alloc([64, 512], F32, "fm")
            hcr = Ring(A, 3, [128, 512], BF16, "hc")
            for (t, k, src) in ((zf, k_zf, zf_d), (fw1, k_fw1, fw1_d), (fw2, k_fw2, fw2_d), (fw3, k_fw3, fw3_d),
                                (fsm[:, 0:4], k_fsm, fsm_d), (fout, k_fout, fout_d), (dl, k_dl, deltas_d), (tau, k_tau, tau_d)):
                S.dma("sp", t, src, writes=[k])
            for i in range(3):
                S.op("dve", lambda e, i=i: e.tensor_tensor(out=fsm[:, 4 + i:5 + i], in0=fsm[:, i:i + 1], in1=fsm[:, 3:4], op=ALU.mult), reads=[k_fsm], writes=[k_fsm])
            for jb in range(N_ // 512):
                j0 = jb * 512
                prev, kprev = zf[:, j0:j0 + 512], k_zf
                for li, (w, kw) in enumerate(((fw1, k_fw1), (fw2, k_fw2), (fw3, k_fw3))):
                    ps, kp = ps_next()
                    S.op("pe", lambda e, ps=ps, w=w, prev=prev: e.matmul(ps[0:64, :], w, prev, start=True, stop=True), reads=[kw, kprev], writes=[kp])
                    ht, kht = hr[li]
                    S.op("act", lambda e, ps=ps, ht=ht, li=li: e.activation(out=ht, in_=ps[0:64, :], func=AF.Identity, scale=fsm[:, 3:4], bias=fsm[:, 4 + li:5 + li]),
                         reads=[kp, k_fsm], writes=[kht])
                    for rep in range(3):
                        S.op("dve", lambda e, ht=ht: e.tensor_scalar(out=fm, in0=ht, scalar1=math.pi, scalar2=-2.0 * math.pi, op0=ALU.is_gt, op1=ALU.mult), reads=[kht], writes=[k_fm])
                        S.op("dve", lambda e, ht=ht: e.tensor_tensor(out=ht, in0=ht, in1=fm, op=ALU.add), reads=[kht, k_fm], writes=[kht])
                        S.op("dve", lambda e, ht=ht: e.tensor_scalar(out=fm, in0=ht, scalar1=-math.pi, scalar2=2.0 * math.pi, op0=ALU.is_lt, op1=ALU.mult), reads=[kht], writes=[k_fm])
                        S.op("dve", lambda e, ht=ht: e.tensor_tensor(out=ht, in0=ht, in1=fm, op=ALU.add), reads=[kht, k_fm], writes=[kht])
                    S.op("act", lambda e, ht=ht: e.activation(out=ht, in_=ht, func=AF.Sin), reads=[kht], writes=[kht])
                    prev, kprev = ht, kht
                for rt in range(4):
                    jt = jb * 4 + rt
                    fcol = 0 if jt < NOWN // 128 else D
                    for cc in range(4):
                        ps, kp = ps_next()
                        S.op("pe", lambda e, ps=ps, prev=prev, rt=rt, cc=cc, fcol=fcol: e.matmul(
                            ps[:, :], prev[:, rt * 128:(rt + 1) * 128], fout[:, fcol + cc * 512:fcol + (cc + 1) * 512], start=True, stop=True),
                            reads=[kprev, k_fout], writes=[kp])
                        dec, kdec = decr.next()
                        S.op("act", lambda e, dec=dec, cc=cc, jt=jt: e.activation(out=dec, in_=dl[:, cc * 512:(cc + 1) * 512], func=AF.Exp, scale=tau[:, jt:jt + 1]),
                             reads=[k_dl, k_tau], writes=[kdec])
                        hc, khc = hcr.next()
                        S.op("dve", lambda e, hc=hc, ps=ps, dec=dec: e.tensor_tensor(out=hc, in0=ps[:, :], in1=dec, op=ALU.mult), reads=[kp, kdec], writes=[khc])
                        S.dma("poolq", Hc[jt * 128:(jt + 1) * 128, cc * 512:(cc + 1) * 512], hc, reads=[khc], writes=["Hc"])

            def stage1(src, nm_src, is_filter):
                S.barrier(); A.reset(PERSIST)
                W1, k_W1 = A.alloc([128, 48, 2, 128], BF16, "W1")
                S.dma("poolq", W1.rearrange("p a b c -> p (a b c)"), W1_d, writes=[k_W1])
                ztr = Ring(A, 2, [128, 48, CG], BF16, "zt")
                aor = Ring(A, 4, [128, 2, CG], BF16, "ao")
                K1 = 128 if is_filter else 86
                if not is_filter:
                    for (zt, kz) in ztr.items:
                        S.op("pool", lambda e, zt=zt: e.memset(zt[64:128], 0.0), writes=[kz])
                for cg in range(NG):
                    c0 = cg * CG
                    zt, kz = ztr.next()
                    if is_filter:
                        S.dma("sp", zt, src[:, c0:c0 + CG].rearrange("(a b) c -> a b c", b=48), reads=[nm_src], writes=[kz])
                    else:
                        S.dma("sp", zt[0:85], src[0:4080, c0:c0 + CG].rearrange("(a b) c -> a b c", b=48), reads=[nm_src], writes=[kz])
                        S.dma("sp", zt[85:86, 0:16, :], src[4080:4096, c0:c0 + CG].rearrange("(a b) c -> a b c", a=1), reads=[nm_src], writes=[kz])
                    for t2 in range(48):
                        pp, kpp = ps_multi(2)

                        def mm(e, pp=pp, zt=zt, t2=t2):
                            e.matmul(pp[:, 0:512], W1[0:K1, t2, 0, :], zt[0:K1, t2, :], start=True, stop=True)
                            return e.matmul(pp[:, 512:1024], W1[0:K1, t2, 1, :], zt[0:K1, t2, :], start=True, stop=True)
                        S.op("pe", mm, reads=[k_W1, kz], writes=kpp)
                        ao, kao = aor.next()
                        eng = "act" if t2 % 2 == 0 else "dve"
                        if eng == "act":
                            S.op("act", lambda e, ao=ao, pp=pp: e.activation(out=ao, in_=pp.rearrange("p (r c) -> p r c", r=2), func=AF.Copy), reads=kpp, writes=[kao])
                        else:
                            S.op("dve", lambda e, ao=ao, pp=pp: e.tensor_copy(out=ao, in_=pp.rearrange("p (r c) -> p r c", r=2)), reads=kpp, writes=[kao])
                        S.dma("poolq", Ad[:, :, c0:c0 + CG].rearrange("f (r t) c -> f r t c", r=2)[:, :, t2, :], ao, reads=[kao], writes=["Ad"])

            def stage2_filter():
                S.barrier(); A.reset(PERSIST)
                cs = load_consts([("W2a", W2a_d, [96, 96]), ("W2b", W2b_d, [96, 96])])
                atr = Ring(A, 2, [96, 16, CG], BF16, "at")
                kor = Ring(A, 2, [96, 2, 16, CG], BF16, "ko")
                for cg in range(NG):
                    c0 = cg * CG
                    for fb in range(8):
                        at, kat = atr.next()
                        S.dma("sp", at, Ad[fb * 16:(fb + 1) * 16, :, c0:c0 + CG].rearrange("f rt c -> rt f c"), reads=["Ad"], writes=[kat])
                        ko, kko = kor.next()
                        for fi in range(16):
                            pp, kpp = ps_multi(2)

                            def mm(e, pp=pp, at=at, fi=fi):
                                e.matmul(pp[0:96, 0:512], cs["W2a"][0], at[:, fi, :], start=True, stop=True)
                                return e.matmul(pp[0:96, 512:1024], cs["W2b"][0], at[:, fi, :], start=True, stop=True)
                            S.op("pe", mm, reads=[cs["W2a"][1], cs["W2b"][1], kat], writes=kpp)
                            eng = "act" if fi % 2 == 0 else "dve"
                            if eng == "act":
                                S.op("act", lambda e, ko=ko, pp=pp, fi=fi: e.activation(out=ko[:, :, fi, :], in_=pp[0:96].rearrange("p (r c) -> p r c", r=2), func=AF.Copy), reads=kpp, writes=[kko])
                            else:
                                S.op("dve", lambda e, ko=ko, pp=pp, fi=fi: e.tensor_copy(out=ko[:, :, fi, :], in_=pp[0:96].rearrange("p (r c) -> p r c", r=2)), reads=kpp, writes=[kko])
                        for ab in range(2):
                            S.dma("poolq", Kd[ab, :, fb * 16:(fb + 1) * 16, c0:c0 + CG], ko[:, ab, :, :], reads=[kko], writes=["Kd"])

            def stage2_data():
                S.barrier(); A.reset(PERSIST)
                cs = load_consts([("W2", W2_d, [96, 96]), ("L1", L1_d, [96, 96]), ("L2", L2_d, [96, 96])])
                atr = Ring(A, 2, [96, 16, CG], BF16, "at")
                kar = Ring(A, 2, [96, 16, CG], BF16, "ka"); kbr = Ring(A, 2, [96, 16, CG], BF16, "kb")
                btr = Ring(A, 2, [96, 16, CG], BF16, "bt")
                p1r = Ring(A, 3, [96, CG], BF16, "p1"); p2r = Ring(A, 3, [96, CG], BF16, "p2")
                for cg in range(NG):
                    c0 = cg * CG
                    for fb in range(8):
                        at, kat = atr.next(); ka, kka = kar.next(); kb, kkb = kbr.next(); bt, kbt = btr.next()
                        S.dma("sp", at, Ad[fb * 16:(fb + 1) * 16, :, c0:c0 + CG].rearrange("f rt c -> rt f c"), reads=["Ad"], writes=[kat])
                        S.dma("sp", ka, Kd[0, :, fb * 16:(fb + 1) * 16, c0:c0 + CG], reads=["Kd"], writes=[kka])
                        S.dma("sp", kb, Kd[1, :, fb * 16:(fb + 1) * 16, c0:c0 + CG], reads=["Kd"], writes=[kkb])
                        pend = None
                        for fi in range(16):
                            ps, kp = ps_next()
                            S.op("pe", lambda e, ps=ps, at=at, fi=fi: e.matmul(ps[0:96, :], cs["W2"][0], at[:, fi, :], start=True, stop=True), reads=[cs["W2"][1], kat], writes=[kp])
                            p1, kp1 = p1r.next(); p2, kp2 = p2r.next()
                            S.op("dve", lambda e, ps=ps, p1=p1, ka=ka, fi=fi: e.tensor_tensor(out=p1, in0=ps[0:96, :], in1=ka[:, fi, :], op=ALU.mult), reads=[kp, kka], writes=[kp1])
                            S.op("dve", lambda e, ps=ps, p2=p2, kb=kb, fi=fi: e.tensor_tensor(out=p2, in0=ps[0:96, :], in1=kb[:, fi, :], op=ALU.mult), reads=[kp, kkb], writes=[kp2])

                            def part2(p1=p1, kp1=kp1, p2=p2, kp2=kp2, fi=fi, bt=bt, kbt=kbt):
                                ps2, kps2 = ps_next()

                                def mm(e):
                                    e.matmul(ps2[0:96, :], cs["L1"][0], p1, start=True, stop=False)
                                    return e.matmul(ps2[0:96, :], cs["L2"][0], p2, start=False, stop=True)
                                S.op("pe", mm, reads=[cs["L1"][1], cs["L2"][1], kp1, kp2], writes=[kps2])
                                S.op("act", lambda e: e.activation(out=bt[:, fi, :], in_=ps2[0:96, :], func=AF.Copy), reads=[kps2], writes=[kbt])
                            if pend is not None:
                                pend()
                            pend = part2
                        pend()
                        S.dma("poolq", Bd[fb * 16:(fb + 1) * 16, :, c0:c0 + CG].rearrange("f rt c -> rt f c"), bt, reads=[kbt], writes=["Bd"])

            def stage1_inv():
                S.barrier(); A.reset(PERSIST)
                Vt, k_V = A.alloc([128, 48, 2, 64], BF16, "V")
                S.dma("poolq", Vt.rearrange("p a b c -> p (a b c)"), V_d, writes=[k_V])
                b2r = Ring(A, 2, [128, 2, 8, CG], BF16, "b2")
                yor = Ring(A, 2, [64, 8, CG], BF16, "yo")
                Ydv = Yd[0:43 * 48, :].rearrange("(a b) c -> a b c", b=48)
                for cg in range(NG):
                    c0 = cg * CG
                    for tb in range(6):
                        b2, kb2 = b2r.next(); yo, kyo = yor.next()
                        for ri in range(2):
                            S.dma("sp", b2[:, ri, :, :], Bd[:, ri * 48 + tb * 8:ri * 48 + tb * 8 + 8, c0:c0 + CG], reads=["Bd"], writes=[kb2])
                        for ti in range(8):
                            t2 = tb * 8 + ti
                            ps, kp = ps_next()

                            def mm(e, ps=ps, b2=b2, t2=t2, ti=ti):
                                e.matmul(ps[0:43, :], Vt[:, t2, 0, 0:43], b2[:, 0, ti, :], start=True, stop=False)
                                return e.matmul(ps[0:43, :], Vt[:, t2, 1, 0:43], b2[:, 1, ti, :], start=False, stop=True)
                            S.op("pe", mm, reads=[k_V, kb2], writes=[kp])
                            if ti % 2 == 0:
                                S.op("act", lambda e, ps=ps, yo=yo, ti=ti: e.activation(out=yo[0:43, ti, :], in_=ps[0:43, :], func=AF.Copy), reads=[kp], writes=[kyo])
                            else:
                                S.op("dve", lambda e, ps=ps, yo=yo, ti=ti: e.tensor_copy(out=yo[0:43, ti, :], in_=ps[0:43, :]), reads=[kp], writes=[kyo])
                        S.dma("poolq", Ydv[:, tb * 8:(tb + 1) * 8, c0:c0 + CG], yo[0:43], reads=[kyo], writes=["Yd"])

            phase_e0 = None
            stage1(Hc, "Hc", True)
            stage2_filter()
            stage1(Zt, "Zt", False)
            stage2_data()
            stage1_inv()

        if "E" in phases:
            phase_e()

        def phase_fgh():
            S.barrier(); A.reset(PERSIST)
            R1, k_R1 = A.alloc([128, 24576], BF16, "R1")
            GTt = R1[:, 0:16384].rearrange("p (a b) -> p a b", a=32)
            ZGt = R1[:, 16384:24576].rearrange("p (a b) -> p a b", a=16)
            actT = R1[:, 0:22528].rearrange("p (a b) -> p a b", a=44)
            hy, k_hy = A.alloc([128, 16, 512], BF16, "hy")
            gn, k_gn = A.alloc([128, 16, 512], BF16, "gn")
            mixed, k_mx = A.alloc([128, 16, 512], BF16, "mixed")
            x1T, k_x1 = A.alloc([128, 16, 512], F32, "x1T")
            rbc, k_rbc = A.alloc([128, 512], F32, "rbc")
            wr = Ring(A, 3, [128, 16, 128], BF16, "w16")
            wdr = Ring(A, 2, [128, 44, 128], BF16, "w44")
            tfr = Ring(A, 2, [128, 512], F32, "tf")
            tbr = Ring(A, 3, [128, 512], BF16, "tb")
            xtr = Ring(A, 3, [128, D], F32, "xt")
            onr = Ring(A, 1, [128, D], BF16, "on")
            ytr = Ring(A, 2, [128, 4, 128], BF16, "yt")
            st, k_st = A.alloc([128, 64], F32, "st")
            hf, k_hf = hy, k_hy
            sqT, k_sqT = gn, k_gn
            ga_a = mod[:, 32:48, 0]; sh_f = mod[:, 48:64, 0]; ga_f = mod[:, 80:96, 0]
            whv, wgv, wov, wuv, wdv = w_hy_out_d, w_gdn_out_d, w_o_d, w_up_d, w_down_d

            def proj16(wv, c0, rhs, krhs, nkc=16, ring=None):
                wt, kw = (ring or wr).next()
                S.dma("poolq", wt.rearrange("p a b -> p (a b)"), wv[c0 // 128], writes=[kw])
                ps, kp = ps_next()

                def mm(e):
                    last = None
                    for kc in range(nkc):
                        last = e.matmul(ps[:, :], wt[:, kc, :], rhs[:, kc, :], start=(kc == 0), stop=(kc == nkc - 1))
                    return last
                S.op("pe", mm, reads=[kw, krhs], writes=[kp])
                return ps, kp

            def rms_bc(srcT, ksrc):
                S.op("act", lambda e: e.activation(out=sqT, in_=srcT, func=AF.Square), reads=[ksrc], writes=[k_sqT])
                ps, kp = ps_next()

                def mm(e):
                    last = None
                    for fc in range(16):
                        last = e.matmul(ps[:, :], ones_b, sqT[:, fc, :], start=(fc == 0), stop=(fc == 15))
                    return last
                S.op("pe", mm, reads=[k_sqT, k_ones], writes=[kp])
                S.op("dve", lambda e: e.tensor_scalar(out=rbc, in0=ps[:, :], scalar1=1.0 / D, scalar2=EPS, op0=ALU.mult, op1=ALU.add), reads=[kp], writes=[k_rbc])
                S.op("dve", lambda e: e.reciprocal(out=rbc, in_=rbc), reads=[k_rbc], writes=[k_rbc])
                S.op("act", lambda e: e.activation(out=rbc, in_=rbc, func=AF.Sqrt), reads=[k_rbc], writes=[k_rbc])

            for tb in range(NOWN // 512):
                t0 = tb * 512
                S.dma("sp", GTt, GT[:, t0:t0 + 512].rearrange("(a p) t -> p a t", p=128), writes=[k_R1])
                S.dma("sp", ZGt, ZGT[:, t0:t0 + 512].rearrange("(a p) t -> p a t", p=128), writes=[k_R1])
                for cc in range(16):
                    x0t, kx0 = tbr.next(); zt, kz = tbr.next(); yt, kyt = ytr.next()
                    S.dma("sp", x0t, X0T[cc * 128:(cc + 1) * 128, t0:t0 + 512], reads=["X0T"], writes=[kx0])
                    S.dma("sp", zt, ZT[cc * 128:(cc + 1) * 128, t0:t0 + 512], reads=["ZT"], writes=[kz])
                    S.dma("sp", yt, Yd[t0:t0 + 512, cc * 128:(cc + 1) * 128].rearrange("(a p) c -> p a c", p=128), reads=["Yd"], writes=[kyt])
                    ps, kp = ps_next(); psv = ps.bitcast(BF16)

                    def trY(e, yt=yt, psv=psv):
                        last = None
                        for a in range(4):
                            last = e.transpose(psv[:, a * 128:(a + 1) * 128], yt[:, a, :], ident_b)
                        return last
                    S.op("pe", trY, reads=[kyt, k_ident], writes=[kp])
                    tf, ktf = tfr.next()
                    S.op("dve", lambda e, tf=tf, zt=zt, psv=psv, cc=cc: e.scalar_tensor_tensor(
                        out=tf, in0=zt, scalar=hybT[:, cc:cc + 1], in1=psv[:, 0:512], op0=ALU.mult, op1=ALU.add), reads=[kz, kp, k_hyb], writes=[ktf])
                    S.op("dve", lambda e, tf=tf, x0t=x0t, cc=cc: e.tensor_tensor(out=hy[:, cc, :], in0=tf, in1=x0t, op=ALU.mult), reads=[ktf, kx0], writes=[k_hy])
                for tt in range(4):
                    ot, kot = xtr.next(); on, kon = onr.next()
                    tq, ktq = xtr.next()
                    S.dma("sp", ot, Od0[t0 + tt * 128:t0 + (tt + 1) * 128, :], reads=["Od0_0", "Od0_1"], writes=[kot])
                    S.dma("sp", tq, Od1[t0 + tt * 128:t0 + (tt + 1) * 128, :], reads=["Od1_0", "Od1_1"], writes=[ktq])
                    S.op("dve", lambda e, tq=tq, ot=ot: e.tensor_tensor(out=ot, in0=ot, in1=tq, op=ALU.add), reads=[kot, ktq], writes=[kot])
                    S.op("act", lambda e, tq=tq, ot=ot: e.activation(out=tq, in_=ot, func=AF.Square), reads=[kot], writes=[ktq])
                    S.op("dve", lambda e, tq=tq: e.reduce_sum(out=st[:, 0:16], in_=tq.rearrange("p (h e) -> p h e", h=16), axis=AX.X), reads=[ktq], writes=[k_st])
                    S.op("dve", lambda e: e.tensor_scalar(out=st[:, 16:32], in0=st[:, 0:16], scalar1=1.0 / 128, scalar2=EPS, op0=ALU.mult, op1=ALU.add), reads=[k_st], writes=[k_st])
                    S.op("dve", lambda e: e.reciprocal(out=st[:, 32:48], in_=st[:, 16:32]), reads=[k_st], writes=[k_st])
                    S.op("act", lambda e: e.activation(out=st[:, 48:64], in_=st[:, 32:48], func=AF.Sqrt), reads=[k_st], writes=[k_st])
                    for h in range(16):
                        eng = "act" if h % 2 == 0 else "dve"
                        if eng == "act":
                            S.op("act", lambda e, on=on, ot=ot, h=h: e.activation(out=on[:, h * 128:(h + 1) * 128], in_=ot[:, h * 128:(h + 1) * 128],
                                                                            func=AF.Copy, scale=st[:, 48 + h:49 + h]), reads=[kot, k_st], writes=[kon])
                        else:
                            S.op("dve", lambda e, on=on, ot=ot, h=h: e.tensor_scalar(out=on[:, h * 128:(h + 1) * 128], in0=ot[:, h * 128:(h + 1) * 128],
                                                                               scalar1=st[:, 48 + h:49 + h], scalar2=None, op0=ALU.mult), reads=[kot, k_st], writes=[kon])
                    for g in range(4):
                        ps, kp = ps_next(); psv = ps.bitcast(BF16)

                        def trO(e, on=on, psv=psv, g=g):
                            last = None
                            for j in range(4):
                                h = g * 4 + j
                                last = e.transpose(psv[:, j * 128:(j + 1) * 128], on[:, h * 128:(h + 1) * 128], ident_b)
                            return last
                        S.op("pe", trO, reads=[kon, k_ident], writes=[kp])
                        for j in range(4):
                            h = g * 4 + j
                            S.op("dve", lambda e, psv=psv, j=j, h=h, tt=tt: e.scalar_tensor_tensor(
                                out=gn[:, h, tt * 128:(tt + 1) * 128], in0=psv[:, j * 128:(j + 1) * 128], scalar=gnormT[:, 0:1],
                                in1=ZGt[:, h, tt * 128:(tt + 1) * 128], op0=ALU.mult, op1=ALU.mult), reads=[kp, k_gnorm, k_R1], writes=[k_gn])
                for m in range(16):
                    ps, kp = proj16(whv, m * 128, hy, k_hy)
                    tf, ktf = tfr.next()
                    S.op("dve", lambda e, tf=tf, ps=ps, m=m: e.tensor_tensor(out=tf, in0=ps[:, :], in1=GTt[:, m, :], op=ALU.mult), reads=[kp, k_R1], writes=[ktf])
                    ps2, kp2 = proj16(wgv, m * 128, gn, k_gn)
                    tf2, ktf2 = tfr.next()
                    S.op("dve", lambda e, tf2=tf2, ps2=ps2, m=m: e.tensor_tensor(out=tf2, in0=ps2[:, :], in1=GTt[:, 16 + m, :], op=ALU.mult), reads=[kp2, k_R1], writes=[ktf2])
                    S.op("dve", lambda e, tf=tf, tf2=tf2, m=m: e.tensor_tensor(out=mixed[:, m, :], in0=tf, in1=tf2, op=ALU.add), reads=[ktf, ktf2], writes=[k_mx])
                for tt in range(4):
                    xt, kx = xtr.next()
                    S.dma("sp", xt, x_d[t0 + tt * 128:t0 + (tt + 1) * 128, :], writes=[kx])
                    for g in range(4):
                        ps, kp = ps_next()

                        def trX(e, xt=xt, ps=ps, g=g):
                            last = None
                            for j in range(4):
                                fc = g * 4 + j
                                last = e.transpose(ps[:, j * 128:(j + 1) * 128], xt[:, fc * 128:(fc + 1) * 128], ident_f)
                            return last
                        S.op("pe", trX, reads=[kx, k_identf], writes=[kp])
                        S.op("act", lambda e, ps=ps, g=g, tt=tt: e.activation(
                            out=x1T[:, g * 4:(g + 1) * 4, tt * 128:(tt + 1) * 128], in_=ps[:, :].rearrange("p (a b) -> p a b", a=4), func=AF.Copy),
                            reads=[kp], writes=[k_x1])
                for m in range(16):
                    ps, kp = proj16(wov, m * 128, mixed, k_mx)
                    S.op("dve", lambda e, ps=ps, m=m: e.scalar_tensor_tensor(out=x1T[:, m, :], in0=ps[:, :], scalar=ga_a[:, m:m + 1], in1=x1T[:, m, :],
                                                                           op0=ALU.mult, op1=ALU.add), reads=[kp, k_mod, k_x1], writes=[k_x1])
                rms_bc(x1T, k_x1)
                for fc in range(16):
                    tf, ktf = tfr.next()
                    S.op("dve", lambda e, tf=tf, fc=fc: e.scalar_tensor_tensor(out=tf, in0=x1T[:, fc, :], scalar=scale_f[:, fc:fc + 1], in1=rbc,
                                                                             op0=ALU.mult, op1=ALU.mult), reads=[k_x1, k_scf, k_rbc], writes=[ktf])
                    S.op("act", lambda e, tf=tf, fc=fc: e.activation(out=hf[:, fc, :], in_=tf, func=AF.Identity, bias=sh_f[:, fc:fc + 1]),
                         reads=[ktf, k_mod], writes=[k_hf])
                for j in range(44):
                    psg, kpg = proj16(wuv, j * 128, hf, k_hf)
                    psu, kpu = proj16(wuv, D_FF + j * 128, hf, k_hf)
                    tf, ktf = tfr.next()
                    S.op("act", lambda e, tf=tf, psg=psg: e.activation(out=tf, in_=psg[:, :], func=AF.Silu), reads=[kpg], writes=[ktf])
                    S.op("dve", lambda e, tf=tf, psu=psu, j=j: e.tensor_tensor(out=actT[:, j, :], in0=tf, in1=psu[:, :], op=ALU.mult), reads=[ktf, kpu], writes=[k_R1])
                for m in range(16):
                    ps, kp = proj16(wdv, m * 128, actT, k_R1, nkc=44, ring=wdr)
                    S.op("dve", lambda e, ps=ps, m=m: e.scalar_tensor_tensor(out=x1T[:, m, :], in0=ps[:, :], scalar=ga_f[:, m:m + 1], in1=x1T[:, m, :],
                                                                           op0=ALU.mult, op1=ALU.add), reads=[kp, k_mod, k_x1], writes=[k_x1])
                rms_bc(x1T, k_x1)
                for fc in range(16):
                    S.op("dve", lambda e, fc=fc: e.scalar_tensor_tensor(out=x1T[:, fc, :], in0=x1T[:, fc, :], scalar=nfinT[:, fc:fc + 1], in1=rbc,
                                                                      op0=ALU.mult, op1=ALU.mult), reads=[k_x1, k_nfin, k_rbc], writes=[k_x1])
                for tt in range(4):
                    xo, kxo = xtr.next()
                    for g in range(4):
                        ps, kp = ps_next()

                        def trB(e, ps=ps, g=g, tt=tt):
                            last = None
                            for j in range(4):
                                fc = g * 4 + j
                                last = e.transpose(ps[:, j * 128:(j + 1) * 128], x1T[:, fc, tt * 128:(tt + 1) * 128], ident_f)
                            return last
                        S.op("pe", trB, reads=[k_x1, k_identf], writes=[kp])
                        S.op("act", lambda e, ps=ps, xo=xo, g=g: e.activation(out=xo[:, g * 512:(g + 1) * 512], in_=ps[:, :], func=AF.Copy), reads=[kp], writes=[kxo])
                    S.final_tokens.append(S.dma("sp", out_d[t0 + tt * 128:t0 + (tt + 1) * 128, :], xo, reads=[kxo], writes=["out"]))

        if "F" in phases:
            phase_fgh()
        S.emit()
    return nc


def _fm(v, n):
    return np.ascontiguousarray(np.asarray(v, np.float32).reshape(n, 128).T)


def _hyena_consts():
    n = L
    j = np.arange(N_)
    idx = np.where(j < NOWN, j, N_ - j).astype(np.float64)
    idx[NOWN] = 0
    tt = idx / (n - 1)
    bands = 16
    w = 2.0 * np.pi * idx / n
    f = np.linspace(1e-4, bands - 1, bands)
    zf = np.concatenate([tt[None, :], np.cos(f[:, None] * w[None, :]), -np.sin(f[:, None] * w[None, :])], axis=0)
    tau = -tt.copy(); tau[NOWN] = -30.0
    deltas = np.abs(np.linspace(math.log(1e-2) / 1.5, math.log(1e-2) / 0.3, D))
    t1 = np.arange(128)[:, None, None, None]; t2 = np.arange(48)[None, :, None, None]; f1 = np.arange(128)[None, None, None, :]
    th = 2 * np.pi * (t1 * f1 / 128.0 + t2 * f1 / float(N_))
    W1 = np.concatenate([np.cos(th), -np.sin(th)], axis=2)
    a = np.arange(48)
    th2 = 2 * np.pi * np.outer(a, a) / 48.0
    c2, s2 = np.cos(th2), np.sin(th2)
    W2 = np.block([[c2, -s2], [s2, c2]])
    W2a = np.block([[c2, c2], [s2, s2]])
    W2b = np.block([[-s2, -s2], [c2, c2]])
    L1 = np.block([[c2, s2], [-s2, c2]])
    L2 = np.block([[-s2, c2], [-c2, -s2]])
    f1v = np.arange(128)[:, None, None, None]; t2v = np.arange(48)[None, :, None, None]; t1v = np.arange(64)[None, None, None, :]
    ph = 2 * np.pi * f1v * (t2v / float(N_) + t1v / 128.0)
    V = np.concatenate([np.cos(ph), -np.sin(ph)], axis=2) / float(N_)
    f32 = lambda x: np.ascontiguousarray(x, dtype=np.float32)
    return dict(zf=f32(zf), tau=f32(tau.reshape(48, 128).T), deltas=f32(np.broadcast_to(deltas[None, :], (128, D))),
                W1=f32(W1.reshape(128, -1)), Vc=f32(V.reshape(128, -1)), W2=f32(W2), W2a=f32(W2a), W2b=f32(W2b), L1=f32(L1), L2=f32(L2))


def make_in_maps(inp):
    maps = []
    ident = np.eye(128, dtype=np.float32)
    tri = np.tril(np.ones((64, 64), np.float32))
    cmask = np.ascontiguousarray(np.stack([tri, np.tril(tri, -1), tri.T, np.triu(tri.T, 1)], axis=1))
    ii = np.arange(64)
    def same(b): return (ii[:, None] // b == ii[None, :] // b).astype(np.float32)
    cm2 = np.ascontiguousarray(np.stack([same(8), same(16) - same(8), same(32) - same(16), same(64) - same(32), -same(8)], axis=1))
    hyc_consts = _hyena_consts()
    w_in0 = inp["w_in"][0]
    cols = [SEG[t] + j * 128 for (t, j) in PLAN]
    w_inc_base = _chunk_major(w_in0, cols)
    shared = dict(
        w_hy_outc=_chunk_major(inp["w_hy_out"][0], [m * 128 for m in range(16)]),
        w_gdn_outc=_chunk_major(inp["w_gdn_out"][0], [m * 128 for m in range(16)]),
        w_oc=_chunk_major(inp["w_o"][0], [m * 128 for m in range(16)]),
        w_upc=_chunk_major(inp["w_up"][0], [j * 128 for j in range(88)]),
        w_downc=_chunk_major(inp["w_down"][0], [m * 128 for m in range(16)]),
    )
    w_inc_flip = None
    cache = {}

    def _w_inc_for(flip):
        if flip not in cache:
            if not flip:
                cache[flip] = w_inc_base
            else:
                wsc = w_in0[:, 14336:14400].reshape(D, 2, 2, 16)[:, :, ::-1, :].reshape(D, 64)
                arr = w_inc_base.copy()
                arr[PLAN_INDEX[("scal", 0)]] = _chunk_major(wsc, [0])[0]
                cache[flip] = arr
        return cache[flip]

    for core in range(8):
        b, half = core // 2, core % 2
        flip = half == 1
        x = inp["x"][b]; ctx = inp["ctx"][b]
        if flip:
            x = x[::-1]; ctx = ctx[::-1]
        cT = np.stack([_fm(inp["c"][b], 16), _fm(inp["c_ctx"], 16)], axis=-1)
        w_in = inp["w_in"][0]
        w_scal = w_in[:, 14336:14400]
        a_log = inp["gdn_a_log"][0]; dtb = inp["gdn_dt_bias"][0]
        hyc = inp["hy_conv"][0]; gdc = inp["gdn_conv"][0]
        if flip:
            w_scal = w_scal.reshape(D, 2, 2, 16)[:, :, ::-1, :].reshape(D, 64)
            a_log = a_log[::-1]; dtb = dtb[::-1]
            hyc = hyc[::-1]; gdc = gdc[::-1]
        scalp = np.zeros((64, 2), np.float32)
        scalp[32:64, 0] = a_log.reshape(32); scalp[32:64, 1] = dtb.reshape(32)
        hyconvT = np.ascontiguousarray(hyc.reshape(3, 48, 128).transpose(2, 1, 0))
        gdnconvT = np.ascontiguousarray(gdc.reshape(3, 48, 128).transpose(2, 1, 0))
        maps.append(dict(
            x=np.ascontiguousarray(x), ctx=np.ascontiguousarray(ctx), cT=np.ascontiguousarray(cT),
            w_ada=inp["w_ada"][0], b_adaT=_fm(inp["b_ada"][0], 96),
            nmixT=_fm(inp["norm_mix"][0], 16), nffnT=_fm(inp["norm_ffn"][0], 16),
            w_inc=_w_inc_for(flip),
            hyconvT=hyconvT, gdnconvT=gdnconvT, scalp=scalp, ident=ident, cmask=cmask, cm2=cm2,
            fw1=inp["hy_fw1"][0], fw2=inp["hy_fw2"][0], fw3=inp["hy_fw3"][0],
            fsm=np.ascontiguousarray(np.stack([inp["hy_fb1"][0], inp["hy_fb2"][0], inp["hy_fb3"][0], inp["hy_freq"][0]], axis=1)),
            fout=(np.ascontiguousarray(np.concatenate([inp["hy_fout"][0][:, D:], inp["hy_fout"][0][:, :D]], axis=1)) if flip else inp["hy_fout"][0]),
            **hyc_consts,
            **shared,
            hybT=_fm(inp["hy_bias"][0], 16), gnormT=_fm(inp["gdn_norm"][0], 1), nfinT=_fm(inp["norm_final"], 16),
        ))
    return maps


def kernel(**inputs):
    inp = {k: np.asarray(v) for k, v in inputs.items()}
    nc = build()
    maps = make_in_maps(inp)
    res = run_bass_kernel_spmd(nc, maps, core_ids=list(range(8)))
    out = np.empty((4, L, D), np.float32)
    for core in range(8):
        b, half = core // 2, core % 2
        o = res.results[core]["out"]
        if half == 0:
            out[b, :NOWN] = o
        else:
            out[b, NOWN:] = o[::-1]
    return out
```

```python
import contextlib
import math
import numpy as np
import ml_dtypes
import concourse.bass as bass
import concourse.mybir as mybir
from concourse.bass_utils import run_bass_kernel_spmd

F32 = mybir.dt.float32
BF16 = mybir.dt.bfloat16
AF = mybir.ActivationFunctionType
ALU = mybir.AluOpType
AX = mybir.AxisListType

D = 2048
L = 4096
NOWN = 2048
CTX = 256
TT = L + CTX
HEADS = 16
D_IN = 18496
D_FF = 5632
EPS = 1e-6
N_ = 6144

COMPUTE = ("pe", "act", "dve", "pool")
NSLOT = {"sp": 12, "poolq": 12}
QENG = {"sp": "sp", "poolq": "pool"}


class Op:
    __slots__ = ("fn", "waits", "sem", "inc")


class Sched:
    def __init__(self, nc):
        self.nc = nc
        self.streams = {e: [] for e in ("pe", "act", "dve", "pool", "sp")}
        self.cnt = {e: 0 for e in COMPUTE}
        self.slot_uses = {q: [0] * NSLOT[q] for q in NSLOT}
        self.dma_i = {q: 0 for q in NSLOT}
        self.last_w = {}
        self.reads = {}
        self.waited = {e: {} for e in self.streams}
        self.final_tokens = []

    def _need(self, stream, tok, waits):
        if tok is None:
            return
        semname, val, pstream, is_pe = tok
        if pstream == stream and is_pe:
            return
        if self.waited[stream].get(semname, 0) >= val:
            return
        waits[semname] = max(waits.get(semname, 0), val)

    def _deps(self, stream, reads, writes):
        waits = {}
        for k in reads:
            self._need(stream, self.last_w.get(k), waits)
        for k in writes:
            self._need(stream, self.last_w.get(k), waits)
            for t in self.reads.get(k, {}).values():
                self._need(stream, t, waits)
        for s, v in waits.items():
            self.waited[stream][s] = v
        return waits

    def _record(self, tok, reads, writes):
        for k in reads:
            d = self.reads.setdefault(k, {})
            o = d.get(tok[0])
            if o is None or o[1] < tok[1]:
                d[tok[0]] = tok
        for k in writes:
            self.last_w[k] = tok
            self.reads[k] = {}

    def op(self, eng, fn, reads=(), writes=()):
        waits = self._deps(eng, reads, writes)
        self.cnt[eng] += 1
        tok = (eng, self.cnt[eng], eng, eng == "pe")
        o = Op(); o.fn = fn; o.waits = sorted(waits.items()); o.sem = eng; o.inc = 1
        self.streams[eng].append(o)
        self._record(tok, reads, writes)
        return tok

    def dma(self, q, out, in_, reads=(), writes=()):
        reads = [k for k in reads if "#" in k]
        writes = [k for k in writes if "#" in k]
        stream = QENG[q]
        waits = self._deps(stream, reads, writes)
        i = self.dma_i[q]; self.dma_i[q] += 1
        slot = i % NSLOT[q]
        semname = "%s%d" % (q, slot)
        prev = self.slot_uses[q][slot] * 16
        if prev and self.waited[stream].get(semname, 0) < prev:
            waits[semname] = prev
            self.waited[stream][semname] = prev
        self.slot_uses[q][slot] += 1
        tok = (semname, self.slot_uses[q][slot] * 16, None, False)
        o = Op(); o.waits = sorted(waits.items()); o.sem = semname; o.inc = 16
        o.fn = (lambda e, out=out, in_=in_: e.dma_start(out=out, in_=in_))
        self.streams[stream].append(o)
        self._record(tok, reads, writes)
        return tok

    def barrier(self):
        allv = {e: self.cnt[e] for e in COMPUTE}
        for q in NSLOT:
            for s in range(NSLOT[q]):
                allv["%s%d" % (q, s)] = self.slot_uses[q][s] * 16
        for stream in self.streams:
            waits = {}
            for s, v in allv.items():
                if v and self.waited[stream].get(s, 0) < v and not (s == "pe" and stream == "pe"):
                    waits[s] = v
                    self.waited[stream][s] = v
            if waits:
                o = Op(); o.fn = None; o.waits = sorted(waits.items()); o.sem = None; o.inc = 0
                self.streams[stream].append(o)

    def emit(self):
        nc = self.nc
        names = list(COMPUTE) + ["%s%d" % (q, s) for q in NSLOT for s in range(NSLOT[q])]
        with contextlib.ExitStack() as st:
            sems = {n: st.enter_context(nc.semaphore("s_" + n)) for n in names}
            block = st.enter_context(nc.Block())

            def run(stream, final=False):
                def body(e):
                    for o in self.streams[stream]:
                        for s, v in o.waits:
                            e.wait_ge(sems[s], v)
                        if o.fn is not None:
                            o.fn(e).then_inc(sems[o.sem], o.inc)
                    if final:
                        for (s, v, _, _) in self.final_tokens:
                            e.wait_ge(sems[s], v)
                return body

            block.sync(run("sp", True))
            block.tensor(run("pe"))
            block.scalar(run("act"))
            block.vector(run("dve"))
            block.gpsimd(run("pool"))


class Arena:
    def __init__(self, big, total):
        self.big = big; self.total = total; self.off = 0; self.n = 0

    def reset(self, to=0):
        self.off = to

    def alloc(self, shape, dtype, name=None):
        n = int(np.prod(shape[1:]))
        size = n * (2 if dtype == F32 else 1)
        o = self.off
        self.off += (size + 15) // 16 * 16
        assert self.off <= self.total, ("SBUF arena overflow", self.off, self.total)
        v = self.big[:, o:o + size]
        if dtype == F32:
            v = v.bitcast(F32)
        if len(shape) == 3:
            v = v.rearrange("p (a b) -> p a b", a=shape[1])
        elif len(shape) == 4:
            v = v.rearrange("p (a b c) -> p a b c", a=shape[1], b=shape[2])
        self.n += 1
        key = "%s#%d" % (name or "t", self.n)
        return v[0:shape[0]], key


class Ring:
    def __init__(self, arena, n, shape, dtype, name):
        self.items = [arena.alloc(shape, dtype, name) for _ in range(n)]
        self.i = 0

    def next(self):
        it = self.items[self.i % len(self.items)]
        self.i += 1
        return it


def _make_plan():
    plan = [("scal", 0)]
    for j in range(16):
        plan += [("x1", j), ("hv", j)]
    for typ in ("q", "k", "gv", "x0", "zg"):
        plan += [(typ, j) for j in range(16)]
    plan += [("gate", j) for j in range(32)]
    return plan


SEG = dict(x0=0, x1=2048, hv=4096, q=6144, k=8192, gv=10240, zg=12288, scal=14336, gate=14400)


PLAN = _make_plan()
PLAN_INDEX = {k: i for i, k in enumerate(PLAN)}


def _chunk_major(w, cols):
    K = w.shape[0]
    out = np.empty((len(cols), 128, (K // 128) * 128), np.float32)
    for i, c0 in enumerate(cols):
        blk = w[:, c0:c0 + 128]
        if blk.shape[1] < 128:
            blk = np.concatenate([blk, np.zeros((K, 128 - blk.shape[1]), np.float32)], axis=1)
        out[i] = blk.reshape(K // 128, 128, 128).transpose(1, 0, 2).reshape(128, -1)
    return out


def build(upto=99, dbg=(), phases="DEF"):
    nc = bass.Bass("TRN2", target_bir_lowering=False)
    S = Sched(nc)
    ext = {}

    def inp(name, shape, dt=F32):
        ext[name] = nc.dram_tensor(name, list(shape), dt, kind="ExternalInput")
        return ext[name].ap()

    def scratch(name, shape, dt):
        kind = "ExternalOutput" if name in dbg else "Internal"
        t = nc.dram_tensor(name, list(shape), dt, kind=kind)
        return t.ap()

    x_d = inp("x", [L, D]); ctx_d = inp("ctx", [CTX, D]); cT_d = inp("cT", [128, 16, 2])
    w_ada_d = inp("w_ada", [D, 6 * D]); b_adaT_d = inp("b_adaT", [128, 96])
    nmixT_d = inp("nmixT", [128, 16]); nffnT_d = inp("nffnT", [128, 16])
    w_in_d = inp("w_inc", [145, 128, 16 * 128])
    hyconvT_d = inp("hyconvT", [128, 48, 3]); gdnconvT_d = inp("gdnconvT", [128, 48, 3])
    scalp_d = inp("scalp", [64, 2])
    ident_d = inp("ident", [128, 128])
    out_d = nc.dram_tensor("out", [NOWN, D], F32, kind="ExternalOutput").ap()
    w_hy_out_d = inp("w_hy_outc", [16, 128, 16 * 128]); w_gdn_out_d = inp("w_gdn_outc", [16, 128, 16 * 128]); w_o_d = inp("w_oc", [16, 128, 16 * 128])
    w_up_d = inp("w_upc", [88, 128, 16 * 128]); w_down_d = inp("w_downc", [16, 128, 44 * 128])
    hybT_d = inp("hybT", [128, 16]); gnormT_d = inp("gnormT", [128, 1]); nfinT_d = inp("nfinT", [128, 16])
    Yd = scratch("Yd", [NOWN + 64, D], BF16); Od0 = scratch("Od0", [NOWN, D], F32); Od1 = scratch("Od1", [NOWN, D], F32)
    cmask_d = inp("cmask", [64, 4, 64]); cm2_d = inp("cm2", [64, 5, 64])
    zf_d = inp("zf", [33, N_]); fw1_d = inp("fw1", [33, 64]); fw2_d = inp("fw2", [64, 64]); fw3_d = inp("fw3", [64, 64])
    fsm_d = inp("fsm", [64, 4]); fout_d = inp("fout", [64, 2 * D]); deltas_d = inp("deltas", [128, D]); tau_d = inp("tau", [128, 48])
    W1_d = inp("W1", [128, 48 * 2 * 128]); V_d = inp("Vc", [128, 48 * 2 * 64])
    W2_d = inp("W2", [96, 96]); W2a_d = inp("W2a", [96, 96]); W2b_d = inp("W2b", [96, 96]); L1_d = inp("L1", [96, 96]); L2_d = inp("L2", [96, 96])
    Hc = scratch("Hc", [N_, D], BF16); Ad = scratch("Ad", [128, 96, D], BF16); Bd = scratch("Bd", [128, 96, D], BF16)
    Kd = scratch("Kd", [2, 96, 128, D], BF16)

    X0T = scratch("X0T", [D, NOWN], BF16); ZT = scratch("ZT", [D, L], BF16); Zt = scratch("Zt", [L, D], BF16)
    QT = scratch("QT", [D, TT], BF16); KT = scratch("KT", [D, TT], BF16); VT = scratch("VT", [D, TT], BF16)
    ZGT = scratch("ZGT", [D, NOWN], BF16); SCT = scratch("SCT", [64, TT], F32); GT = scratch("GT", [2 * D, NOWN], BF16)
    MODd = scratch("MODd", [128, 96, 2], F32)

    with contextlib.ExitStack() as st:
        TOTAL = 106000
        big = st.enter_context(nc.sbuf_tensor("big", [128, TOTAL], BF16))
        PSALL = st.enter_context(nc.psum_tensor("psall", [128, 4096], F32))
        psb = [PSALL[:, i * 512:(i + 1) * 512] for i in range(8)]
        A = Arena(big, TOTAL)
        psi = [0]

        def ps_next():
            i = psi[0] % 8; psi[0] += 1
            return psb[i], "psb%d" % i

        def ps_multi(nb):
            i = ((psi[0] + nb - 1) // nb * nb) % 8
            psi[0] = i + nb
            return PSALL[:, i * 512:(i + nb) * 512], ["psb%d" % (i + j) for j in range(nb)]

        ident_f, k_identf = A.alloc([128, 128], F32, "identf")
        ident_b, k_ident = A.alloc([128, 128], BF16, "ident")
        ones_b, k_ones = A.alloc([128, 128], BF16, "ones")
        ones_f, k_onesf = A.alloc([128, 128], F32, "onesf")
        mod, k_mod = A.alloc([128, 96, 2], F32, "mod")
        nmixT, k_nmix = A.alloc([128, 16], F32, "nmix")
        nffnT, k_nffn = A.alloc([128, 16], F32, "nffn")
        scale_a, k_sca = A.alloc([128, 16], F32, "scale_a")
        scale_c, k_scc = A.alloc([128, 16], F32, "scale_c")
        scale_f, k_scf = A.alloc([128, 16], F32, "scale_f")
        hyconvT, k_hyc = A.alloc([128, 48, 3], F32, "hyc")
        gdnconvT, k_gdc = A.alloc([128, 48, 3], F32, "gdc")
        scalp, k_scalp = A.alloc([64, 2], F32, "scalp")
        nega, k_nega = A.alloc([64, 1], F32, "nega")
        negpi, k_negpi = A.alloc([128, 1], F32, "negpi")
        S.op("dve", lambda e: e.memset(negpi, -math.pi), writes=[k_negpi])
        hybT, k_hyb = A.alloc([128, 16], F32, "hyb")
        gnormT, k_gnorm = A.alloc([128, 1], F32, "gnorm")
        nfinT, k_nfin = A.alloc([128, 16], F32, "nfin")
        PERSIST = A.off
        S.dma("sp", hybT, hybT_d, writes=[k_hyb]); S.dma("sp", gnormT, gnormT_d, writes=[k_gnorm]); S.dma("sp", nfinT, nfinT_d, writes=[k_nfin])

        S.dma("sp", ident_f, ident_d, writes=[k_identf])
        S.op("dve", lambda e: e.tensor_copy(out=ident_b, in_=ident_f), reads=[k_identf], writes=[k_ident])
        S.op("dve", lambda e: e.memset(ones_b, 1.0), writes=[k_ones])
        S.op("dve", lambda e: e.memset(ones_f, 1.0), writes=[k_onesf])
        S.dma("sp", nmixT, nmixT_d, writes=[k_nmix]); S.dma("sp", nffnT, nffnT_d, writes=[k_nffn])
        S.dma("sp", hyconvT, hyconvT_d, writes=[k_hyc]); S.dma("sp", gdnconvT, gdnconvT_d, writes=[k_gdc])
        S.dma("sp", scalp, scalp_d, writes=[k_scalp])
        S.op("act", lambda e: e.activation(out=nega[32:64], in_=scalp[32:64, 0:1], func=AF.Exp), reads=[k_scalp], writes=[k_nega])
        S.op("dve", lambda e: e.tensor_scalar(out=nega[32:64], in0=nega[32:64], scalar1=-1.0, scalar2=None, op0=ALU.mult), reads=[k_nega], writes=[k_nega])

        cT, k_cT = A.alloc([128, 16, 2], F32, "cT")
        sT, k_sT = A.alloc([128, 16, 2], BF16, "sT")
        bT, k_bT = A.alloc([128, 96], F32, "bT")
        wr = Ring(A, 2, [128, 16, 1536], BF16, "wada")
        S.dma("sp", cT, cT_d, writes=[k_cT]); S.dma("sp", bT, b_adaT_d, writes=[k_bT])
        S.op("act", lambda e: e.activation(out=sT, in_=cT, func=AF.Silu), reads=[k_cT], writes=[k_sT])
        w_ada_v = w_ada_d.rearrange("(kc p) n -> p kc n", p=128)
        for blk in range(8):
            wt, kw = wr.next()
            S.dma("poolq", wt, w_ada_v[:, :, blk * 1536:(blk + 1) * 1536], writes=[kw])
            ps, kp = ps_next()

            def mmA(e, wt=wt, ps=ps):
                last = None
                for j in range(12):
                    for kc in range(16):
                        last = e.matmul(ps[:, 2 * j:2 * j + 2], wt[:, kc, j * 128:(j + 1) * 128], sT[:, kc, :],
                                        start=(kc == 0), stop=(kc == 15))
                return last
            S.op("pe", mmA, reads=[kw, k_sT], writes=[kp])
            S.op("dve", lambda e, ps=ps, blk=blk: e.tensor_copy(
                out=mod[:, blk * 12:(blk + 1) * 12, :], in_=ps[:, 0:24].rearrange("p (a b) -> p a b", b=2)),
                reads=[kp], writes=[k_mod])
        for j in range(2):
            S.op("dve", lambda e, j=j: e.tensor_tensor(out=mod[:, :, j], in0=mod[:, :, j], in1=bT, op=ALU.add),
                 reads=[k_mod, k_bT], writes=[k_mod])
        for (dst, kd, src, nrm, kn) in ((scale_a, k_sca, mod[:, 16:32, 0], nmixT, k_nmix),
                                        (scale_c, k_scc, mod[:, 16:32, 1], nmixT, k_nmix),
                                        (scale_f, k_scf, mod[:, 64:80, 0], nffnT, k_nffn)):
            S.op("dve", lambda e, dst=dst, src=src, nrm=nrm: e.scalar_tensor_tensor(
                out=dst, in0=src, scalar=1.0, in1=nrm, op0=ALU.add, op1=ALU.mult), reads=[k_mod, kn], writes=[kd])
        if "MODd" in dbg:
            S.final_tokens.append(S.dma("sp", MODd, mod, reads=[k_mod], writes=["MODd"]))
        if upto <= 1:
            S.emit(); return nc

        S.barrier(); A.reset(PERSIST)
        hT, k_hT = A.alloc([128, 16, TT], BF16, "hT")
        PB = A.off
        xr = Ring(A, 2, [128, D], F32, "xt")
        sq, k_sq = A.alloc([128, D], F32, "sq")
        xnr = Ring(A, 2, [128, D], BF16, "xn")
        str_ = Ring(A, 2, [128, 4], F32, "stat")
        for i in range(TT // 128):
            lat = i < L // 128
            src = x_d[i * 128:(i + 1) * 128, :] if lat else ctx_d[(i - 32) * 128:(i - 31) * 128, :]
            sc_t, k_sc = (scale_a, k_sca) if lat else (scale_c, k_scc)
            sh_t = mod[:, 0:16, 0] if lat else mod[:, 0:16, 1]
            xt, kx = xr.next(); xn, kxn = xnr.next(); stt, kst = str_.next()
            S.dma("sp", xt, src, writes=[kx])
            S.op("act", lambda e, xt=xt: e.activation(out=sq, in_=xt, func=AF.Square), reads=[kx], writes=[k_sq])
            S.op("dve", lambda e, stt=stt: e.reduce_sum(out=stt[:, 0:1], in_=sq, axis=AX.X), reads=[k_sq], writes=[kst])
            S.op("dve", lambda e, stt=stt: e.tensor_scalar(out=stt[:, 1:2], in0=stt[:, 0:1], scalar1=1.0 / D, scalar2=EPS,
                                                           op0=ALU.mult, op1=ALU.add), reads=[kst], writes=[kst])
            S.op("dve", lambda e, stt=stt: e.reciprocal(out=stt[:, 2:3], in_=stt[:, 1:2]), reads=[kst], writes=[kst])
            S.op("act", lambda e, stt=stt: e.activation(out=stt[:, 3:4], in_=stt[:, 2:3], func=AF.Sqrt), reads=[kst], writes=[kst])
            S.op("dve", lambda e, stt=stt: e.tensor_tensor(out=stt[:, 0:1], in0=stt[:, 3:4], in1=stt[:, 3:4], op=ALU.mult), reads=[kst], writes=[kst])
            S.op("dve", lambda e, stt=stt: e.tensor_tensor(out=stt[:, 0:1], in0=stt[:, 0:1], in1=stt[:, 1:2], op=ALU.mult), reads=[kst], writes=[kst])
            S.op("dve", lambda e, stt=stt: e.tensor_scalar(out=stt[:, 0:1], in0=stt[:, 0:1], scalar1=-0.5, scalar2=1.5,
                                                           op0=ALU.mult, op1=ALU.add), reads=[kst], writes=[kst])
            S.op("dve", lambda e, stt=stt: e.tensor_tensor(out=stt[:, 3:4], in0=stt[:, 3:4], in1=stt[:, 0:1], op=ALU.mult), reads=[kst], writes=[kst])
            S.op("act", lambda e, xt=xt, xn=xn, stt=stt: e.activation(out=xn, in_=xt, func=AF.Copy, scale=stt[:, 3:4]),
                 reads=[kx, kst], writes=[kxn])
            for g in range(4):
                ps, kp = ps_next()
                psv = ps.bitcast(BF16)

                def tr(e, xn=xn, psv=psv, g=g):
                    last = None
                    for j in range(4):
                        fc = g * 4 + j
                        last = e.transpose(psv[:, j * 128:(j + 1) * 128], xn[:, fc * 128:(fc + 1) * 128], ident_b)
                    return last
                S.op("pe", tr, reads=[kxn, k_ident], writes=[kp])
                for j in range(4):
                    fc = g * 4 + j
                    dst = hT[:, fc, i * 128:(i + 1) * 128]
                    if j % 2 == 0:
                        S.op("act", lambda e, dst=dst, psv=psv, j=j, fc=fc, sc_t=sc_t, sh_t=sh_t: e.activation(
                            out=dst, in_=psv[:, j * 128:(j + 1) * 128], func=AF.Identity, scale=sc_t[:, fc:fc + 1], bias=sh_t[:, fc:fc + 1]),
                            reads=[kp, k_sc, k_mod], writes=[k_hT])
                    else:
                        S.op("dve", lambda e, dst=dst, psv=psv, j=j, fc=fc, sc_t=sc_t, sh_t=sh_t: e.tensor_scalar(
                            out=dst, in0=psv[:, j * 128:(j + 1) * 128], scalar1=sc_t[:, fc:fc + 1], scalar2=sh_t[:, fc:fc + 1],
                            op0=ALU.mult, op1=ALU.add), reads=[kp, k_sc, k_mod], writes=[k_hT])
        if "HTd" in dbg:
            HTd = nc.dram_tensor("HTd", [128, 16, TT], BF16, kind="ExternalOutput").ap()
            S.final_tokens.append(S.dma("sp", HTd, hT, reads=[k_hT], writes=["HTd"]))
        if upto <= 2:
            S.emit(); return nc

        S.barrier(); A.reset(PB)
        wr = Ring(A, 3, [128, 16, 128], BF16, "win")
        ur = Ring(A, 3, [128, 512], F32, "u")
        t2r = Ring(A, 2, [128, 512], F32, "tmp2")
        obr = Ring(A, 3, [128, 512], BF16, "ob")
        sqr = Ring(A, 3, [128, 512], BF16, "sq")
        ofr = Ring(A, 2, [64, 512], F32, "of")
        u1, k_u1 = A.alloc([128, L], BF16, "u1")
        ztr = Ring(A, 2, [128, 4, 128], BF16, "zt")
        BLK_ALL = [(b * 512, 512) for b in range(8)]
        BLK_OWN = BLK_ALL[:NOWN // 512]
        BLK_CTX = [(L, CTX)]

        def conv_epilogue(ps, kp, n, rw, taps, ktaps):
            u, ku = ur.next()
            S.op("act", lambda e: e.activation(out=u[:, 0:n], in_=ps[:, 0:n], func=AF.Copy, scale=taps[:, 1:2]),
                 reads=[kp, ktaps], writes=[ku])
            pv = ps[:, 0:n].rearrange("p (r w) -> p r w", w=rw)
            uv = u[:, 0:n].rearrange("p (r w) -> p r w", w=rw)
            S.op("dve", lambda e: e.scalar_tensor_tensor(out=uv[:, :, 1:rw], in0=pv[:, :, 0:rw - 1], scalar=taps[:, 0:1],
                                                         in1=uv[:, :, 1:rw], op0=ALU.mult, op1=ALU.add), reads=[kp, ktaps, ku], writes=[ku])
            S.op("dve", lambda e: e.scalar_tensor_tensor(out=uv[:, :, 0:rw - 1], in0=pv[:, :, 1:rw], scalar=taps[:, 2:3],
                                                         in1=uv[:, :, 0:rw - 1], op0=ALU.mult, op1=ALU.add), reads=[kp, ktaps, ku], writes=[ku])
            return u, ku

        def do_chunk(typ, j):
            M = 64 if typ == "scal" else 128
            wt, kw = wr.next()
            S.dma("poolq", wt.rearrange("p a b -> p (a b)"), w_in_d[PLAN_INDEX[(typ, j)]], writes=[kw])
            own = typ in ("x0", "zg", "gate")
            blocks = BLK_OWN if own else BLK_ALL
            if typ in ("q", "k", "gv", "scal"):
                blocks = blocks + BLK_CTX
            def blk(t0, n):
                ps, kp = ps_next()

                def mm(e, ps=ps, t0=t0, n=n):
                    last = None
                    for kc in range(16):
                        last = e.matmul(ps[0:M, 0:n], wt[:, kc, 0:M], hT[:, kc, t0:t0 + n], start=(kc == 0), stop=(kc == 15))
                    return last
                S.op("pe", mm, reads=[kw, k_hT], writes=[kp])
                isctx = t0 >= L
                rw = CTX if isctx else 64
                if typ in ("x0", "x1", "hv"):
                    ci = {"x0": 0, "x1": 16, "hv": 32}[typ] + j
                    u, ku = conv_epilogue(ps, kp, n, rw, hyconvT[:, ci, :], k_hyc)
                    if typ == "x0":
                        ob, kob = obr.next()
                        S.op("act", lambda e, ob=ob, u=u: e.activation(out=ob, in_=u, func=AF.Copy), reads=[ku], writes=[kob])
                        S.dma("sp", X0T[j * 128:(j + 1) * 128, t0:t0 + n], ob, reads=[kob], writes=["X0T"])
                    elif typ == "x1":
                        S.op("act", lambda e, u=u, t0=t0: e.activation(out=u1[:, t0:t0 + 512], in_=u, func=AF.Copy), reads=[ku], writes=[k_u1 + "@%d" % t0])
                    else:
                        ob, kob = obr.next()
                        S.op("pool", lambda e, ob=ob, u=u, t0=t0: e.tensor_tensor(out=ob, in0=u, in1=u1[:, t0:t0 + 512], op=ALU.mult),
                             reads=[ku, k_u1 + "@%d" % t0], writes=[kob])
                        S.dma("sp", ZT[j * 128:(j + 1) * 128, t0:t0 + n], ob, reads=[kob], writes=["ZT"])
                        yield
                        ps2, kp2 = ps_next()
                        ps2v = ps2.bitcast(BF16)

                        def trz(e, ob=ob, ps2v=ps2v):
                            last = None
                            for tb in range(4):
                                last = e.transpose(ps2v[:, tb * 128:(tb + 1) * 128], ob[:, tb * 128:(tb + 1) * 128], ident_b)
                            return last
                        S.op("pe", trz, reads=[kob, k_ident], writes=[kp2])
                        zt, kzt = ztr.next()
                        S.op("act", lambda e, zt=zt, ps2v=ps2v: e.activation(
                            out=zt, in_=ps2v[:, 0:512].rearrange("p (a b) -> p a b", b=128), func=AF.Copy), reads=[kp2], writes=[kzt])
                        S.dma("sp", Zt[t0:t0 + 512, j * 128:(j + 1) * 128].rearrange("(a p) c -> p a c", p=128), zt,
                              reads=[kzt], writes=["Zt"])
                elif typ in ("q", "k", "gv"):
                    ci = {"q": 0, "k": 16, "gv": 32}[typ] + j
                    u, ku = conv_epilogue(ps, kp, n, rw, gdnconvT[:, ci, :], k_gdc)
                    S.op("act", lambda e, u=u, n=n: e.activation(out=u[:, 0:n], in_=u[:, 0:n], func=AF.Silu), reads=[ku], writes=[ku])
                    ob, kob = obr.next()
                    dstT = {"q": QT, "k": KT, "gv": VT}[typ]
                    if typ == "gv":
                        S.op("pool", lambda e, ob=ob, u=u, n=n: e.tensor_copy(out=ob[:, 0:n], in_=u[:, 0:n]), reads=[ku], writes=[kob])
                    else:
                        sqb, ksqb = sqr.next()
                        S.op("pool", lambda e, sqb=sqb, u=u, n=n: e.tensor_tensor(out=sqb[:, 0:n], in0=u[:, 0:n], in1=u[:, 0:n], op=ALU.mult),
                             reads=[ku], writes=[ksqb])
                        yield
                        t2, kt2 = t2r.next()
                        ps2, kp2 = ps_next()
                        S.op("pe", lambda e, ps2=ps2, sqb=sqb, n=n: e.matmul(ps2[:, 0:n], ones_b, sqb[:, 0:n], start=True, stop=True),
                             reads=[ksqb, k_ones], writes=[kp2])
                        S.op("dve", lambda e, t2=t2, ps2=ps2, n=n: e.tensor_scalar(out=t2[:, 0:n], in0=ps2[:, 0:n], scalar1=EPS, scalar2=None, op0=ALU.add),
                             reads=[kp2], writes=[kt2])
                        S.op("dve", lambda e, t2=t2, n=n: e.reciprocal(out=t2[:, 0:n], in_=t2[:, 0:n]), reads=[kt2], writes=[kt2])
                        S.op("act", lambda e, t2=t2, n=n: e.activation(out=t2[:, 0:n], in_=t2[:, 0:n], func=AF.Sqrt), reads=[kt2], writes=[kt2])
                        ob, kob = obr.next()
                        qs = (128.0 ** -0.5) if typ == "q" else 1.0
                        S.op("dve", lambda e, ob=ob, u=u, t2=t2, n=n, qs=qs: e.scalar_tensor_tensor(
                            out=ob[:, 0:n], in0=u[:, 0:n], scalar=qs, in1=t2[:, 0:n], op0=ALU.mult, op1=ALU.mult), reads=[ku, kt2], writes=[kob])
                    S.dma("sp", dstT[j * 128:(j + 1) * 128, t0:t0 + n], ob[:, 0:n], reads=[kob], writes=[typ + "T"])
                elif typ in ("zg", "gate"):
                    ob, kob = obr.next()
                    fn = AF.Silu if typ == "zg" else AF.Sigmoid
                    S.op("act", lambda e, ob=ob, ps=ps, fn=fn: e.activation(out=ob, in_=ps[:, 0:512], func=fn), reads=[kp], writes=[kob])
                    dstT = ZGT if typ == "zg" else GT
                    S.dma("sp", dstT[j * 128:(j + 1) * 128, t0:t0 + n], ob, reads=[kob], writes=[typ + "T"])
                else:
                    of, kof = ofr.next()
                    S.op("act", lambda e, of=of, ps=ps, n=n: e.activation(out=of[0:32, 0:n], in_=ps[0:32, 0:n], func=AF.Sigmoid), reads=[kp], writes=[kof])
                    S.op("act", lambda e, of=of, ps=ps, n=n: e.activation(out=of[32:64, 0:n], in_=ps[32:64, 0:n], func=AF.Exp, bias=scalp[32:64, 1:2]),
                         reads=[kp, k_scalp], writes=[kof])
                    S.op("act", lambda e, of=of, n=n: e.activation(out=of[32:64, 0:n], in_=of[32:64, 0:n], func=AF.Ln, bias=1.0), reads=[kof], writes=[kof])
                    S.op("dve", lambda e, of=of, n=n: e.tensor_scalar(out=of[32:64, 0:n], in0=of[32:64, 0:n], scalar1=nega[32:64, 0:1], scalar2=None, op0=ALU.mult),
                         reads=[kof, k_nega], writes=[kof])
                    S.dma("sp", SCT[:, t0:t0 + n], of[:, 0:n], reads=[kof], writes=["SCT"])
                return
                yield

            pend = None
            for (t0, n) in blocks:
                g = blk(t0, n)
                alive = True
                try:
                    next(g)
                except StopIteration:
                    alive = False
                if pend is not None:
                    for _ in pend:
                        pass
                pend = g if alive else None
            if pend is not None:
                for _ in pend:
                    pass

        plan = list(PLAN)
        import os
        if os.environ.get("K_PLAN_TEST") == "gdn":
            plan = [("scal", 0)] + [(t, j) for t in ("q", "k", "gv") for j in range(16)]
        elif os.environ.get("K_PLAN_TEST") == "hy":
            plan = [pp for j in (0, 7) for pp in (("x1", j), ("hv", j))] + [("x0", 0), ("x0", 7)]
        elif os.environ.get("K_PLAN_TEST"):
            plan = [("scal", 0), ("x1", 1), ("hv", 1), ("q", 2), ("k", 3), ("gv", 4), ("x0", 5), ("zg", 6), ("gate", 7), ("gate", 17)]
        for (typ, j) in plan:
            do_chunk(typ, j)
        for nm in ("X0T", "ZT", "Zt", "QT", "KT", "VT", "ZGT", "SCT", "GT"):
            if nm in dbg:
                pass
        if upto <= 3:
            S.barrier()
            S.emit(); return nc


        def phase_d():
            S.barrier(); A.reset(PERSIST)
            H = HEADS
            HH = 8
            cmask, k_cm = A.alloc([64, 4, 64], F32, "cmask")
            S.dma("sp", cmask, cmask_d, writes=[k_cm])
            cm2, k_cm2 = A.alloc([64, 5, 64], F32, "cm2")
            S.dma("sp", cm2, cm2_d, writes=[k_cm2])
            cmask_b, k_cmb = A.alloc([64, 4, 64], BF16, "cmask_b")
            cm2_b, k_cm2b = A.alloc([64, 5, 64], BF16, "cm2_b")
            S.op("dve", lambda e: e.tensor_copy(out=cm2_b, in_=cm2), reads=[k_cm2], writes=[k_cm2b])
            S.op("dve", lambda e: e.tensor_copy(out=cmask_b, in_=cmask), reads=[k_cm], writes=[k_cmb])
            qr = Ring(A, 2, [128, H, 256], BF16, "qg"); kr = Ring(A, 2, [128, H, 256], BF16, "kg"); vr = Ring(A, 2, [128, H, 256], BF16, "vg")
            scr = Ring(A, 2, [64, 256], F32, "scg")
            identb64 = ident_b[0:64, 0:64].unsqueeze(1).broadcast_to([64, HH, 64])
            identf64 = ident_f[0:64, 0:64].unsqueeze(1).broadcast_to([64, HH, 64])
            QTv = QT.rearrange("(h p) t -> p h t", p=128); KTv = KT.rearrange("(h p) t -> p h t", p=128); VTv = VT.rearrange("(h p) t -> p h t", p=128)

            def al(shape, dt, nm):
                return A.alloc(shape, dt, nm)

            class TS:
                pass
            halves = []
            for hh in range(2):
                T_ = TS()
                T_.S = al([128, HH, 128], F32, "S"); T_.Sb = al([128, HH, 128], BF16, "Sb")
                T_.sct = al([64, 64], F32, "sct"); T_.sm = al([64, 8, HH], F32, "sm")
                T_.ktok = al([64, HH, 128], BF16, "ktok"); T_.vtok = al([64, HH, 128], BF16, "vtok")
                T_.Xg = al([64, HH, 64], F32, "Xg"); T_.Xb = al([64, HH, 64], F32, "Xb")
                T_.Y = al([64, HH, 64], F32, "Y"); T_.EA = al([64, HH, 64], F32, "EA"); T_.EB = al([64, HH, 64], F32, "EB")
                T_.tmpM = al([64, HH, 64], BF16, "tmpM"); T_.tmpB = al([64, HH, 64], BF16, "tmpB"); T_.tmpQ = al([64, HH, 64], BF16, "tmpQ")
                T_.bbc = al([64, HH, 64], BF16, "bbc")
                T_.AO = [al([64, HH, 64], BF16, "AO%d" % i) for i in range(3)]
                T_.P0 = al([64, HH, 64], BF16, "P0"); T_.P1 = al([64, HH, 64], BF16, "P1")
                T_.PT0 = al([64, HH, 64], BF16, "PT0"); T_.PT1 = al([64, HH, 64], BF16, "PT1")
                T_.TT = al([64, HH, 64], BF16, "TT"); T_.T = al([64, HH, 64], BF16, "T")
                T_.Am = al([64, HH, 64], BF16, "Am"); T_.ATm = al([64, HH, 64], BF16, "ATm")
                T_.eGbc = al([128, HH, 64], F32, "eGbc")
                T_.vb = al([64, HH, 128], BF16, "vb"); T_.kbg = al([64, HH, 128], BF16, "kbg")
                T_.vnew = al([64, HH, 128], BF16, "vnew")
                T_.slot = []
                for sl in range(2):
                    d = dict(wT=al([128, HH, 64], BF16, "wT"), uu=al([64, HH, 128], F32, "uu"), qdT=al([128, HH, 64], BF16, "qdT"),
                             QKT=al([64, HH, 64], BF16, "QKT"), kdec=al([64, HH, 128], BF16, "kdec"), egl=al([128, HH], F32, "egl"))
                    T_.slot.append(d)
                T_.orr = Ring(A, 2, [64, HH, 128], F32, "o")
                halves.append(T_)

            def bc_s(ap):
                return ap.unsqueeze(2).broadcast_to([64, HH, 64])

            def bc_e(ap):
                return ap.unsqueeze(2).broadcast_to([64, HH, 128])

            def mk(i):
                return cmask[:, i, :].unsqueeze(1).broadcast_to([64, HH, 64])

            def mkb(i):
                return cmask_b[:, i, :].unsqueeze(1).broadcast_to([64, HH, 64])

            def m2(i):
                return cm2_b[:, i, :].unsqueeze(1).broadcast_to([64, HH, 64])

            def hm(psx):
                return psx[0:64, :].rearrange("p (h s) -> p h s", h=HH)

            def hm128(psx):
                return psx[0:64, :].rearrange("p (h d) -> p h d", h=HH)

            def run_dir(dr):
                mA_s, mB_i, mB_s, cumi = ((1, 2, 3, 2) if dr == 0 else (3, 0, 1, 0))
                last_t = 63 if dr == 0 else 0
                for T_ in halves:
                    S.op("dve", lambda e, T_=T_: e.memset(T_.S[0], 0.0), reads=[T_.S[1]], writes=[T_.S[1]])
                    S.op("dve", lambda e, T_=T_: e.memset(T_.Sb[0], 0.0), reads=[T_.Sb[1]], writes=[T_.Sb[1]])
                if dr == 0:
                    groups = [L // 256] + list(range(NOWN // 256))
                else:
                    groups = [L // 256] + list(range(L // 256 - 1, -1, -1))

                def chunk(hh, gi, ci, need_o, g0, qg, kqg, kg, kkg, vg, kvg, scg, kscg, slot):
                    T_ = halves[hh]
                    h0 = hh * HH
                    c0 = ci * 64
                    (Sst, k_S), (Sb, k_Sb), (sct, k_sct), (sm, k_sm) = T_.S, T_.Sb, T_.sct, T_.sm
                    (ktok, k_ktok), (vtok, k_vtok), (Xg, k_Xg), (Xb, k_Xb) = T_.ktok, T_.vtok, T_.Xg, T_.Xb
                    (Y, k_Y), (EA, k_EA), (EB, k_EB), (tmpM, k_tmpM) = T_.Y, T_.EA, T_.EB, T_.tmpM
                    (tmpB, k_tmpB), (tmpQ, k_tmpQ), (bbc, k_bbc) = T_.tmpB, T_.tmpQ, T_.bbc
                    (P0, kP0), (P1, kP1), (PT0, kPT0), (PT1, kPT1) = T_.P0, T_.P1, T_.PT0, T_.PT1
                    (TTm, k_TT), (Tm, k_T), (Am, k_Am), (ATm, k_ATm) = T_.TT, T_.T, T_.Am, T_.ATm
                    (eGbc, k_eGbc), (vb, k_vb), (kbg, k_kbg), (vnew, k_vnew) = T_.eGbc, T_.vb, T_.kbg, T_.vnew
                    sd = T_.slot[slot]
                    (wT, k_wT), (uu, k_uu), (qdT, k_qdT), (QKT, k_QKT), (kdec, k_kdec), (egl, k_egl) = (
                        sd["wT"], sd["uu"], sd["qdT"], sd["QKT"], sd["kdec"], sd["egl"])

                    def mmset(lhs, klhs, rhs, krhs):
                        pX, kpX = ps_next()

                        def f(e):
                            last = None
                            for h in range(HH):
                                last = e.matmul(pX[0:64, h * 64:(h + 1) * 64], lhs[:, h, :], rhs[:, h, :], start=True, stop=True)
                            return last
                        S.op("pe", f, reads=[klhs, krhs], writes=[kpX])
                        return pX, [kpX]
                    ps, kp = ps_next()
                    S.op("pe", lambda e: e.transpose(ps[0:64, 0:64], scg[:, c0:c0 + 64], ident_f[0:64, 0:64]), reads=[kscg, k_identf], writes=[kp])
                    S.op("act", lambda e: e.activation(out=sct, in_=ps[0:64, 0:64], func=AF.Copy), reads=[kp], writes=[k_sct])
                    beta = sct[:, dr * 16 + h0:dr * 16 + h0 + HH]; gg = sct[:, 32 + dr * 16 + h0:32 + dr * 16 + h0 + HH]
                    yield "p"
                    ps2, kp2 = ps_next()
                    S.op("pe", lambda e: e.matmul(ps2[0:64, 0:HH], cmask[:, cumi, :], gg, start=True, stop=True), reads=[k_sct, k_cm], writes=[kp2])
                    S.op("act", lambda e: e.activation(out=sm[:, 0, :], in_=ps2[0:64, 0:HH], func=AF.Copy), reads=[kp2], writes=[k_sm])
                    yield "p"
                    S.op("dve", lambda e: e.tensor_tensor(out=Xg, in0=identf64, in1=bc_s(sm[:, 0, :]), op=ALU.mult), reads=[k_sm, k_identf], writes=[k_Xg])
                    S.op("pool", lambda e: e.tensor_tensor(out=Xb, in0=identf64, in1=bc_s(beta), op=ALU.mult), reads=[k_sct, k_identf], writes=[k_Xb])
                    pG, kpG1 = ps_next(); kpG = [kpG1]
                    pGv = pG.rearrange("p (h s) -> p h s", h=HH)
                    S.op("pe", lambda e: e.matmul(pG[:, 0:512], ones_f[0:64, :], Xg, start=True, stop=True), reads=[k_Xg, k_onesf], writes=kpG)
                    pB, kpB1 = ps_next(); kpB = [kpB1]
                    pBv = pB.rearrange("p (h s) -> p h s", h=HH)
                    S.op("pe", lambda e: e.matmul(pB[0:64, 0:512], ones_f[0:64, 0:64], Xb, start=True, stop=True), reads=[k_Xb, k_onesf], writes=kpB)
                    S.op("dve", lambda e: e.tensor_tensor(out=Y, in0=pGv[0:64], in1=bc_s(sm[:, 0, :]), op=ALU.subtract), reads=kpG + [k_sm], writes=[k_Y])
                    S.op("act", lambda e: e.activation(out=bbc, in_=pBv[0:64], func=AF.Copy), reads=kpB, writes=[k_bbc])
                    S.op("dve", lambda e: e.tensor_tensor(out=sm[:, 3, :], in0=pGv[0:64, :, last_t], in1=sm[:, 0, :], op=ALU.subtract), reads=kpG + [k_sm], writes=[k_sm])
                    S.op("dve", lambda e: e.tensor_scalar(out=egl, in0=pGv[:, :, last_t], scalar1=-80.0, scalar2=None, op0=ALU.max), reads=kpG, writes=[k_egl])
                    if need_o:
                        S.op("dve", lambda e: e.tensor_scalar(out=eGbc, in0=pGv, scalar1=-80.0, scalar2=None, op0=ALU.max), reads=kpG, writes=[k_eGbc])
                    yield "p"
                    S.op("dve", lambda e: e.scalar_tensor_tensor(out=EA, in0=Y, scalar=80.0, in1=mk(mA_s), op0=ALU.min, op1=ALU.mult), reads=[k_Y, k_cm], writes=[k_EA])
                    S.op("act", lambda e: e.activation(out=EA, in_=EA, func=AF.Exp, scale=-1.0), reads=[k_EA], writes=[k_EA])
                    S.op("dve", lambda e: e.scalar_tensor_tensor(out=EB, in0=Y, scalar=-80.0, in1=mk(mB_i), op0=ALU.max, op1=ALU.mult), reads=[k_Y, k_cm], writes=[k_EB])
                    S.op("act", lambda e: e.activation(out=EB, in_=EB, func=AF.Exp), reads=[k_EB], writes=[k_EB])
                    yield "p"
                    S.op("dve", lambda e: e.tensor_scalar(out=sm[:, 1, :], in0=sm[:, 0, :], scalar1=-80.0, scalar2=None, op0=ALU.max), reads=[k_sm], writes=[k_sm])
                    S.op("act", lambda e: e.activation(out=sm[:, 1, :], in_=sm[:, 1, :], func=AF.Exp), reads=[k_sm], writes=[k_sm])
                    S.op("dve", lambda e: e.tensor_tensor(out=sm[:, 2, :], in0=sm[:, 1, :], in1=beta, op=ALU.mult), reads=[k_sm, k_sct], writes=[k_sm])
                    S.op("dve", lambda e: e.tensor_scalar(out=sm[:, 3, :], in0=sm[:, 3, :], scalar1=-80.0, scalar2=None, op0=ALU.max), reads=[k_sm], writes=[k_sm])
                    S.op("act", lambda e: e.activation(out=sm[:, 3, :], in_=sm[:, 3, :], func=AF.Exp), reads=[k_sm], writes=[k_sm])
                    S.op("act", lambda e: e.activation(out=egl, in_=egl, func=AF.Exp), reads=[k_egl], writes=[k_egl])
                    if need_o:
                        S.op("act", lambda e: e.activation(out=eGbc, in_=eGbc, func=AF.Exp), reads=[k_eGbc], writes=[k_eGbc])
                        S.op("pool", lambda e: e.tensor_tensor(out=qdT, in0=qg[:, h0:h0 + HH, c0:c0 + 64], in1=eGbc, op=ALU.mult), reads=[kqg, k_eGbc], writes=[k_qdT])
                    yield "p"
                    for (src, ksrc, dst, kdst) in ((kg, kkg, ktok, k_ktok), (vg, kvg, vtok, k_vtok)):
                        pT, kpT = ps_next()
                        pTv = pT.bitcast(BF16)

                        def trk(e, src=src, pTv=pTv):
                            last = None
                            for h in range(HH):
                                last = e.transpose(pTv[0:64, h * 128:(h + 1) * 128], src[:, h0 + h, c0:c0 + 64], ident_b)
                            return last
                        S.op("pe", trk, reads=[ksrc, k_ident], writes=[kpT])
                        S.op("act", lambda e, dst=dst, pTv=pTv: e.activation(out=dst, in_=pTv[0:64, :].rearrange("p (h d) -> p h d", h=HH), func=AF.Copy),
                             reads=[kpT], writes=[kdst])
                    pK, kpK1 = ps_next(); kpK = [kpK1]
                    pKv = pK.rearrange("p (h s) -> p h s", h=HH)

                    def mmK(e):
                        last = None
                        for h in range(HH):
                            last = e.matmul(pK[0:64, h * 64:(h + 1) * 64], kg[:, h0 + h, c0:c0 + 64], kg[:, h0 + h, c0:c0 + 64], start=True, stop=True)
                        return last
                    S.op("pe", mmK, reads=[kkg], writes=kpK)
                    S.op("dve", lambda e: e.tensor_tensor(out=tmpM, in0=pKv[0:64], in1=EA, op=ALU.mult), reads=kpK + [k_EA], writes=[k_tmpM])
                    S.op("dve", lambda e: e.tensor_tensor(out=tmpB, in0=pKv[0:64], in1=EB, op=ALU.mult), reads=kpK + [k_EB], writes=[k_tmpB])
                    if need_o:
                        pQ, kpQ1 = ps_next(); kpQ = [kpQ1]
                        pQv = pQ.rearrange("p (h s) -> p h s", h=HH)

                        def mmQ(e):
                            last = None
                            for h in range(HH):
                                last = e.matmul(pQ[0:64, h * 64:(h + 1) * 64], kg[:, h0 + h, c0:c0 + 64], qg[:, h0 + h, c0:c0 + 64], start=True, stop=True)
                            return last
                        S.op("pe", mmQ, reads=[kkg, kqg], writes=kpQ)
                        S.op("dve", lambda e: e.tensor_tensor(out=tmpQ, in0=pQv[0:64], in1=EB, op=ALU.mult), reads=kpQ + [k_EB], writes=[k_tmpQ])
                    yield "p"
                    S.op("dve", lambda e: e.tensor_tensor(out=tmpM, in0=tmpM, in1=bc_s(beta), op=ALU.mult), reads=[k_tmpM, k_sct], writes=[k_tmpM])
                    S.op("pool", lambda e: e.tensor_tensor(out=Am, in0=tmpM, in1=mkb(mA_s), op=ALU.mult), reads=[k_tmpM, k_cmb], writes=[k_Am])
                    S.op("dve", lambda e: e.tensor_tensor(out=tmpB, in0=tmpB, in1=bbc, op=ALU.mult), reads=[k_tmpB, k_bbc], writes=[k_tmpB])
                    S.op("pool", lambda e: e.tensor_tensor(out=ATm, in0=tmpB, in1=mkb(mB_s), op=ALU.mult), reads=[k_tmpB, k_cmb], writes=[k_ATm])
                    if need_o:
                        S.op("pool", lambda e: e.tensor_tensor(out=QKT, in0=tmpQ, in1=mkb(mB_i), op=ALU.mult), reads=[k_tmpQ, k_cmb], writes=[k_QKT])
                    for lv in range(3):
                        AOt, kAO = T_.AO[lv]
                        S.op("pool", lambda e, lv=lv, AOt=AOt: e.tensor_tensor(out=AOt, in0=ATm, in1=m2(1 + lv), op=ALU.mult), reads=[k_ATm, k_cm2b], writes=[kAO])
                    yield "p"
                    S.op("pool", lambda e: e.tensor_tensor(out=P0, in0=Am, in1=m2(4), op=ALU.mult), reads=[k_Am, k_cm2b], writes=[kP0])
                    S.op("pool", lambda e: e.tensor_tensor(out=PT0, in0=ATm, in1=m2(4), op=ALU.mult), reads=[k_ATm, k_cm2b], writes=[kPT0])
                    S.op("pool", lambda e: e.tensor_tensor(out=Tm, in0=P0, in1=identb64, op=ALU.add), reads=[kP0, k_ident], writes=[k_T])
                    S.op("pool", lambda e: e.tensor_tensor(out=TTm, in0=PT0, in1=identb64, op=ALU.add), reads=[kPT0, k_ident], writes=[k_TT])
                    p1, kp1 = mmset(PT0, kPT0, P0, kP0)
                    S.op("act", lambda e: e.activation(out=P1, in_=hm(p1), func=AF.Copy), reads=kp1, writes=[kP1])
                    p2_, kp2_ = mmset(P0, kP0, PT0, kPT0)
                    S.op("act", lambda e: e.activation(out=PT1, in_=hm(p2_), func=AF.Copy), reads=kp2_, writes=[kPT1])
                    yield "p"
                    p3, kp3 = mmset(TTm, k_TT, P1, kP1)
                    p4, kp4 = mmset(P1, kP1, TTm, k_TT)
                    S.op("dve", lambda e: e.tensor_tensor(out=Tm, in0=Tm, in1=hm(p3), op=ALU.add), reads=kp3 + [k_T], writes=[k_T])
                    S.op("dve", lambda e: e.tensor_tensor(out=TTm, in0=TTm, in1=hm(p4), op=ALU.add), reads=kp4 + [k_TT], writes=[k_TT])
                    p5, kp5 = mmset(PT1, kPT1, P1, kP1)
                    S.op("act", lambda e: e.activation(out=P0, in_=hm(p5), func=AF.Copy), reads=kp5, writes=[kP0])
                    yield "p"
                    p6, kp6 = mmset(TTm, k_TT, P0, kP0)
                    p7, kp7 = mmset(P0, kP0, TTm, k_TT)
                    S.op("dve", lambda e: e.tensor_tensor(out=Tm, in0=Tm, in1=hm(p6), op=ALU.add), reads=kp6 + [k_T], writes=[k_T])
                    S.op("dve", lambda e: e.tensor_tensor(out=TTm, in0=TTm, in1=hm(p7), op=ALU.add), reads=kp7 + [k_TT], writes=[k_TT])
                    yield "p"
                    for lv in range(3):
                        AOt, kAO = T_.AO[lv]
                        pX, kpX = mmset(AOt, kAO, Tm, k_T)
                        S.op("act", lambda e, pX=pX: e.activation(out=P1, in_=hm(pX), func=AF.Copy), reads=kpX, writes=[kP1])
                        yield "p"
                        p8, kp8 = mmset(P1, kP1, TTm, k_TT)
                        if lv < 2:
                            p9, kp9 = mmset(TTm, k_TT, P1, kP1)
                            S.op("dve", lambda e, p9=p9: e.tensor_tensor(out=Tm, in0=Tm, in1=hm(p9), op=ALU.subtract), reads=kp9 + [k_T], writes=[k_T])
                        S.op("dve", lambda e, p8=p8: e.tensor_tensor(out=TTm, in0=TTm, in1=hm(p8), op=ALU.subtract), reads=kp8 + [k_TT], writes=[k_TT])
                        yield "p"
                    S.op("pool", lambda e: e.tensor_tensor(out=vb, in0=vtok, in1=bc_e(beta), op=ALU.mult), reads=[k_vtok, k_sct], writes=[k_vb])
                    S.op("pool", lambda e: e.tensor_tensor(out=kbg, in0=ktok, in1=bc_e(sm[:, 2, :]), op=ALU.mult), reads=[k_ktok, k_sm], writes=[k_kbg])
                    S.op("pool", lambda e: e.tensor_tensor(out=kdec, in0=ktok, in1=bc_e(sm[:, 3, :]), op=ALU.mult), reads=[k_ktok, k_sm], writes=[k_kdec])
                    pU, kpU = ps_multi(2)

                    def mmU(e):
                        last = None
                        for h in range(HH):
                            last = e.matmul(pU[0:64, h * 128:(h + 1) * 128], TTm[:, h, :], vb[:, h, :], start=True, stop=True)
                        return last
                    S.op("pe", mmU, reads=[k_TT, k_vb], writes=kpU)
                    S.op("act", lambda e: e.activation(out=uu, in_=hm128(pU), func=AF.Copy), reads=kpU, writes=[k_uu])
                    pW, kpW1 = ps_next(); kpW = [kpW1]

                    def mmW(e):
                        last = None
                        for h in range(HH):
                            last = e.matmul(pW[:, h * 64:(h + 1) * 64], kbg[:, h, :], TTm[:, h, :], start=True, stop=True)
                        return last
                    S.op("pe", mmW, reads=[k_kbg, k_TT], writes=kpW)
                    S.op("act", lambda e: e.activation(out=wT, in_=pW.rearrange("p (h s) -> p h s", h=HH), func=AF.Copy), reads=kpW, writes=[k_wT])
                    yield "scan"
                    pV, kpV = ps_multi(2)

                    def mmV(e):
                        last = None
                        for h in range(HH):
                            last = e.matmul(pV[0:64, h * 128:(h + 1) * 128], wT[:, h, :], Sb[:, h, :], start=True, stop=True)
                        return last
                    S.op("pe", mmV, reads=[k_wT, k_Sb], writes=kpV)
                    S.op("dve", lambda e: e.tensor_tensor(out=vnew, in0=uu, in1=hm128(pV), op=ALU.subtract), reads=kpV + [k_uu], writes=[k_vnew])
                    yield "s"
                    if need_o:
                        pO, kpO = ps_multi(2)

                        def mmO(e):
                            last = None
                            for h in range(HH):
                                e.matmul(pO[0:64, h * 128:(h + 1) * 128], qdT[:, h, :], Sb[:, h, :], start=True, stop=False)
                                last = e.matmul(pO[0:64, h * 128:(h + 1) * 128], QKT[:, h, :], vnew[:, h, :], start=False, stop=True)
                            return last
                        S.op("pe", mmO, reads=[k_qdT, k_Sb, k_QKT, k_vnew], writes=kpO)
                        ot, kot = T_.orr.next()
                        S.op("act", lambda e: e.activation(out=ot, in_=hm128(pO), func=AF.Copy), reads=kpO, writes=[kot])
                        Odst = Od0 if dr == 0 else Od1
                        S.dma("sp", Odst[g0 + c0:g0 + c0 + 64, h0 * 128:(h0 + HH) * 128], ot.rearrange("p h d -> p (h d)"), reads=[kot], writes=["Od%d_%d" % (dr, hh)])
                    pS, kpS = ps_multi(2)

                    def mmS(e):
                        last = None
                        for h in range(HH):
                            last = e.matmul(pS[:, h * 128:(h + 1) * 128], kdec[:, h, :], vnew[:, h, :], start=True, stop=True)
                        return last
                    S.op("pe", mmS, reads=[k_kdec, k_vnew], writes=kpS)
                    S.op("dve", lambda e: e.tensor_tensor(out=Sst, in0=Sst, in1=egl.unsqueeze(2).broadcast_to([128, HH, 128]), op=ALU.mult), reads=[k_S, k_egl], writes=[k_S])
                    S.op("dve", lambda e: e.tensor_tensor(out=Sst, in0=Sst, in1=pS.rearrange("p (h d) -> p h d", h=HH), op=ALU.add), reads=kpS + [k_S], writes=[k_S])
                    S.op("act", lambda e: e.activation(out=Sb, in_=Sst, func=AF.Copy), reads=[k_S], writes=[k_Sb])
                    yield "done"

                def advance(gens, until):
                    live = list(gens)
                    while live:
                        for g in list(live):
                            try:
                                v = next(g)
                            except StopIteration:
                                live.remove(g); continue
                            if v == until:
                                live.remove(g)

                pending = None
                nchunk = 0
                for gi in groups:
                    g0 = gi * 256
                    need_o = gi < NOWN // 256
                    qg, kqg = qr.next(); kg, kkg = kr.next(); vg, kvg = vr.next(); scg, kscg = scr.next()
                    if need_o:
                        S.dma("sp", qg, QTv[:, :, g0:g0 + 256], reads=["qT"], writes=[kqg])
                    S.dma("sp", kg, KTv[:, :, g0:g0 + 256], reads=["kT"], writes=[kkg])
                    S.dma("sp", vg, VTv[:, :, g0:g0 + 256], reads=["gvT"], writes=[kvg])
                    S.dma("sp", scg, SCT[:, g0:g0 + 256], reads=["SCT"], writes=[kscg])
                    for ci in (range(4) if dr == 0 else range(3, -1, -1)):
                        gens = [chunk(hh, gi, ci, need_o, g0, qg, kqg, kg, kkg, vg, kvg, scg, kscg, nchunk % 2) for hh in range(2)]
                        advance(gens, "scan")
                        if pending is not None:
                            advance(pending, "done")
                        pending = gens
                        nchunk += 1
                advance(pending, "done")

            for dr_ in range(2):
                run_dir(dr_)

        if "D" in phases:
            phase_d()


        def phase_e():
            CG = 512
            NG = D // CG

            def load_consts(names):
                out = {}
                for nm, src, shp in names:
                    t, k = A.alloc(shp, BF16, nm)
                    S.dma("poolq", t, src, writes=[k])
                    out[nm] = (t, k)
                return out

            S.barrier(); A.reset(PERSIST)
            zf, k_zf = A.alloc([33, N_], F32, "zf")
            fw1, k_fw1 = A.alloc([33, 64], F32, "fw1"); fw2, k_fw2 = A.alloc([64, 64], F32, "fw2"); fw3, k_fw3 = A.alloc([64, 64], F32, "fw3")
            fsm, k_fsm = A.alloc([64, 8], F32, "fsm")
            fout, k_fout = A.alloc([64, 2 * D], F32, "fout")
            dl, k_dl = A.alloc([128, D], F32, "deltas")
            tau, k_tau = A.alloc([128, 48], F32, "tau")
            hr = [A.alloc([64, 512], F32, "hmlp%d" % i) for i in range(3)]
            decr = Ring(A, 2, [128, 512], F32, "dec")
            fm, k_fm = A.alloc([64, 512], F32, "fm")
            hcr = Ring(A, 3, [128, 512], BF16, "hc")
            for (t, k, src) in ((zf, k_zf, zf_d), (fw1, k_fw1, fw1_d), (fw2, k_fw2, fw2_d), (fw3, k_fw3, fw3_d),
                                (fsm[:, 0:4], k_fsm, fsm_d), (fout, k_fout, fout_d), (dl, k_dl, deltas_d), (tau, k_tau, tau_d)):
                S.dma("sp", t, src, writes=[k])
            for i in range(3):
                S.op("dve", lambda e, i=i: e.tensor_tensor(out=fsm[:, 4 + i:5 + i], in0=fsm[:, i:i + 1], in1=fsm[:, 3:4], op=ALU.mult), reads=[k_fsm], writes=[k_fsm])
            for jb in range(N_ // 512):
                j0 = jb * 512
                prev, kprev = zf[:, j0:j0 + 512], k_zf
                for li, (w, kw) in enumerate(((fw1, k_fw1), (fw2, k_fw2), (fw3, k_fw3))):
                    ps, kp = ps_next()
                    S.op("pe", lambda e, ps=ps, w=w, prev=prev: e.matmul(ps[0:64, :], w, prev, start=True, stop=True), reads=[kw, kprev], writes=[kp])
                    ht, kht = hr[li]
                    S.op("act", lambda e, ps=ps, ht=ht, li=li: e.activation(out=ht, in_=ps[0:64, :], func=AF.Identity, scale=fsm[:, 3:4], bias=fsm[:, 4 + li:5 + li]),
                         reads=[kp, k_fsm], writes=[kht])
                    for rep in range(3):
                        S.op("dve", lambda e, ht=ht: e.tensor_scalar(out=fm, in0=ht, scalar1=math.pi, scalar2=-2.0 * math.pi, op0=ALU.is_gt, op1=ALU.mult), reads=[kht], writes=[k_fm])
                        S.op("dve", lambda e, ht=ht: e.tensor_tensor(out=ht, in0=ht, in1=fm, op=ALU.add), reads=[kht, k_fm], writes=[kht])
                        S.op("dve", lambda e, ht=ht: e.tensor_scalar(out=fm, in0=ht, scalar1=-math.pi, scalar2=2.0 * math.pi, op0=ALU.is_lt, op1=ALU.mult), reads=[kht], writes=[k_fm])
                        S.op("dve", lambda e, ht=ht: e.tensor_tensor(out=ht, in0=ht, in1=fm, op=ALU.add), reads=[kht, k_fm], writes=[kht])
                    S.op("act", lambda e, ht=ht: e.activation(out=ht, in_=ht, func=AF.Sin), reads=[kht], writes=[kht])
                    prev, kprev = ht, kht
                for rt in range(4):
                    jt = jb * 4 + rt
                    fcol = 0 if jt < NOWN // 128 else D
                    for cc in range(4):
                        ps, kp = ps_next()
                        S.op("pe", lambda e, ps=ps, prev=prev, rt=rt, cc=cc, fcol=fcol: e.matmul(
                            ps[:, :], prev[:, rt * 128:(rt + 1) * 128], fout[:, fcol + cc * 512:fcol + (cc + 1) * 512], start=True, stop=True),
                            reads=[kprev, k_fout], writes=[kp])
                        dec, kdec = decr.next()
                        S.op("act", lambda e, dec=dec, cc=cc, jt=jt: e.activation(out=dec, in_=dl[:, cc * 512:(cc + 1) * 512], func=AF.Exp, scale=tau[:, jt:jt + 1]),
                             reads=[k_dl, k_tau], writes=[kdec])
                        hc, khc = hcr.next()
                        S.op("dve", lambda e, hc=hc, ps=ps, dec=dec: e.tensor_tensor(out=hc, in0=ps[:, :], in1=dec, op=ALU.mult), reads=[kp, kdec], writes=[khc])
                        S.dma("poolq", Hc[jt * 128:(jt + 1) * 128, cc * 512:(cc + 1) * 512], hc, reads=[khc], writes=["Hc"])

            def stage1(src, nm_src, is_filter):
                S.barrier(); A.reset(PERSIST)
                W1, k_W1 = A.alloc([128, 48, 2, 128], BF16, "W1")
                S.dma("poolq", W1.rearrange("p a b c -> p (a b c)"), W1_d, writes=[k_W1])
                ztr = Ring(A, 2, [128, 48, CG], BF16, "zt")
                aor = Ring(A, 4, [128, 2, CG], BF16, "ao")
                K1 = 128 if is_filter else 86
                if not is_filter:
                    for (zt, kz) in ztr.items:
                        S.op("pool", lambda e, zt=zt: e.memset(zt[64:128], 0.0), writes=[kz])
                for cg in range(NG):
                    c0 = cg * CG
                    zt, kz = ztr.next()
                    if is_filter:
                        S.dma("sp", zt, src[:, c0:c0 + CG].rearrange("(a b) c -> a b c", b=48), reads=[nm_src], writes=[kz])
                    else:
                        S.dma("sp", zt[0:85], src[0:4080, c0:c0 + CG].rearrange("(a b) c -> a b c", b=48), reads=[nm_src], writes=[kz])
                        S.dma("sp", zt[85:86, 0:16, :], src[4080:4096, c0:c0 + CG].rearrange("(a b) c -> a b c", a=1), reads=[nm_src], writes=[kz])
                    for t2 in range(48):
                        pp, kpp = ps_multi(2)

                        def mm(e, pp=pp, zt=zt, t2=t2):
                            e.matmul(pp[:, 0:512], W1[0:K1, t2, 0, :], zt[0:K1, t2, :], start=True, stop=True)
                            return e.matmul(pp[:, 512:1024], W1[0:K1, t2, 1, :], zt[0:K1, t2, :], start=True, stop=True)
                        S.op("pe", mm, reads=[k_W1, kz], writes=kpp)
                        ao, kao = aor.next()
                        eng = "act" if t2 % 2 == 0 else "dve"
                        if eng == "act":
                            S.op("act", lambda e, ao=ao, pp=pp: e.activation(out=ao, in_=pp.rearrange("p (r c) -> p r c", r=2), func=AF.Copy), reads=kpp, writes=[kao])
                        else:
                            S.op("dve", lambda e, ao=ao, pp=pp: e.tensor_copy(out=ao, in_=pp.rearrange("p (r c) -> p r c", r=2)), reads=kpp, writes=[kao])
                        S.dma("poolq", Ad[:, :, c0:c0 + CG].rearrange("f (r t) c -> f r t c", r=2)[:, :, t2, :], ao, reads=[kao], writes=["Ad"])

            def stage2_filter():
                S.barrier(); A.reset(PERSIST)
                cs = load_consts([("W2a", W2a_d, [96, 96]), ("W2b", W2b_d, [96, 96])])
                atr = Ring(A, 2, [96, 16, CG], BF16, "at")
                kor = Ring(A, 2, [96, 2, 16, CG], BF16, "ko")
                for cg in range(NG):
                    c0 = cg * CG
                    for fb in range(8):
                        at, kat = atr.next()
                        S.dma("sp", at, Ad[fb * 16:(fb + 1) * 16, :, c0:c0 + CG].rearrange("f rt c -> rt f c"), reads=["Ad"], writes=[kat])
                        ko, kko = kor.next()
                        for fi in range(16):
                            pp, kpp = ps_multi(2)

                            def mm(e, pp=pp, at=at, fi=fi):
                                e.matmul(pp[0:96, 0:512], cs["W2a"][0], at[:, fi, :], start=True, stop=True)
                                return e.matmul(pp[0:96, 512:1024], cs["W2b"][0], at[:, fi, :], start=True, stop=True)
                            S.op("pe", mm, reads=[cs["W2a"][1], cs["W2b"][1], kat], writes=kpp)
                            eng = "act" if fi % 2 == 0 else "dve"
                            if eng == "act":
                                S.op("act", lambda e, ko=ko, pp=pp, fi=fi: e.activation(out=ko[:, :, fi, :], in_=pp[0:96].rearrange("p (r c) -> p r c", r=2), func=AF.Copy), reads=kpp, writes=[kko])
                            else:
                                S.op("dve", lambda e, ko=ko, pp=pp, fi=fi: e.tensor_copy(out=ko[:, :, fi, :], in_=pp[0:96].rearrange("p (r c) -> p r c", r=2)), reads=kpp, writes=[kko])
                        for ab in range(2):
                            S.dma("poolq", Kd[ab, :, fb * 16:(fb + 1) * 16, c0:c0 + CG], ko[:, ab, :, :], reads=[kko], writes=["Kd"])

            def stage2_data():
                S.barrier(); A.reset(PERSIST)
                cs = load_consts([("W2", W2_d, [96, 96]), ("L1", L1_d, [96, 96]), ("L2", L2_d, [96, 96])])
                atr = Ring(A, 2, [96, 16, CG], BF16, "at")
                kar = Ring(A, 2, [96, 16, CG], BF16, "ka"); kbr = Ring(A, 2, [96, 16, CG], BF16, "kb")
                btr = Ring(A, 2, [96, 16, CG], BF16, "bt")
                p1r = Ring(A, 3, [96, CG], BF16, "p1"); p2r = Ring(A, 3, [96, CG], BF16, "p2")
                for cg in range(NG):
                    c0 = cg * CG
                    for fb in range(8):
                        at, kat = atr.next(); ka, kka = kar.next(); kb, kkb = kbr.next(); bt, kbt = btr.next()
                        S.dma("sp", at, Ad[fb * 16:(fb + 1) * 16, :, c0:c0 + CG].rearrange("f rt c -> rt f c"), reads=["Ad"], writes=[kat])
                        S.dma("sp", ka, Kd[0, :, fb * 16:(fb + 1) * 16, c0:c0 + CG], reads=["Kd"], writes=[kka])
                        S.dma("sp", kb, Kd[1, :, fb * 16:(fb + 1) * 16, c0:c0 + CG], reads=["Kd"], writes=[kkb])
                        pend = None
                        for fi in range(16):
                            ps, kp = ps_next()
                            S.op("pe", lambda e, ps=ps, at=at, fi=fi: e.matmul(ps[0:96, :], cs["W2"][0], at[:, fi, :], start=True, stop=True), reads=[cs["W2"][1], kat], writes=[kp])
                            p1, kp1 = p1r.next(); p2, kp2 = p2r.next()
                            S.op("dve", lambda e, ps=ps, p1=p1, ka=ka, fi=fi: e.tensor_tensor(out=p1, in0=ps[0:96, :], in1=ka[:, fi, :], op=ALU.mult), reads=[kp, kka], writes=[kp1])
                            S.op("dve", lambda e, ps=ps, p2=p2, kb=kb, fi=fi: e.tensor_tensor(out=p2, in0=ps[0:96, :], in1=kb[:, fi, :], op=ALU.mult), reads=[kp, kkb], writes=[kp2])

                            def part2(p1=p1, kp1=kp1, p2=p2, kp2=kp2, fi=fi, bt=bt, kbt=kbt):
                                ps2, kps2 = ps_next()

                                def mm(e):
                                    e.matmul(ps2[0:96, :], cs["L1"][0], p1, start=True, stop=False)
                                    return e.matmul(ps2[0:96, :], cs["L2"][0], p2, start=False, stop=True)
                                S.op("pe", mm, reads=[cs["L1"][1], cs["L2"][1], kp1, kp2], writes=[kps2])
                                S.op("act", lambda e: e.activation(out=bt[:, fi, :], in_=ps2[0:96, :], func=AF.Copy), reads=[kps2], writes=[kbt])
                            if pend is not None:
                                pend()
                            pend = part2
                        pend()
                        S.dma("poolq", Bd[fb * 16:(fb + 1) * 16, :, c0:c0 + CG].rearrange("f rt c -> rt f c"), bt, reads=[kbt], writes=["Bd"])

            def stage1_inv():
                S.barrier(); A.reset(PERSIST)
                Vt, k_V = A.alloc([128, 48, 2, 64], BF16, "V")
                S.dma("poolq", Vt.rearrange("p a b c -> p (a b c)"), V_d, writes=[k_V])
                b2r = Ring(A, 2, [128, 2, 8, CG], BF16, "b2")
                yor = Ring(A, 2, [64, 8, CG], BF16, "yo")
                Ydv = Yd[0:43 * 48, :].rearrange("(a b) c -> a b c", b=48)
                for cg in range(NG):
                    c0 = cg * CG
                    for tb in range(6):
                        b2, kb2 = b2r.next(); yo, kyo = yor.next()
                        for ri in range(2):
                            S.dma("sp", b2[:, ri, :, :], Bd[:, ri * 48 + tb * 8:ri * 48 + tb * 8 + 8, c0:c0 + CG], reads=["Bd"], writes=[kb2])
                        for ti in range(8):
                            t2 = tb * 8 + ti
                            ps, kp = ps_next()

                            def mm(e, ps=ps, b2=b2, t2=t2, ti=ti):
                                e.matmul(ps[0:43, :], Vt[:, t2, 0, 0:43], b2[:, 0, ti, :], start=True, stop=False)
                                return e.matmul(ps[0:43, :], Vt[:, t2, 1, 0:43], b2[:, 1, ti, :], start=False, stop=True)
                            S.op("pe", mm, reads=[k_V, kb2], writes=[kp])
                            if ti % 2 == 0:
                                S.op("act", lambda e, ps=ps, yo=yo, ti=ti: e.activation(out=yo[0:43, ti, :], in_=ps[0:43, :], func=AF.Copy), reads=[kp], writes=[kyo])
                            else:
                                S.op("dve", lambda e, ps=ps, yo=yo, ti=ti: e.tensor_copy(out=yo[0:43, ti, :], in_=ps[0:43, :]), reads=[kp], writes=[kyo])
                        S.dma("poolq", Ydv[:, tb * 8:(tb + 1) * 8, c0:c0 + CG], yo[0:43], reads=[kyo], writes=["Yd"])

            phase_e0 = None
            stage1(Hc, "Hc", True)
            stage2_filter()
            stage1(Zt, "Zt", False)
            stage2_data()
            stage1_inv()

        if "E" in phases:
            phase_e()

        def phase_fgh():
            S.barrier(); A.reset(PERSIST)
            R1, k_R1 = A.alloc([128, 24576], BF16, "R1")
            GTt = R1[:, 0:16384].rearrange("p (a b) -> p a b", a=32)
            ZGt = R1[:, 16384:24576].rearrange("p (a b) -> p a b", a=16)
            actT = R1[:, 0:22528].rearrange("p (a b) -> p a b", a=44)
            hy, k_hy = A.alloc([128, 16, 512], BF16, "hy")
            gn, k_gn = A.alloc([128, 16, 512], BF16, "gn")
            mixed, k_mx = A.alloc([128, 16, 512], BF16, "mixed")
            x1T, k_x1 = A.alloc([128, 16, 512], F32, "x1T")
            rbc, k_rbc = A.alloc([128, 512], F32, "rbc")
            wr = Ring(A, 3, [128, 16, 128], BF16, "w16")
            wdr = Ring(A, 2, [128, 44, 128], BF16, "w44")
            tfr = Ring(A, 2, [128, 512], F32, "tf")
            tbr = Ring(A, 3, [128, 512], BF16, "tb")
            xtr = Ring(A, 3, [128, D], F32, "xt")
            onr = Ring(A, 1, [128, D], BF16, "on")
            ytr = Ring(A, 2, [128, 4, 128], BF16, "yt")
            st, k_st = A.alloc([128, 64], F32, "st")
            hf, k_hf = hy, k_hy
            sqT, k_sqT = gn, k_gn
            ga_a = mod[:, 32:48, 0]; sh_f = mod[:, 48:64, 0]; ga_f = mod[:, 80:96, 0]
            whv, wgv, wov, wuv, wdv = w_hy_out_d, w_gdn_out_d, w_o_d, w_up_d, w_down_d

            def proj16(wv, c0, rhs, krhs, nkc=16, ring=None):
                wt, kw = (ring or wr).next()
                S.dma("poolq", wt.rearrange("p a b -> p (a b)"), wv[c0 // 128], writes=[kw])
                ps, kp = ps_next()

                def mm(e):
                    last = None
                    for kc in range(nkc):
                        last = e.matmul(ps[:, :], wt[:, kc, :], rhs[:, kc, :], start=(kc == 0), stop=(kc == nkc - 1))
                    return last
                S.op("pe", mm, reads=[kw, krhs], writes=[kp])
                return ps, kp

            def rms_bc(srcT, ksrc):
                S.op("act", lambda e: e.activation(out=sqT, in_=srcT, func=AF.Square), reads=[ksrc], writes=[k_sqT])
                ps, kp = ps_next()

                def mm(e):
                    last = None
                    for fc in range(16):
                        last = e.matmul(ps[:, :], ones_b, sqT[:, fc, :], start=(fc == 0), stop=(fc == 15))
                    return last
                S.op("pe", mm, reads=[k_sqT, k_ones], writes=[kp])
                S.op("dve", lambda e: e.tensor_scalar(out=rbc, in0=ps[:, :], scalar1=1.0 / D, scalar2=EPS, op0=ALU.mult, op1=ALU.add), reads=[kp], writes=[k_rbc])
                S.op("dve", lambda e: e.reciprocal(out=rbc, in_=rbc), reads=[k_rbc], writes=[k_rbc])
                S.op("act", lambda e: e.activation(out=rbc, in_=rbc, func=AF.Sqrt), reads=[k_rbc], writes=[k_rbc])

            for tb in range(NOWN // 512):
                t0 = tb * 512
                S.dma("sp", GTt, GT[:, t0:t0 + 512].rearrange("(a p) t -> p a t", p=128), writes=[k_R1])
                S.dma("sp", ZGt, ZGT[:, t0:t0 + 512].rearrange("(a p) t -> p a t", p=128), writes=[k_R1])
                for cc in range(16):
                    x0t, kx0 = tbr.next(); zt, kz = tbr.next(); yt, kyt = ytr.next()
                    S.dma("sp", x0t, X0T[cc * 128:(cc + 1) * 128, t0:t0 + 512], reads=["X0T"], writes=[kx0])
                    S.dma("sp", zt, ZT[cc * 128:(cc + 1) * 128, t0:t0 + 512], reads=["ZT"], writes=[kz])
                    S.dma("sp", yt, Yd[t0:t0 + 512, cc * 128:(cc + 1) * 128].rearrange("(a p) c -> p a c", p=128), reads=["Yd"], writes=[kyt])
                    ps, kp = ps_next(); psv = ps.bitcast(BF16)

                    def trY(e, yt=yt, psv=psv):
                        last = None
                        for a in range(4):
                            last = e.transpose(psv[:, a * 128:(a + 1) * 128], yt[:, a, :], ident_b)
                        return last
                    S.op("pe", trY, reads=[kyt, k_ident], writes=[kp])
                    tf, ktf = tfr.next()
                    S.op("dve", lambda e, tf=tf, zt=zt, psv=psv, cc=cc: e.scalar_tensor_tensor(
                        out=tf, in0=zt, scalar=hybT[:, cc:cc + 1], in1=psv[:, 0:512], op0=ALU.mult, op1=ALU.add), reads=[kz, kp, k_hyb], writes=[ktf])
                    S.op("dve", lambda e, tf=tf, x0t=x0t, cc=cc: e.tensor_tensor(out=hy[:, cc, :], in0=tf, in1=x0t, op=ALU.mult), reads=[ktf, kx0], writes=[k_hy])
                for tt in range(4):
                    ot, kot = xtr.next(); on, kon = onr.next()
                    tq, ktq = xtr.next()
                    S.dma("sp", ot, Od0[t0 + tt * 128:t0 + (tt + 1) * 128, :], reads=["Od0_0", "Od0_1"], writes=[kot])
                    S.dma("sp", tq, Od1[t0 + tt * 128:t0 + (tt + 1) * 128, :], reads=["Od1_0", "Od1_1"], writes=[ktq])
                    S.op("dve", lambda e, tq=tq, ot=ot: e.tensor_tensor(out=ot, in0=ot, in1=tq, op=ALU.add), reads=[kot, ktq], writes=[kot])
                    S.op("act", lambda e, tq=tq, ot=ot: e.activation(out=tq, in_=ot, func=AF.Square), reads=[kot], writes=[ktq])
                    S.op("dve", lambda e, tq=tq: e.reduce_sum(out=st[:, 0:16], in_=tq.rearrange("p (h e) -> p h e", h=16), axis=AX.X), reads=[ktq], writes=[k_st])
                    S.op("dve", lambda e: e.tensor_scalar(out=st[:, 16:32], in0=st[:, 0:16], scalar1=1.0 / 128, scalar2=EPS, op0=ALU.mult, op1=ALU.add), reads=[k_st], writes=[k_st])
                    S.op("dve", lambda e: e.reciprocal(out=st[:, 32:48], in_=st[:, 16:32]), reads=[k_st], writes=[k_st])
                    S.op("act", lambda e: e.activation(out=st[:, 48:64], in_=st[:, 32:48], func=AF.Sqrt), reads=[k_st], writes=[k_st])
                    for h in range(16):
                        eng = "act" if h % 2 == 0 else "dve"
                        if eng == "act":
                            S.op("act", lambda e, on=on, ot=ot, h=h: e.activation(out=on[:, h * 128:(h + 1) * 128], in_=ot[:, h * 128:(h + 1) * 128],
                                                                            func=AF.Copy, scale=st[:, 48 + h:49 + h]), reads=[kot, k_st], writes=[kon])
                        else:
                            S.op("dve", lambda e, on=on, ot=ot, h=h: e.tensor_scalar(out=on[:, h * 128:(h + 1) * 128], in0=ot[:, h * 128:(h + 1) * 128],
                                                                               scalar1=st[:, 48 + h:49 + h], scalar2=None, op0=ALU.mult), reads=[kot, k_st], writes=[kon])
                    for g in range(4):
                        ps, kp = ps_next(); psv = ps.bitcast(BF16)

                        def trO(e, on=on, psv=psv, g=g):
                            last = None
                            for j in range(4):
                                h = g * 4 + j
                                last = e.transpose(psv[:, j * 128:(j + 1) * 128], on[:, h * 128:(h + 1) * 128], ident_b)
                            return last
                        S.op("pe", trO, reads=[kon, k_ident], writes=[kp])
                        for j in range(4):
                            h = g * 4 + j
                            S.op("dve", lambda e, psv=psv, j=j, h=h, tt=tt: e.scalar_tensor_tensor(
                                out=gn[:, h, tt * 128:(tt + 1) * 128], in0=psv[:, j * 128:(j + 1) * 128], scalar=gnormT[:, 0:1],
                                in1=ZGt[:, h, tt * 128:(tt + 1) * 128], op0=ALU.mult, op1=ALU.mult), reads=[kp, k_gnorm, k_R1], writes=[k_gn])
                for m in range(16):
                    ps, kp = proj16(whv, m * 128, hy, k_hy)
                    tf, ktf = tfr.next()
                    S.op("dve", lambda e, tf=tf, ps=ps, m=m: e.tensor_tensor(out=tf, in0=ps[:, :], in1=GTt[:, m, :], op=ALU.mult), reads=[kp, k_R1], writes=[ktf])
                    ps2, kp2 = proj16(wgv, m * 128, gn, k_gn)
                    tf2, ktf2 = tfr.next()
                    S.op("dve", lambda e, tf2=tf2, ps2=ps2, m=m: e.tensor_tensor(out=tf2, in0=ps2[:, :], in1=GTt[:, 16 + m, :], op=ALU.mult), reads=[kp2, k_R1], writes=[ktf2])
                    S.op("dve", lambda e, tf=tf, tf2=tf2, m=m: e.tensor_tensor(out=mixed[:, m, :], in0=tf, in1=tf2, op=ALU.add), reads=[ktf, ktf2], writes=[k_mx])
                for tt in range(4):
                    xt, kx = xtr.next()
                    S.dma("sp", xt, x_d[t0 + tt * 128:t0 + (tt + 1) * 128, :], writes=[kx])
                    for g in range(4):
                        ps, kp = ps_next()

                        def trX(e, xt=xt, ps=ps, g=g):
                            last = None
                            for j in range(4):
                                fc = g * 4 + j
                                last = e.transpose(ps[:, j * 128:(j + 1) * 128], xt[:, fc * 128:(fc + 1) * 128], ident_f)
                            return last
                        S.op("pe", trX, reads=[kx, k_identf], writes=[kp])
                        S.op("act", lambda e, ps=ps, g=g, tt=tt: e.activation(
                            out=x1T[:, g * 4:(g + 1) * 4, tt * 128:(tt + 1) * 128], in_=ps[:, :].rearrange("p (a b) -> p a b", a=4), func=AF.Copy),
                            reads=[kp], writes=[k_x1])
                for m in range(16):
                    ps, kp = proj16(wov, m * 128, mixed, k_mx)
                    S.op("dve", lambda e, ps=ps, m=m: e.scalar_tensor_tensor(out=x1T[:, m, :], in0=ps[:, :], scalar=ga_a[:, m:m + 1], in1=x1T[:, m, :],
                                                                           op0=ALU.mult, op1=ALU.add), reads=[kp, k_mod, k_x1], writes=[k_x1])
                rms_bc(x1T, k_x1)
                for fc in range(16):
                    tf, ktf = tfr.next()
                    S.op("dve", lambda e, tf=tf, fc=fc: e.scalar_tensor_tensor(out=tf, in0=x1T[:, fc, :], scalar=scale_f[:, fc:fc + 1], in1=rbc,
                                                                             op0=ALU.mult, op1=ALU.mult), reads=[k_x1, k_scf, k_rbc], writes=[ktf])
                    S.op("act", lambda e, tf=tf, fc=fc: e.activation(out=hf[:, fc, :], in_=tf, func=AF.Identity, bias=sh_f[:, fc:fc + 1]),
                         reads=[ktf, k_mod], writes=[k_hf])
                for j in range(44):
                    psg, kpg = proj16(wuv, j * 128, hf, k_hf)
                    psu, kpu = proj16(wuv, D_FF + j * 128, hf, k_hf)
                    tf, ktf = tfr.next()
                    S.op("act", lambda e, tf=tf, psg=psg: e.activation(out=tf, in_=psg[:, :], func=AF.Silu), reads=[kpg], writes=[ktf])
                    S.op("dve", lambda e, tf=tf, psu=psu, j=j: e.tensor_tensor(out=actT[:, j, :], in0=tf, in1=psu[:, :], op=ALU.mult), reads=[ktf, kpu], writes=[k_R1])
                for m in range(16):
                    ps, kp = proj16(wdv, m * 128, actT, k_R1, nkc=44, ring=wdr)
                    S.op("dve", lambda e, ps=ps, m=m: e.scalar_tensor_tensor(out=x1T[:, m, :], in0=ps[:, :], scalar=ga_f[:, m:m + 1], in1=x1T[:, m, :],
                                                                           op0=ALU.mult, op1=ALU.add), reads=[kp, k_mod, k_x1], writes=[k_x1])
                rms_bc(x1T, k_x1)
                for fc in range(16):
                    S.op("dve", lambda e, fc=fc: e.scalar_tensor_tensor(out=x1T[:, fc, :], in0=x1T[:, fc, :], scalar=nfinT[:, fc:fc + 1], in1=rbc,
                                                                      op0=ALU.mult, op1=ALU.mult), reads=[k_x1, k_nfin, k_rbc], writes=[k_x1])
                for tt in range(4):
                    xo, kxo = xtr.next()
                    for g in range(4):
                        ps, kp = ps_next()

                        def trB(e, ps=ps, g=g, tt=tt):
                            last = None
                            for j in range(4):
                                fc = g * 4 + j
                                last = e.transpose(ps[:, j * 128:(j + 1) * 128], x1T[:, fc, tt * 128:(tt + 1) * 128], ident_f)
                            return last
                        S.op("pe", trB, reads=[k_x1, k_identf], writes=[kp])
                        S.op("act", lambda e, ps=ps, xo=xo, g=g: e.activation(out=xo[:, g * 512:(g + 1) * 512], in_=ps[:, :], func=AF.Copy), reads=[kp], writes=[kxo])
                    S.final_tokens.append(S.dma("sp", out_d[t0 + tt * 128:t0 + (tt + 1) * 128, :], xo, reads=[kxo], writes=["out"]))

        if "F" in phases:
            phase_fgh()
        S.emit()
    return nc


def _fm(v, n):
    return np.ascontiguousarray(np.asarray(v, np.float32).reshape(n, 128).T)


def _hyena_consts():
    n = L
    j = np.arange(N_)
    idx = np.where(j < NOWN, j, N_ - j).astype(np.float64)
    idx[NOWN] = 0
    tt = idx / (n - 1)
    bands = 16
    w = 2.0 * np.pi * idx / n
    f = np.linspace(1e-4, bands - 1, bands)
    zf = np.concatenate([tt[None, :], np.cos(f[:, None] * w[None, :]), -np.sin(f[:, None] * w[None, :])], axis=0)
    tau = -tt.copy(); tau[NOWN] = -30.0
    deltas = np.abs(np.linspace(math.log(1e-2) / 1.5, math.log(1e-2) / 0.3, D))
    t1 = np.arange(128)[:, None, None, None]; t2 = np.arange(48)[None, :, None, None]; f1 = np.arange(128)[None, None, None, :]
    th = 2 * np.pi * (t1 * f1 / 128.0 + t2 * f1 / float(N_))
    W1 = np.concatenate([np.cos(th), -np.sin(th)], axis=2)
    a = np.arange(48)
    th2 = 2 * np.pi * np.outer(a, a) / 48.0
    c2, s2 = np.cos(th2), np.sin(th2)
    W2 = np.block([[c2, -s2], [s2, c2]])
    W2a = np.block([[c2, c2], [s2, s2]])
    W2b = np.block([[-s2, -s2], [c2, c2]])
    L1 = np.block([[c2, s2], [-s2, c2]])
    L2 = np.block([[-s2, c2], [-c2, -s2]])
    f1v = np.arange(128)[:, None, None, None]; t2v = np.arange(48)[None, :, None, None]; t1v = np.arange(64)[None, None, None, :]
    ph = 2 * np.pi * f1v * (t2v / float(N_) + t1v / 128.0)
    V = np.concatenate([np.cos(ph), -np.sin(ph)], axis=2) / float(N_)
    f32 = lambda x: np.ascontiguousarray(x, dtype=np.float32)
    return dict(zf=f32(zf), tau=f32(tau.reshape(48, 128).T), deltas=f32(np.broadcast_to(deltas[None, :], (128, D))),
                W1=f32(W1.reshape(128, -1)), Vc=f32(V.reshape(128, -1)), W2=f32(W2), W2a=f32(W2a), W2b=f32(W2b), L1=f32(L1), L2=f32(L2))


def make_in_maps(inp):
    maps = []
    ident = np.eye(128, dtype=np.float32)
    tri = np.tril(np.ones((64, 64), np.float32))
    cmask = np.ascontiguousarray(np.stack([tri, np.tril(tri, -1), tri.T, np.triu(tri.T, 1)], axis=1))
    ii = np.arange(64)
    def same(b): return (ii[:, None] // b == ii[None, :] // b).astype(np.float32)
    cm2 = np.ascontiguousarray(np.stack([same(8), same(16) - same(8), same(32) - same(16), same(64) - same(32), -same(8)], axis=1))
    hyc_consts = _hyena_consts()
    w_in0 = inp["w_in"][0]
    cols = [SEG[t] + j * 128 for (t, j) in PLAN]
    w_inc_base = _chunk_major(w_in0, cols)
    shared = dict(
        w_hy_outc=_chunk_major(inp["w_hy_out"][0], [m * 128 for m in range(16)]),
        w_gdn_outc=_chunk_major(inp["w_gdn_out"][0], [m * 128 for m in range(16)]),
        w_oc=_chunk_major(inp["w_o"][0], [m * 128 for m in range(16)]),
        w_upc=_chunk_major(inp["w_up"][0], [j * 128 for j in range(88)]),
        w_downc=_chunk_major(inp["w_down"][0], [m * 128 for m in range(16)]),
    )
    w_inc_flip = None
    cache = {}

    def _w_inc_for(flip):
        if flip not in cache:
            if not flip:
                cache[flip] = w_inc_base
            else:
                wsc = w_in0[:, 14336:14400].reshape(D, 2, 2, 16)[:, :, ::-1, :].reshape(D, 64)
                arr = w_inc_base.copy()
                arr[PLAN_INDEX[("scal", 0)]] = _chunk_major(wsc, [0])[0]
                cache[flip] = arr
        return cache[flip]

    for core in range(8):
        b, half = core // 2, core % 2
        flip = half == 1
        x = inp["x"][b]; ctx = inp["ctx"][b]
        if flip:
            x = x[::-1]; ctx = ctx[::-1]
        cT = np.stack([_fm(inp["c"][b], 16), _fm(inp["c_ctx"], 16)], axis=-1)
        w_in = inp["w_in"][0]
        w_scal = w_in[:, 14336:14400]
        a_log = inp["gdn_a_log"][0]; dtb = inp["gdn_dt_bias"][0]
        hyc = inp["hy_conv"][0]; gdc = inp["gdn_conv"][0]
        if flip:
            w_scal = w_scal.reshape(D, 2, 2, 16)[:, :, ::-1, :].reshape(D, 64)
            a_log = a_log[::-1]; dtb = dtb[::-1]
            hyc = hyc[::-1]; gdc = gdc[::-1]
        scalp = np.zeros((64, 2), np.float32)
        scalp[32:64, 0] = a_log.reshape(32); scalp[32:64, 1] = dtb.reshape(32)
        hyconvT = np.ascontiguousarray(hyc.reshape(3, 48, 128).transpose(2, 1, 0))
        gdnconvT = np.ascontiguousarray(gdc.reshape(3, 48, 128).transpose(2, 1, 0))
        maps.append(dict(
            x=np.ascontiguousarray(x), ctx=np.ascontiguousarray(ctx), cT=np.ascontiguousarray(cT),
            w_ada=inp["w_ada"][0], b_adaT=_fm(inp["b_ada"][0], 96),
            nmixT=_fm(inp["norm_mix"][0], 16), nffnT=_fm(inp["norm_ffn"][0], 16),
            w_inc=_w_inc_for(flip),
            hyconvT=hyconvT, gdnconvT=gdnconvT, scalp=scalp, ident=ident, cmask=cmask, cm2=cm2,
            fw1=inp["hy_fw1"][0], fw2=inp["hy_fw2"][0], fw3=inp["hy_fw3"][0],
            fsm=np.ascontiguousarray(np.stack([inp["hy_fb1"][0], inp["hy_fb2"][0], inp["hy_fb3"][0], inp["hy_freq"][0]], axis=1)),
            fout=(np.ascontiguousarray(np.concatenate([inp["hy_fout"][0][:, D:], inp["hy_fout"][0][:, :D]], axis=1)) if flip else inp["hy_fout"][0]),
            **hyc_consts,
            **shared,
            hybT=_fm(inp["hy_bias"][0], 16), gnormT=_fm(inp["gdn_norm"][0], 1), nfinT=_fm(inp["norm_final"], 16),
        ))
    return maps


def kernel(**inputs):
    inp = {k: np.asarray(v) for k, v in inputs.items()}
    nc = build()
    maps = make_in_maps(inp)
    res = run_bass_kernel_spmd(nc, maps, core_ids=list(range(8)))
    out = np.empty((4, L, D), np.float32)
    for core in range(8):
        b, half = core // 2, core % 2
        o = res.results[core]["out"]
        if half == 0:
            out[b, :NOWN] = o
        else:
            out[b, NOWN:] = o[::-1]
    return out
```

```python
import contextlib
import math
import numpy as np
import ml_dtypes
import concourse.bass as bass
import concourse.mybir as mybir
from concourse.bass_utils import run_bass_kernel_spmd

F32 = mybir.dt.float32
BF16 = mybir.dt.bfloat16
AF = mybir.ActivationFunctionType
ALU = mybir.AluOpType
AX = mybir.AxisListType

D = 2048
L = 4096
NOWN = 2048
CTX = 256
TT = L + CTX
HEADS = 16
D_IN = 18496
D_FF = 5632
EPS = 1e-6
N_ = 6144

COMPUTE = ("pe", "act", "dve", "pool")
NSLOT = {"sp": 12, "poolq": 12}
QENG = {"sp": "sp", "poolq": "pool"}


class Op:
    __slots__ = ("fn", "waits", "sem", "inc")


class Sched:
    def __init__(self, nc):
        self.nc = nc
        self.streams = {e: [] for e in ("pe", "act", "dve", "pool", "sp")}
        self.cnt = {e: 0 for e in COMPUTE}
        self.slot_uses = {q: [0] * NSLOT[q] for q in NSLOT}
        self.dma_i = {q: 0 for q in NSLOT}
        self.last_w = {}
        self.reads = {}
        self.waited = {e: {} for e in self.streams}
        self.final_tokens = []

    def _need(self, stream, tok, waits):
        if tok is None:
            return
        semname, val, pstream, is_pe = tok
        if pstream == stream and is_pe:
            return
        if self.waited[stream].get(semname, 0) >= val:
            return
        waits[semname] = max(waits.get(semname, 0), val)

    def _deps(self, stream, reads, writes):
        waits = {}
        for k in reads:
            self._need(stream, self.last_w.get(k), waits)
        for k in writes:
            self._need(stream, self.last_w.get(k), waits)
            for t in self.reads.get(k, {}).values():
                self._need(stream, t, waits)
        for s, v in waits.items():
            self.waited[stream][s] = v
        return waits

    def _record(self, tok, reads, writes):
        for k in reads:
            d = self.reads.setdefault(k, {})
            o = d.get(tok[0])
            if o is None or o[1] < tok[1]:
                d[tok[0]] = tok
        for k in writes:
            self.last_w[k] = tok
            self.reads[k] = {}

    def op(self, eng, fn, reads=(), writes=()):
        waits = self._deps(eng, reads, writes)
        self.cnt[eng] += 1
        tok = (eng, self.cnt[eng], eng, eng == "pe")
        o = Op(); o.fn = fn; o.waits = sorted(waits.items()); o.sem = eng; o.inc = 1
        self.streams[eng].append(o)
        self._record(tok, reads, writes)
        return tok

    def dma(self, q, out, in_, reads=(), writes=()):
        reads = [k for k in reads if "#" in k]
        writes = [k for k in writes if "#" in k]
        stream = QENG[q]
        waits = self._deps(stream, reads, writes)
        i = self.dma_i[q]; self.dma_i[q] += 1
        slot = i % NSLOT[q]
        semname = "%s%d" % (q, slot)
        prev = self.slot_uses[q][slot] * 16
        if prev and self.waited[stream].get(semname, 0) < prev:
            waits[semname] = prev
            self.waited[stream][semname] = prev
        self.slot_uses[q][slot] += 1
        tok = (semname, self.slot_uses[q][slot] * 16, None, False)
        o = Op(); o.waits = sorted(waits.items()); o.sem = semname; o.inc = 16
        o.fn = (lambda e, out=out, in_=in_: e.dma_start(out=out, in_=in_))
        self.streams[stream].append(o)
        self._record(tok, reads, writes)
        return tok

    def barrier(self):
        allv = {e: self.cnt[e] for e in COMPUTE}
        for q in NSLOT:
            for s in range(NSLOT[q]):
                allv["%s%d" % (q, s)] = self.slot_uses[q][s] * 16
        for stream in self.streams:
            waits = {}
            for s, v in allv.items():
                if v and self.waited[stream].get(s, 0) < v and not (s == "pe" and stream == "pe"):
                    waits[s] = v
                    self.waited[stream][s] = v
            if waits:
                o = Op(); o.fn = None; o.waits = sorted(waits.items()); o.sem = None; o.inc = 0
                self.streams[stream].append(o)

    def emit(self):
        nc = self.nc
        names = list(COMPUTE) + ["%s%d" % (q, s) for q in NSLOT for s in range(NSLOT[q])]
        with contextlib.ExitStack() as st:
            sems = {n: st.enter_context(nc.semaphore("s_" + n)) for n in names}
            block = st.enter_context(nc.Block())

            def run(stream, final=False):
                def body(e):
                    for o in self.streams[stream]:
                        for s, v in o.waits:
                            e.wait_ge(sems[s], v)
                        if o.fn is not None:
                            o.fn(e).then_inc(sems[o.sem], o.inc)
                    if final:
                        for (s, v, _, _) in self.final_tokens:
                            e.wait_ge(sems[s], v)
                return body

            block.sync(run("sp", True))
            block.tensor(run("pe"))
            block.scalar(run("act"))
            block.vector(run("dve"))
            block.gpsimd(run("pool"))


class Arena:
    def __init__(self, big, total):
        self.big = big; self.total = total; self.off = 0; self.n = 0

    def reset(self, to=0):
        self.off = to

    def alloc(self, shape, dtype, name=None):
        n = int(np.prod(shape[1:]))
        size = n * (2 if dtype == F32 else 1)
        o = self.off
        self.off += (size + 15) // 16 * 16
        assert self.off <= self.total, ("SBUF arena overflow", self.off, self.total)
        v = self.big[:, o:o + size]
        if dtype == F32:
            v = v.bitcast(F32)
        if len(shape) == 3:
            v = v.rearrange("p (a b) -> p a b", a=shape[1])
        elif len(shape) == 4:
            v = v.rearrange("p (a b c) -> p a b c", a=shape[1], b=shape[2])
        self.n += 1
        key = "%s#%d" % (name or "t", self.n)
        return v[0:shape[0]], key


class Ring:
    def __init__(self, arena, n, shape, dtype, name):
        self.items = [arena.alloc(shape, dtype, name) for _ in range(n)]
        self.i = 0

    def next(self):
        it = self.items[self.i % len(self.items)]
        self.i += 1
        return it


def _make_plan():
    plan = [("scal", 0)]
    for j in range(16):
        plan += [("x1", j), ("hv", j)]
    for typ in ("q", "k", "gv", "x0", "zg"):
        plan += [(typ, j) for j in range(16)]
    plan += [("gate", j) for j in range(32)]
    return plan


SEG = dict(x0=0, x1=2048, hv=4096, q=6144, k=8192, gv=10240, zg=12288, scal=14336, gate=14400)


PLAN = _make_plan()
PLAN_INDEX = {k: i for i, k in enumerate(PLAN)}


def _chunk_major(w, cols):
    K = w.shape[0]
    out = np.empty((len(cols), 128, (K // 128) * 128), np.float32)
    for i, c0 in enumerate(cols):
        blk = w[:, c0:c0 + 128]
        if blk.shape[1] < 128:
            blk = np.concatenate([blk, np.zeros((K, 128 - blk.shape[1]), np.float32)], axis=1)
        out[i] = blk.reshape(K // 128, 128, 128).transpose(1, 0, 2).reshape(128, -1)
    return out


def build(upto=99, dbg=(), phases="DEF"):
    nc = bass.Bass("TRN2", target_bir_lowering=False)
    S = Sched(nc)
    ext = {}

    def inp(name, shape, dt=F32):
        ext[name] = nc.dram_tensor(name, list(shape), dt, kind="ExternalInput")
        return ext[name].ap()

    def scratch(name, shape, dt):
        kind = "ExternalOutput" if name in dbg else "Internal"
        t = nc.dram_tensor(name, list(shape), dt, kind=kind)
        return t.ap()

    x_d = inp("x", [L, D]); ctx_d = inp("ctx", [CTX, D]); cT_d = inp("cT", [128, 16, 2])
    w_ada_d = inp("w_ada", [D, 6 * D]); b_adaT_d = inp("b_adaT", [128, 96])
    nmixT_d = inp("nmixT", [128, 16]); nffnT_d = inp("nffnT", [128, 16])
    w_in_d = inp("w_inc", [145, 128, 16 * 128])
    hyconvT_d = inp("hyconvT", [128, 48, 3]); gdnconvT_d = inp("gdnconvT", [128, 48, 3])
    scalp_d = inp("scalp", [64, 2])
    ident_d = inp("ident", [128, 128])
    out_d = nc.dram_tensor("out", [NOWN, D], F32, kind="ExternalOutput").ap()
    w_hy_out_d = inp("w_hy_outc", [16, 128, 16 * 128]); w_gdn_out_d = inp("w_gdn_outc", [16, 128, 16 * 128]); w_o_d = inp("w_oc", [16, 128, 16 * 128])
    w_up_d = inp("w_upc", [88, 128, 16 * 128]); w_down_d = inp("w_downc", [16, 128, 44 * 128])
    hybT_d = inp("hybT", [128, 16]); gnormT_d = inp("gnormT", [128, 1]); nfinT_d = inp("nfinT", [128, 16])
    Yd = scratch("Yd", [NOWN + 64, D], BF16); Od0 = scratch("Od0", [NOWN, D], F32); Od1 = scratch("Od1", [NOWN, D], F32)
    cmask_d = inp("cmask", [64, 4, 64]); cm2_d = inp("cm2", [64, 5, 64])
    zf_d = inp("zf", [33, N_]); fw1_d = inp("fw1", [33, 64]); fw2_d = inp("fw2", [64, 64]); fw3_d = inp("fw3", [64, 64])
    fsm_d = inp("fsm", [64, 4]); fout_d = inp("fout", [64, 2 * D]); deltas_d = inp("deltas", [128, D]); tau_d = inp("tau", [128, 48])
    W1_d = inp("W1", [128, 48 * 2 * 128]); V_d = inp("Vc", [128, 48 * 2 * 64])
    W2_d = inp("W2", [96, 96]); W2a_d = inp("W2a", [96, 96]); W2b_d = inp("W2b", [96, 96]); L1_d = inp("L1", [96, 96]); L2_d = inp("L2", [96, 96])
    Hc = scratch("Hc", [N_, D], BF16); Ad = scratch("Ad", [128, 96, D], BF16); Bd = scratch("Bd", [128, 96, D], BF16)
    Kd = scratch("Kd", [2, 96, 128, D], BF16)

    X0T = scratch("X0T", [D, NOWN], BF16); ZT = scratch("ZT", [D, L], BF16); Zt = scratch("Zt", [L, D], BF16)
    QT = scratch("QT", [D, TT], BF16); KT = scratch("KT", [D, TT], BF16); VT = scratch("VT", [D, TT], BF16)
    ZGT = scratch("ZGT", [D, NOWN], BF16); SCT = scratch("SCT", [64, TT], F32); GT = scratch("GT", [2 * D, NOWN], BF16)
    MODd = scratch("MODd", [128, 96, 2], F32)

    with contextlib.ExitStack() as st:
        TOTAL = 106000
        big = st.enter_context(nc.sbuf_tensor("big", [128, TOTAL], BF16))
        PSALL = st.enter_context(nc.psum_tensor("psall", [128, 4096], F32))
        psb = [PSALL[:, i * 512:(i + 1) * 512] for i in range(8)]
        A = Arena(big, TOTAL)
        psi = [0]

        def ps_next():
            i = psi[0] % 8; psi[0] += 1
            return psb[i], "psb%d" % i

        def ps_multi(nb):
            i = ((psi[0] + nb - 1) // nb * nb) % 8
            psi[0] = i + nb
            return PSALL[:, i * 512:(i + nb) * 512], ["psb%d" % (i + j) for j in range(nb)]

        ident_f, k_identf = A.alloc([128, 128], F32, "identf")
        ident_b, k_ident = A.alloc([128, 128], BF16, "ident")
        ones_b, k_ones = A.alloc([128, 128], BF16, "ones")
        ones_f, k_onesf = A.alloc([128, 128], F32, "onesf")
        mod, k_mod = A.alloc([128, 96, 2], F32, "mod")
        nmixT, k_nmix = A.alloc([128, 16], F32, "nmix")
        nffnT, k_nffn = A.alloc([128, 16], F32, "nffn")
        scale_a, k_sca = A.alloc([128, 16], F32, "scale_a")
        scale_c, k_scc = A.alloc([128, 16], F32, "scale_c")
        scale_f, k_scf = A.alloc([128, 16], F32, "scale_f")
        hyconvT, k_hyc = A.alloc([128, 48, 3], F32, "hyc")
        gdnconvT, k_gdc = A.alloc([128, 48, 3], F32, "gdc")
        scalp, k_scalp = A.alloc([64, 2], F32, "scalp")
        nega, k_nega = A.alloc([64, 1], F32, "nega")
        negpi, k_negpi = A.alloc([128, 1], F32, "negpi")
        S.op("dve", lambda e: e.memset(negpi, -math.pi), writes=[k_negpi])
        hybT, k_hyb = A.alloc([128, 16], F32, "hyb")
        gnormT, k_gnorm = A.alloc([128, 1], F32, "gnorm")
        nfinT, k_nfin = A.alloc([128, 16], F32, "nfin")
        PERSIST = A.off
        S.dma("sp", hybT, hybT_d, writes=[k_hyb]); S.dma("sp", gnormT, gnormT_d, writes=[k_gnorm]); S.dma("sp", nfinT, nfinT_d, writes=[k_nfin])

        S.dma("sp", ident_f, ident_d, writes=[k_identf])
        S.op("dve", lambda e: e.tensor_copy(out=ident_b, in_=ident_f), reads=[k_identf], writes=[k_ident])
        S.op("dve", lambda e: e.memset(ones_b, 1.0), writes=[k_ones])
        S.op("dve", lambda e: e.memset(ones_f, 1.0), writes=[k_onesf])
        S.dma("sp", nmixT, nmixT_d, writes=[k_nmix]); S.dma("sp", nffnT, nffnT_d, writes=[k_nffn])
        S.dma("sp", hyconvT, hyconvT_d, writes=[k_hyc]); S.dma("sp", gdnconvT, gdnconvT_d, writes=[k_gdc])
        S.dma("sp", scalp, scalp_d, writes=[k_scalp])
        S.op("act", lambda e: e.activation(out=nega[32:64], in_=scalp[32:64, 0:1], func=AF.Exp), reads=[k_scalp], writes=[k_nega])
        S.op("dve", lambda e: e.tensor_scalar(out=nega[32:64], in0=nega[32:64], scalar1=-1.0, scalar2=None, op0=ALU.mult), reads=[k_nega], writes=[k_nega])

        cT, k_cT = A.alloc([128, 16, 2], F32, "cT")
        sT, k_sT = A.alloc([128, 16, 2], BF16, "sT")
        bT, k_bT = A.alloc([128, 96], F32, "bT")
        wr = Ring(A, 2, [128, 16, 1536], BF16, "wada")
        S.dma("sp", cT, cT_d, writes=[k_cT]); S.dma("sp", bT, b_adaT_d, writes=[k_bT])
        S.op("act", lambda e: e.activation(out=sT, in_=cT, func=AF.Silu), reads=[k_cT], writes=[k_sT])
        w_ada_v = w_ada_d.rearrange("(kc p) n -> p kc n", p=128)
        for blk in range(8):
            wt, kw = wr.next()
            S.dma("poolq", wt, w_ada_v[:, :, blk * 1536:(blk + 1) * 1536], writes=[kw])
            ps, kp = ps_next()

            def mmA(e, wt=wt, ps=ps):
                last = None
                for j in range(12):
                    for kc in range(16):
                        last = e.matmul(ps[:, 2 * j:2 * j + 2], wt[:, kc, j * 128:(j + 1) * 128], sT[:, kc, :],
                                        start=(kc == 0), stop=(kc == 15))
                return last
            S.op("pe", mmA, reads=[kw, k_sT], writes=[kp])
            S.op("dve", lambda e, ps=ps, blk=blk: e.tensor_copy(
                out=mod[:, blk * 12:(blk + 1) * 12, :], in_=ps[:, 0:24].rearrange("p (a b) -> p a b", b=2)),
                reads=[kp], writes=[k_mod])
        for j in range(2):
            S.op("dve", lambda e, j=j: e.tensor_tensor(out=mod[:, :, j], in0=mod[:, :, j], in1=bT, op=ALU.add),
                 reads=[k_mod, k_bT], writes=[k_mod])
        for (dst, kd, src, nrm, kn) in ((scale_a, k_sca, mod[:, 16:32, 0], nmixT, k_nmix),
                                        (scale_c, k_scc, mod[:, 16:32, 1], nmixT, k_nmix),
                                        (scale_f, k_scf, mod[:, 64:80, 0], nffnT, k_nffn)):
            S.op("dve", lambda e, dst=dst, src=src, nrm=nrm: e.scalar_tensor_tensor(
                out=dst, in0=src, scalar=1.0, in1=nrm, op0=ALU.add, op1=ALU.mult), reads=[k_mod, kn], writes=[kd])
        if "MODd" in dbg:
            S.final_tokens.append(S.dma("sp", MODd, mod, reads=[k_mod], writes=["MODd"]))
        if upto <= 1:
            S.emit(); return nc

        S.barrier(); A.reset(PERSIST)
        hT, k_hT = A.alloc([128, 16, TT], BF16, "hT")
        PB = A.off
        xr = Ring(A, 2, [128, D], F32, "xt")
        sq, k_sq = A.alloc([128, D], F32, "sq")
        xnr = Ring(A, 2, [128, D], BF16, "xn")
        str_ = Ring(A, 2, [128, 4], F32, "stat")
        for i in range(TT // 128):
            lat = i < L // 128
            src = x_d[i * 128:(i + 1) * 128, :] if lat else ctx_d[(i - 32) * 128:(i - 31) * 128, :]
            sc_t, k_sc = (scale_a, k_sca) if lat else (scale_c, k_scc)
            sh_t = mod[:, 0:16, 0] if lat else mod[:, 0:16, 1]
            xt, kx = xr.next(); xn, kxn = xnr.next(); stt, kst = str_.next()
            S.dma("sp", xt, src, writes=[kx])
            S.op("act", lambda e, xt=xt: e.activation(out=sq, in_=xt, func=AF.Square), reads=[kx], writes=[k_sq])
            S.op("dve", lambda e, stt=stt: e.reduce_sum(out=stt[:, 0:1], in_=sq, axis=AX.X), reads=[k_sq], writes=[kst])
            S.op("dve", lambda e, stt=stt: e.tensor_scalar(out=stt[:, 1:2], in0=stt[:, 0:1], scalar1=1.0 / D, scalar2=EPS,
                                                           op0=ALU.mult, op1=ALU.add), reads=[kst], writes=[kst])
            S.op("dve", lambda e, stt=stt: e.reciprocal(out=stt[:, 2:3], in_=stt[:, 1:2]), reads=[kst], writes=[kst])
            S.op("act", lambda e, stt=stt: e.activation(out=stt[:, 3:4], in_=stt[:, 2:3], func=AF.Sqrt), reads=[kst], writes=[kst])
            S.op("dve", lambda e, stt=stt: e.tensor_tensor(out=stt[:, 0:1], in0=stt[:, 3:4], in1=stt[:, 3:4], op=ALU.mult), reads=[kst], writes=[kst])
            S.op("dve", lambda e, stt=stt: e.tensor_tensor(out=stt[:, 0:1], in0=stt[:, 0:1], in1=stt[:, 1:2], op=ALU.mult), reads=[kst], writes=[kst])
            S.op("dve", lambda e, stt=stt: e.tensor_scalar(out=stt[:, 0:1], in0=stt[:, 0:1], scalar1=-0.5, scalar2=1.5,
                                                           op0=ALU.mult, op1=ALU.add), reads=[kst], writes=[kst])
            S.op("dve", lambda e, stt=stt: e.tensor_tensor(out=stt[:, 3:4], in0=stt[:, 3:4], in1=stt[:, 0:1], op=ALU.mult), reads=[kst], writes=[kst])
            S.op("act", lambda e, xt=xt, xn=xn, stt=stt: e.activation(out=xn, in_=xt, func=AF.Copy, scale=stt[:, 3:4]),
                 reads=[kx, kst], writes=[kxn])
            for g in range(4):
                ps, kp = ps_next()
                psv = ps.bitcast(BF16)

                def tr(e, xn=xn, psv=psv, g=g):
                    last = None
                    for j in range(4):
                        fc = g * 4 + j
                        last = e.transpose(psv[:, j * 128:(j + 1) * 128], xn[:, fc * 128:(fc + 1) * 128], ident_b)
                    return last
                S.op("pe", tr, reads=[kxn, k_ident], writes=[kp])
                for j in range(4):
                    fc = g * 4 + j
                    dst = hT[:, fc, i * 128:(i + 1) * 128]
                    if j % 2 == 0:
                        S.op("act", lambda e, dst=dst, psv=psv, j=j, fc=fc, sc_t=sc_t, sh_t=sh_t: e.activation(
                            out=dst, in_=psv[:, j * 128:(j + 1) * 128], func=AF.Identity, scale=sc_t[:, fc:fc + 1], bias=sh_t[:, fc:fc + 1]),
                            reads=[kp, k_sc, k_mod], writes=[k_hT])
                    else:
                        S.op("dve", lambda e, dst=dst, psv=psv, j=j, fc=fc, sc_t=sc_t, sh_t=sh_t: e.tensor_scalar(
                            out=dst, in0=psv[:, j * 128:(j + 1) * 128], scalar1=sc_t[:, fc:fc + 1], scalar2=sh_t[:, fc:fc + 1],
                            op0=ALU.mult, op1=ALU.add), reads=[kp, k_sc, k_mod], writes=[k_hT])
        if "HTd" in dbg:
            HTd = nc.dram_tensor("HTd", [128, 16, TT], BF16, kind="ExternalOutput").ap()
            S.final_tokens.append(S.dma("sp", HTd, hT, reads=[k_hT], writes=["HTd"]))
        if upto <= 2:
            S.emit(); return nc

        S.barrier(); A.reset(PB)
        wr = Ring(A, 3, [128, 16, 128], BF16, "win")
        ur = Ring(A, 3, [128, 512], F32, "u")
        t2r = Ring(A, 2, [128, 512], F32, "tmp2")
        obr = Ring(A, 3, [128, 512], BF16, "ob")
        sqr = Ring(A, 3, [128, 512], BF16, "sq")
        ofr = Ring(A, 2, [64, 512], F32, "of")
        u1, k_u1 = A.alloc([128, L], BF16, "u1")
        ztr = Ring(A, 2, [128, 4, 128], BF16, "zt")
        BLK_ALL = [(b * 512, 512) for b in range(8)]
        BLK_OWN = BLK_ALL[:NOWN // 512]
        BLK_CTX = [(L, CTX)]

        def conv_epilogue(ps, kp, n, rw, taps, ktaps):
            u, ku = ur.next()
            S.op("act", lambda e: e.activation(out=u[:, 0:n], in_=ps[:, 0:n], func=AF.Copy, scale=taps[:, 1:2]),
                 reads=[kp, ktaps], writes=[ku])
            pv = ps[:, 0:n].rearrange("p (r w) -> p r w", w=rw)
            uv = u[:, 0:n].rearrange("p (r w) -> p r w", w=rw)
            S.op("dve", lambda e: e.scalar_tensor_tensor(out=uv[:, :, 1:rw], in0=pv[:, :, 0:rw - 1], scalar=taps[:, 0:1],
                                                         in1=uv[:, :, 1:rw], op0=ALU.mult, op1=ALU.add), reads=[kp, ktaps, ku], writes=[ku])
            S.op("dve", lambda e: e.scalar_tensor_tensor(out=uv[:, :, 0:rw - 1], in0=pv[:, :, 1:rw], scalar=taps[:, 2:3],
                                                         in1=uv[:, :, 0:rw - 1], op0=ALU.mult, op1=ALU.add), reads=[kp, ktaps, ku], writes=[ku])
            return u, ku

        def do_chunk(typ, j):
            M = 64 if typ == "scal" else 128
            wt, kw = wr.next()
            S.dma("poolq", wt.rearrange("p a b -> p (a b)"), w_in_d[PLAN_INDEX[(typ, j)]], writes=[kw])
            own = typ in ("x0", "zg", "gate")
            blocks = BLK_OWN if own else BLK_ALL
            if typ in ("q", "k", "gv", "scal"):
                blocks = blocks + BLK_CTX
            def blk(t0, n):
                ps, kp = ps_next()

                def mm(e, ps=ps, t0=t0, n=n):
                    last = None
                    for kc in range(16):
                        last = e.matmul(ps[0:M, 0:n], wt[:, kc, 0:M], hT[:, kc, t0:t0 + n], start=(kc == 0), stop=(kc == 15))
                    return last
                S.op("pe", mm, reads=[kw, k_hT], writes=[kp])
                isctx = t0 >= L
                rw = CTX if isctx else 64
                if typ in ("x0", "x1", "hv"):
                    ci = {"x0": 0, "x1": 16, "hv": 32}[typ] + j
                    u, ku = conv_epilogue(ps, kp, n, rw, hyconvT[:, ci, :], k_hyc)
                    if typ == "x0":
                        ob, kob = obr.next()
                        S.op("act", lambda e, ob=ob, u=u: e.activation(out=ob, in_=u, func=AF.Copy), reads=[ku], writes=[kob])
                        S.dma("sp", X0T[j * 128:(j + 1) * 128, t0:t0 + n], ob, reads=[kob], writes=["X0T"])
                    elif typ == "x1":
                        S.op("act", lambda e, u=u, t0=t0: e.activation(out=u1[:, t0:t0 + 512], in_=u, func=AF.Copy), reads=[ku], writes=[k_u1 + "@%d" % t0])
                    else:
                        ob, kob = obr.next()
                        S.op("pool", lambda e, ob=ob, u=u, t0=t0: e.tensor_tensor(out=ob, in0=u, in1=u1[:, t0:t0 + 512], op=ALU.mult),
                             reads=[ku, k_u1 + "@%d" % t0], writes=[kob])
                        S.dma("sp", ZT[j * 128:(j + 1) * 128, t0:t0 + n], ob, reads=[kob], writes=["ZT"])
                        yield
                        ps2, kp2 = ps_next()
                        ps2v = ps2.bitcast(BF16)

                        def trz(e, ob=ob, ps2v=ps2v):
                            last = None
                            for tb in range(4):
                                last = e.transpose(ps2v[:, tb * 128:(tb + 1) * 128], ob[:, tb * 128:(tb + 1) * 128], ident_b)
                            return last
                        S.op("pe", trz, reads=[kob, k_ident], writes=[kp2])
                        zt, kzt = ztr.next()
                        S.op("act", lambda e, zt=zt, ps2v=ps2v: e.activation(
                            out=zt, in_=ps2v[:, 0:512].rearrange("p (a b) -> p a b", b=128), func=AF.Copy), reads=[kp2], writes=[kzt])
                        S.dma("sp", Zt[t0:t0 + 512, j * 128:(j + 1) * 128].rearrange("(a p) c -> p a c", p=128), zt,
                              reads=[kzt], writes=["Zt"])
                elif typ in ("q", "k", "gv"):
                    ci = {"q": 0, "k": 16, "gv": 32}[typ] + j
                    u, ku = conv_epilogue(ps, kp, n, rw, gdnconvT[:, ci, :], k_gdc)
                    S.op("act", lambda e, u=u, n=n: e.activation(out=u[:, 0:n], in_=u[:, 0:n], func=AF.Silu), reads=[ku], writes=[ku])
                    ob, kob = obr.next()
                    dstT = {"q": QT, "k": KT, "gv": VT}[typ]
                    if typ == "gv":
                        S.op("pool", lambda e, ob=ob, u=u, n=n: e.tensor_copy(out=ob[:, 0:n], in_=u[:, 0:n]), reads=[ku], writes=[kob])
                    else:
                        sqb, ksqb = sqr.next()
                        S.op("pool", lambda e, sqb=sqb, u=u, n=n: e.tensor_tensor(out=sqb[:, 0:n], in0=u[:, 0:n], in1=u[:, 0:n], op=ALU.mult),
                             reads=[ku], writes=[ksqb])
                        yield
                        t2, kt2 = t2r.next()
                        ps2, kp2 = ps_next()
                        S.op("pe", lambda e, ps2=ps2, sqb=sqb, n=n: e.matmul(ps2[:, 0:n], ones_b, sqb[:, 0:n], start=True, stop=True),
                             reads=[ksqb, k_ones], writes=[kp2])
                        S.op("dve", lambda e, t2=t2, ps2=ps2, n=n: e.tensor_scalar(out=t2[:, 0:n], in0=ps2[:, 0:n], scalar1=EPS, scalar2=None, op0=ALU.add),
                             reads=[kp2], writes=[kt2])
                        S.op("dve", lambda e, t2=t2, n=n: e.reciprocal(out=t2[:, 0:n], in_=t2[:, 0:n]), reads=[kt2], writes=[kt2])
                        S.op("act", lambda e, t2=t2, n=n: e.activation(out=t2[:, 0:n], in_=t2[:, 0:n], func=AF.Sqrt), reads=[kt2], writes=[kt2])
                        ob, kob = obr.next()
                        qs = (128.0 ** -0.5) if typ == "q" else 1.0
                        S.op("dve", lambda e, ob=ob, u=u, t2=t2, n=n, qs=qs: e.scalar_tensor_tensor(
                            out=ob[:, 0:n], in0=u[:, 0:n], scalar=qs, in1=t2[:, 0:n], op0=ALU.mult, op1=ALU.mult), reads=[ku, kt2], writes=[kob])
                    S.dma("sp", dstT[j * 128:(j + 1) * 128, t0:t0 + n], ob[:, 0:n], reads=[kob], writes=[typ + "T"])
                elif typ in ("zg", "gate"):
                    ob, kob = obr.next()
                    fn = AF.Silu if typ == "zg" else AF.Sigmoid
                    S.op("act", lambda e, ob=ob, ps=ps, fn=fn: e.activation(out=ob, in_=ps[:, 0:512], func=fn), reads=[kp], writes=[kob])
                    dstT = ZGT if typ == "zg" else GT
                    S.dma("sp", dstT[j * 128:(j + 1) * 128, t0:t0 + n], ob, reads=[kob], writes=[typ + "T"])
                else:
                    of, kof = ofr.next()
                    S.op("act", lambda e, of=of, ps=ps, n=n: e.activation(out=of[0:32, 0:n], in_=ps[0:32, 0:n], func=AF.Sigmoid), reads=[kp], writes=[kof])
                    S.op("act", lambda e, of=of, ps=ps, n=n: e.activation(out=of[32:64, 0:n], in_=ps[32:64, 0:n], func=AF.Exp, bias=scalp[32:64, 1:2]),
                         reads=[kp, k_scalp], writes=[kof])
                    S.op("act", lambda e, of=of, n=n: e.activation(out=of[32:64, 0:n], in_=of[32:64, 0:n], func=AF.Ln, bias=1.0), reads=[kof], writes=[kof])
                    S.op("dve", lambda e, of=of, n=n: e.tensor_scalar(out=of[32:64, 0:n], in0=of[32:64, 0:n], scalar1=nega[32:64, 0:1], scalar2=None, op0=ALU.mult),
                         reads=[kof, k_nega], writes=[kof])
                    S.dma("sp", SCT[:, t0:t0 + n], of[:, 0:n], reads=[kof], writes=["SCT"])
                return
                yield

            pend = None
            for (t0, n) in blocks:
                g = blk(t0, n)
                alive = True
                try:
                    next(g)
                except StopIteration:
                    alive = False
                if pend is not None:
                    for _ in pend:
                        pass
                pend = g if alive else None
            if pend is not None:
                for _ in pend:
                    pass

        plan = list(PLAN)
        import os
        if os.environ.get("K_PLAN_TEST") == "gdn":
            plan = [("scal", 0)] + [(t, j) for t in ("q", "k", "gv") for j in range(16)]
        elif os.environ.get("K_PLAN_TEST") == "hy":
            plan = [pp for j in (0, 7) for pp in (("x1", j), ("hv", j))] + [("x0", 0), ("x0", 7)]
        elif os.environ.get("K_PLAN_TEST"):
            plan = [("scal", 0), ("x1", 1), ("hv", 1), ("q", 2), ("k", 3), ("gv", 4), ("x0", 5), ("zg", 6), ("gate", 7), ("gate", 17)]
        for (typ, j) in plan:
            do_chunk(typ, j)
        for nm in ("X0T", "ZT", "Zt", "QT", "KT", "VT", "ZGT", "SCT", "GT"):
            if nm in dbg:
                pass
        if upto <= 3:
            S.barrier()
            S.emit(); return nc


        def phase_d():
            S.barrier(); A.reset(PERSIST)
            H = HEADS
            HH = 8
            cmask, k_cm = A.alloc([64, 4, 64], F32, "cmask")
            S.dma("sp", cmask, cmask_d, writes=[k_cm])
            cm2, k_cm2 = A.alloc([64, 5, 64], F32, "cm2")
            S.dma("sp", cm2, cm2_d, writes=[k_cm2])
            cmask_b, k_cmb = A.alloc([64, 4, 64], BF16, "cmask_b")
            cm2_b, k_cm2b = A.alloc([64, 5, 64], BF16, "cm2_b")
            S.op("dve", lambda e: e.tensor_copy(out=cm2_b, in_=cm2), reads=[k_cm2], writes=[k_cm2b])
            S.op("dve", lambda e: e.tensor_copy(out=cmask_b, in_=cmask), reads=[k_cm], writes=[k_cmb])
            qr = Ring(A, 2, [128, H, 256], BF16, "qg"); kr = Ring(A, 2, [128, H, 256], BF16, "kg"); vr = Ring(A, 2, [128, H, 256], BF16, "vg")
            scr = Ring(A, 2, [64, 256], F32, "scg")
            identb64 = ident_b[0:64, 0:64].unsqueeze(1).broadcast_to([64, HH, 64])
            identf64 = ident_f[0:64, 0:64].unsqueeze(1).broadcast_to([64, HH, 64])
            QTv = QT.rearrange("(h p) t -> p h t", p=128); KTv = KT.rearrange("(h p) t -> p h t", p=128); VTv = VT.rearrange("(h p) t -> p h t", p=128)

            def al(shape, dt, nm):
                return A.alloc(shape, dt, nm)

            class TS:
                pass
            halves = []
            for hh in range(2):
                T_ = TS()
                T_.S = al([128, HH, 128], F32, "S"); T_.Sb = al([128, HH, 128], BF16, "Sb")
                T_.sct = al([64, 64], F32, "sct"); T_.sm = al([64, 8, HH], F32, "sm")
                T_.ktok = al([64, HH, 128], BF16, "ktok"); T_.vtok = al([64, HH, 128], BF16, "vtok")
                T_.Xg = al([64, HH, 64], F32, "Xg"); T_.Xb = al([64, HH, 64], F32, "Xb")
                T_.Y = al([64, HH, 64], F32, "Y"); T_.EA = al([64, HH, 64], F32, "EA"); T_.EB = al([64, HH, 64], F32, "EB")
                T_.tmpM = al([64, HH, 64], BF16, "tmpM"); T_.tmpB = al([64, HH, 64], BF16, "tmpB"); T_.tmpQ = al([64, HH, 64], BF16, "tmpQ")
                T_.bbc = al([64, HH, 64], BF16, "bbc")
                T_.AO = [al([64, HH, 64], BF16, "AO%d" % i) for i in range(3)]
                T_.P0 = al([64, HH, 64], BF16, "P0"); T_.P1 = al([64, HH, 64], BF16, "P1")
                T_.PT0 = al([64, HH, 64], BF16, "PT0"); T_.PT1 = al([64, HH, 64], BF16, "PT1")
                T_.TT = al([64, HH, 64], BF16, "TT"); T_.T = al([64, HH, 64], BF16, "T")
                T_.Am = al([64, HH, 64], BF16, "Am"); T_.ATm = al([64, HH, 64], BF16, "ATm")
                T_.eGbc = al([128, HH, 64], F32, "eGbc")
                T_.vb = al([64, HH, 128], BF16, "vb"); T_.kbg = al([64, HH, 128], BF16, "kbg")
                T_.vnew = al([64, HH, 128], BF16, "vnew")
                T_.slot = []
                for sl in range(2):
                    d = dict(wT=al([128, HH, 64], BF16, "wT"), uu=al([64, HH, 128], F32, "uu"), qdT=al([128, HH, 64], BF16, "qdT"),
                             QKT=al([64, HH, 64], BF16, "QKT"), kdec=al([64, HH, 128], BF16, "kdec"), egl=al([128, HH], F32, "egl"))
                    T_.slot.append(d)
                T_.orr = Ring(A, 2, [64, HH, 128], F32, "o")
                halves.append(T_)

            def bc_s(ap):
                return ap.unsqueeze(2).broadcast_to([64, HH, 64])

            def bc_e(ap):
                return ap.unsqueeze(2).broadcast_to([64, HH, 128])

            def mk(i):
                return cmask[:, i, :].unsqueeze(1).broadcast_to([64, HH, 64])

            def mkb(i):
                return cmask_b[:, i, :].unsqueeze(1).broadcast_to([64, HH, 64])

            def m2(i):
                return cm2_b[:, i, :].unsqueeze(1).broadcast_to([64, HH, 64])

            def hm(psx):
                return psx[0:64, :].rearrange("p (h s) -> p h s", h=HH)

            def hm128(psx):
                return psx[0:64, :].rearrange("p (h d) -> p h d", h=HH)

            def run_dir(dr):
                mA_s, mB_i, mB_s, cumi = ((1, 2, 3, 2) if dr == 0 else (3, 0, 1, 0))
                last_t = 63 if dr == 0 else 0
                for T_ in halves:
                    S.op("dve", lambda e, T_=T_: e.memset(T_.S[0], 0.0), reads=[T_.S[1]], writes=[T_.S[1]])
                    S.op("dve", lambda e, T_=T_: e.memset(T_.Sb[0], 0.0), reads=[T_.Sb[1]], writes=[T_.Sb[1]])
                if dr == 0:
                    groups = [L // 256] + list(range(NOWN // 256))
                else:
                    groups = [L // 256] + list(range(L // 256 - 1, -1, -1))

                def chunk(hh, gi, ci, need_o, g0, qg, kqg, kg, kkg, vg, kvg, scg, kscg, slot):
                    T_ = halves[hh]
                    h0 = hh * HH
                    c0 = ci * 64
                    (Sst, k_S), (Sb, k_Sb), (sct, k_sct), (sm, k_sm) = T_.S, T_.Sb, T_.sct, T_.sm
                    (ktok, k_ktok), (vtok, k_vtok), (Xg, k_Xg), (Xb, k_Xb) = T_.ktok, T_.vtok, T_.Xg, T_.Xb
                    (Y, k_Y), (EA, k_EA), (EB, k_EB), (tmpM, k_tmpM) = T_.Y, T_.EA, T_.EB, T_.tmpM
                    (tmpB, k_tmpB), (tmpQ, k_tmpQ), (bbc, k_bbc) = T_.tmpB, T_.tmpQ, T_.bbc
                    (P0, kP0), (P1, kP1), (PT0, kPT0), (PT1, kPT1) = T_.P0, T_.P1, T_.PT0, T_.PT1
                    (TTm, k_TT), (Tm, k_T), (Am, k_Am), (ATm, k_ATm) = T_.TT, T_.T, T_.Am, T_.ATm
                    (eGbc, k_eGbc), (vb, k_vb), (kbg, k_kbg), (vnew, k_vnew) = T_.eGbc, T_.vb, T_.kbg, T_.vnew
                    sd = T_.slot[slot]
                    (wT, k_wT), (uu, k_uu), (qdT, k_qdT), (QKT, k_QKT), (kdec, k_kdec), (egl, k_egl) = (
                        sd["wT"], sd["uu"], sd["qdT"], sd["QKT"], sd["kdec"], sd["egl"])

                    def mmset(lhs, klhs, rhs, krhs):
                        pX, kpX = ps_next()

                        def f(e):
                            last = None
                            for h in range(HH):
                                last = e.matmul(pX[0:64, h * 64:(h + 1) * 64], lhs[:, h, :], rhs[:, h, :], start=True, stop=True)
                            return last
                        S.op("pe", f, reads=[klhs, krhs], writes=[kpX])
                        return pX, [kpX]
                    ps, kp = ps_next()
                    S.op("pe", lambda e: e.transpose(ps[0:64, 0:64], scg[:, c0:c0 + 64], ident_f[0:64, 0:64]), reads=[kscg, k_identf], writes=[kp])
                    S.op("act", lambda e: e.activation(out=sct, in_=ps[0:64, 0:64], func=AF.Copy), reads=[kp], writes=[k_sct])
                    beta = sct[:, dr * 16 + h0:dr * 16 + h0 + HH]; gg = sct[:, 32 + dr * 16 + h0:32 + dr * 16 + h0 + HH]
                    yield "p"
                    ps2, kp2 = ps_next()
                    S.op("pe", lambda e: e.matmul(ps2[0:64, 0:HH], cmask[:, cumi, :], gg, start=True, stop=True), reads=[k_sct, k_cm], writes=[kp2])
                    S.op("act", lambda e: e.activation(out=sm[:, 0, :], in_=ps2[0:64, 0:HH], func=AF.Copy), reads=[kp2], writes=[k_sm])
                    yield "p"
                    S.op("dve", lambda e: e.tensor_tensor(out=Xg, in0=identf64, in1=bc_s(sm[:, 0, :]), op=ALU.mult), reads=[k_sm, k_identf], writes=[k_Xg])
                    S.op("pool", lambda e: e.tensor_tensor(out=Xb, in0=identf64, in1=bc_s(beta), op=ALU.mult), reads=[k_sct, k_identf], writes=[k_Xb])
                    pG, kpG1 = ps_next(); kpG = [kpG1]
                    pGv = pG.rearrange("p (h s) -> p h s", h=HH)
                    S.op("pe", lambda e: e.matmul(pG[:, 0:512], ones_f[0:64, :], Xg, start=True, stop=True), reads=[k_Xg, k_onesf], writes=kpG)
                    pB, kpB1 = ps_next(); kpB = [kpB1]
                    pBv = pB.rearrange("p (h s) -> p h s", h=HH)
                    S.op("pe", lambda e: e.matmul(pB[0:64, 0:512], ones_f[0:64, 0:64], Xb, start=True, stop=True), reads=[k_Xb, k_onesf], writes=kpB)
                    S.op("dve", lambda e: e.tensor_tensor(out=Y, in0=pGv[0:64], in1=bc_s(sm[:, 0, :]), op=ALU.subtract), reads=kpG + [k_sm], writes=[k_Y])
                    S.op("act", lambda e: e.activation(out=bbc, in_=pBv[0:64], func=AF.Copy), reads=kpB, writes=[k_bbc])
                    S.op("dve", lambda e: e.tensor_tensor(out=sm[:, 3, :], in0=pGv[0:64, :, last_t], in1=sm[:, 0, :], op=ALU.subtract), reads=kpG + [k_sm], writes=[k_sm])
                    S.op("dve", lambda e: e.tensor_scalar(out=egl, in0=pGv[:, :, last_t], scalar1=-80.0, scalar2=None, op0=ALU.max), reads=kpG, writes=[k_egl])
                    if need_o:
                        S.op("dve", lambda e: e.tensor_scalar(out=eGbc, in0=pGv, scalar1=-80.0, scalar2=None, op0=ALU.max), reads=kpG, writes=[k_eGbc])
                    yield "p"
                    S.op("dve", lambda e: e.scalar_tensor_tensor(out=EA, in0=Y, scalar=80.0, in1=mk(mA_s), op0=ALU.min, op1=ALU.mult), reads=[k_Y, k_cm], writes=[k_EA])
                    S.op("act", lambda e: e.activation(out=EA, in_=EA, func=AF.Exp, scale=-1.0), reads=[k_EA], writes=[k_EA])
                    S.op("dve", lambda e: e.scalar_tensor_tensor(out=EB, in0=Y, scalar=-80.0, in1=mk(mB_i), op0=ALU.max, op1=ALU.mult), reads=[k_Y, k_cm], writes=[k_EB])
                    S.op("act", lambda e: e.activation(out=EB, in_=EB, func=AF.Exp), reads=[k_EB], writes=[k_EB])
                    yield "p"
                    S.op("dve", lambda e: e.tensor_scalar(out=sm[:, 1, :], in0=sm[:, 0, :], scalar1=-80.0, scalar2=None, op0=ALU.max), reads=[k_sm], writes=[k_sm])
                    S.op("act", lambda e: e.activation(out=sm[:, 1, :], in_=sm[:, 1, :], func=AF.Exp), reads=[k_sm], writes=[k_sm])
                    S.op("dve", lambda e: e.tensor_tensor(out=sm[:, 2, :], in0=sm[:, 1, :], in1=beta, op=ALU.mult), reads=[k_sm, k_sct], writes=[k_sm])
                    S.op("dve", lambda e: e.tensor_scalar(out=sm[:, 3, :], in0=sm[:, 3, :], scalar1=-80.0, scalar2=None, op0=ALU.max), reads=[k_sm], writes=[k_sm])
                    S.op("act", lambda e: e.activation(out=sm[:, 3, :], in_=sm[:, 3, :], func=AF.Exp), reads=[k_sm], writes=[k_sm])
                    S.op("act", lambda e: e.activation(out=egl, in_=egl, func=AF.Exp), reads=[k_egl], writes=[k_egl])
                    if need_o:
                        S.op("act", lambda e: e.activation(out=eGbc, in_=eGbc, func=AF.Exp), reads=[k_eGbc], writes=[k_eGbc])
                        S.op("pool", lambda e: e.tensor_tensor(out=qdT, in0=qg[:, h0:h0 + HH, c0:c0 + 64], in1=eGbc, op=ALU.mult), reads=[kqg, k_eGbc], writes=[k_qdT])
                    yield "p"
                    for (src, ksrc, dst, kdst) in ((kg, kkg, ktok, k_ktok), (vg, kvg, vtok, k_vtok)):
                        pT, kpT = ps_next()
                        pTv = pT.bitcast(BF16)

                        def trk(e, src=src, pTv=pTv):
                            last = None
                            for h in range(HH):
                                last = e.transpose(pTv[0:64, h * 128:(h + 1) * 128], src[:, h0 + h, c0:c0 + 64], ident_b)
                            return last
                        S.op("pe", trk, reads=[ksrc, k_ident], writes=[kpT])
                        S.op("act", lambda e, dst=dst, pTv=pTv: e.activation(out=dst, in_=pTv[0:64, :].rearrange("p (h d) -> p h d", h=HH), func=AF.Copy),
                             reads=[kpT], writes=[kdst])
                    pK, kpK1 = ps_next(); kpK = [kpK1]
                    pKv = pK.rearrange("p (h s) -> p h s", h=HH)

                    def mmK(e):
                        last = None
                        for h in range(HH):
                            last = e.matmul(pK[0:64, h * 64:(h + 1) * 64], kg[:, h0 + h, c0:c0 + 64], kg[:, h0 + h, c0:c0 + 64], start=True, stop=True)
                        return last
                    S.op("pe", mmK, reads=[kkg], writes=kpK)
                    S.op("dve", lambda e: e.tensor_tensor(out=tmpM, in0=pKv[0:64], in1=EA, op=ALU.mult), reads=kpK + [k_EA], writes=[k_tmpM])
                    S.op("dve", lambda e: e.tensor_tensor(out=tmpB, in0=pKv[0:64], in1=EB, op=ALU.mult), reads=kpK + [k_EB], writes=[k_tmpB])
                    if need_o:
                        pQ, kpQ1 = ps_next(); kpQ = [kpQ1]
                        pQv = pQ.rearrange("p (h s) -> p h s", h=HH)

                        def mmQ(e):
                            last = None
                            for h in range(HH):
                                last = e.matmul(pQ[0:64, h * 64:(h + 1) * 64], kg[:, h0 + h, c0:c0 + 64], qg[:, h0 + h, c0:c0 + 64], start=True, stop=True)
                            return last
                        S.op("pe", mmQ, reads=[kkg, kqg], writes=kpQ)
                        S.op("dve", lambda e: e.tensor_tensor(out=tmpQ, in0=pQv[0:64], in1=EB, op=ALU.mult), reads=kpQ + [k_EB], writes=[k_tmpQ])
                    yield "p"
                    S.op("dve", lambda e: e.tensor_tensor(out=tmpM, in0=tmpM, in1=bc_s(beta), op=ALU.mult), reads=[k_tmpM, k_sct], writes=[k_tmpM])
                    S.op("pool", lambda e: e.tensor_tensor(out=Am, in0=tmpM, in1=mkb(mA_s), op=ALU.mult), reads=[k_tmpM, k_cmb], writes=[k_Am])
                    S.op("dve", lambda e: e.tensor_tensor(out=tmpB, in0=tmpB, in1=bbc, op=ALU.mult), reads=[k_tmpB, k_bbc], writes=[k_tmpB])
                    S.op("pool", lambda e: e.tensor_tensor(out=ATm, in0=tmpB, in1=mkb(mB_s), op=ALU.mult), reads=[k_tmpB, k_cmb], writes=[k_ATm])
                    if need_o:
                        S.op("pool", lambda e: e.tensor_tensor(out=QKT, in0=tmpQ, in1=mkb(mB_i), op=ALU.mult), reads=[k_tmpQ, k_cmb], writes=[k_QKT])
                    for lv in range(3):
                        AOt, kAO = T_.AO[lv]
                        S.op("pool", lambda e, lv=lv, AOt=AOt: e.tensor_tensor(out=AOt, in0=ATm, in1=m2(1 + lv), op=ALU.mult), reads=[k_ATm, k_cm2b], writes=[kAO])
                    yield "p"
                    S.op("pool", lambda e: e.tensor_tensor(out=P0, in0=Am, in1=m2(4), op=ALU.mult), reads=[k_Am, k_cm2b], writes=[kP0])
                    S.op("pool", lambda e: e.tensor_tensor(out=PT0, in0=ATm, in1=m2(4), op=ALU.mult), reads=[k_ATm, k_cm2b], writes=[kPT0])
                    S.op("pool", lambda e: e.tensor_tensor(out=Tm, in0=P0, in1=identb64, op=ALU.add), reads=[kP0, k_ident], writes=[k_T])
                    S.op("pool", lambda e: e.tensor_tensor(out=TTm, in0=PT0, in1=identb64, op=ALU.add), reads=[kPT0, k_ident], writes=[k_TT])
                    p1, kp1 = mmset(PT0, kPT0, P0, kP0)
                    S.op("act", lambda e: e.activation(out=P1, in_=hm(p1), func=AF.Copy), reads=kp1, writes=[kP1])
                    p2_, kp2_ = mmset(P0, kP0, PT0, kPT0)
                    S.op("act", lambda e: e.activation(out=PT1, in_=hm(p2_), func=AF.Copy), reads=kp2_, writes=[kPT1])
                    yield "p"
                    p3, kp3 = mmset(TTm, k_TT, P1, kP1)
                    p4, kp4 = mmset(P1, kP1, TTm, k_TT)
                    S.op("dve", lambda e: e.tensor_tensor(out=Tm, in0=Tm, in1=hm(p3), op=ALU.add), reads=kp3 + [k_T], writes=[k_T])
                    S.op("dve", lambda e: e.tensor_tensor(out=TTm, in0=TTm, in1=hm(p4), op=ALU.add), reads=kp4 + [k_TT], writes=[k_TT])
                    p5, kp5 = mmset(PT1, kPT1, P1, kP1)
                    S.op("act", lambda e: e.activation(out=P0, in_=hm(p5), func=AF.Copy), reads=kp5, writes=[kP0])
                    yield "p"
                    p6, kp6 = mmset(TTm, k_TT, P0, kP0)
                    p7, kp7 = mmset(P0, kP0, TTm, k_TT)
                    S.op("dve", lambda e: e.tensor_tensor(out=Tm, in0=Tm, in1=hm(p6), op=ALU.add), reads=kp6 + [k_T], writes=[k_T])
                    S.op("dve", lambda e: e.tensor_tensor(out=TTm, in0=TTm, in1=hm(p7), op=ALU.add), reads=kp7 + [k_TT], writes=[k_TT])
                    yield "p"
                    for lv in range(3):
                        AOt, kAO = T_.AO[lv]
                        pX, kpX = mmset(AOt, kAO, Tm, k_T)
                        S.op("act", lambda e, pX=pX: e.activation(out=P1, in_=hm(pX), func=AF.Copy), reads=kpX, writes=[kP1])
                        yield "p"
                        p8, kp8 = mmset(P1, kP1, TTm, k_TT)
                        if lv < 2:
                            p9, kp9 = mmset(TTm, k_TT, P1, kP1)
                            S.op("dve", lambda e, p9=p9: e.tensor_tensor(out=Tm, in0=Tm, in1=hm(p9), op=ALU.subtract), reads=kp9 + [k_T], writes=[k_T])
                        S.op("dve", lambda e, p8=p8: e.tensor_tensor(out=TTm, in0=TTm, in1=hm(p8), op=ALU.subtract), reads=kp8 + [k_TT], writes=[k_TT])
                        yield "p"
                    S.op("pool", lambda e: e.tensor_tensor(out=vb, in0=vtok, in1=bc_e(beta), op=ALU.mult), reads=[k_vtok, k_sct], writes=[k_vb])
                    S.op("pool", lambda e: e.tensor_tensor(out=kbg, in0=ktok, in1=bc_e(sm[:, 2, :]), op=ALU.mult), reads=[k_ktok, k_sm], writes=[k_kbg])
                    S.op("pool", lambda e: e.tensor_tensor(out=kdec, in0=ktok, in1=bc_e(sm[:, 3, :]), op=ALU.mult), reads=[k_ktok, k_sm], writes=[k_kdec])
                    pU, kpU = ps_multi(2)

                    def mmU(e):
                        last = None
                        for h in range(HH):
                            last = e.matmul(pU[0:64, h * 128:(h + 1) * 128], TTm[:, h, :], vb[:, h, :], start=True, stop=True)
                        return last
                    S.op("pe", mmU, reads=[k_TT, k_vb], writes=kpU)
                    S.op("act", lambda e: e.activation(out=uu, in_=hm128(pU), func=AF.Copy), reads=kpU, writes=[k_uu])
                    pW, kpW1 = ps_next(); kpW = [kpW1]

                    def mmW(e):
                        last = None
                        for h in range(HH):
                            last = e.matmul(pW[:, h * 64:(h + 1) * 64], kbg[:, h, :], TTm[:, h, :], start=True, stop=True)
                        return last
                    S.op("pe", mmW, reads=[k_kbg, k_TT], writes=kpW)
                    S.op("act", lambda e: e.activation(out=wT, in_=pW.rearrange("p (h s) -> p h s", h=HH), func=AF.Copy), reads=kpW, writes=[k_wT])
                    yield "scan"
                    pV, kpV = ps_multi(2)

                    def mmV(e):
                        last = None
                        for h in range(HH):
                            last = e.matmul(pV[0:64, h * 128:(h + 1) * 128], wT[:, h, :], Sb[:, h, :], start=True, stop=True)
                        return last
                    S.op("pe", mmV, reads=[k_wT, k_Sb], writes=kpV)
                    S.op("dve", lambda e: e.tensor_tensor(out=vnew, in0=uu, in1=hm128(pV), op=ALU.subtract), reads=kpV + [k_uu], writes=[k_vnew])
                    yield "s"
                    if need_o:
                        pO, kpO = ps_multi(2)

                        def mmO(e):
                            last = None
                            for h in range(HH):
                                e.matmul(pO[0:64, h * 128:(h + 1) * 128], qdT[:, h, :], Sb[:, h, :], start=True, stop=False)
                                last = e.matmul(pO[0:64, h * 128:(h + 1) * 128], QKT[:, h, :], vnew[:, h, :], start=False, stop=True)
                            return last
                        S.op("pe", mmO, reads=[k_qdT, k_Sb, k_QKT, k_vnew], writes=kpO)
                        ot, kot = T_.orr.next()
                        S.op("act", lambda e: e.activation(out=ot, in_=hm128(pO), func=AF.Copy), reads=kpO, writes=[kot])
                        Odst = Od0 if dr == 0 else Od1
                        S.dma("sp", Odst[g0 + c0:g0 + c0 + 64, h0 * 128:(h0 + HH) * 128], ot.rearrange("p h d -> p (h d)"), reads=[kot], writes=["Od%d_%d" % (dr, hh)])
                    pS, kpS = ps_multi(2)

                    def mmS(e):
                        last = None
                        for h in range(HH):
                            last = e.matmul(pS[:, h * 128:(h + 1) * 128], kdec[:, h, :], vnew[:, h, :], start=True, stop=True)
                        return last
                    S.op("pe", mmS, reads=[k_kdec, k_vnew], writes=kpS)
                    S.op("dve", lambda e: e.tensor_tensor(out=Sst, in0=Sst, in1=egl.unsqueeze(2).broadcast_to([128, HH, 128]), op=ALU.mult), reads=[k_S, k_egl], writes=[k_S])
                    S.op("dve", lambda e: e.tensor_tensor(out=Sst, in0=Sst, in1=pS.rearrange("p (h d) -> p h d", h=HH), op=ALU.add), reads=kpS + [k_S], writes=[k_S])
                    S.op("act", lambda e: e.activation(out=Sb, in_=Sst, func=AF.Copy), reads=[k_S], writes=[k_Sb])
                    yield "done"

                def advance(gens, until):
                    live = list(gens)
                    while live:
                        for g in list(live):
                            try:
                                v = next(g)
                            except StopIteration:
                                live.remove(g); continue
                            if v == until:
                                live.remove(g)

                pending = None
                nchunk = 0
                def load_group(gi):
                    g0 = gi * 256
                    need_o = gi < NOWN // 256
                    qg, kqg = qr.next(); kg, kkg = kr.next(); vg, kvg = vr.next(); scg, kscg = scr.next()
                    if need_o:
                        S.dma("sp", qg, QTv[:, :, g0:g0 + 256], reads=["qT"], writes=[kqg])
                    S.dma("sp", kg, KTv[:, :, g0:g0 + 256], reads=["kT"], writes=[kkg])
                    S.dma("sp", vg, VTv[:, :, g0:g0 + 256], reads=["gvT"], writes=[kvg])
                    S.dma("sp", scg, SCT[:, g0:g0 + 256], reads=["SCT"], writes=[kscg])
                    return (g0, need_o, qg, kqg, kg, kkg, vg, kvg, scg, kscg)

                loaded = load_group(groups[0])
                for gidx, gi in enumerate(groups):
                    (g0, need_o, qg, kqg, kg, kkg, vg, kvg, scg, kscg) = loaded
                    if gidx + 1 < len(groups):
                        loaded = load_group(groups[gidx + 1])
                    for ci in (range(4) if dr == 0 else range(3, -1, -1)):
                        gens = [chunk(hh, gi, ci, need_o, g0, qg, kqg, kg, kkg, vg, kvg, scg, kscg, nchunk % 2) for hh in range(2)]
                        advance(gens, "scan")
                        if pending is not None:
                            advance(pending, "done")
                        pending = gens
                        nchunk += 1
                advance(pending, "done")

            for dr_ in range(2):
                run_dir(dr_)

        if "D" in phases:
            phase_d()


        def phase_e():
            CG = 512
            NG = D // CG

            def load_consts(names):
                out = {}
                for nm, src, shp in names:
                    t, k = A.alloc(shp, BF16, nm)
                    S.dma("poolq", t, src, writes=[k])
                    out[nm] = (t, k)
                return out

            S.barrier(); A.reset(PERSIST)
            zf, k_zf = A.alloc([33, N_], F32, "zf")
            fw1, k_fw1 = A.alloc([33, 64], F32, "fw1"); fw2, k_fw2 = A.alloc([64, 64], F32, "fw2"); fw3, k_fw3 = A.alloc([64, 64], F32, "fw3")
            fsm, k_fsm = A.alloc([64, 8], F32, "fsm")
            fout, k_fout = A.alloc([64, 2 * D], F32, "fout")
            dl, k_dl = A.alloc([128, D], F32, "deltas")
            tau, k_tau = A.alloc([128, 48], F32, "tau")
            hr = [A.alloc([64, 512], F32, "hmlp%d" % i) for i in range(3)]
            decr = Ring(A, 2, [128, 512], F32, "dec")
            fm, k_fm = A.alloc([64, 512], F32, "fm")
            hcr = Ring(A, 3, [128, 512], BF16, "hc")
            for (t, k, src) in ((zf, k_zf, zf_d), (fw1, k_fw1, fw1_d), (fw2, k_fw2, fw2_d), (fw3, k_fw3, fw3_d),
                                (fsm[:, 0:4], k_fsm, fsm_d), (fout, k_fout, fout_d), (dl, k_dl, deltas_d), (tau, k_tau, tau_d)):
                S.dma("sp", t, src, writes=[k])
            for i in range(3):
                S.op("dve", lambda e, i=i: e.tensor_tensor(out=fsm[:, 4 + i:5 + i], in0=fsm[:, i:i + 1], in1=fsm[:, 3:4], op=ALU.mult), reads=[k_fsm], writes=[k_fsm])
            for jb in range(N_ // 512):
                j0 = jb * 512
                prev, kprev = zf[:, j0:j0 + 512], k_zf
                for li, (w, kw) in enumerate(((fw1, k_fw1), (fw2, k_fw2), (fw3, k_fw3))):
                    ps, kp = ps_next()
                    S.op("pe", lambda e, ps=ps, w=w, prev=prev: e.matmul(ps[0:64, :], w, prev, start=True, stop=True), reads=[kw, kprev], writes=[kp])
                    ht, kht = hr[li]
                    S.op("act", lambda e, ps=ps, ht=ht, li=li: e.activation(out=ht, in_=ps[0:64, :], func=AF.Identity, scale=fsm[:, 3:4], bias=fsm[:, 4 + li:5 + li]),
                         reads=[kp, k_fsm], writes=[kht])
                    for rep in range(3):
                        S.op("dve", lambda e, ht=ht: e.tensor_scalar(out=fm, in0=ht, scalar1=math.pi, scalar2=-2.0 * math.pi, op0=ALU.is_gt, op1=ALU.mult), reads=[kht], writes=[k_fm])
                        S.op("dve", lambda e, ht=ht: e.tensor_tensor(out=ht, in0=ht, in1=fm, op=ALU.add), reads=[kht, k_fm], writes=[kht])
                        S.op("dve", lambda e, ht=ht: e.tensor_scalar(out=fm, in0=ht, scalar1=-math.pi, scalar2=2.0 * math.pi, op0=ALU.is_lt, op1=ALU.mult), reads=[kht], writes=[k_fm])
                        S.op("dve", lambda e, ht=ht: e.tensor_tensor(out=ht, in0=ht, in1=fm, op=ALU.add), reads=[kht, k_fm], writes=[kht])
                    S.op("act", lambda e, ht=ht: e.activation(out=ht, in_=ht, func=AF.Sin), reads=[kht], writes=[kht])
                    prev, kprev = ht, kht
                for rt in range(4):
                    jt = jb * 4 + rt
                    fcol = 0 if jt < NOWN // 128 else D
                    for cc in range(4):
                        ps, kp = ps_next()
                        S.op("pe", lambda e, ps=ps, prev=prev, rt=rt, cc=cc, fcol=fcol: e.matmul(
                            ps[:, :], prev[:, rt * 128:(rt + 1) * 128], fout[:, fcol + cc * 512:fcol + (cc + 1) * 512], start=True, stop=True),
                            reads=[kprev, k_fout], writes=[kp])
                        dec, kdec = decr.next()
                        S.op("act", lambda e, dec=dec, cc=cc, jt=jt: e.activation(out=dec, in_=dl[:, cc * 512:(cc + 1) * 512], func=AF.Exp, scale=tau[:, jt:jt + 1]),
                             reads=[k_dl, k_tau], writes=[kdec])
                        hc, khc = hcr.next()
                        S.op("dve", lambda e, hc=hc, ps=ps, dec=dec: e.tensor_tensor(out=hc, in0=ps[:, :], in1=dec, op=ALU.mult), reads=[kp, kdec], writes=[khc])
                        S.dma("poolq", Hc[jt * 128:(jt + 1) * 128, cc * 512:(cc + 1) * 512], hc, reads=[khc], writes=["Hc"])

            def stage1(src, nm_src, is_filter):
                S.barrier(); A.reset(PERSIST)
                W1, k_W1 = A.alloc([128, 48, 2, 128], BF16, "W1")
                S.dma("poolq", W1.rearrange("p a b c -> p (a b c)"), W1_d, writes=[k_W1])
                ztr = Ring(A, 2, [128, 48, CG], BF16, "zt")
                aor = Ring(A, 4, [128, 2, CG], BF16, "ao")
                K1 = 128 if is_filter else 86
                if not is_filter:
                    for (zt, kz) in ztr.items:
                        S.op("pool", lambda e, zt=zt: e.memset(zt[64:128], 0.0), writes=[kz])
                for cg in range(NG):
                    c0 = cg * CG
                    zt, kz = ztr.next()
                    if is_filter:
                        S.dma("sp", zt, src[:, c0:c0 + CG].rearrange("(a b) c -> a b c", b=48), reads=[nm_src], writes=[kz])
                    else:
                        S.dma("sp", zt[0:85], src[0:4080, c0:c0 + CG].rearrange("(a b) c -> a b c", b=48), reads=[nm_src], writes=[kz])
                        S.dma("sp", zt[85:86, 0:16, :], src[4080:4096, c0:c0 + CG].rearrange("(a b) c -> a b c", a=1), reads=[nm_src], writes=[kz])
                    for t2 in range(48):
                        pp, kpp = ps_multi(2)

                        def mm(e, pp=pp, zt=zt, t2=t2):
                            e.matmul(pp[:, 0:512], W1[0:K1, t2, 0, :], zt[0:K1, t2, :], start=True, stop=True)
                            return e.matmul(pp[:, 512:1024], W1[0:K1, t2, 1, :], zt[0:K1, t2, :], start=True, stop=True)
                        S.op("pe", mm, reads=[k_W1, kz], writes=kpp)
                        ao, kao = aor.next()
                        eng = "act" if t2 % 2 == 0 else "dve"
                        if eng == "act":
                            S.op("act", lambda e, ao=ao, pp=pp: e.activation(out=ao, in_=pp.rearrange("p (r c) -> p r c", r=2), func=AF.Copy), reads=kpp, writes=[kao])
                        else:
                            S.op("dve", lambda e, ao=ao, pp=pp: e.tensor_copy(out=ao, in_=pp.rearrange("p (r c) -> p r c", r=2)), reads=kpp, writes=[kao])
                        S.dma("poolq", Ad[:, :, c0:c0 + CG].rearrange("f (r t) c -> f r t c", r=2)[:, :, t2, :], ao, reads=[kao], writes=["Ad"])

            def stage2_filter():
                S.barrier(); A.reset(PERSIST)
                cs = load_consts([("W2a", W2a_d, [96, 96]), ("W2b", W2b_d, [96, 96])])
                atr = Ring(A, 2, [96, 16, CG], BF16, "at")
                kor = Ring(A, 2, [96, 2, 16, CG], BF16, "ko")
                for cg in range(NG):
                    c0 = cg * CG
                    for fb in range(8):
                        at, kat = atr.next()
                        S.dma("sp", at, Ad[fb * 16:(fb + 1) * 16, :, c0:c0 + CG].rearrange("f rt c -> rt f c"), reads=["Ad"], writes=[kat])
                        ko, kko = kor.next()
                        for fi in range(16):
                            pp, kpp = ps_multi(2)

                            def mm(e, pp=pp, at=at, fi=fi):
                                e.matmul(pp[0:96, 0:512], cs["W2a"][0], at[:, fi, :], start=True, stop=True)
                                return e.matmul(pp[0:96, 512:1024], cs["W2b"][0], at[:, fi, :], start=True, stop=True)
                            S.op("pe", mm, reads=[cs["W2a"][1], cs["W2b"][1], kat], writes=kpp)
                            eng = "act" if fi % 2 == 0 else "dve"
                            if eng == "act":
                                S.op("act", lambda e, ko=ko, pp=pp, fi=fi: e.activation(out=ko[:, :, fi, :], in_=pp[0:96].rearrange("p (r c) -> p r c", r=2), func=AF.Copy), reads=kpp, writes=[kko])
                            else:
                                S.op("dve", lambda e, ko=ko, pp=pp, fi=fi: e.tensor_copy(out=ko[:, :, fi, :], in_=pp[0:96].rearrange("p (r c) -> p r c", r=2)), reads=kpp, writes=[kko])
                        for ab in range(2):
                            S.dma("poolq", Kd[ab, :, fb * 16:(fb + 1) * 16, c0:c0 + CG], ko[:, ab, :, :], reads=[kko], writes=["Kd"])

            def stage2_data():
                S.barrier(); A.reset(PERSIST)
                cs = load_consts([("W2", W2_d, [96, 96]), ("L1", L1_d, [96, 96]), ("L2", L2_d, [96, 96])])
                atr = Ring(A, 2, [96, 16, CG], BF16, "at")
                kar = Ring(A, 2, [96, 16, CG], BF16, "ka"); kbr = Ring(A, 2, [96, 16, CG], BF16, "kb")
                btr = Ring(A, 2, [96, 16, CG], BF16, "bt")
                p1r = Ring(A, 3, [96, CG], BF16, "p1"); p2r = Ring(A, 3, [96, CG], BF16, "p2")
                for cg in range(NG):
                    c0 = cg * CG
                    for fb in range(8):
                        at, kat = atr.next(); ka, kka = kar.next(); kb, kkb = kbr.next(); bt, kbt = btr.next()
                        S.dma("sp", at, Ad[fb * 16:(fb + 1) * 16, :, c0:c0 + CG].rearrange("f rt c -> rt f c"), reads=["Ad"], writes=[kat])
                        S.dma("sp", ka, Kd[0, :, fb * 16:(fb + 1) * 16, c0:c0 + CG], reads=["Kd"], writes=[kka])
                        S.dma("sp", kb, Kd[1, :, fb * 16:(fb + 1) * 16, c0:c0 + CG], reads=["Kd"], writes=[kkb])
                        pend = None
                        for fi in range(16):
                            ps, kp = ps_next()
                            S.op("pe", lambda e, ps=ps, at=at, fi=fi: e.matmul(ps[0:96, :], cs["W2"][0], at[:, fi, :], start=True, stop=True), reads=[cs["W2"][1], kat], writes=[kp])
                            p1, kp1 = p1r.next(); p2, kp2 = p2r.next()
                            S.op("dve", lambda e, ps=ps, p1=p1, ka=ka, fi=fi: e.tensor_tensor(out=p1, in0=ps[0:96, :], in1=ka[:, fi, :], op=ALU.mult), reads=[kp, kka], writes=[kp1])
                            S.op("dve", lambda e, ps=ps, p2=p2, kb=kb, fi=fi: e.tensor_tensor(out=p2, in0=ps[0:96, :], in1=kb[:, fi, :], op=ALU.mult), reads=[kp, kkb], writes=[kp2])

                            def part2(p1=p1, kp1=kp1, p2=p2, kp2=kp2, fi=fi, bt=bt, kbt=kbt):
                                ps2, kps2 = ps_next()

                                def mm(e):
                                    e.matmul(ps2[0:96, :], cs["L1"][0], p1, start=True, stop=False)
                                    return e.matmul(ps2[0:96, :], cs["L2"][0], p2, start=False, stop=True)
                                S.op("pe", mm, reads=[cs["L1"][1], cs["L2"][1], kp1, kp2], writes=[kps2])
                                S.op("act", lambda e: e.activation(out=bt[:, fi, :], in_=ps2[0:96, :], func=AF.Copy), reads=[kps2], writes=[kbt])
                            if pend is not None:
                                pend()
                            pend = part2
                        pend()
                        S.dma("poolq", Bd[fb * 16:(fb + 1) * 16, :, c0:c0 + CG].rearrange("f rt c -> rt f c"), bt, reads=[kbt], writes=["Bd"])

            def stage1_inv():
                S.barrier(); A.reset(PERSIST)
                Vt, k_V = A.alloc([128, 48, 2, 64], BF16, "V")
                S.dma("poolq", Vt.rearrange("p a b c -> p (a b c)"), V_d, writes=[k_V])
                b2r = Ring(A, 2, [128, 2, 8, CG], BF16, "b2")
                yor = Ring(A, 2, [64, 8, CG], BF16, "yo")
                Ydv = Yd[0:43 * 48, :].rearrange("(a b) c -> a b c", b=48)
                for cg in range(NG):
                    c0 = cg * CG
                    for tb in range(6):
                        b2, kb2 = b2r.next(); yo, kyo = yor.next()
                        for ri in range(2):
                            S.dma("sp", b2[:, ri, :, :], Bd[:, ri * 48 + tb * 8:ri * 48 + tb * 8 + 8, c0:c0 + CG], reads=["Bd"], writes=[kb2])
                        for ti in range(8):
                            t2 = tb * 8 + ti
                            ps, kp = ps_next()

                            def mm(e, ps=ps, b2=b2, t2=t2, ti=ti):
                                e.matmul(ps[0:43, :], Vt[:, t2, 0, 0:43], b2[:, 0, ti, :], start=True, stop=False)
                                return e.matmul(ps[0:43, :], Vt[:, t2, 1, 0:43], b2[:, 1, ti, :], start=False, stop=True)
                            S.op("pe", mm, reads=[k_V, kb2], writes=[kp])
                            if ti % 2 == 0:
                                S.op("act", lambda e, ps=ps, yo=yo, ti=ti: e.activation(out=yo[0:43, ti, :], in_=ps[0:43, :], func=AF.Copy), reads=[kp], writes=[kyo])
                            else:
                                S.op("dve", lambda e, ps=ps, yo=yo, ti=ti: e.tensor_copy(out=yo[0:43, ti, :], in_=ps[0:43, :]), reads=[kp], writes=[kyo])
                        S.dma("poolq", Ydv[:, tb * 8:(tb + 1) * 8, c0:c0 + CG], yo[0:43], reads=[kyo], writes=["Yd"])

            phase_e0 = None
            stage1(Hc, "Hc", True)
            stage2_filter()
            stage1(Zt, "Zt", False)
            stage2_data()
            stage1_inv()

        if "E" in phases:
            phase_e()

        def phase_fgh():
            S.barrier(); A.reset(PERSIST)
            R1, k_R1 = A.alloc([128, 24576], BF16, "R1")
            GTt = R1[:, 0:16384].rearrange("p (a b) -> p a b", a=32)
            ZGt = R1[:, 16384:24576].rearrange("p (a b) -> p a b", a=16)
            actT = R1[:, 0:22528].rearrange("p (a b) -> p a b", a=44)
            hy, k_hy = A.alloc([128, 16, 512], BF16, "hy")
            gn, k_gn = A.alloc([128, 16, 512], BF16, "gn")
            mixed, k_mx = A.alloc([128, 16, 512], BF16, "mixed")
            x1T, k_x1 = A.alloc([128, 16, 512], F32, "x1T")
            rbc, k_rbc = A.alloc([128, 512], F32, "rbc")
            wr = Ring(A, 3, [128, 16, 128], BF16, "w16")
            wdr = Ring(A, 2, [128, 44, 128], BF16, "w44")
            tfr = Ring(A, 2, [128, 512], F32, "tf")
            tbr = Ring(A, 3, [128, 512], BF16, "tb")
            xtr = Ring(A, 3, [128, D], F32, "xt")
            onr = Ring(A, 1, [128, D], BF16, "on")
            ytr = Ring(A, 2, [128, 4, 128], BF16, "yt")
            st, k_st = A.alloc([128, 64], F32, "st")
            hf, k_hf = hy, k_hy
            sqT, k_sqT = gn, k_gn
            ga_a = mod[:, 32:48, 0]; sh_f = mod[:, 48:64, 0]; ga_f = mod[:, 80:96, 0]
            whv, wgv, wov, wuv, wdv = w_hy_out_d, w_gdn_out_d, w_o_d, w_up_d, w_down_d

            def proj16(wv, c0, rhs, krhs, nkc=16, ring=None):
                wt, kw = (ring or wr).next()
                S.dma("poolq", wt.rearrange("p a b -> p (a b)"), wv[c0 // 128], writes=[kw])
                ps, kp = ps_next()

                def mm(e):
                    last = None
                    for kc in range(nkc):
                        last = e.matmul(ps[:, :], wt[:, kc, :], rhs[:, kc, :], start=(kc == 0), stop=(kc == nkc - 1))
                    return last
                S.op("pe", mm, reads=[kw, krhs], writes=[kp])
                return ps, kp

            def rms_bc(srcT, ksrc):
                S.op("act", lambda e: e.activation(out=sqT, in_=srcT, func=AF.Square), reads=[ksrc], writes=[k_sqT])
                ps, kp = ps_next()

                def mm(e):
                    last = None
                    for fc in range(16):
                        last = e.matmul(ps[:, :], ones_b, sqT[:, fc, :], start=(fc == 0), stop=(fc == 15))
                    return last
                S.op("pe", mm, reads=[k_sqT, k_ones], writes=[kp])
                S.op("dve", lambda e: e.tensor_scalar(out=rbc, in0=ps[:, :], scalar1=1.0 / D, scalar2=EPS, op0=ALU.mult, op1=ALU.add), reads=[kp], writes=[k_rbc])
                S.op("dve", lambda e: e.reciprocal(out=rbc, in_=rbc), reads=[k_rbc], writes=[k_rbc])
                S.op("act", lambda e: e.activation(out=rbc, in_=rbc, func=AF.Sqrt), reads=[k_rbc], writes=[k_rbc])

            for tb in range(NOWN // 512):
                t0 = tb * 512
                S.dma("sp", GTt, GT[:, t0:t0 + 512].rearrange("(a p) t -> p a t", p=128), writes=[k_R1])
                S.dma("sp", ZGt, ZGT[:, t0:t0 + 512].rearrange("(a p) t -> p a t", p=128), writes=[k_R1])
                for cc in range(16):
                    x0t, kx0 = tbr.next(); zt, kz = tbr.next(); yt, kyt = ytr.next()
                    S.dma("sp", x0t, X0T[cc * 128:(cc + 1) * 128, t0:t0 + 512], reads=["X0T"], writes=[kx0])
                    S.dma("sp", zt, ZT[cc * 128:(cc + 1) * 128, t0:t0 + 512], reads=["ZT"], writes=[kz])
                    S.dma("sp", yt, Yd[t0:t0 + 512, cc * 128:(cc + 1) * 128].rearrange("(a p) c -> p a c", p=128), reads=["Yd"], writes=[kyt])
                    ps, kp = ps_next(); psv = ps.bitcast(BF16)

                    def trY(e, yt=yt, psv=psv):
                        last = None
                        for a in range(4):
                            last = e.transpose(psv[:, a * 128:(a + 1) * 128], yt[:, a, :], ident_b)
                        return last
                    S.op("pe", trY, reads=[kyt, k_ident], writes=[kp])
                    tf, ktf = tfr.next()
                    S.op("dve", lambda e, tf=tf, zt=zt, psv=psv, cc=cc: e.scalar_tensor_tensor(
                        out=tf, in0=zt, scalar=hybT[:, cc:cc + 1], in1=psv[:, 0:512], op0=ALU.mult, op1=ALU.add), reads=[kz, kp, k_hyb], writes=[ktf])
                    S.op("dve", lambda e, tf=tf, x0t=x0t, cc=cc: e.tensor_tensor(out=hy[:, cc, :], in0=tf, in1=x0t, op=ALU.mult), reads=[ktf, kx0], writes=[k_hy])
                for tt in range(4):
                    ot, kot = xtr.next(); on, kon = onr.next()
                    tq, ktq = xtr.next()
                    S.dma("sp", ot, Od0[t0 + tt * 128:t0 + (tt + 1) * 128, :], reads=["Od0_0", "Od0_1"], writes=[kot])
                    S.dma("sp", tq, Od1[t0 + tt * 128:t0 + (tt + 1) * 128, :], reads=["Od1_0", "Od1_1"], writes=[ktq])
                    S.op("dve", lambda e, tq=tq, ot=ot: e.tensor_tensor(out=ot, in0=ot, in1=tq, op=ALU.add), reads=[kot, ktq], writes=[kot])
                    S.op("act", lambda e, tq=tq, ot=ot: e.activation(out=tq, in_=ot, func=AF.Square), reads=[kot], writes=[ktq])
                    S.op("dve", lambda e, tq=tq: e.reduce_sum(out=st[:, 0:16], in_=tq.rearrange("p (h e) -> p h e", h=16), axis=AX.X), reads=[ktq], writes=[k_st])
                    S.op("dve", lambda e: e.tensor_scalar(out=st[:, 16:32], in0=st[:, 0:16], scalar1=1.0 / 128, scalar2=EPS, op0=ALU.mult, op1=ALU.add), reads=[k_st], writes=[k_st])
                    S.op("dve", lambda e: e.reciprocal(out=st[:, 32:48], in_=st[:, 16:32]), reads=[k_st], writes=[k_st])
                    S.op("act", lambda e: e.activation(out=st[:, 48:64], in_=st[:, 32:48], func=AF.Sqrt), reads=[k_st], writes=[k_st])
                    for h in range(16):
                        eng = "act" if h % 2 == 0 else "dve"
                        if eng == "act":
                            S.op("act", lambda e, on=on, ot=ot, h=h: e.activation(out=on[:, h * 128:(h + 1) * 128], in_=ot[:, h * 128:(h + 1) * 128],
                                                                            func=AF.Copy, scale=st[:, 48 + h:49 + h]), reads=[kot, k_st], writes=[kon])
                        else:
                            S.op("dve", lambda e, on=on, ot=ot, h=h: e.tensor_scalar(out=on[:, h * 128:(h + 1) * 128], in0=ot[:, h * 128:(h + 1) * 128],
                                                                               scalar1=st[:, 48 + h:49 + h], scalar2=None, op0=ALU.mult), reads=[kot, k_st], writes=[kon])
                    for g in range(4):
                        ps, kp = ps_next(); psv = ps.bitcast(BF16)

                        def trO(e, on=on, psv=psv, g=g):
                            last = None
                            for j in range(4):
                                h = g * 4 + j
                                last = e.transpose(psv[:, j * 128:(j + 1) * 128], on[:, h * 128:(h + 1) * 128], ident_b)
                            return last
                        S.op("pe", trO, reads=[kon, k_ident], writes=[kp])
                        for j in range(4):
                            h = g * 4 + j
                            S.op("dve", lambda e, psv=psv, j=j, h=h, tt=tt: e.scalar_tensor_tensor(
                                out=gn[:, h, tt * 128:(tt + 1) * 128], in0=psv[:, j * 128:(j + 1) * 128], scalar=gnormT[:, 0:1],
                                in1=ZGt[:, h, tt * 128:(tt + 1) * 128], op0=ALU.mult, op1=ALU.mult), reads=[kp, k_gnorm, k_R1], writes=[k_gn])
                for m in range(16):
                    ps, kp = proj16(whv, m * 128, hy, k_hy)
                    tf, ktf = tfr.next()
                    S.op("dve", lambda e, tf=tf, ps=ps, m=m: e.tensor_tensor(out=tf, in0=ps[:, :], in1=GTt[:, m, :], op=ALU.mult), reads=[kp, k_R1], writes=[ktf])
                    ps2, kp2 = proj16(wgv, m * 128, gn, k_gn)
                    tf2, ktf2 = tfr.next()
                    S.op("dve", lambda e, tf2=tf2, ps2=ps2, m=m: e.tensor_tensor(out=tf2, in0=ps2[:, :], in1=GTt[:, 16 + m, :], op=ALU.mult), reads=[kp2, k_R1], writes=[ktf2])
                    S.op("dve", lambda e, tf=tf, tf2=tf2, m=m: e.tensor_tensor(out=mixed[:, m, :], in0=tf, in1=tf2, op=ALU.add), reads=[ktf, ktf2], writes=[k_mx])
                for tt in range(4):
                    xt, kx = xtr.next()
                    S.dma("sp", xt, x_d[t0 + tt * 128:t0 + (tt + 1) * 128, :], writes=[kx])
                    for g in range(4):
                        ps, kp = ps_next()

                        def trX(e, xt=xt, ps=ps, g=g):
                            last = None
                            for j in range(4):
                                fc = g * 4 + j
                                last = e.transpose(ps[:, j * 128:(j + 1) * 128], xt[:, fc * 128:(fc + 1) * 128], ident_f)
                            return last
                        S.op("pe", trX, reads=[kx, k_identf], writes=[kp])
                        S.op("act", lambda e, ps=ps, g=g, tt=tt: e.activation(
                            out=x1T[:, g * 4:(g + 1) * 4, tt * 128:(tt + 1) * 128], in_=ps[:, :].rearrange("p (a b) -> p a b", a=4), func=AF.Copy),
                            reads=[kp], writes=[k_x1])
                for m in range(16):
                    ps, kp = proj16(wov, m * 128, mixed, k_mx)
                    S.op("dve", lambda e, ps=ps, m=m: e.scalar_tensor_tensor(out=x1T[:, m, :], in0=ps[:, :], scalar=ga_a[:, m:m + 1], in1=x1T[:, m, :],
                                                                           op0=ALU.mult, op1=ALU.add), reads=[kp, k_mod, k_x1], writes=[k_x1])
                rms_bc(x1T, k_x1)
                for fc in range(16):
                    tf, ktf = tfr.next()
                    S.op("dve", lambda e, tf=tf, fc=fc: e.scalar_tensor_tensor(out=tf, in0=x1T[:, fc, :], scalar=scale_f[:, fc:fc + 1], in1=rbc,
                                                                             op0=ALU.mult, op1=ALU.mult), reads=[k_x1, k_scf, k_rbc], writes=[ktf])
                    S.op("act", lambda e, tf=tf, fc=fc: e.activation(out=hf[:, fc, :], in_=tf, func=AF.Identity, bias=sh_f[:, fc:fc + 1]),
                         reads=[ktf, k_mod], writes=[k_hf])
                for j in range(44):
                    psg, kpg = proj16(wuv, j * 128, hf, k_hf)
                    psu, kpu = proj16(wuv, D_FF + j * 128, hf, k_hf)
                    tf, ktf = tfr.next()
                    S.op("act", lambda e, tf=tf, psg=psg: e.activation(out=tf, in_=psg[:, :], func=AF.Silu), reads=[kpg], writes=[ktf])
                    S.op("dve", lambda e, tf=tf, psu=psu, j=j: e.tensor_tensor(out=actT[:, j, :], in0=tf, in1=psu[:, :], op=ALU.mult), reads=[ktf, kpu], writes=[k_R1])
                for m in range(16):
                    ps, kp = proj16(wdv, m * 128, actT, k_R1, nkc=44, ring=wdr)
                    S.op("dve", lambda e, ps=ps, m=m: e.scalar_tensor_tensor(out=x1T[:, m, :], in0=ps[:, :], scalar=ga_f[:, m:m + 1], in1=x1T[:, m, :],
                                                                           op0=ALU.mult, op1=ALU.add), reads=[kp, k_mod, k_x1], writes=[k_x1])
                rms_bc(x1T, k_x1)
                for fc in range(16):
                    S.op("dve", lambda e, fc=fc: e.scalar_tensor_tensor(out=x1T[:, fc, :], in0=x1T[:, fc, :], scalar=nfinT[:, fc:fc + 1], in1=rbc,
                                                                      op0=ALU.mult, op1=ALU.mult), reads=[k_x1, k_nfin, k_rbc], writes=[k_x1])
                for tt in range(4):
                    xo, kxo = xtr.next()
                    for g in range(4):
                        ps, kp = ps_next()

                        def trB(e, ps=ps, g=g, tt=tt):
                            last = None
                            for j in range(4):
                                fc = g * 4 + j
                                last = e.transpose(ps[:, j * 128:(j + 1) * 128], x1T[:, fc, tt * 128:(tt + 1) * 128], ident_f)
                            return last
                        S.op("pe", trB, reads=[k_x1, k_identf], writes=[kp])
                        S.op("act", lambda e, ps=ps, xo=xo, g=g: e.activation(out=xo[:, g * 512:(g + 1) * 512], in_=ps[:, :], func=AF.Copy), reads=[kp], writes=[kxo])
                    S.final_tokens.append(S.dma("sp", out_d[t0 + tt * 128:t0 + (tt + 1) * 128, :], xo, reads=[kxo], writes=["out"]))

        if "F" in phases:
            phase_fgh()
        S.emit()
    return nc


def _fm(v, n):
    return np.ascontiguousarray(np.asarray(v, np.float32).reshape(n, 128).T)


def _hyena_consts():
    n = L
    j = np.arange(N_)
    idx = np.where(j < NOWN, j, N_ - j).astype(np.float64)
    idx[NOWN] = 0
    tt = idx / (n - 1)
    bands = 16
    w = 2.0 * np.pi * idx / n
    f = np.linspace(1e-4, bands - 1, bands)
    zf = np.concatenate([tt[None, :], np.cos(f[:, None] * w[None, :]), -np.sin(f[:, None] * w[None, :])], axis=0)
    tau = -tt.copy(); tau[NOWN] = -30.0
    deltas = np.abs(np.linspace(math.log(1e-2) / 1.5, math.log(1e-2) / 0.3, D))
    t1 = np.arange(128)[:, None, None, None]; t2 = np.arange(48)[None, :, None, None]; f1 = np.arange(128)[None, None, None, :]
    th = 2 * np.pi * (t1 * f1 / 128.0 + t2 * f1 / float(N_))
    W1 = np.concatenate([np.cos(th), -np.sin(th)], axis=2)
    a = np.arange(48)
    th2 = 2 * np.pi * np.outer(a, a) / 48.0
    c2, s2 = np.cos(th2), np.sin(th2)
    W2 = np.block([[c2, -s2], [s2, c2]])
    W2a = np.block([[c2, c2], [s2, s2]])
    W2b = np.block([[-s2, -s2], [c2, c2]])
    L1 = np.block([[c2, s2], [-s2, c2]])
    L2 = np.block([[-s2, c2], [-c2, -s2]])
    f1v = np.arange(128)[:, None, None, None]; t2v = np.arange(48)[None, :, None, None]; t1v = np.arange(64)[None, None, None, :]
    ph = 2 * np.pi * f1v * (t2v / float(N_) + t1v / 128.0)
    V = np.concatenate([np.cos(ph), -np.sin(ph)], axis=2) / float(N_)
    f32 = lambda x: np.ascontiguousarray(x, dtype=np.float32)
    return dict(zf=f32(zf), tau=f32(tau.reshape(48, 128).T), deltas=f32(np.broadcast_to(deltas[None, :], (128, D))),
                W1=f32(W1.reshape(128, -1)), Vc=f32(V.reshape(128, -1)), W2=f32(W2), W2a=f32(W2a), W2b=f32(W2b), L1=f32(L1), L2=f32(L2))


def make_in_maps(inp):
    maps = []
    ident = np.eye(128, dtype=np.float32)
    tri = np.tril(np.ones((64, 64), np.float32))
    cmask = np.ascontiguousarray(np.stack([tri, np.tril(tri, -1), tri.T, np.triu(tri.T, 1)], axis=1))
    ii = np.arange(64)
    def same(b): return (ii[:, None] // b == ii[None, :] // b).astype(np.float32)
    cm2 = np.ascontiguousarray(np.stack([same(8), same(16) - same(8), same(32) - same(16), same(64) - same(32), -same(8)], axis=1))
    hyc_consts = _hyena_consts()
    w_in0 = inp["w_in"][0]
    cols = [SEG[t] + j * 128 for (t, j) in PLAN]
    w_inc_base = _chunk_major(w_in0, cols)
    shared = dict(
        w_hy_outc=_chunk_major(inp["w_hy_out"][0], [m * 128 for m in range(16)]),
        w_gdn_outc=_chunk_major(inp["w_gdn_out"][0], [m * 128 for m in range(16)]),
        w_oc=_chunk_major(inp["w_o"][0], [m * 128 for m in range(16)]),
        w_upc=_chunk_major(inp["w_up"][0], [j * 128 for j in range(88)]),
        w_downc=_chunk_major(inp["w_down"][0], [m * 128 for m in range(16)]),
    )
    w_inc_flip = None
    cache = {}

    def _w_inc_for(flip):
        if flip not in cache:
            if not flip:
                cache[flip] = w_inc_base
            else:
                wsc = w_in0[:, 14336:14400].reshape(D, 2, 2, 16)[:, :, ::-1, :].reshape(D, 64)
                arr = w_inc_base.copy()
                arr[PLAN_INDEX[("scal", 0)]] = _chunk_major(wsc, [0])[0]
                cache[flip] = arr
        return cache[flip]

    for core in range(8):
        b, half = core // 2, core % 2
        flip = half == 1
        x = inp["x"][b]; ctx = inp["ctx"][b]
        if flip:
            x = x[::-1]; ctx = ctx[::-1]
        cT = np.stack([_fm(inp["c"][b], 16), _fm(inp["c_ctx"], 16)], axis=-1)
        w_in = inp["w_in"][0]
        w_scal = w_in[:, 14336:14400]
        a_log = inp["gdn_a_log"][0]; dtb = inp["gdn_dt_bias"][0]
        hyc = inp["hy_conv"][0]; gdc = inp["gdn_conv"][0]
        if flip:
            w_scal = w_scal.reshape(D, 2, 2, 16)[:, :, ::-1, :].reshape(D, 64)
            a_log = a_log[::-1]; dtb = dtb[::-1]
            hyc = hyc[::-1]; gdc = gdc[::-1]
        scalp = np.zeros((64, 2), np.float32)
        scalp[32:64, 0] = a_log.reshape(32); scalp[32:64, 1] = dtb.reshape(32)
        hyconvT = np.ascontiguousarray(hyc.reshape(3, 48, 128).transpose(2, 1, 0))
        gdnconvT = np.ascontiguousarray(gdc.reshape(3, 48, 128).transpose(2, 1, 0))
        maps.append(dict(
            x=np.ascontiguousarray(x), ctx=np.ascontiguousarray(ctx), cT=np.ascontiguousarray(cT),
            w_ada=inp["w_ada"][0], b_adaT=_fm(inp["b_ada"][0], 96),
            nmixT=_fm(inp["norm_mix"][0], 16), nffnT=_fm(inp["norm_ffn"][0], 16),
            w_inc=_w_inc_for(flip),
            hyconvT=hyconvT, gdnconvT=gdnconvT, scalp=scalp, ident=ident, cmask=cmask, cm2=cm2,
            fw1=inp["hy_fw1"][0], fw2=inp["hy_fw2"][0], fw3=inp["hy_fw3"][0],
            fsm=np.ascontiguousarray(np.stack([inp["hy_fb1"][0], inp["hy_fb2"][0], inp["hy_fb3"][0], inp["hy_freq"][0]], axis=1)),
            fout=(np.ascontiguousarray(np.concatenate([inp["hy_fout"][0][:, D:], inp["hy_fout"][0][:, :D]], axis=1)) if flip else inp["hy_fout"][0]),
            **hyc_consts,
            **shared,
            hybT=_fm(inp["hy_bias"][0], 16), gnormT=_fm(inp["gdn_norm"][0], 1), nfinT=_fm(inp["norm_final"], 16),
        ))
    return maps


def kernel(**inputs):
    inp = {k: np.asarray(v) for k, v in inputs.items()}
    nc = build()
    maps = make_in_maps(inp)
    res = run_bass_kernel_spmd(nc, maps, core_ids=list(range(8)))
    out = np.empty((4, L, D), np.float32)
    for core in range(8):
        b, half = core // 2, core % 2
        o = res.results[core]["out"]
        if half == 0:
            out[b, :NOWN] = o
        else:
            out[b, NOWN:] = o[::-1]
    return out
```
